# Optimizing a Trainium2 kernel written in Bass

```python
import jax, jax.numpy as jnp
from jax import lax
import numpy as np

D_MODEL = 1024
BATCH = 8
SEQ = 2048
DEPTH = 2
DEC_BATCH = 32
DEC_SEQ = 8
PAST_LEN = 16384
PAGE_SIZE = 128

EPS = 1e-6
D_FF = 2816
HG_HEADS = 4
HG_DK = 64
HG_DV = 64
HG_KW = HG_HEADS * HG_DK
HG_WIDTH = HG_HEADS * HG_DV
HG_CHUNK = 16
HG_MIN_F = 1e-30
RW_HEADS = 4
RW_DH = 64
RW_WIDTH = RW_HEADS * RW_DH
RW_DECAY_LORA = 32
RW_AAA_LORA = 32
RW_GATE_LORA = 64
RW_GN_EPS = 64e-5
MLA_HEADS = 8
MLA_NOPE = 64
MLA_ROPE = 32
MLA_VDIM = 64
MLA_Q_RANK = 384
MLA_KV_RANK = 256
MLA_WIDTH = MLA_HEADS * MLA_VDIM
MLA_SCALE = (MLA_NOPE + MLA_ROPE) ** -0.5
ROPE_THETA = 10000.0
Q_BLOCK = 128
MIX_WIDTH = HG_WIDTH + RW_WIDTH + MLA_WIDTH
HG_COLS = 2 * HG_KW + 2 * HG_WIDTH
RW_COLS = 3 * RW_WIDTH + RW_DECAY_LORA + RW_AAA_LORA + RW_GATE_LORA
MLA_COLS = MLA_Q_RANK + MLA_KV_RANK + MLA_ROPE
IN_COLS = HG_COLS + RW_COLS + MLA_COLS
RW_SPLITS = [RW_WIDTH, 2 * RW_WIDTH, 3 * RW_WIDTH, 3 * RW_WIDTH + RW_DECAY_LORA, 3 * RW_WIDTH + RW_DECAY_LORA + RW_AAA_LORA]

kernel_name = 'hymba_hgrn2_rwkv7_mla_macaron_step'


def rms_norm(x, g):
    xf = x.astype(jnp.float32)
    y = xf * lax.rsqrt(jnp.mean(xf * xf, axis=-1, keepdims=True) + EPS)
    return (y * g.astype(jnp.float32)).astype(x.dtype)


def swiglu(x, wi, wo):
    a, b = jnp.split(x @ wi, 2, axis=-1)
    return (jax.nn.silu(a) * b) @ wo


def rope(x, pos):
    half = x.shape[-1] // 2
    inv = jnp.exp(-jnp.log(ROPE_THETA) * jnp.arange(half, dtype=jnp.float32) / half)
    ang = pos[:, None] * inv[None, :]
    cos = jnp.cos(ang)[:, None, :]
    sin = jnp.sin(ang)[:, None, :]
    xf = x.astype(jnp.float32)
    x1, x2 = xf[..., :half], xf[..., half:]
    return jnp.concatenate([x1 * cos - x2 * sin, x2 * cos + x1 * sin], axis=-1).astype(x.dtype)


def hgrn_lower_bounds(logits):
    p = jax.nn.softmax(logits.astype(jnp.float32), axis=0)
    return jnp.clip(jnp.cumsum(p, axis=0) - p[0:1], 0.0, 1.0)


def gla_chunked(q, k, v, log_f, s0):
    B, T, H, _ = q.shape
    C = min(HG_CHUNK, T)
    n = -(-T // C)
    pad = n * C - T

    def prep(a):
        a = jnp.pad(a, ((0, 0), (0, pad), (0, 0), (0, 0)))
        return a.reshape(B, n, C, H, a.shape[-1]).transpose(1, 0, 3, 2, 4)

    qc, kc, vc, gc = prep(q), prep(k), prep(v), prep(log_f)
    mask = jnp.tril(jnp.ones((C, C), dtype=bool))[:, :, None]

    def step(S, inp):
        qb, kb, vb, gb = inp
        b = jnp.cumsum(gb, axis=-2)
        o_inter = jnp.einsum('bhck,bhkv->bhcv', qb * jnp.exp(b), S)
        diff = b[:, :, :, None, :] - b[:, :, None, :, :]
        decay = jnp.where(mask, jnp.exp(jnp.where(mask, diff, 0.0)), 0.0)
        att = jnp.einsum('bhtk,bhtsk,bhsk->bhts', qb, decay, kb)
        o = o_inter + jnp.einsum('bhts,bhsv->bhtv', att, vb)
        b_last = b[:, :, -1:, :]
        S_new = jnp.exp(b_last[:, :, 0, :])[..., None] * S + jnp.einsum('bhsk,bhsv->bhkv', kb * jnp.exp(b_last - b), vb)
        return S_new, o

    S, o = lax.scan(step, s0, (qc, kc, vc, gc))
    o = o.transpose(1, 0, 3, 2, 4).reshape(B, n * C, H, -1)[:, :T]
    return o, S


def hgrn2_mixer(p, lb, norm_g, s0):
    B, T, _ = p.shape
    q, fz, i, g = jnp.split(p, [HG_KW, 2 * HG_KW, 2 * HG_KW + HG_WIDTH], axis=-1)
    fz = fz.astype(jnp.float32)
    f = lb + (1.0 - lb) * jax.nn.sigmoid(fz)
    log_f = jnp.log(jnp.maximum(f, HG_MIN_F))
    k = (1.0 - lb) * jax.nn.sigmoid(-fz)
    heads = lambda t, d: t.astype(jnp.float32).reshape(B, T, HG_HEADS, d)
    o, s = gla_chunked(heads(q, HG_DK), heads(k, HG_DK), heads(i, HG_DV), heads(log_f, HG_DK), s0.astype(jnp.float32))
    o = rms_norm(o, norm_g).reshape(B, T, HG_WIDTH)
    return o * jax.nn.silu(g.astype(jnp.float32)), s


def rwkv7_mixer(p, shift0, s0, l, W):
    f32 = jnp.float32
    B, T, _ = p.shape
    prev = jnp.concatenate([shift0[:, None, :].astype(p.dtype), p[:, :-1]], axis=1)
    ps = p + (prev - p) * W['rw_mu'][l]
    r, k, v, wd, ad, gd = jnp.split(ps, RW_SPLITS, axis=-1)
    w = -jax.nn.softplus(-(W['rw_w0'][l] + jnp.tanh(wd) @ W['rw_w2'][l])) - 0.5
    log_w = -jnp.exp(w.astype(f32))
    a = jax.nn.sigmoid((W['rw_a0'][l] + ad @ W['rw_a2'][l]).astype(f32))
    g = jax.nn.sigmoid(gd) @ W['rw_g2'][l]
    hd = lambda t: t.astype(f32).reshape(B, T, RW_HEADS, RW_DH)
    r_, k_, v_, a_, lw_ = hd(r), hd(k), hd(v), hd(a), hd(log_w)
    kk = k_ * W['rw_kk'][l].astype(f32).reshape(RW_HEADS, RW_DH)
    kk = kk / jnp.maximum(jnp.sqrt(jnp.sum(kk * kk, axis=-1, keepdims=True)), 1e-12)
    k_ = k_ * (1.0 + (a_ - 1.0) * W['rw_ka'][l].astype(f32).reshape(RW_HEADS, RW_DH))

    def step(S, inp):
        r_t, lw_t, k_t, v_t, kk_t, a_t = inp
        S = (S * jnp.exp(lw_t)[:, :, None, :]
             - jnp.einsum('bhvk,bhk->bhv', S, kk_t)[..., None] * (kk_t * a_t)[:, :, None, :]
             + v_t[..., None] * k_t[:, :, None, :])
        return S, jnp.einsum('bhvk,bhk->bhv', S, r_t)

    tm = lambda t: jnp.moveaxis(t, 1, 0)
    S, y = lax.scan(step, s0.astype(f32), (tm(r_), tm(lw_), tm(k_), tm(v_), tm(kk), tm(a_)))
    y = jnp.moveaxis(y, 0, 1)
    mu = jnp.mean(y, axis=-1, keepdims=True)
    var = jnp.mean(jnp.square(y - mu), axis=-1, keepdims=True)
    y = ((y - mu) * lax.rsqrt(var + RW_GN_EPS)).reshape(B, T, RW_WIDTH)
    y = y * W['rw_ln_w'][l].astype(f32) + W['rw_ln_b'][l].astype(f32)
    bonus = jnp.sum(r_ * k_ * W['rw_rk'][l].astype(f32), axis=-1, keepdims=True) * v_
    y = (y + bonus.reshape(B, T, RW_WIDTH)) * g.astype(f32)
    return y, S, p[:, -1]


def mla_project(p, pos, l, W):
    B, T, _ = p.shape
    qa, kva, kpe = jnp.split(p, [MLA_Q_RANK, MLA_Q_RANK + MLA_KV_RANK], axis=-1)
    q = (rms_norm(qa, W['mla_q_norm'][l]) @ W['mla_wqb'][l]).reshape(B, T, MLA_HEADS, MLA_NOPE + MLA_ROPE)
    q_nope = q[..., :MLA_NOPE]
    q_pe = rope(q[..., MLA_NOPE:], pos)
    c = rms_norm(kva, W['mla_kv_norm'][l])
    k_pe = rope(kpe[:, :, None, :], pos)[:, :, 0]
    return q_nope, q_pe, c, k_pe


def mla_prompt_attn(q_nope, q_pe, c, k_pe, wuk, wuv):
    B, T, H, _ = q_nope.shape
    k_nope = jnp.einsum('btr,rhd->bthd', c, wuk)
    v = jnp.einsum('btr,rhd->bthd', c, wuv)
    qb = min(Q_BLOCK, T)
    nb = T // qb
    kpos = jnp.arange(T)

    def block(i):
        start = i * qb
        qn = lax.dynamic_slice_in_dim(q_nope, start, qb, axis=1)
        qp = lax.dynamic_slice_in_dim(q_pe, start, qb, axis=1)
        s = jnp.einsum('bqhd,bkhd->bhqk', qn, k_nope) + jnp.einsum('bqhe,bke->bhqk', qp, k_pe)
        s = s.astype(jnp.float32) * MLA_SCALE
        qpos = start + jnp.arange(qb)
        s = jnp.where(kpos[None, :] <= qpos[:, None], s, -jnp.inf)
        pr = jax.nn.softmax(s, axis=-1).astype(v.dtype)
        return jnp.einsum('bhqk,bkhd->bqhd', pr, v)

    o = lax.map(block, jnp.arange(nb))
    return o.transpose(1, 0, 2, 3, 4).reshape(B, T, H * MLA_VDIM)


def mla_sample_attn(q_nope, q_pe, c, k_pe, ckv_pages, kpe_pages, wuk, wuv):
    B, T, H, _ = q_nope.shape
    c_past = ckv_pages.reshape(B, -1, MLA_KV_RANK).astype(c.dtype)
    kpe_past = kpe_pages.reshape(B, -1, MLA_ROPE).astype(k_pe.dtype)
    P = c_past.shape[1]
    q_lat = jnp.einsum('bqhd,rhd->bqhr', q_nope, wuk)
    s_past = jnp.einsum('bqhr,bkr->bhqk', q_lat, c_past) + jnp.einsum('bqhe,bke->bhqk', q_pe, kpe_past)
    s_new = jnp.einsum('bqhr,bkr->bhqk', q_lat, c) + jnp.einsum('bqhe,bke->bhqk', q_pe, k_pe)
    causal = jnp.tril(jnp.ones((T, T), dtype=bool))
    s_new = jnp.where(causal, s_new.astype(jnp.float32) * MLA_SCALE, -jnp.inf)
    s = jnp.concatenate([s_past.astype(jnp.float32) * MLA_SCALE, s_new], axis=-1)
    pr = jax.nn.softmax(s, axis=-1).astype(c.dtype)
    o_lat = jnp.einsum('bhqk,bkr->bqhr', pr[..., :P], c_past) + jnp.einsum('bhqk,bkr->bqhr', pr[..., P:], c)
    o = jnp.einsum('bqhr,rhd->bqhd', o_lat, wuv)
    return o.reshape(B, T, H * MLA_VDIM)


def trunk(x, pos, hg_s0, rw_s0, rw_sh0, attend, W):
    lbs = hgrn_lower_bounds(W['hg_lb_logits'])
    hg_out, rw_out, sh_out, c_out, kpe_out = [], [], [], [], []
    for l in range(DEPTH):
        x = x + 0.5 * swiglu(rms_norm(x, W['norm_ffn1'][l]), W['ffn1_wi'][l], W['ffn1_wo'][l])
        p = rms_norm(x, W['norm_mix'][l]) @ W['w_in'][l]
        p_hg, p_rw, p_mla = jnp.split(p, [HG_COLS, HG_COLS + RW_COLS], axis=-1)
        o_hg, hg_s = hgrn2_mixer(p_hg, lbs[l], W['hg_norm'][l], hg_s0[l])
        o_rw, rw_s, rw_sh = rwkv7_mixer(p_rw, rw_sh0[l], rw_s0[l], l, W)
        q_nope, q_pe, c, k_pe = mla_project(p_mla, pos, l, W)
        o_mla = attend(l, q_nope, q_pe, c, k_pe)
        o = jnp.concatenate([o_hg.astype(x.dtype), o_rw.astype(x.dtype), o_mla.astype(x.dtype)], axis=-1)
        x = x + o @ W['w_out'][l]
        x = x + 0.5 * swiglu(rms_norm(x, W['norm_ffn2'][l]), W['ffn2_wi'][l], W['ffn2_wo'][l])
        hg_out.append(hg_s)
        rw_out.append(rw_s)
        sh_out.append(rw_sh)
        c_out.append(c)
        kpe_out.append(k_pe)
    y = rms_norm(x, W['norm_final'])
    return y, jnp.stack(hg_out), jnp.stack(rw_out), jnp.stack(sh_out), jnp.stack(c_out), jnp.stack(kpe_out)


def setup_inputs(seed: int = 0) -> dict:
    key = jax.random.key(seed)
    keys = iter(jax.random.split(key, 48))
    f32 = jnp.float32
    nrm = lambda shape, s=1.0: jax.random.normal(next(keys), shape, f32) * s
    lin = lambda shape, fan_in: nrm(shape, fan_in ** -0.5)
    gain = lambda shape: 1.0 + nrm(shape, 0.02)
    n_pages = PAST_LEN // PAGE_SIZE
    n_used = DEC_BATCH * n_pages
    n_pool = n_used + n_used // 4
    perm = jax.random.permutation(next(keys), n_pool)
    page_table = perm[:n_used].reshape(DEC_BATCH, n_pages).astype(jnp.int32)
    return {
        'x_prompt': nrm((BATCH, SEQ, D_MODEL)),
        'x_sample': nrm((DEC_BATCH, DEC_SEQ, D_MODEL)),
        'cache_mla_ckv': nrm((DEPTH, n_pool, PAGE_SIZE, MLA_KV_RANK)),
        'cache_mla_kpe': nrm((DEPTH, n_pool, PAGE_SIZE, MLA_ROPE)),
        'state_hgrn': nrm((DEPTH, DEC_BATCH, HG_HEADS, HG_DK, HG_DV), 0.5),
        'state_rwkv': nrm((DEPTH, DEC_BATCH, RW_HEADS, RW_DH, RW_DH), 0.5),
        'state_rwkv_shift': nrm((DEPTH, DEC_BATCH, RW_COLS)),
        'page_table': page_table,
        'norm_ffn1': gain((DEPTH, D_MODEL)),
        'ffn1_wi': lin((DEPTH, D_MODEL, 2 * D_FF), D_MODEL),
        'ffn1_wo': lin((DEPTH, D_FF, D_MODEL), D_FF),
        'norm_mix': gain((DEPTH, D_MODEL)),
        'w_in': lin((DEPTH, D_MODEL, IN_COLS), D_MODEL),
        'w_out': lin((DEPTH, MIX_WIDTH, D_MODEL), MIX_WIDTH),
        'hg_lb_logits': nrm((DEPTH, HG_KW)),
        'hg_norm': gain((DEPTH, HG_DV)),
        'rw_mu': jax.random.uniform(next(keys), (DEPTH, RW_COLS), f32),
        'rw_w0': jax.random.uniform(next(keys), (DEPTH, RW_WIDTH), f32, minval=-6.0, maxval=-1.0),
        'rw_w2': lin((DEPTH, RW_DECAY_LORA, RW_WIDTH), RW_DECAY_LORA),
        'rw_a0': nrm((DEPTH, RW_WIDTH), 0.1),
        'rw_a2': lin((DEPTH, RW_AAA_LORA, RW_WIDTH), RW_AAA_LORA),
        'rw_g2': lin((DEPTH, RW_GATE_LORA, RW_WIDTH), RW_GATE_LORA),
        'rw_kk': 0.85 + nrm((DEPTH, RW_WIDTH), 0.02),
        'rw_ka': gain((DEPTH, RW_WIDTH)),
        'rw_rk': nrm((DEPTH, RW_HEADS, RW_DH), 0.1),
        'rw_ln_w': gain((DEPTH, RW_WIDTH)),
        'rw_ln_b': nrm((DEPTH, RW_WIDTH), 0.02),
        'mla_q_norm': gain((DEPTH, MLA_Q_RANK)),
        'mla_wqb': lin((DEPTH, MLA_Q_RANK, MLA_HEADS * (MLA_NOPE + MLA_ROPE)), MLA_Q_RANK),
        'mla_kv_norm': gain((DEPTH, MLA_KV_RANK)),
        'mla_wuk': lin((DEPTH, MLA_KV_RANK, MLA_HEADS, MLA_NOPE), MLA_KV_RANK),
        'mla_wuv': lin((DEPTH, MLA_KV_RANK, MLA_HEADS, MLA_VDIM), MLA_KV_RANK),
        'norm_ffn2': gain((DEPTH, D_MODEL)),
        'ffn2_wi': lin((DEPTH, D_MODEL, 2 * D_FF), D_MODEL),
        'ffn2_wo': lin((DEPTH, D_FF, D_MODEL), D_FF),
        'norm_final': gain((D_MODEL,)),
    }


def reference(x_prompt, x_sample, cache_mla_ckv, cache_mla_kpe, state_hgrn, state_rwkv, state_rwkv_shift, page_table,
              norm_ffn1, ffn1_wi, ffn1_wo, norm_mix, w_in, w_out, hg_lb_logits, hg_norm,
              rw_mu, rw_w0, rw_w2, rw_a0, rw_a2, rw_g2, rw_kk, rw_ka, rw_rk, rw_ln_w, rw_ln_b,
              mla_q_norm, mla_wqb, mla_kv_norm, mla_wuk, mla_wuv, norm_ffn2, ffn2_wi, ffn2_wo, norm_final):
    W = dict(norm_ffn1=norm_ffn1, ffn1_wi=ffn1_wi, ffn1_wo=ffn1_wo, norm_mix=norm_mix, w_in=w_in, w_out=w_out,
             hg_lb_logits=hg_lb_logits, hg_norm=hg_norm, rw_mu=rw_mu, rw_w0=rw_w0, rw_w2=rw_w2, rw_a0=rw_a0,
             rw_a2=rw_a2, rw_g2=rw_g2, rw_kk=rw_kk, rw_ka=rw_ka, rw_rk=rw_rk, rw_ln_w=rw_ln_w, rw_ln_b=rw_ln_b,
             mla_q_norm=mla_q_norm, mla_wqb=mla_wqb, mla_kv_norm=mla_kv_norm, mla_wuk=mla_wuk, mla_wuv=mla_wuv,
             norm_ffn2=norm_ffn2, ffn2_wi=ffn2_wi, ffn2_wo=ffn2_wo, norm_final=norm_final)
    Bp, Tp, _ = x_prompt.shape
    pos_p = jnp.arange(Tp, dtype=jnp.float32)
    hg0 = jnp.zeros((DEPTH, Bp, HG_HEADS, HG_DK, HG_DV), jnp.float32)
    rw0 = jnp.zeros((DEPTH, Bp, RW_HEADS, RW_DH, RW_DH), jnp.float32)
    sh0 = jnp.zeros((DEPTH, Bp, RW_COLS), x_prompt.dtype)
    attend_p = lambda l, qn, qp, c, kp: mla_prompt_attn(qn, qp, c, kp, mla_wuk[l], mla_wuv[l])
    y_p, hg_p, rw_p, sh_p, ckv_p, kpe_p = trunk(x_prompt, pos_p, hg0, rw0, sh0, attend_p, W)
    past_len = page_table.shape[1] * cache_mla_ckv.shape[2]
    pos_s = past_len + jnp.arange(x_sample.shape[1], dtype=jnp.float32)
    attend_s = lambda l, qn, qp, c, kp: mla_sample_attn(qn, qp, c, kp, cache_mla_ckv[l, page_table],
                                                         cache_mla_kpe[l, page_table], mla_wuk[l], mla_wuv[l])
    y_s, hg_s, rw_s, sh_s, ckv_s, kpe_s = trunk(x_sample, pos_s, state_hgrn, state_rwkv, state_rwkv_shift, attend_s, W)
    return (y_p, y_s, hg_p, hg_s, rw_p, rw_s, sh_p, sh_s, ckv_p, ckv_s, kpe_p, kpe_s)
```

```python
import contextlib
import numpy as np
import concourse.bass as bass
import concourse.mybir as mybir
from concourse.bass_utils import run_bass_kernel_spmd

F32 = mybir.dt.float32
BF16 = mybir.dt.bfloat16
I32 = mybir.dt.int32
U32 = mybir.dt.uint32
AF = mybir.ActivationFunctionType
ALU = mybir.AluOpType
AX = mybir.AxisListType

D = 1024
SEQ = 2048
DEPTH = 2
DFF = 2816
NJ = DFF // 128
KC = D // 128
EPS = 1e-6
TB = 512
NPB = SEQ // TB
NS = 32
IN_COLS = 2592
GJ = 2


WIN_COLS = 2656
C_ID, C_MP, C_MPS, C_BLK, C_SEGP, C_SEGS, C_MS, C_MSS, C_ROWS = 0, 128, 256, 384, 512, 1024, 1056, 1088, 1120
C_MPL, C_MSL, C_TRI = 1124, 1252, 1284
NCST = 1412


def vec_layout():
    L = {}
    c = [0]

    def add(name, n):
        L[name] = c[0]
        c[0] += n
    for l in range(DEPTH):
        add(f"nf1_{l}", 8)
    for l in range(DEPTH):
        add(f"nmix_{l}", 8)
    for l in range(DEPTH):
        add(f"nf2_{l}", 8)
    add("nfin", 8)
    for l in range(DEPTH):
        add(f"hglog_{l}", 2)
    for l in range(DEPTH):
        add(f"hgn_{l}", 1)
    for l in range(DEPTH):
        for nm, n in (("mu", 7), ("w0", 2), ("a0", 2), ("kk", 2), ("ka", 2), ("rk", 2), ("lnw", 2), ("lnb", 2), ("qn", 3), ("kvn", 2)):
            add(f"{nm}_{l}", n)
    L["_n"] = c[0]
    return L


VL = vec_layout()
NV = VL["_n"]


def make_consts():
    c = np.zeros((128, NCST), np.float32)
    i = np.arange(128)
    c[:, C_ID:C_ID + 128] = np.eye(128)
    same = (i[:, None] // 64) == (i[None, :] // 64)
    c[:, C_MP:C_MP + 128] = same & (i[:, None] <= i[None, :])
    c[:, C_MPS:C_MPS + 128] = same & (i[:, None] < i[None, :])
    c[:, C_BLK:C_BLK + 128] = same
    t = np.arange(512)
    c[:, C_SEGP:C_SEGP + 512] = (t % 64 != 0)[None, :]
    t = np.arange(32)
    c[:, C_SEGS:C_SEGS + 32] = (t % 8 != 0)[None, :]
    j = np.arange(32)
    same8 = (j[:, None] // 8) == (j[None, :] // 8)
    c[:32, C_MS:C_MS + 32] = same8 & (j[:, None] <= j[None, :])
    c[:32, C_MSS:C_MSS + 32] = same8 & (j[:, None] < j[None, :])
    for b in range(4):
        c[8 * b:8 * b + 8, C_ROWS + b] = 1.0
    c[:, C_MPL:C_MPL + 128] = same & (i[:, None] > i[None, :])
    c[:32, C_MSL:C_MSL + 32] = same8 & (j[:, None] > j[None, :])
    c[:, C_TRI:C_TRI + 128] = (i[:, None] <= i[None, :])
    return c


DBG = {}


class Chan:
    def __init__(self, sem):
        self.sem = sem
        self.count = 0


class Eng:
    def __init__(self, name, b, chan, selfsync):
        self.name = name
        self.b = b
        self.chan = chan
        self.seen = {}
        self.selfsync = selfsync


class TT:
    def __init__(self, t, name):
        self.t = t
        self.name = name
        self.w = None
        self.r = []
        self.dchan = None

    def __getitem__(self, idx):
        return self.t[idx]


class Sched:
    def __init__(self, nc, es):
        self.nc = nc
        self.es = es
        self.nsem = 0
        self.eng = {}
        for name, b, ss in (("pe", nc.tensor, False), ("act", nc.scalar, DBG.get("ss", True)), ("dve", nc.vector, DBG.get("ss", True)),
                            ("pool", nc.gpsimd, True), ("sp", nc.sync, False)):
            self.eng[name] = Eng(name, b, self.new_chan("e_" + name), ss)
        self.dchans = []
        self.named = {}
        self.ntile = 0

    def new_chan(self, name):
        self.nsem += 1
        return Chan(self.es.enter_context(self.nc.semaphore(name)))

    def sb(self, shape, dt, name, es=None):
        self.ntile += 1
        t = (es or self.es).enter_context(self.nc.sbuf_tensor(f"{name}_{self.ntile}", list(shape), dt))
        return TT(t, name)

    def ps(self, shape, dt, name, es=None):
        self.ntile += 1
        t = (es or self.es).enter_context(self.nc.psum_tensor(f"{name}_{self.ntile}", list(shape), dt))
        return TT(t, name)

    def _wait(self, E, reads, writes):
        deps = {}

        def add(d):
            ch, cnt = d
            if deps.get(ch, 0) < cnt:
                deps[ch] = cnt
        for t in reads:
            if t.w is not None:
                add(t.w)
        for t in writes:
            if t.w is not None:
                add(t.w)
            for r in t.r:
                add(r)
        for ch, cnt in deps.items():
            if ch is E.chan and not E.selfsync:
                continue
            if E.seen.get(ch, 0) < cnt:
                E.b.wait_ge(ch.sem, cnt)
                E.seen[ch] = cnt

    def op(self, eng, fn, reads=(), writes=()):
        E = self.eng[eng]
        self._wait(E, reads, writes)
        if E.selfsync and DBG.get("serial", True) and E.seen.get(E.chan, 0) < E.chan.count:
            E.b.wait_ge(E.chan.sem, E.chan.count)
            E.seen[E.chan] = E.chan.count
        ins = fn(E.b)
        E.chan.count += 1
        ins.then_inc(E.chan.sem, 1)
        stamp = (E.chan, E.chan.count)
        for t in writes:
            t.w = stamp
            t.r = []
        for t in reads:
            t.r.append(stamp)
        return ins

    def dma(self, q, out, in_, reads=(), writes=(), **kw):
        E = self.eng[q]
        self._wait(E, reads, writes)
        owner = (list(writes) + list(reads))[0]
        if owner.dchan is None:
            if owner.name not in self.named:
                self.named[owner.name] = self.new_chan("d_" + owner.name)
                self.dchans.append(self.named[owner.name])
            owner.dchan = self.named[owner.name]
        ch = owner.dchan
        ins = E.b.dma_start(out=out, in_=in_, **kw)
        ch.count += 16
        ins.then_inc(ch.sem, 16)
        stamp = (ch, ch.count)
        for t in writes:
            t.w = stamp
            t.r = []
        for t in reads:
            t.r.append(stamp)
        return ins

    def barrier(self, engines=("pe", "act", "dve", "sp", "pool"), dchans=True):
        chans = [self.eng[e].chan for e in self.eng] + (self.dchans if dchans else [])
        for e in engines:
            E = self.eng[e]
            for ch in chans:
                if ch is E.chan:
                    continue
                if E.seen.get(ch, 0) < ch.count:
                    E.b.wait_ge(ch.sem, ch.count)
                    E.seen[ch] = ch.count

    def finish(self):
        self.barrier(engines=("sp",))


def build():
    nc = bass.Bass("TRN2", target_bir_lowering=False)
    dt_in = lambda name, shape, dt=F32: nc.dram_tensor(name, list(shape), dt, kind="ExternalInput").ap()
    dt_out = lambda name, shape, dt=F32: nc.dram_tensor(name, list(shape), dt, kind="ExternalOutput").ap()

    xpT = dt_in("xpT", [D, SEQ])
    xsT = dt_in("xsT", [D, NS])
    wi_d = [[dt_in(f"wi{l}{f}", [D, 2 * DFF]) for f in range(2)] for l in range(DEPTH)]
    wo_d = [[dt_in(f"wo{l}{f}", [DFF, D]) for f in range(2)] for l in range(DEPTH)]
    vec_d = dt_in("vecs", [128, NV])
    cst_d = dt_in("cst", [128, NCST])
    win_d = [dt_in(f"win{l}", [D, WIN_COLS]) for l in range(DEPTH)]
    wout_d = [dt_in(f"wout{l}", [D, D]) for l in range(DEPTH)]
    hgst_d = dt_in("hgst", [DEPTH, 4, 4, 64, 64])
    hg_p_d = dt_out("hg_p", [DEPTH, 4, 64, 64])
    lora_d = dt_in("lora", [DEPTH, 128, 256])
    wqb_d = dt_in("wqb", [DEPTH, 384, 1280])
    wuk_d = dt_in("wuk", [DEPTH, 256, 512])
    wuv_d = dt_in("wuv", [DEPTH, 256, 512])
    rope_d = dt_in("rope", [128, 2, SEQ + NS])
    NPOOL = DBG.get('npool', 5120)
    ckvc_d = [dt_in(f"ckvc{l}", [NPOOL, 128, 256]) for l in range(DEPTH)]
    kpec_d = [dt_in(f"kpec{l}", [NPOOL, 128, 32]) for l in range(DEPTH)]
    pt_d = dt_in("pt", [128, 4], I32)
    wukT_d = dt_in("wukT", [DEPTH, 128, 4, 256])
    ckv_p_d = dt_out("ckv_p", [DEPTH, 256, SEQ])
    ckv_s_d = dt_out("ckv_s", [DEPTH, 256, NS])
    kpe_p_d = dt_out("kpe_p", [DEPTH, 32, SEQ])
    kpe_s_d = dt_out("kpe_s", [DEPTH, 32, NS])
    rwst_d = dt_in("rwst", [DEPTH, 4, 4, 64, 64])
    rwsh_d = dt_in("rwsh", [DEPTH, 128, 7, 4])
    rw_p_d = dt_out("rw_p", [DEPTH, 4, 64, 64])
    rw_s_d = dt_out("rw_s", [DEPTH, 4, 4, 64, 64])
    sh_p_d = dt_out("sh_p", [DEPTH, 128, 7])
    sh_s_d = dt_out("sh_s", [DEPTH, 128, 7, 4])
    hg_s_d = dt_out("hg_s", [DEPTH, 4, 4, 64, 64])
    yT_p = dt_out("yT_p", [D, SEQ])
    yT_s = dt_out("yT_s", [D, NS])

    with contextlib.ExitStack() as es:
        S = Sched(nc, es)
        vecs = S.sb([128, NV], F32, "vecs")
        S.dma("sp", vecs[:], vec_d[:, :], writes=[vecs])
        cst = S.sb([128, NCST], F32, "cst")
        S.dma("sp", cst[:], cst_d[:, :], writes=[cst])
        cstb = S.sb([128, NCST], BF16, "cstb")
        S.op("dve", lambda e: e.tensor_copy(out=cstb[:], in_=cst[:]), reads=[cst], writes=[cstb])
        ones_bf = S.sb([128, 128], BF16, "ones")
        S.op("dve", lambda e: e.memset(ones_bf[:], 1.0), writes=[ones_bf])

        x = S.sb([128, KC, TB], F32, "x")
        NWI, NWO = 2, 4
        wi_buf = [S.sb([128, KC, 2, GJ * 128], BF16, f"wi{i}") for i in range(NWI)]
        wo_buf = [S.sb([128, GJ, 512], BF16, f"wo{i}") for i in range(NWO)]
        wi_ctr = [0]
        wo_ctr = [0]
        PS = [S.ps([128, 512], F32, f"ps{i}") for i in range(8)]

        def rmsnorm(xt, ncols, gcol, out_bf, sq, rstd, psb):
            for kc in range(KC):
                S.op("act", lambda e, kc=kc: e.activation(out=sq[:, kc, :ncols], in_=xt[:, kc, :ncols], func=AF.Square),
                     reads=[xt], writes=[sq])

            def mm(e):
                ins = None
                for kc in range(KC):
                    ins = e.matmul(psb[:, :ncols], lhsT=ones_bf[:], rhs=sq[:, kc, :ncols], start=(kc == 0), stop=(kc == KC - 1))
                return ins
            S.op("pe", mm, reads=[sq, ones_bf], writes=[psb])
            S.op("act", lambda e: e.activation(out=rstd[:, :ncols], in_=psb[:, :ncols], func=AF.Sqrt, scale=1.0 / D, bias=epsb[:, 0:1]),
                 reads=[psb, epsb], writes=[rstd])
            S.op("dve", lambda e: e.reciprocal(out=rstd[:, :ncols], in_=rstd[:, :ncols]), reads=[rstd], writes=[rstd])
            for kc in range(KC):
                S.op("dve", lambda e, kc=kc: e.scalar_tensor_tensor(out=out_bf[:, kc, :ncols], in0=xt[:, kc, :ncols],
                                                                  scalar=vecs[:, gcol + kc:gcol + kc + 1], in1=rstd[:, :ncols],
                                                                  op0=ALU.mult, op1=ALU.mult),
                     reads=[xt, vecs, rstd], writes=[out_bf])

        epsb = S.sb([128, 1], F32, "epsb")
        S.op("dve", lambda e: e.memset(epsb[:], EPS), writes=[epsb])

        def ffn(l, f, ncols, gcol):
            with contextlib.ExitStack() as fs:
                xn = S.sb([128, KC, TB], BF16, "xn", fs)
                sq = S.sb([128, KC, TB], BF16, "sq", fs)
                rstd = S.sb([128, TB], F32, "rstd", fs)
                h = [S.sb([128, TB], BF16, f"h{j}", fs) for j in range(NJ)]
                sa = [S.sb([128, TB], BF16, f"sa{i}", fs) for i in range(2)]
                rmsnorm(x, ncols, gcol, xn, sq, rstd, PS[7])
                wi_v = wi_d[l][f].rearrange("(kc p) (two n) -> p kc two n", p=128, two=2)
                for g in range(NJ // GJ):
                    wb = wi_buf[wi_ctr[0] % NWI]
                    wi_ctr[0] += 1
                    for two in range(2):
                        S.dma("pool", wb[:, :, two, :], wi_v[:, :, two, g * GJ * 128:(g + 1) * GJ * 128], writes=[wb])
                    for jj in range(GJ):
                        j = g * GJ + jj
                        pa, pb = PS[(j % 2) * 2], PS[(j % 2) * 2 + 1]

                        def mma(e, wb=wb, jj=jj, pa=pa):
                            ins = None
                            for kc in range(KC):
                                ins = e.matmul(pa[:, :ncols], lhsT=wb[:, kc, 0, jj * 128:(jj + 1) * 128], rhs=xn[:, kc, :ncols],
                                               start=(kc == 0), stop=(kc == KC - 1))
                            return ins

                        def mmb(e, wb=wb, jj=jj, pb=pb):
                            ins = None
                            for kc in range(KC):
                                ins = e.matmul(pb[:, :ncols], lhsT=wb[:, kc, 1, jj * 128:(jj + 1) * 128], rhs=xn[:, kc, :ncols],
                                               start=(kc == 0), stop=(kc == KC - 1))
                            return ins
                        S.op("pe", mma, reads=[wb, xn], writes=[pa])
                        S.op("pe", mmb, reads=[wb, xn], writes=[pb])
                        st = sa[j % 2]
                        S.op("act", lambda e, pa=pa, st=st: e.activation(out=st[:, :ncols], in_=pa[:, :ncols], func=AF.Silu),
                             reads=[pa], writes=[st])
                        S.op("dve", lambda e, pb=pb, st=st, j=j: e.tensor_tensor(out=h[j][:, :ncols], in0=pb[:, :ncols], in1=st[:, :ncols], op=ALU.mult),
                             reads=[pb, st], writes=[h[j]])
                wo_v = wo_d[l][f].rearrange("(j p) n -> p j n", p=128)
                for half in range(2):
                    acc = [PS[4 + i] for i in range(4)]
                    for g in range(NJ // GJ):
                        wb = wo_buf[wo_ctr[0] % NWO]
                        wo_ctr[0] += 1
                        S.dma("pool", wb[:], wo_v[:, g * GJ:(g + 1) * GJ, half * 512:(half + 1) * 512], writes=[wb])
                        for jj in range(GJ):
                            j = g * GJ + jj
                            for fo in range(4):
                                S.op("pe", lambda e, wb=wb, jj=jj, j=j, fo=fo: e.matmul(
                                    acc[fo][:, :ncols], lhsT=wb[:, jj, fo * 128:(fo + 1) * 128], rhs=h[j][:, :ncols],
                                    start=(j == 0), stop=(j == NJ - 1)),
                                    reads=[wb, h[j]], writes=[acc[fo]])
                    for fo in range(4):
                        kc = half * 4 + fo
                        S.op("dve", lambda e, fo=fo, kc=kc: e.scalar_tensor_tensor(
                            out=x[:, kc, :ncols], in0=acc[fo][:, :ncols], scalar=0.5, in1=x[:, kc, :ncols],
                            op0=ALU.mult, op1=ALU.add), reads=[acc[fo], x], writes=[x])
                S.barrier()


        NWB = 2
        win_buf = [S.sb([128, KC, 256], BF16, f"win{i}") for i in range(NWB)]
        win_ctr = [0]
        lbt = S.sb([128, 8], F32, "lbt")
        S.op("dve", lambda e: e.memset(lbt[:], 0.0), writes=[lbt])
        S.op("dve", lambda e: e.tensor_tensor(out=lbt[:, 2:4], in0=vecs[:, VL["hglog_1"]:VL["hglog_1"] + 2],
                                              in1=vecs[:, VL["hglog_0"]:VL["hglog_0"] + 2], op=ALU.subtract),
             reads=[vecs, lbt], writes=[lbt])
        S.op("act", lambda e: e.activation(out=lbt[:, 2:4], in_=lbt[:, 2:4], func=AF.Sigmoid), reads=[lbt], writes=[lbt])
        S.op("dve", lambda e: e.tensor_scalar(out=lbt[:, 4:8], in0=lbt[:, 0:4], scalar1=-1.0, scalar2=1.0, op0=ALU.mult, op1=ALU.add),
             reads=[lbt], writes=[lbt])
        S_hg = [S.sb([128, 2, 64], F32, f"S_hg{l}") for l in range(DEPTH)]
        for l in range(DEPTH):
            S.op("dve", lambda e, l=l: e.memset(S_hg[l][:], 0.0), writes=[S_hg[l]])

        def wgroup(l, col0, ncol):
            wb = win_buf[win_ctr[0] % NWB]
            win_ctr[0] += 1
            S.dma("pool", wb[:, :, :ncol], win_d[l].rearrange("(kc p) n -> p kc n", p=128)[:, :, col0:col0 + ncol], writes=[wb])
            return wb

        def proj_fm(wb, wcol, M, ps, xn, ncols, pbase=0):
            def f(e):
                ins = None
                for kc in range(KC):
                    ins = e.matmul(ps[pbase:pbase + M, :ncols], lhsT=wb[:, kc, wcol:wcol + M], rhs=xn[:, kc, :ncols],
                                   start=(kc == 0), stop=(kc == KC - 1))
                return ins
            S.op("pe", f, reads=[wb, xn], writes=[ps])

        def A(eng, out, in_, func, R, W, **kw):
            S.op(eng, lambda e: e.activation(out=out, in_=in_, func=func, **kw), reads=R, writes=W)

        def TTo(out, in0, in1, op, R, W, eng="dve"):
            S.op(eng, lambda e: e.tensor_tensor(out=out, in0=in0, in1=in1, op=op), reads=R, writes=W)

        def TS(out, in0, s1, s2, op0, op1, R, W, eng="dve"):
            if op1 is None:
                S.op(eng, lambda e: e.tensor_scalar(out=out, in0=in0, scalar1=s1, scalar2=None, op0=op0), reads=R, writes=W)
            else:
                S.op(eng, lambda e: e.tensor_scalar(out=out, in0=in0, scalar1=s1, scalar2=s2, op0=op0, op1=op1), reads=R, writes=W)

        def STT(out, in0, scalar, in1, op0, op1, R, W):
            S.op("dve", lambda e: e.scalar_tensor_tensor(out=out, in0=in0, scalar=scalar, in1=in1, op0=op0, op1=op1), reads=R, writes=W)

        def MM(out, lhsT, rhs, R, W, start=True, stop=True):
            S.op("pe", lambda e: e.matmul(out, lhsT=lhsT, rhs=rhs, start=start, stop=stop), reads=R, writes=W)

        def hgrn(l, ncols, xn, o_all):
            samp = (ncols == NS)
            C = 8 if samp else 64
            nseg = ncols // C
            groups = [(0, 32, [0, 1, 2, 3])] if samp else [(g * 128, 128, [2 * g, 2 * g + 1]) for g in range(4)]
            ng = len(groups)
            segm = cst[:, C_SEGS:C_SEGS + 32] if samp else cst[:, C_SEGP:C_SEGP + 512]
            mcol = C_MS if samp else C_MP
            with contextlib.ExitStack() as hs:
                qT = S.sb([128, 2, TB], F32, "hq", hs)
                sg_ = S.sb([128, 2, TB], F32, "hsig", hs)
                sn = S.sb([128, 2, TB], F32, "hsn", hs)
                bb = S.sb([128, 2, TB], F32, "hb", hs)
                t1 = S.sb([128, 2, TB], F32, "ht1", hs)
                t2 = S.sb([128, 2, TB], F32, "ht2", hs)
                Qi = S.sb([128, 2, TB], BF16, "hQi", hs)
                Ki = S.sb([128, 2, TB], BF16, "hKi", hs)
                Qs = S.sb([128, 2, TB], BF16, "hQs", hs)
                Kd = S.sb([128, 2, TB], BF16, "hKd", hs)
                gate = S.sb([128, 2, TB], F32, "hgate", hs)
                Vt = S.sb([128, 4, 256], BF16, "hVt", hs)
                KdT = S.sb([128, 4, 2, 128], BF16, "hKdT", hs)
                KdTm = S.sb([32, 4, 2, 128], BF16, "hKdTm", hs)
                attm = S.sb([128, 4, 2, 2, 128], BF16, "hattm", hs)
                Sbf = S.sb([128, 8, 2, 64], BF16, "hSbf", hs)
                dseg = S.sb([128, 2, 8], F32, "hdseg", hs)
                osb = S.sb([128, 2, TB], F32, "hosb", hs)
                o2 = S.sb([128, 2, TB], BF16, "ho2", hs)
                Ssm = [S.sb([128, 2, 64], F32, f"hSs{b}", hs) for b in range(4)] if samp else None
                lbc = lambda pc: lbt[:, l * 2 + pc:l * 2 + pc + 1]
                omlc = lambda pc: lbt[:, 4 + l * 2 + pc:4 + l * 2 + pc + 1]
                wb = wgroup(l, 0, 256)
                for pc in range(2):
                    proj_fm(wb, pc * 128, 128, PS[pc], xn, ncols)
                    A("act", qT[:, pc, :ncols], PS[pc][:, :ncols], AF.Copy, [PS[pc]], [qT])
                wb = wgroup(l, 256, 256)
                for pc in range(2):
                    p_ = PS[2 + pc]
                    proj_fm(wb, pc * 128, 128, p_, xn, ncols)
                    A("act", sg_[:, pc, :ncols], p_[:, :ncols], AF.Sigmoid, [p_], [sg_])
                    A("act", sn[:, pc, :ncols], p_[:, :ncols], AF.Sigmoid, [p_], [sn], scale=-1.0)
                    TS(sg_[:, pc, :ncols], sg_[:, pc, :ncols], omlc(pc), lbc(pc), ALU.mult, ALU.add, [sg_, lbt], [sg_])
                    TS(sg_[:, pc, :ncols], sg_[:, pc, :ncols], 1e-30, None, ALU.max, None, [sg_], [sg_])
                    A("act", sg_[:, pc, :ncols], sg_[:, pc, :ncols], AF.Ln, [sg_], [sg_])
                    TS(sn[:, pc, :ncols], sn[:, pc, :ncols], omlc(pc), None, ALU.mult, None, [sn, lbt], [sn])
                    S.op("dve", lambda e, pc=pc: e.tensor_tensor_scan(out=bb[:, pc, :ncols], data0=segm[:, :ncols], data1=sg_[:, pc, :ncols],
                                                                     initial=0.0, op0=ALU.mult, op1=ALU.add), reads=[sg_, cst], writes=[bb])
                    bv = bb[:, pc, :ncols].rearrange("p (s c) -> p s c", c=C)
                    v3 = lambda t, pc=pc: t[:, pc, :ncols].rearrange("p (s c) -> p s c", c=C)
                    TTo(v3(t1), bv, bv[:, :, C // 2 - 1:C // 2].to_broadcast([128, nseg, C]), ALU.subtract, [bb], [t1])
                    A("act", t2[:, pc, :ncols], t1[:, pc, :ncols], AF.Exp, [t1], [t2])
                    TTo(Qi[:, pc, :ncols], qT[:, pc, :ncols], t2[:, pc, :ncols], ALU.mult, [qT, t2], [Qi])
                    A("act", t2[:, pc, :ncols], t1[:, pc, :ncols], AF.Exp, [t1], [t2], scale=-1.0)
                    TTo(Ki[:, pc, :ncols], sn[:, pc, :ncols], t2[:, pc, :ncols], ALU.mult, [sn, t2], [Ki])
                    A("act", t2[:, pc, :ncols], bb[:, pc, :ncols], AF.Exp, [bb], [t2])
                    TTo(Qs[:, pc, :ncols], qT[:, pc, :ncols], t2[:, pc, :ncols], ALU.mult, [qT, t2], [Qs])
                    TTo(v3(t1), bv, bv[:, :, C - 1:C].to_broadcast([128, nseg, C]), ALU.subtract, [bb], [t1])
                    A("act", t2[:, pc, :ncols], t1[:, pc, :ncols], AF.Exp, [t1], [t2], scale=-1.0)
                    TTo(Kd[:, pc, :ncols], sn[:, pc, :ncols], t2[:, pc, :ncols], ALU.mult, [sn, t2], [Kd])
                    A("act", dseg[:, pc, :nseg], bv[:, :, C - 1], AF.Exp, [bb], [dseg])
                wb = wgroup(l, 512, 256)
                for g, (c0g, gsz, segs) in enumerate(groups):
                    def f(e, c0g=c0g, gsz=gsz, wb=wb):
                        ins = None
                        for kc in range(KC):
                            ins = e.matmul(PS[4][:gsz, 0:256], lhsT=xn[:, kc, c0g:c0g + gsz], rhs=wb[:, kc, 0:256], start=(kc == 0), stop=(kc == KC - 1))
                        return ins
                    S.op("pe", f, reads=[xn, wb], writes=[PS[4]])
                    A("act", Vt[:gsz, g, :], PS[4][:gsz, 0:256], AF.Copy, [PS[4]], [Vt])
                wb = wgroup(l, 768, 256)
                for pc in range(2):
                    proj_fm(wb, pc * 128, 128, PS[pc], xn, ncols)
                    A("act", gate[:, pc, :ncols], PS[pc][:, :ncols], AF.Silu, [PS[pc]], [gate])
                if DBG.get('hg_stage', 9) < 1:
                    S.barrier()
                    return
                psT = PS[5]
                for g, (c0g, gsz, segs) in enumerate(groups):
                    for pc in range(2):
                        S.op("pe", lambda e, g=g, pc=pc, c0g=c0g, gsz=gsz: e.transpose(
                            psT[:, :].bitcast(BF16)[:gsz, pc * 128:(pc + 1) * 128], Kd[:, pc, c0g:c0g + gsz], cstb[:, C_ID:C_ID + 128]),
                            reads=[Kd, cstb], writes=[psT])
                    S.op("dve", lambda e, g=g, gsz=gsz: e.tensor_copy(out=KdT[:gsz, g, :, :].rearrange("p a b -> p (a b)"),
                                                                     in_=psT[:, :].bitcast(BF16)[:gsz, 0:256]), reads=[psT], writes=[KdT])
                if samp:
                    for b in range(4):
                        TS(KdTm[:, b, :, :].rearrange("p a b -> p (a b)"), KdT[:32, 0, :, :].rearrange("p a b -> p (a b)"),
                           cst[:32, C_ROWS + b:C_ROWS + b + 1], None, ALU.mult, None, [KdT, cst], [KdTm])
                if DBG.get('hg_stage', 9) < 2:
                    S.barrier()
                    return
                psA = [PS[6], PS[5]]
                for g, (c0g, gsz, segs) in enumerate(groups):
                    for hb in range(2):
                        base = hb * 64

                        def f(e, c0g=c0g, gsz=gsz, hb=hb, base=base):
                            ins = None
                            for pc in range(2):
                                ins = e.matmul(psA[hb][:gsz, pc * 128:pc * 128 + gsz], lhsT=Ki[base:base + 64, pc, c0g:c0g + gsz],
                                               rhs=Qi[base:base + 64, pc, c0g:c0g + gsz], start=True, stop=True)
                            return ins
                        S.op("pe", f, reads=[Ki, Qi], writes=[psA[hb]])
                        TTo(attm[:gsz, g, hb, :, :gsz], psA[hb][:gsz, 0:256].rearrange("p (h t) -> p h t", h=2)[:, :, :gsz],
                            cst[:gsz, mcol:mcol + gsz].unsqueeze(1).to_broadcast([gsz, 2, gsz]), ALU.mult, [psA[hb], cst], [attm])
                if DBG.get('hg_stage', 9) < 3:
                    S.barrier()
                    return
                psU = PS[7]
                if not samp:
                    Sst = S_hg[l]
                    for seg in range(nseg):
                        g, r0 = seg // 2, (seg % 2) * 64
                        psU = PS[7] if r0 == 0 else PS[4]
                        A("act", Sbf[:, seg, :, :].rearrange("p a b -> p (a b)"), Sst[:, :, :].rearrange("p a b -> p (a b)"), AF.Copy, [Sst], [Sbf])

                        def f(e, g=g, r0=r0):
                            ins = None
                            for h in range(4):
                                pc, base = h // 2, (h % 2) * 64
                                ins = e.matmul(psU[base:base + 64, pc * 64:(pc + 1) * 64], lhsT=KdT[r0:r0 + 64, g, pc, base:base + 64],
                                               rhs=Vt[r0:r0 + 64, g, h * 64:(h + 1) * 64], start=True, stop=True)
                            return ins
                        S.op("pe", f, reads=[KdT, Vt], writes=[psU])
                        for pc in range(2):
                            STT(Sst[:, pc, :], Sst[:, pc, :], dseg[:, pc, seg:seg + 1], psU[:, pc * 64:(pc + 1) * 64], ALU.mult, ALU.add,
                                [Sst, dseg, psU], [Sst])
                else:
                    for b in range(4):
                        Sst = Ssm[b]
                        S.dma("sp", Sst[:], hgst_d[l, b].rearrange("(pc hb) k v -> (hb k) pc v", hb=2), writes=[Sst])
                        A("act", Sbf[:, b, :, :].rearrange("p a b -> p (a b)"), Sst[:, :, :].rearrange("p a b -> p (a b)"), AF.Copy, [Sst], [Sbf])

                        def f(e, b=b):
                            ins = None
                            for h in range(4):
                                pc, base = h // 2, (h % 2) * 64
                                ins = e.matmul(psU[base:base + 64, pc * 64:(pc + 1) * 64], lhsT=KdTm[:32, b, pc, base:base + 64],
                                               rhs=Vt[:32, 0, h * 64:(h + 1) * 64], start=True, stop=True)
                            return ins
                        S.op("pe", f, reads=[KdTm, Vt], writes=[psU])
                        for pc in range(2):
                            STT(Sst[:, pc, :], Sst[:, pc, :], dseg[:, pc, b:b + 1], psU[:, pc * 64:(pc + 1) * 64], ALU.mult, ALU.add,
                                [Sst, dseg, psU], [Sst])
                        S.dma("sp", hg_s_d[l, b].rearrange("(pc hb) k v -> (hb k) pc v", hb=2), Sst[:], reads=[Sst])
                if DBG.get('hg_stage', 9) < 4:
                    S.barrier()
                    return
                for g, (c0g, gsz, segs) in enumerate(groups):
                    for h in range(4):
                        pc, hb, base = h // 2, h % 2, (h % 2) * 64

                        def f(e, g=g, h=h, pc=pc, hb=hb, base=base, c0g=c0g, gsz=gsz, segs=segs):
                            ins = e.matmul(PS[h][base:base + 64, c0g:c0g + gsz], lhsT=Vt[:gsz, g, h * 64:(h + 1) * 64],
                                           rhs=attm[:gsz, g, hb, pc, :gsz], start=True, stop=False)
                            for si, seg in enumerate(segs):
                                ins = e.matmul(PS[h][base:base + 64, seg * C:(seg + 1) * C], lhsT=Sbf[base:base + 64, seg, pc, :],
                                               rhs=Qs[base:base + 64, pc, seg * C:(seg + 1) * C], start=False, stop=(si == len(segs) - 1))
                            return ins
                        S.op("pe", f, reads=[Vt, attm, Sbf, Qs], writes=[PS[h]])
                for pc in range(2):
                    for hb in range(2):
                        h, base = 2 * pc + hb, hb * 64
                        A("act", osb[base:base + 64, pc, :ncols], PS[h][base:base + 64, :ncols], AF.Copy, [PS[h]], [osb])
                        A("act", o2[base:base + 64, pc, :ncols], PS[h][base:base + 64, :ncols], AF.Square, [PS[h]], [o2])
                for pc in range(2):
                    MM(PS[4 + pc][:, :ncols], cstb[:, C_BLK:C_BLK + 128], o2[:, pc, :ncols], [cstb, o2], [PS[4 + pc]])
                    A("act", t1[:, pc, :ncols], PS[4 + pc][:, :ncols], AF.Sqrt, [PS[4 + pc], epsb], [t1], scale=1.0 / 64, bias=epsb[:, 0:1])
                    S.op("dve", lambda e, pc=pc: e.reciprocal(out=t1[:, pc, :ncols], in_=t1[:, pc, :ncols]), reads=[t1], writes=[t1])
                    STT(t2[:, pc, :ncols], osb[:, pc, :ncols], vecs[:, VL[f"hgn_{l}"]:VL[f"hgn_{l}"] + 1], t1[:, pc, :ncols], ALU.mult, ALU.mult,
                        [osb, vecs, t1], [t2])
                    TTo(o_all[:, pc, :ncols], t2[:, pc, :ncols], gate[:, pc, :ncols], ALU.mult, [t2, gate], [o_all])
                S.barrier()


        H_rw = [S.sb([128, 2, 64], F32, f"H_rw{l}") for l in range(DEPTH)]
        rw_prev = [S.sb([128, 8], F32, f"rwprev{l}") for l in range(DEPTH)]
        lora = [S.sb([128, 256], BF16, f"lora{l}") for l in range(DEPTH)]
        omka = S.sb([128, 4], F32, "omka")
        for l in range(DEPTH):
            S.op("dve", lambda e, l=l: e.memset(H_rw[l][:], 0.0), writes=[H_rw[l]])
            S.op("dve", lambda e, l=l: e.memset(rw_prev[l][:], 0.0), writes=[rw_prev[l]])
            S.dma("pool", lora[l][:], lora_d[l], writes=[lora[l]])
            TS(omka[:, 2 * l:2 * l + 2], vecs[:, VL[f"ka_{l}"]:VL[f"ka_{l}"] + 2], -1.0, 1.0, ALU.mult, ALU.add, [vecs], [omka])

        def rwkv(l, bi, ncols, xn, o_all):
            samp = (ncols == NS)
            C = 8 if samp else 64
            nseg = ncols // C
            groups = [(0, 32, [0, 1, 2, 3])] if samp else [(g * 128, 128, [2 * g, 2 * g + 1]) for g in range(4)]
            segm = cst[:, C_SEGS:C_SEGS + 32] if samp else cst[:, C_SEGP:C_SEGP + 512]
            m_incl, m_str, m_low = (C_MS, C_MSS, C_MSL) if samp else (C_MP, C_MPS, C_MPL)
            nlev = 2 if samp else 5
            V_ = lambda nm, pc=0: vecs[:, VL[f"{nm}_{l}"] + pc:VL[f"{nm}_{l}"] + pc + 1]
            with contextlib.ExitStack() as rs:
                rkv = S.sb([128, 6, TB], F32, "rw_rkv", rs)
                l6 = S.sb([128, TB], BF16, "rw_l6", rs)
                pb = S.sb([128, TB + 1], F32, "rw_pb", rs)
                prevb = S.sb([128, TB], F32, "rw_prevb", rs)
                dtmp = S.sb([128, TB], F32, "rw_d", rs)
                l6f = S.sb([128, TB], F32, "rw_l6f", rs)
                sh0 = S.sb([128, 7, 4], F32, "rw_sh0", rs)
                shs = S.sb([128, 7, 4], F32, "rw_shs", rs)
                if samp:
                    S.dma("sp", sh0[:], rwsh_d[l], writes=[sh0])
                for c in range(7):
                    if c % 2 == 0:
                        wb = wgroup(l, 1024 + 128 * c, 256 if c < 6 else 128)
                    pp = PS[c % 2]
                    proj_fm(wb, (c % 2) * 128, 128, pp, xn, ncols)
                    A("act", pb[:, 1:ncols + 1], pp[:, :ncols], AF.Copy, [pp], [pb])
                    dest = rkv[:, c, :ncols] if c < 6 else l6f[:, :ncols]
                    dT = rkv if c < 6 else l6f
                    if not samp:
                        S.op("dve", lambda e, c=c: e.tensor_copy(out=pb[:, 0:1], in_=rw_prev[l][:, c:c + 1]), reads=[rw_prev[l], pb], writes=[pb])
                        TTo(dtmp[:, :ncols], pb[:, 0:ncols], pb[:, 1:ncols + 1], ALU.subtract, [pb], [dtmp])
                        S.op("dve", lambda e, c=c: e.tensor_copy(out=rw_prev[l][:, c:c + 1], in_=pb[:, ncols:ncols + 1]), reads=[pb, rw_prev[l]], writes=[rw_prev[l]])
                    else:
                        S.op("dve", lambda e: e.tensor_copy(out=prevb[:, 1:ncols], in_=pb[:, 1:ncols]), reads=[pb], writes=[prevb])
                        S.op("dve", lambda e, c=c: e.tensor_copy(out=prevb[:, :ncols].rearrange("p (b t) -> p b t", t=8)[:, :, 0], in_=sh0[:, c, :]),
                             reads=[sh0, prevb], writes=[prevb])
                        TTo(dtmp[:, :ncols], prevb[:, :ncols], pb[:, 1:ncols + 1], ALU.subtract, [pb, prevb], [dtmp])
                        S.op("dve", lambda e, c=c: e.tensor_copy(out=shs[:, c, :], in_=pb[:, 1:ncols + 1].rearrange("p (b t) -> p b t", t=8)[:, :, 7]),
                             reads=[pb, shs], writes=[shs])
                    STT(dest, dtmp[:, :ncols], V_("mu", c), pb[:, 1:ncols + 1], ALU.mult, ALU.add, [dtmp, vecs, pb], [dT])
                if samp:
                    S.dma("sp", sh_s_d[l], shs[:], reads=[shs])
                elif bi == NPB - 1:
                    S.dma("sp", sh_p_d[l], rw_prev[l][:, 0:7], reads=[rw_prev[l]])
                A("act", l6[0:32, :ncols], l6f[0:32, :ncols], AF.Tanh, [l6f], [l6])
                A("act", l6[32:64, :ncols], l6f[32:64, :ncols], AF.Copy, [l6f], [l6])
                A("act", l6[64:128, :ncols], l6f[64:128, :ncols], AF.Sigmoid, [l6f], [l6])
                if DBG.get('rw_stage', 9) < 1:
                    S.barrier(); return
                for pc in range(2):
                    with contextlib.ExitStack() as bs:
                        f2 = lambda nm: S.sb([128, TB], F32, nm, bs)
                        b2 = lambda nm: S.sb([128, TB], BF16, nm, bs)
                        lw, aa, gg, al, be, km, cw, tA, tB_ = f2("rw_lw"), f2("rw_a"), f2("rw_g"), f2("rw_al"), f2("rw_be"), f2("rw_km"), f2("rw_cw"), f2("rw_tA"), f2("rw_tB")
                        At, Bt, Kt, Rt = b2("rw_At"), b2("rw_Bt"), b2("rw_Kt"), b2("rw_Rt")
                        vb = b2("rw_vb")
                        tokT = S.sb([128, 4, 4, 128], BF16, "rw_tokT", bs)
                        tokM = S.sb([32, 4, 2, 128], BF16, "rw_tokM", bs)
                        pCt = S.sb([128, 8], F32, "rw_pC", bs)
                        Hc = S.sb([128, 8, 64], BF16, "rw_Hc", bs)
                        Hp = S.sb([128, 64], F32, "rw_Hp", bs)
                        ysb = f2("rw_y")
                        Hsm = [S.sb([128, 64], F32, f"rw_Hs{b}", bs) for b in range(4)] if samp else None
                        r_, k_, v_ = rkv[:, pc, :ncols], rkv[:, 2 + pc, :ncols], rkv[:, 4 + pc, :ncols]
                        n = ncols
                        MM(PS[2][:, :n], lora[l][0:32, pc * 128:(pc + 1) * 128], l6[0:32, :n], [lora[l], l6], [PS[2]])
                        MM(PS[3][:, :n], lora[l][32:64, pc * 128:(pc + 1) * 128], l6[32:64, :n], [lora[l], l6], [PS[3]])
                        MM(PS[4][:, :n], lora[l][64:128, pc * 128:(pc + 1) * 128], l6[64:128, :n], [lora[l], l6], [PS[4]])
                        A("act", lw[:, :n], PS[2][:, :n], AF.Sigmoid, [PS[2], vecs], [lw], bias=V_("w0", pc))
                        TS(lw[:, :n], lw[:, :n], -float(np.exp(-0.5)), None, ALU.mult, None, [lw], [lw])
                        A("act", aa[:, :n], PS[3][:, :n], AF.Sigmoid, [PS[3], vecs], [aa], bias=V_("a0", pc))
                        A("act", gg[:, :n], PS[4][:, :n], AF.Copy, [PS[4]], [gg])
                        TS(al[:, :n], k_, V_("kk", pc), None, ALU.mult, None, [rkv, vecs], [al])
                        A("act", tA[:, :n], al[:, :n], AF.Square, [al], [tA])
                        MM(PS[5][:, :n], cst[:, C_BLK:C_BLK + 128], tA[:, :n], [cst, tA], [PS[5]])
                        A("act", tA[:, :n], PS[5][:, :n], AF.Sqrt, [PS[5]], [tA])
                        TS(tA[:, :n], tA[:, :n], 1e-12, None, ALU.max, None, [tA], [tA])
                        S.op("dve", lambda e: e.reciprocal(out=tA[:, :n], in_=tA[:, :n]), reads=[tA], writes=[tA])
                        TTo(al[:, :n], al[:, :n], tA[:, :n], ALU.mult, [al, tA], [al])
                        TS(tA[:, :n], aa[:, :n], V_("ka", pc), omka[:, 2 * l + pc:2 * l + pc + 1], ALU.mult, ALU.add, [aa, vecs, omka], [tA])
                        TTo(km[:, :n], k_, tA[:, :n], ALU.mult, [rkv, tA], [km])
                        STT(be[:, :n], al[:, :n], -1.0, aa[:, :n], ALU.mult, ALU.mult, [al, aa], [be])
                        STT(tA[:, :n], r_, V_("rk", pc), km[:, :n], ALU.mult, ALU.mult, [rkv, vecs, km], [tA])
                        MM(PS[6][:, :n], cst[:, C_BLK:C_BLK + 128], tA[:, :n], [cst, tA], [PS[6]])
                        TTo(tB_[:, :n], PS[6][:, :n], v_, ALU.mult, [PS[6], rkv], [tB_])
                        S.op("dve", lambda e: e.tensor_tensor_scan(out=cw[:, :n], data0=segm[:, :n], data1=lw[:, :n], initial=0.0,
                                                                   op0=ALU.mult, op1=ALU.add), reads=[lw, cst], writes=[cw])
                        TTo(tA[:, :n], cw[:, :n], lw[:, :n], ALU.subtract, [cw, lw], [tA])
                        A("act", tA[:, :n], tA[:, :n], AF.Exp, [tA], [tA])
                        TTo(At[:, :n], al[:, :n], tA[:, :n], ALU.mult, [al, tA], [At])
                        A("act", tA[:, :n], cw[:, :n], AF.Exp, [cw], [tA], scale=-1.0)
                        TTo(Bt[:, :n], be[:, :n], tA[:, :n], ALU.mult, [be, tA], [Bt])
                        TTo(Kt[:, :n], km[:, :n], tA[:, :n], ALU.mult, [km, tA], [Kt])
                        A("act", tA[:, :n], cw[:, :n], AF.Exp, [cw], [tA])
                        TTo(Rt[:, :n], r_, tA[:, :n], ALU.mult, [rkv, tA], [Rt])
                        A("act", pCt[:, :nseg], cw[:, :n].rearrange("p (s c) -> p s c", c=C)[:, :, C - 1], AF.Exp, [cw], [pCt])
                        A("act", vb[:, :n], v_, AF.Copy, [rkv], [vb])
                        if DBG.get('rw_stage', 9) < 2:
                            S.barrier(); continue
                        for g, (c0g, gsz, segs) in enumerate(groups):
                            for wi_, src in enumerate((At, Bt, Kt, vb)):
                                S.op("pe", lambda e, wi_=wi_, src=src, c0g=c0g, gsz=gsz: e.transpose(
                                    PS[7][:, :].bitcast(BF16)[:gsz, wi_ * 128:(wi_ + 1) * 128], src[:, c0g:c0g + gsz], cstb[:, C_ID:C_ID + 128]),
                                    reads=[src, cstb], writes=[PS[7]])
                            S.op("dve", lambda e, g=g, gsz=gsz: e.tensor_copy(out=tokT[:gsz, g, :, :].rearrange("p a b -> p (a b)"),
                                                                             in_=PS[7][:, :].bitcast(BF16)[:gsz, 0:512]), reads=[PS[7]], writes=[tokT])
                        if samp:
                            for b in range(4):
                                TS(tokM[:, b, :, :].rearrange("p a b -> p (a b)"), tokT[:32, 0, 1:3, :].rearrange("p a b -> p (a b)"),
                                   cst[:32, C_ROWS + b:C_ROWS + b + 1], None, ALU.mult, None, [tokT, cst], [tokM])
                        if DBG.get('rw_stage', 9) < 3:
                            S.barrier(); continue
                        for hb in range(2):
                            h, base = 2 * pc + hb, hb * 64
                            bk = PS[0:4] if hb == 0 else PS[4:8]
                            if samp:
                                for b in range(4):
                                    S.dma("sp", Hsm[b][base:base + 64, :], rwst_d[l, b, h], writes=[Hsm[b]])
                            for g, (c0g, gsz, segs) in enumerate(groups):
                                with contextlib.ExitStack() as gs:
                                    sq_ = lambda nm: S.sb([128, 128], BF16, nm, gs)
                                    Nn, Aa, IA, Pp, AakT, ArbT, ArkT, WT = (sq_("rw_N"), sq_("rw_A"), sq_("rw_IA"), sq_("rw_P"), sq_("rw_AakT"),
                                                                            sq_("rw_ArbT"), sq_("rw_ArkT"), sq_("rw_WT"))
                                    X0 = S.sb([128, 64], BF16, "rw_X0", gs)
                                    Ut = S.sb([128, 64], F32, "rw_Ut", gs)
                                    Usb = S.sb([128, 64], BF16, "rw_Usb", gs)
                                    Uf = S.sb([128, 64], F32, "rw_Uf", gs)
                                    gsl = slice(c0g, c0g + gsz)
                                    fm = lambda t: t[base:base + 64, gsl]
                                    mk = lambda col: cst[:gsz, col:col + gsz]
                                    MM(bk[0][:gsz, :gsz], fm(Bt), fm(At), [Bt, At], [bk[0]])
                                    TTo(Nn[:gsz, :gsz], bk[0][:gsz, :gsz], mk(m_str), ALU.mult, [bk[0], cst], [Nn])
                                    MM(bk[1][:gsz, :gsz], fm(At), fm(Bt), [Bt, At], [bk[1]])
                                    TTo(Aa[:gsz, :gsz], bk[1][:gsz, :gsz], mk(m_low), ALU.mult, [bk[1], cst], [Aa])
                                    MM(bk[2][:gsz, :gsz], fm(Kt), fm(At), [Kt, At], [bk[2]])
                                    TTo(AakT[:gsz, :gsz], bk[2][:gsz, :gsz], mk(m_str), ALU.mult, [bk[2], cst], [AakT])
                                    MM(bk[3][:gsz, :gsz], fm(Bt), fm(Rt), [Bt, Rt], [bk[3]])
                                    TTo(ArbT[:gsz, :gsz], bk[3][:gsz, :gsz], mk(m_incl), ALU.mult, [bk[3], cst], [ArbT])
                                    MM(bk[0][:gsz, :gsz], fm(Kt), fm(Rt), [Kt, Rt], [bk[0]])
                                    TTo(ArkT[:gsz, :gsz], bk[0][:gsz, :gsz], mk(m_incl), ALU.mult, [bk[0], cst], [ArkT])
                                    if DBG.get('rw_stage', 9) < 4:
                                        S.barrier(); continue
                                    TTo(Pp[:gsz, :gsz], Nn[:gsz, :gsz], mk(C_ID), ALU.add, [Nn, cst], [Pp])
                                    for j in range(1, nlev + 1):
                                        MM(bk[1][:gsz, :gsz], Nn[:gsz, :gsz], Aa[:gsz, :gsz], [Nn, Aa], [bk[1]])
                                        if j < nlev:
                                            MM(bk[2][:gsz, :gsz], Aa[:gsz, :gsz], Nn[:gsz, :gsz], [Nn, Aa], [bk[2]])
                                        TTo(IA[:gsz, :gsz], bk[1][:gsz, :gsz], mk(C_ID), ALU.add, [bk[1], cst], [IA])
                                        if j < nlev:
                                            S.op("dve", lambda e, gsz=gsz: e.tensor_copy(out=Aa[:gsz, :gsz], in_=bk[1][:gsz, :gsz]), reads=[bk[1]], writes=[Aa])
                                            A("act", Nn[:gsz, :gsz], bk[2][:gsz, :gsz], AF.Copy, [bk[2]], [Nn])
                                        MM(bk[3][:gsz, :gsz], IA[:gsz, :gsz], Pp[:gsz, :gsz], [IA, Pp], [bk[3]])
                                        A("act", Pp[:gsz, :gsz], bk[3][:gsz, :gsz], AF.Copy, [bk[3]], [Pp])
                                    if DBG.get('rw_stage', 9) < 5:
                                        S.barrier(); continue
                                    Vtok = tokT[:gsz, g, 3, base:base + 64]
                                    MM(bk[0][:gsz, 0:64], AakT[:gsz, :gsz], Vtok, [AakT, tokT], [bk[0]])
                                    A("act", X0[:gsz, :], bk[0][:gsz, 0:64], AF.Copy, [bk[0]], [X0])
                                    MM(bk[1][:gsz, 0:64], Pp[:gsz, :gsz], X0[:gsz, :], [Pp, X0], [bk[1]])
                                    A("act", Ut[:gsz, :], bk[1][:gsz, 0:64], AF.Copy, [bk[1]], [Ut])
                                    MM(bk[2][base:base + 64, :gsz], tokT[:gsz, g, 0, base:base + 64], Pp[:gsz, :gsz], [tokT, Pp], [bk[2]])
                                    A("act", WT[base:base + 64, :gsz], bk[2][base:base + 64, :gsz], AF.Copy, [bk[2]], [WT])
                                    if DBG.get('rw_stage', 9) < 6:
                                        S.barrier(); continue
                                    if not samp:
                                        Hst = H_rw[l]
                                        for si, seg in enumerate(segs):
                                            r0 = si * 64
                                            pu = bk[si % 2]
                                            ph = bk[2 + si % 2]
                                            A("act", Hc[base:base + 64, seg, :], Hst[base:base + 64, pc, :], AF.Copy, [Hst], [Hc])
                                            A("act", Hp[base:base + 64, :], Hst[base:base + 64, pc, :], AF.Identity, [Hst, pCt], [Hp], scale=pCt[base:base + 64, seg:seg + 1])
                                            MM(pu[r0:r0 + 64, 0:64], WT[base:base + 64, r0:r0 + 64], Hc[base:base + 64, seg, :], [WT, Hc], [pu])
                                            TTo(Uf[r0:r0 + 64, :], pu[r0:r0 + 64, 0:64], Ut[r0:r0 + 64, :], ALU.add, [pu, Ut], [Uf])
                                            A("act", Usb[r0:r0 + 64, :], Uf[r0:r0 + 64, :], AF.Copy, [Uf], [Usb])

                                            def fH(e, r0=r0, g=g, ph=ph):
                                                e.matmul(ph[base:base + 64, 0:64], lhsT=tokT[r0:r0 + 64, g, 2, base:base + 64], rhs=tokT[r0:r0 + 64, g, 3, base:base + 64],
                                                         start=True, stop=False)
                                                return e.matmul(ph[base:base + 64, 0:64], lhsT=tokT[r0:r0 + 64, g, 1, base:base + 64], rhs=Usb[r0:r0 + 64, :],
                                                                start=False, stop=True)
                                            S.op("pe", fH, reads=[tokT, Usb], writes=[ph])
                                            STT(Hst[base:base + 64, pc, :], ph[base:base + 64, 0:64], pCt[base:base + 64, seg:seg + 1], Hp[base:base + 64, :],
                                                ALU.mult, ALU.add, [ph, pCt, Hp, Hst], [Hst])
                                    else:
                                        for b in range(4):
                                            A("act", Hc[base:base + 64, b, :], Hsm[b][base:base + 64, :], AF.Copy, [Hsm[b]], [Hc])

                                        def fU(e):
                                            ins = None
                                            for b in range(4):
                                                ins = e.matmul(bk[0][:32, b * 64:(b + 1) * 64], lhsT=WT[base:base + 64, 0:32], rhs=Hc[base:base + 64, b, :], start=True, stop=True)
                                            return ins
                                        S.op("pe", fU, reads=[WT, Hc], writes=[bk[0]])
                                        S.op("dve", lambda e: e.tensor_copy(out=Uf[:32, :], in_=Ut[:32, :]), reads=[Ut], writes=[Uf])
                                        for b in range(4):
                                            STT(Uf[:32, :], bk[0][:32, b * 64:(b + 1) * 64], cst[:32, C_ROWS + b:C_ROWS + b + 1], Uf[:32, :], ALU.mult, ALU.add,
                                                [bk[0], cst, Uf], [Uf])
                                        A("act", Usb[:32, :], Uf[:32, :], AF.Copy, [Uf], [Usb])
                                        for b in range(4):
                                            ph = bk[2 + b % 2]

                                            def fH(e, b=b, ph=ph):
                                                e.matmul(ph[base:base + 64, 0:64], lhsT=tokM[:32, b, 1, base:base + 64], rhs=tokT[:32, 0, 3, base:base + 64], start=True, stop=False)
                                                return e.matmul(ph[base:base + 64, 0:64], lhsT=tokM[:32, b, 0, base:base + 64], rhs=Usb[:32, :], start=False, stop=True)
                                            S.op("pe", fH, reads=[tokM, tokT, Usb], writes=[ph])
                                            TTo(Hp[base:base + 64, :], ph[base:base + 64, 0:64], Hsm[b][base:base + 64, :], ALU.add, [ph, Hsm[b]], [Hp])
                                            TS(Hsm[b][base:base + 64, :], Hp[base:base + 64, :], pCt[base:base + 64, b:b + 1], None, ALU.mult, None, [Hp, pCt], [Hsm[b]])
                                            S.dma("sp", rw_s_d[l, b, h], Hsm[b][base:base + 64, :], reads=[Hsm[b]])
                                    if DBG.get('rw_stage', 9) < 7:
                                        S.barrier(); continue
                                    py = bk[1]

                                    def fY(e, g=g, gsz=gsz, segs=segs, py=py):
                                        e.matmul(py[base:base + 64, :gsz], lhsT=tokT[:gsz, g, 3, base:base + 64], rhs=ArkT[:gsz, :gsz], start=True, stop=False)
                                        ins = e.matmul(py[base:base + 64, :gsz], lhsT=Usb[:gsz, :], rhs=ArbT[:gsz, :gsz], start=False, stop=False)
                                        for si, seg in enumerate(segs):
                                            ins = e.matmul(py[base:base + 64, si * C:(si + 1) * C], lhsT=Hc[base:base + 64, seg, :],
                                                           rhs=Rt[base:base + 64, c0g + si * C:c0g + (si + 1) * C], start=False, stop=(si == len(segs) - 1))
                                        return ins
                                    S.op("pe", fY, reads=[tokT, ArkT, Usb, ArbT, Hc, Rt], writes=[py])
                                    A("act", ysb[base:base + 64, gsl], py[base:base + 64, :gsz], AF.Copy, [py], [ysb])
                                    S.barrier(engines=("pe", "act", "dve"), dchans=False)
                        if DBG.get('rw_stage', 9) < 8:
                            S.barrier(); continue
                        if DBG.get('rw_post', 99) > 0:
                            MM(PS[0][:, :n], cst[:, C_BLK:C_BLK + 128], ysb[:, :n], [cst, ysb], [PS[0]])
                        if DBG.get('rw_post', 99) > 1:
                            A("act", tA[:, :n], ysb[:, :n], AF.Square, [ysb], [tA])
                        if DBG.get('rw_post', 99) > 2:
                            MM(PS[1][:, :n], cst[:, C_BLK:C_BLK + 128], tA[:, :n], [cst, tA], [PS[1]])
                        if DBG.get('rw_post', 99) > 3:
                            TS(cw[:, :n], PS[0][:, :n], 1.0 / 64, None, ALU.mult, None, [PS[0]], [cw])
                        if DBG.get('rw_post', 99) > 4:
                            TTo(tA[:, :n], cw[:, :n], cw[:, :n], ALU.mult, [cw], [tA])
                        if DBG.get('rw_post', 99) > 5:
                            STT(tA[:, :n], PS[1][:, :n], 1.0 / 64, tA[:, :n], ALU.mult, ALU.subtract, [PS[1], tA], [tA])
                        if DBG.get('rw_post', 99) > 6:
                            TS(tA[:, :n], tA[:, :n], 0.0, 64e-5, ALU.max, ALU.add, [tA], [tA])
                        if DBG.get('rw_post', 99) > 7:
                            A("act", tA[:, :n], tA[:, :n], AF.Sqrt, [tA], [tA])
                        if DBG.get('rw_post', 99) > 8:
                            S.op("dve", lambda e: e.reciprocal(out=tA[:, :n], in_=tA[:, :n]), reads=[tA], writes=[tA])
                        if DBG.get('rw_post', 99) > 9:
                            TTo(ysb[:, :n], ysb[:, :n], cw[:, :n], ALU.subtract, [ysb, cw], [ysb])
                        if DBG.get('rw_post', 99) > 10:
                            if DBG.get('exp1'):
                                TTo(ysb[:, :n], ysb[:, :n], cw[:, :n], ALU.mult, [ysb, cw], [ysb])
                            else:
                                TTo(ysb[:, :n], ysb[:, :n], tA[:, :n], ALU.mult, [ysb, tA], [ysb])
                        if DBG.get('rw_post', 99) > 11:
                            TS(ysb[:, :n], ysb[:, :n], V_("lnw", pc), V_("lnb", pc), ALU.mult, ALU.add, [ysb, vecs], [ysb])
                        if DBG.get('rw_post', 99) > 12:
                            TTo(ysb[:, :n], ysb[:, :n], tB_[:, :n], ALU.add, [ysb, tB_], [ysb])
                        if DBG.get('rw_post', 99) > 13:
                            TTo(o_all[:, 2 + pc, :n], ysb[:, :n], gg[:, :n], ALU.mult, [ysb, gg], [o_all])
                        S.barrier()
                if (not samp) and bi == NPB - 1:
                    S.dma("sp", rw_p_d[l].rearrange("(pc hb) k v -> (hb k) pc v", hb=2), H_rw[l][:], reads=[H_rw[l]])
                S.barrier()


        MLA_SCALE = float((64 + 32) ** -0.5)
        knope_h, v_h, kpe_h = [], [], []
        kv_es = contextlib.ExitStack()

        def alloc_kv():
            for l in range(DEPTH):
                knope_h.append(S.sb([128, 4, SEQ], BF16, f"knope{l}", kv_es))
                v_h.append(S.sb([128, 16, 8, 65], BF16, f"vh{l}", kv_es))
                kpe_h.append(S.sb([128, SEQ], BF16, f"kpeh{l}", kv_es))
                S.op("dve", lambda e, l=l: e.memset(v_h[l][:, :, :, 64:65], 1.0), writes=[v_h[l]])

        def norm_fm(src, nch, ncols, gname, l, dst_f, dst_b, sqm, rs):
            for c in range(nch):
                A("act", sqm[:, c, :ncols], src[:, c, :ncols], AF.Square, [src], [sqm])

            def f(e):
                ins = None
                for c in range(nch):
                    ins = e.matmul(PS[7][:, :ncols], lhsT=ones_bf[:], rhs=sqm[:, c, :ncols], start=(c == 0), stop=(c == nch - 1))
                return ins
            S.op("pe", f, reads=[sqm, ones_bf], writes=[PS[7]])
            A("act", rs[:, :ncols], PS[7][:, :ncols], AF.Sqrt, [PS[7], epsb], [rs], scale=1.0 / (nch * 128), bias=epsb[:, 0:1])
            S.op("dve", lambda e: e.reciprocal(out=rs[:, :ncols], in_=rs[:, :ncols]), reads=[rs], writes=[rs])
            for c in range(nch):
                g_ = vecs[:, VL[f"{gname}_{l}"] + c:VL[f"{gname}_{l}"] + c + 1]
                if dst_f is not None:
                    STT(dst_f[:, c, :ncols], src[:, c, :ncols], g_, rs[:, :ncols], ALU.mult, ALU.mult, [src, vecs, rs], [dst_f])
                    if dst_b is not None:
                        A("act", dst_b[:, c, :ncols], dst_f[:, c, :ncols], AF.Copy, [dst_f], [dst_b])
                else:
                    STT(dst_b[:, c, :ncols], src[:, c, :ncols], g_, rs[:, :ncols], ALU.mult, ALU.mult, [src, vecs, rs], [dst_b])

        def mla(l, bi, ncols, c0, xn, o_all):
            samp = (ncols == NS)
            n = ncols
            with contextlib.ExitStack() as ms_:
                wqb = S.sb([128, 3, 1280], BF16, "mla_wqb", ms_)
                S.dma("pool", wqb[:], wqb_d[l].rearrange("(kc p) n -> p kc n", p=128), writes=[wqb])
                wuk = S.sb([128, 2, 512], BF16, "mla_wuk", ms_)
                wuv = S.sb([128, 2, 512], BF16, "mla_wuv", ms_)
                S.dma("pool", wuk[:], wuk_d[l].rearrange("(kc p) n -> p kc n", p=128), writes=[wuk])
                S.dma("pool", wuv[:], wuv_d[l].rearrange("(kc p) n -> p kc n", p=128), writes=[wuv])
                ropt = S.sb([128, 2, TB], F32, "mla_rope", ms_)
                rc0 = SEQ if samp else c0
                S.dma("sp", ropt[:, :, :n], rope_d[:, :, rc0:rc0 + n], writes=[ropt])
                qn = S.sb([128, 3, TB], BF16, "mla_qn", ms_)
                cb = S.sb([128, 2, TB], BF16, "mla_cb", ms_)
                qnp = S.sb([128, 4, TB], BF16, "mla_qnp", ms_)
                qpe = S.sb([128, 8, TB], BF16, "mla_qpe", ms_)
                rt1 = S.sb([128, TB], F32, "mla_rt1", ms_)
                rt2 = S.sb([128, TB], F32, "mla_rt2", ms_)
                kpef = S.sb([32, TB], F32, "mla_kpef", ms_)
                with contextlib.ExitStack() as ps_:
                    qa_f = S.sb([128, 3, TB], F32, "mla_qaf", ps_)
                    kv_f = S.sb([128, 2, TB], F32, "mla_kvf", ps_)
                    c_f = kv_f
                    sqm = S.sb([128, 3, TB], BF16, "mla_sq", ps_)
                    rs = S.sb([128, TB], F32, "mla_rs", ps_)
                    wb = wgroup(l, 1920, 256)
                    for c in range(3):
                        if c == 2:
                            wb = wgroup(l, 2176, 128)
                        proj_fm(wb, (c % 2) * 128, 128, PS[c % 2], xn, n)
                        A("act", qa_f[:, c, :n], PS[c % 2][:, :n], AF.Copy, [PS[c % 2]], [qa_f])
                    wb = wgroup(l, 2304, 256)
                    for c in range(2):
                        proj_fm(wb, c * 128, 128, PS[2 + c], xn, n)
                        A("act", kv_f[:, c, :n], PS[2 + c][:, :n], AF.Copy, [PS[2 + c]], [kv_f])
                    norm_fm(qa_f, 3, n, "qn", l, None, qn, sqm, rs)
                    norm_fm(kv_f, 2, n, "kvn", l, c_f, cb, sqm, rs)
                    cdst = ckv_s_d[l] if samp else ckv_p_d[l][:, c0:c0 + n]
                    S.dma("sp", cdst.rearrange("(c p) t -> p c t", p=128), c_f[:, :, :n], reads=[c_f])
                    wb = wgroup(l, 2560, 96)
                    proj_fm(wb, 0, 64, PS[4], xn, n)
                    proj_fm(wb, 32, 64, PS[5], xn, n)
                    TTo(rt1[0:32, :n], PS[4][0:32, :n], ropt[0:32, 0, :n], ALU.mult, [PS[4], ropt], [rt1])
                    TTo(rt2[0:32, :n], PS[5][0:32, :n], ropt[0:32, 1, :n], ALU.mult, [PS[5], ropt], [rt2])
                    TTo(kpef[:, :n], rt1[0:32, :n], rt2[0:32, :n], ALU.add, [rt1, rt2], [kpef])
                    if DBG.get("kpe_dbg") == 1:
                        S.op("dve", lambda e: e.tensor_copy(out=kpef[:, :n], in_=PS[4][0:32, :n]), reads=[PS[4]], writes=[kpef])
                    if DBG.get("kpe_dbg") == 2:
                        S.op("dve", lambda e: e.tensor_copy(out=kpef[:, :n], in_=ropt[0:32, 0, :n]), reads=[ropt], writes=[kpef])
                    kdst = kpe_s_d[l] if samp else kpe_p_d[l][:, c0:c0 + n]
                    S.dma("sp", kdst, kpef[:, :n], reads=[kpef])
                    S.barrier()
                for j in range(4):
                    pp = PS[j % 2]

                    def f(e, j=j, pp=pp):
                        ins = None
                        for kc in range(3):
                            ins = e.matmul(pp[:, :n], lhsT=wqb[:, kc, j * 128:(j + 1) * 128], rhs=qn[:, kc, :n], start=(kc == 0), stop=(kc == 2))
                        return ins
                    S.op("pe", f, reads=[wqb, qn], writes=[pp])
                    A("act", qnp[:, j, :n], pp[:, :n], AF.Copy, [pp], [qnp], scale=MLA_SCALE)
                for h in range(8):
                    pb_ = 0 if samp else (h % 2) * 64
                    pa, pbk = PS[2 + (h % 2) * 2], PS[3 + (h % 2) * 2]

                    def f(e, h=h, pb_=pb_, pa=pa, pbk=pbk):
                        ins = None
                        for which, pt in ((0, pa), (1, pbk)):
                            for kc in range(3):
                                col = 512 + h * 96 + which * 32
                                ins = e.matmul(pt[pb_:pb_ + 64, :n], lhsT=wqb[:, kc, col:col + 64], rhs=qn[:, kc, :n], start=(kc == 0), stop=(kc == 2))
                        return ins
                    S.op("pe", f, reads=[wqb, qn], writes=[pa, pbk])
                    TTo(rt1[pb_:pb_ + 32, :n], pa[pb_:pb_ + 32, :n], ropt[pb_:pb_ + 32, 0, :n], ALU.mult, [pa, ropt], [rt1])
                    TTo(rt2[pb_:pb_ + 32, :n], pbk[pb_:pb_ + 32, :n], ropt[pb_:pb_ + 32, 1, :n], ALU.mult, [pbk, ropt], [rt2])
                    TTo(rt1[pb_:pb_ + 32, :n], rt1[pb_:pb_ + 32, :n], rt2[pb_:pb_ + 32, :n], ALU.add, [rt1, rt2], [rt1])
                    TS(qpe[pb_:pb_ + 32, h, :n], rt1[pb_:pb_ + 32, :n], MLA_SCALE, None, ALU.mult, None, [rt1], [qpe])
                if not samp:
                    mla_prompt_attn(l, bi, c0, wuk, wuv, cb, kpef, qnp, qpe, o_all, ms_)
                elif DBG.get("mla_s", 1):
                    mla_sample_attn(l, wuv, cb, kpef, qnp, qpe, o_all, ms_)
                S.barrier()

        def mla_sample_attn(l, wuv, cb, kpef, qnp, qpe, o_all, ms_):
            NT = 16
            NCH = 128 // NT
            wukT = S.sb([128, 4, 256], BF16, "mla_wukT", ms_)
            S.dma("pool", wukT[:], wukT_d[l], writes=[wukT])
            idx = S.sb([128, 4], I32, "mla_idx", ms_)
            S.dma("sp", idx[:], pt_d[:, :], writes=[idx])
            idxa = S.sb([128, 4, NCH], I32, "mla_idxa", ms_)
            for a in range(NCH):
                TS(idxa[:, :, a], idx[:, :], float(NCH), float(a), ALU.mult, ALU.add, [idx], [idxa])
            qlat = S.sb([128, 2, 8, NS], BF16, "mla_qlat", ms_)
            for hb in range(2):
                base = hb * 64
                pq = PS[hb]

                def f(e, hb=hb, base=base, pq=pq):
                    ins = None
                    for jp in range(4):
                        for rc in range(2):
                            ins = e.matmul(pq[:, (rc * 4 + jp) * 32:(rc * 4 + jp + 1) * 32], lhsT=wukT[base:base + 64, jp, rc * 128:(rc + 1) * 128],
                                           rhs=qnp[base:base + 64, jp, 0:NS], start=True, stop=True)
                    return ins
                S.op("pe", f, reads=[wukT, qnp], writes=[pq])
                A("act", qlat[:, :, :, :].rearrange("p r (j two) t -> p r j two t", two=2)[:, :, :, hb, :],
                  pq[:, 0:256].rearrange("p (r j t) -> p r j t", r=2, j=4), AF.Copy, [pq], [qlat])
            if DBG.get("ms_stage", 9) < 1:
                return
            kpeb = S.sb([32, NS], BF16, "mla_kpeb", ms_)
            A("act", kpeb[:, :], kpef[:, :NS], AF.Copy, [kpef], [kpeb])
            cnew = S.sb([32, 256], BF16, "mla_cnew", ms_)
            for rc in range(2):
                S.op("pe", lambda e, rc=rc: e.transpose(PS[2][:, :].bitcast(BF16)[:NS, rc * 128:(rc + 1) * 128], cb[:, rc, 0:NS], cstb[:, C_ID:C_ID + 128]),
                     reads=[cb, cstb], writes=[PS[2]])
            A("act", cnew[:, :], PS[2][:, :].bitcast(BF16)[:NS, 0:256], AF.Copy, [PS[2]], [cnew])
            cbuf = [S.sb([128, NT * 256], BF16, f"mla_cbuf{i}", ms_) for i in range(2)]
            kbuf = S.sb([128, 128 * 32 + 32], BF16, "mla_kbuf", ms_)
            S.op("dve", lambda e: e.memset(kbuf[:, 4096:4128], 0.0), writes=[kbuf])
            cT = [S.sb([128, 256], BF16, f"mla_cT{i}", ms_) for i in range(2)]
            kT = [S.sb([128, 128], BF16, f"mla_kT{i}", ms_) for i in range(2)]
            qpep = S.sb([128, 8, NS], BF16, "mla_qpep", ms_)
            kpebp = S.sb([128, NS], BF16, "mla_kpebp", ms_)
            for t_ in (kT[0], kT[1], qpep, kpebp):
                S.op("dve", lambda e, t_=t_: e.memset(t_[:], 0.0), writes=[t_])
            A("act", qpep[0:32, :, :], qpe[0:32, :, 0:NS], AF.Copy, [qpe, qpep], [qpep])
            A("act", kpebp[0:32, :], kpef[:, :NS], AF.Copy, [kpef, kpebp], [kpebp])
            Pt = [S.sb([128, 64], BF16, f"mla_Pt{i}", ms_) for i in range(2)]
            Ptn = S.sb([32, 64], BF16, "mla_Ptn", ms_)
            olat = S.sb([64, 256], BF16, "mla_olat", ms_)
            olatT = S.sb([128, 2, 64], BF16, "mla_olatT", ms_)
            rcs = S.sb([64, 1], F32, "mla_rcs", ms_)
            ckv2 = ckvc_d[l].rearrange("n (a t) d -> (n a) (t d)", t=NT)
            kpe2 = kpec_d[l].rearrange("n t d -> n (t d)")
            pool_e = S.eng["pool"]

            def gather(dst, src2, idx_ap, R):
                E = pool_e
                S._wait(E, R, [dst])
                if dst.dchan is None:
                    if dst.name not in S.named:
                        S.named[dst.name] = S.new_chan("d_" + dst.name)
                        S.dchans.append(S.named[dst.name])
                    dst.dchan = S.named[dst.name]
                ch = dst.dchan
                ins = E.b.indirect_dma_start(out=dst[:, 0:src2.shape[1]], out_offset=None, in_=src2, in_offset=bass.IndirectOffsetOnAxis(ap=idx_ap, axis=0))
                ch.count += 16
                ins.then_inc(ch.sem, 16)
                dst.w = (ch, ch.count)
                dst.r = []
                for t in R:
                    t.r.append((ch, ch.count))
            pO, pS_ = PS[7], PS[6]
            it = 0
            if DBG.get("ms_stage", 9) < 2:
                return
            for b in range(4):
                qsl = slice(8 * b, 8 * b + 8)
                gather(kbuf, kpe2, idx[:, b:b + 1], [idx])
                for a in range(NCH):
                    cbf = cbuf[a % 2]
                    gather(cbf, ckv2, idxa[:, b, a:a + 1], [idxa])
                    for tt in range(NT):
                        t = a * NT + tt
                        par = it % 2
                        it += 1
                        pT = PS[2 + par]
                        ctile = cbf[:, tt * 256:(tt + 1) * 256]
                        if DBG.get("ms_stage", 9) < 3:
                            continue

                        def fT(e, pT=pT, ctile=ctile, t=t):
                            e.transpose(pT[:, :].bitcast(BF16)[:, 0:128], ctile[:, 0:128], cstb[:, C_ID:C_ID + 128])
                            e.transpose(pT[:, :].bitcast(BF16)[:, 128:256], ctile[:, 128:256], cstb[:, C_ID:C_ID + 128])
                            return e.transpose(pT[:, :].bitcast(BF16)[:64, 256:384], kbuf[:, t * 32:t * 32 + 64], cstb[:, C_ID:C_ID + 128])
                        S.op("pe", fT, reads=[cbf, kbuf, cstb], writes=[pT])
                        if DBG.get("ms_sub", 9) < 2:
                            continue
                        A("act", cT[par][:, :], pT[:, :].bitcast(BF16)[:, 0:256], AF.Copy, [pT], [cT[par]])
                        if DBG.get("ms_sub", 9) < 3:
                            continue
                        if DBG.get("ms_kt", "act") == "dve":
                            S.op("dve", lambda e, par=par, pT=pT: e.tensor_copy(out=kT[par][0:32, :], in_=pT[:, :].bitcast(BF16)[:32, 256:384]), reads=[pT], writes=[kT[par]])
                        else:
                            A("act", kT[par][0:32, :], pT[:, :].bitcast(BF16)[:32, 256:384], AF.Copy, [pT], [kT[par]])
                        pSc = PS[4 + par]
                        if DBG.get("ms_stage", 9) < 4:
                            continue

                        def fS(e, par=par, pSc=pSc):
                            e.matmul(pSc[:, 0:64], lhsT=cT[par][:, 0:128], rhs=qlat[:, 0, :, qsl], start=True, stop=False)
                            e.matmul(pSc[:, 0:64], lhsT=cT[par][:, 128:256], rhs=qlat[:, 1, :, qsl], start=False, stop=False)
                            return e.matmul(pSc[:, 0:64], lhsT=kT[par][:, :], rhs=qpep[:, :, qsl], start=False, stop=True)
                        S.op("pe", fS, reads=[cT[par], kT[par], qlat, qpep], writes=[pSc])
                        A("act", Pt[par][:, :], pSc[:, 0:64], AF.Exp, [pSc], [Pt[par]])

                        if DBG.get("ms_stage", 9) < 5:
                            continue

                        def fP(e, par=par, ctile=ctile, first=(t == 0)):
                            e.matmul(pO[0:64, 0:256], lhsT=Pt[par][:, :], rhs=ctile, start=first, stop=False)
                            return e.matmul(pS_[0:64, 0:2], lhsT=Pt[par][:, :], rhs=ones_bf[:, 0:2], start=first, stop=False)
                        S.op("pe", fP, reads=[Pt[par], cbf, ones_bf], writes=[pO, pS_])
                if DBG.get("ms_stage", 9) < 6:
                    continue
                pSc = PS[4]

                def fSn(e):
                    e.matmul(pSc[:NS, 0:64], lhsT=cb[:, 0, 0:NS], rhs=qlat[:, 0, :, qsl], start=True, stop=False)
                    e.matmul(pSc[:NS, 0:64], lhsT=cb[:, 1, 0:NS], rhs=qlat[:, 1, :, qsl], start=False, stop=False)
                    return e.matmul(pSc[:NS, 0:64], lhsT=kpebp[:, 0:NS], rhs=qpep[:, :, qsl], start=False, stop=True)
                S.op("pe", fSn, reads=[cb, kpebp, qlat, qpep], writes=[pSc])
                A("act", Ptn[:, :], pSc[:NS, 0:64], AF.Exp, [pSc], [Ptn])
                TTo(Ptn[:, :].rearrange("p (h t) -> p h t", h=8), Ptn[:, :].rearrange("p (h t) -> p h t", h=8),
                    cstb[:NS, C_MS + 8 * b:C_MS + 8 * b + 8].unsqueeze(1).to_broadcast([NS, 8, 8]), ALU.mult, [Ptn, cstb], [Ptn])

                def fPn(e):
                    e.matmul(pO[0:64, 0:256], lhsT=Ptn[:, :], rhs=cnew[:, :], start=False, stop=True)
                    return e.matmul(pS_[0:64, 0:2], lhsT=Ptn[:, :], rhs=ones_bf[:NS, 0:2], start=False, stop=True)
                S.op("pe", fPn, reads=[Ptn, cnew, ones_bf], writes=[pO, pS_])
                S.op("dve", lambda e: e.reciprocal(out=rcs[:, :], in_=pS_[0:64, 0:1]), reads=[pS_], writes=[rcs])
                TS(olat[:, :], pO[0:64, 0:256], rcs[:, 0:1], None, ALU.mult, None, [pO, rcs], [olat])
                for rc in range(2):
                    S.op("pe", lambda e, rc=rc: e.transpose(PS[2][:, :].bitcast(BF16)[:, rc * 64:(rc + 1) * 64], olat[:, rc * 128:(rc + 1) * 128], cstb[:64, C_ID:C_ID + 64]),
                         reads=[olat, cstb], writes=[PS[2]])
                A("act", olatT[:, :, :].rearrange("p a b -> p (a b)"), PS[2][:, :].bitcast(BF16)[:, 0:128], AF.Copy, [PS[2]], [olatT])

                def fO(e, b=b):
                    ins = None
                    for h in range(8):
                        jp, base = h // 2, (h % 2) * 64
                        for rc in range(2):
                            ins = e.matmul(PS[0][base:base + 64, jp * 32 + 8 * b:jp * 32 + 8 * b + 8], lhsT=wuv[:, rc, h * 64:(h + 1) * 64],
                                           rhs=olatT[:, rc, h * 8:(h + 1) * 8], start=(rc == 0), stop=(rc == 1))
                    return ins
                S.op("pe", fO, reads=[wuv, olatT], writes=[PS[0]])
            if DBG.get("ms_stage", 9) < 6:
                return
            A("act", o_all[:, 4:8, 0:NS], PS[0][:, 0:128].rearrange("p (j t) -> p j t", j=4), AF.Copy, [PS[0]], [o_all])

        def mla_prompt_attn(l, bi, c0, wuk, wuv, cb, kpef, qnp, qpe, o_all, ms_):
            n = TB
            for j in range(4):
                pp = PS[j % 2]

                def f(e, j=j, pp=pp):
                    ins = None
                    for rc in range(2):
                        ins = e.matmul(pp[:, :n], lhsT=wuk[:, rc, j * 128:(j + 1) * 128], rhs=cb[:, rc, :n], start=(rc == 0), stop=(rc == 1))
                    return ins
                S.op("pe", f, reads=[wuk, cb], writes=[pp])
                A("act", knope_h[l][:, j, c0:c0 + n], pp[:, :n], AF.Copy, [pp], [knope_h[l]])
            A("act", kpe_h[l][0:32, c0:c0 + n], kpef[:, :n], AF.Copy, [kpef], [kpe_h[l]])
            S.dma("sp", kpe_h[l][64:96, c0:c0 + n], kpe_h[l][0:32, c0:c0 + n], reads=[kpe_h[l]], writes=[kpe_h[l]])
            for g in range(4):
                pp = PS[2 + g % 2]

                def f(e, g=g, pp=pp):
                    ins = None
                    for rc in range(2):
                        ins = e.matmul(pp[:, :512], lhsT=cb[:, rc, g * 128:(g + 1) * 128], rhs=wuv[:, rc, :], start=(rc == 0), stop=(rc == 1))
                    return ins
                S.op("pe", f, reads=[wuv, cb], writes=[pp])
                A("act", v_h[l][:, bi * 4 + g, :, 0:64], pp[:, :512].rearrange("p (h d) -> p h d", h=8), AF.Copy, [pp], [v_h[l]])
            QR = 256
            PT = S.sb([128, 16, QR], BF16, "mla_PT", ms_)
            o_tok = S.sb([128, 4, 512], BF16, "mla_otok", ms_)
            rcp = S.sb([128, 1], F32, "mla_rcp", ms_)
            for h in range(8):
                j, base = h // 2, (h % 2) * 64
                pss = (PS[0], PS[1]) if h % 2 == 0 else (PS[2], PS[3])
                for qr in range(TB // QR):
                    q0 = qr * QR
                    kb0 = (c0 + q0) // 128
                    nkb = kb0 + QR // 128
                    for kb in range(nkb):
                        pst = pss[kb % 2]

                        def f(e, kb=kb, pst=pst):
                            e.matmul(pst[:, :QR], lhsT=knope_h[l][base:base + 64, j, kb * 128:(kb + 1) * 128], rhs=qnp[base:base + 64, j, q0:q0 + QR],
                                     start=True, stop=False)
                            return e.matmul(pst[:, :QR], lhsT=kpe_h[l][base:base + 32, kb * 128:(kb + 1) * 128], rhs=qpe[base:base + 32, h, q0:q0 + QR],
                                            start=False, stop=True)
                        S.op("pe", f, reads=[knope_h[l], kpe_h[l], qnp, qpe], writes=[pst])
                        A("act", PT[:, kb, :], pst[:, :QR], AF.Exp, [pst], [PT])
                        dj = kb - kb0
                        if dj >= 0:
                            TTo(PT[:, kb, dj * 128:(dj + 1) * 128], PT[:, kb, dj * 128:(dj + 1) * 128], cstb[:, C_TRI:C_TRI + 128], ALU.mult, [PT, cstb], [PT])
                    for qs in range(QR // 128):
                        po = PS[4 + qs % 2]
                        nk = kb0 + qs + 1

                        def f(e, qs=qs, po=po, nk=nk):
                            ins = None
                            for kb in range(nk):
                                ins = e.matmul(po[:, 0:65], lhsT=PT[:, kb, qs * 128:(qs + 1) * 128], rhs=v_h[l][:, kb, h, :], start=(kb == 0), stop=(kb == nk - 1))
                            return ins
                        S.op("pe", f, reads=[PT, v_h[l]], writes=[po])
                        S.op("dve", lambda e, po=po: e.reciprocal(out=rcp[:, :], in_=po[:, 64:65]), reads=[po], writes=[rcp])
                        TS(o_tok[:, qr * 2 + qs, h * 64:(h + 1) * 64], po[:, 0:64], rcp[:, 0:1], None, ALU.mult, None, [po, rcp], [o_tok])
            for qs in range(4):
                for m in range(4):
                    S.op("pe", lambda e, qs=qs, m=m: e.transpose(PS[6][:, :].bitcast(BF16)[:, m * 128:(m + 1) * 128], o_tok[:, qs, m * 128:(m + 1) * 128],
                                                               cstb[:, C_ID:C_ID + 128]), reads=[o_tok, cstb], writes=[PS[6]])
                for m in range(4):
                    A("act", o_all[:, 4 + m, qs * 128:(qs + 1) * 128], PS[6][:, :].bitcast(BF16)[:, m * 128:(m + 1) * 128], AF.Copy, [PS[6]], [o_all])

        def mixer(l, bi, ncols, c0):
            with contextlib.ExitStack() as ms:
                xn = S.sb([128, KC, TB], BF16, "mxn", ms)
                o_all = S.sb([128, KC, TB], BF16, "oall", ms)
                S.op("dve", lambda e: e.memset(o_all[:], 0.0), writes=[o_all])
                with contextlib.ExitStack() as ns:
                    sq = S.sb([128, KC, TB], BF16, "msq", ns)
                    rstd = S.sb([128, TB], F32, "mrstd", ns)
                    rmsnorm(x, ncols, VL[f"nmix_{l}"], xn, sq, rstd, PS[7])
                    S.barrier()
                if DBG.get('hg', 1):
                    hgrn(l, ncols, xn, o_all)
                if DBG.get('rw', 1):
                    rwkv(l, bi, ncols, xn, o_all)
                if DBG.get('mla', 1):
                    mla(l, bi, ncols, c0, xn, o_all)
                with contextlib.ExitStack() as ws:
                    wout = S.sb([128, KC, D], BF16, "wout", ws)
                    S.dma("pool", wout[:], wout_d[l].rearrange("(kc p) n -> p kc n", p=128), writes=[wout])
                    for fo in range(KC):
                        pp = PS[fo % 2]

                        def f(e, fo=fo, pp=pp):
                            ins = None
                            for c in range(KC):
                                ins = e.matmul(pp[:, :ncols], lhsT=wout[:, c, fo * 128:(fo + 1) * 128], rhs=o_all[:, c, :ncols],
                                               start=(c == 0), stop=(c == KC - 1))
                            return ins
                        S.op("pe", f, reads=[wout, o_all], writes=[pp])
                        TTo(x[:, fo, :ncols], pp[:, :ncols], x[:, fo, :ncols], ALU.add, [pp, x], [x])
                    if bi == NPB - 1:
                        S.dma("sp", hg_p_d[l].rearrange("(pc hb) k v -> (hb k) pc v", hb=2), S_hg[l][:], reads=[S_hg[l]])
                    S.barrier()

        blocks = [(xpT, yT_p, b * TB, TB) for b in range(NPB)] + [(xsT, yT_s, 0, NS)]
        alloc_kv()
        for bi, (src, dst, c0, ncols) in enumerate(blocks):
            if bi == NPB:
                S.barrier()
                kv_es.close()
            if bi not in DBG.get('blocks', range(10)):
                continue
            S.dma("sp", x[:, :, :ncols], src.rearrange("(kc p) t -> p kc t", p=128)[:, :, c0:c0 + ncols], writes=[x])
            for l in DBG.get('layers', range(DEPTH)):
                ffn(l, 0, ncols, 0 + l * 8)
                mixer(l, bi, ncols, c0)
                ffn(l, 1, ncols, 32 + l * 8)
            with contextlib.ExitStack() as fs:
                sq = S.sb([128, KC, TB], BF16, "sq", fs)
                rstd = S.sb([128, TB], F32, "rstd", fs)
                yo = S.sb([128, KC, TB], F32, "yo", fs)
                for kc in range(KC):
                    S.op("act", lambda e, kc=kc: e.activation(out=sq[:, kc, :ncols], in_=x[:, kc, :ncols], func=AF.Square),
                         reads=[x], writes=[sq])

                def mm(e):
                    ins = None
                    for kc in range(KC):
                        ins = e.matmul(PS[7][:, :ncols], lhsT=ones_bf[:], rhs=sq[:, kc, :ncols], start=(kc == 0), stop=(kc == KC - 1))
                    return ins
                S.op("pe", mm, reads=[sq, ones_bf], writes=[PS[7]])
                S.op("act", lambda e: e.activation(out=rstd[:, :ncols], in_=PS[7][:, :ncols], func=AF.Sqrt, scale=1.0 / D, bias=epsb[:, 0:1]),
                     reads=[PS[7], epsb], writes=[rstd])
                S.op("dve", lambda e: e.reciprocal(out=rstd[:, :ncols], in_=rstd[:, :ncols]), reads=[rstd], writes=[rstd])
                for kc in range(KC):
                    S.op("dve", lambda e, kc=kc: e.scalar_tensor_tensor(out=yo[:, kc, :ncols], in0=x[:, kc, :ncols],
                                                                      scalar=vecs[:, VL["nfin"] + kc:VL["nfin"] + kc + 1], in1=rstd[:, :ncols],
                                                                      op0=ALU.mult, op1=ALU.mult),
                         reads=[x, vecs, rstd], writes=[yo])
                S.dma("sp", dst.rearrange("(kc p) t -> p kc t", p=128)[:, :, c0:c0 + ncols], yo[:, :, :ncols], reads=[yo])
                S.barrier()
        S.finish()
    return nc


def _fm(v):
    return np.ascontiguousarray(np.asarray(v, np.float32).reshape(-1, 128).T)


def kernel(**inp):
    ncores = DBG.get('ncores', 8)
    nc = build()
    f32 = np.float32
    vecs = np.zeros((128, NV), f32)

    def put(name, v):
        a = _fm(v)
        vecs[:, VL[name]:VL[name] + a.shape[1]] = a
    for l in range(DEPTH):
        put(f"nf1_{l}", inp["norm_ffn1"][l])
        put(f"nmix_{l}", inp["norm_mix"][l])
        put(f"nf2_{l}", inp["norm_ffn2"][l])
        put(f"hglog_{l}", inp["hg_lb_logits"][l])
        put(f"hgn_{l}", np.tile(np.asarray(inp["hg_norm"][l], f32), 2))
        put(f"mu_{l}", inp["rw_mu"][l])
        put(f"w0_{l}", inp["rw_w0"][l])
        put(f"a0_{l}", inp["rw_a0"][l])
        put(f"kk_{l}", inp["rw_kk"][l])
        put(f"ka_{l}", inp["rw_ka"][l])
        put(f"rk_{l}", np.asarray(inp["rw_rk"][l], f32).reshape(-1))
        put(f"lnw_{l}", inp["rw_ln_w"][l])
        put(f"lnb_{l}", inp["rw_ln_b"][l])
        put(f"qn_{l}", inp["mla_q_norm"][l])
        put(f"kvn_{l}", inp["mla_kv_norm"][l])
    put("nfin", inp["norm_final"])
    shared = {"vecs": vecs, "cst": make_consts()}
    shared["lora"] = np.ascontiguousarray(np.stack([np.concatenate([inp["rw_w2"][l], inp["rw_a2"][l], inp["rw_g2"][l]], axis=0)
                                                    for l in range(DEPTH)]), f32)
    wq = np.asarray(inp["mla_wqb"], f32).reshape(DEPTH, 384, 8, 96)
    sw = (np.arange(32) + 16) % 32
    wr = wq[..., 64:]
    shared["wqb"] = np.ascontiguousarray(np.concatenate([wq[..., :64].reshape(DEPTH, 384, 512),
                                                         np.concatenate([wr, wr[..., sw], wr], axis=-1).reshape(DEPTH, 384, 768)], axis=2))
    shared["wuk"] = np.ascontiguousarray(np.asarray(inp["mla_wuk"], f32).reshape(DEPTH, 256, 512))
    shared["wuv"] = np.ascontiguousarray(np.asarray(inp["mla_wuv"], f32).reshape(DEPTH, 256, 512))
    wk = np.asarray(inp["mla_wuk"], f32).transpose(0, 2, 3, 1).reshape(DEPTH, 4, 2, 64, 256)
    shared["wukT"] = np.ascontiguousarray(wk.transpose(0, 2, 3, 1, 4).reshape(DEPTH, 128, 4, 256))
    for l in range(DEPTH):
        shared[f"ckvc{l}"] = np.ascontiguousarray(inp["cache_mla_ckv"][l][:DBG.get('npool', 5120)])
        shared[f"kpec{l}"] = np.ascontiguousarray(inp["cache_mla_kpe"][l][:DBG.get('npool', 5120)])
    past_len = int(inp["page_table"].shape[1]) * int(inp["cache_mla_ckv"].shape[2])
    pos = np.concatenate([np.arange(SEQ, dtype=f32), (past_len + np.tile(np.arange(8, dtype=f32), 4)).astype(f32)])
    inv = np.exp(-np.log(f32(10000.0)) * np.arange(16, dtype=f32) / f32(16)).astype(f32)
    ang = (pos[None, :] * inv[:, None]).astype(f32)
    cos2 = np.concatenate([np.cos(ang), np.cos(ang)], axis=0).astype(f32)
    sin2 = np.concatenate([-np.sin(ang), np.sin(ang)], axis=0).astype(f32)
    rope = np.zeros((128, 2, SEQ + NS), f32)
    for r0 in (0, 64):
        rope[r0:r0 + 32, 0] = cos2
        rope[r0:r0 + 32, 1] = sin2
    shared["rope"] = rope
    for l in range(DEPTH):
        w = np.asarray(inp["w_in"][l], f32)
        shared[f"win{l}"] = np.ascontiguousarray(np.concatenate([w, w[:, 2576:2592], w[:, 2560:2576], w[:, 2560:2592]], axis=1))
        shared[f"wout{l}"] = np.ascontiguousarray(inp["w_out"][l], f32)
    for l in range(DEPTH):
        shared[f"wi{l}0"] = np.ascontiguousarray(inp["ffn1_wi"][l], f32)
        shared[f"wi{l}1"] = np.ascontiguousarray(inp["ffn2_wi"][l], f32)
        shared[f"wo{l}0"] = np.ascontiguousarray(inp["ffn1_wo"][l], f32)
        shared[f"wo{l}1"] = np.ascontiguousarray(inp["ffn2_wo"][l], f32)
    in_maps = []
    for c in range(ncores):
        m = dict(shared)
        m["xpT"] = np.ascontiguousarray(np.asarray(inp["x_prompt"][c], f32).T)
        m["xsT"] = np.ascontiguousarray(np.asarray(inp["x_sample"][4 * c:4 * c + 4], f32).reshape(NS, D).T)
        m["hgst"] = np.ascontiguousarray(np.asarray(inp["state_hgrn"][:, 4 * c:4 * c + 4], f32))
        m["pt"] = np.ascontiguousarray((np.asarray(inp["page_table"][4 * c:4 * c + 4]) % DBG.get('npool', 1 << 30)).astype(np.int32).T)
        m["rwst"] = np.ascontiguousarray(np.asarray(inp["state_rwkv"][:, 4 * c:4 * c + 4], f32).transpose(0, 1, 2, 4, 3))
        sh = np.asarray(inp["state_rwkv_shift"][:, 4 * c:4 * c + 4], f32)
        m["rwsh"] = np.ascontiguousarray(sh.reshape(DEPTH, 4, 7, 128).transpose(0, 3, 2, 1))
        in_maps.append(m)
    res = run_bass_kernel_spmd(nc, in_maps, core_ids=list(range(ncores)))
    R = res.results
    y_p = np.stack([R[c]["yT_p"].T for c in range(ncores)]).astype(f32)
    y_s = np.concatenate([R[c]["yT_s"].T.reshape(4, 8, D) for c in range(ncores)]).astype(f32)
    hg_p = np.stack([R[c]["hg_p"] for c in range(ncores)], axis=1).astype(f32)
    hg_s = np.concatenate([R[c]["hg_s"] for c in range(ncores)], axis=1).astype(f32)
    z = lambda *sh: np.zeros(sh, f32)
    rw_p = np.stack([R[c]["rw_p"].transpose(0, 1, 3, 2) for c in range(ncores)], axis=1).astype(f32)
    rw_s = np.concatenate([R[c]["rw_s"].transpose(0, 1, 2, 4, 3) for c in range(ncores)], axis=1).astype(f32)
    sh_p = np.stack([R[c]["sh_p"].transpose(0, 2, 1).reshape(DEPTH, 896) for c in range(ncores)], axis=1).astype(f32)
    sh_s = np.concatenate([R[c]["sh_s"].transpose(0, 3, 2, 1).reshape(DEPTH, 4, 896) for c in range(ncores)], axis=1).astype(f32)
    ckv_p = np.stack([R[c]["ckv_p"].transpose(0, 2, 1) for c in range(ncores)], axis=1).astype(f32)
    ckv_s = np.concatenate([R[c]["ckv_s"].transpose(0, 2, 1).reshape(DEPTH, 4, 8, 256) for c in range(ncores)], axis=1).astype(f32)
    kpe_p = np.stack([R[c]["kpe_p"].transpose(0, 2, 1) for c in range(ncores)], axis=1).astype(f32)
    kpe_s = np.concatenate([R[c]["kpe_s"].transpose(0, 2, 1).reshape(DEPTH, 4, 8, 32) for c in range(ncores)], axis=1).astype(f32)
    return (y_p, y_s, hg_p, hg_s, rw_p, rw_s, sh_p, sh_s, ckv_p, ckv_s, kpe_p, kpe_s)
```

```python
import contextlib
import numpy as np
import concourse.bass as bass
import concourse.mybir as mybir
from concourse.bass_utils import run_bass_kernel_spmd

F32 = mybir.dt.float32
BF16 = mybir.dt.bfloat16
I32 = mybir.dt.int32
U32 = mybir.dt.uint32
AF = mybir.ActivationFunctionType
ALU = mybir.AluOpType
AX = mybir.AxisListType

D = 1024
SEQ = 2048
DEPTH = 2
DFF = 2816
NJ = DFF // 128
KC = D // 128
EPS = 1e-6
TB = 512
NPB = SEQ // TB
NS = 32
IN_COLS = 2592
GJ = 2


WIN_COLS = 2656
C_ID, C_MP, C_MPS, C_BLK, C_SEGP, C_SEGS, C_MS, C_MSS, C_ROWS = 0, 128, 256, 384, 512, 1024, 1056, 1088, 1120
C_MPL, C_MSL, C_TRI = 1124, 1252, 1284
NCST = 1412


def vec_layout():
    L = {}
    c = [0]

    def add(name, n):
        L[name] = c[0]
        c[0] += n
    for l in range(DEPTH):
        add(f"nf1_{l}", 8)
    for l in range(DEPTH):
        add(f"nmix_{l}", 8)
    for l in range(DEPTH):
        add(f"nf2_{l}", 8)
    add("nfin", 8)
    for l in range(DEPTH):
        add(f"hglog_{l}", 2)
    for l in range(DEPTH):
        add(f"hgn_{l}", 1)
    for l in range(DEPTH):
        for nm, n in (("mu", 7), ("w0", 2), ("a0", 2), ("kk", 2), ("ka", 2), ("rk", 2), ("lnw", 2), ("lnb", 2), ("qn", 3), ("kvn", 2)):
            add(f"{nm}_{l}", n)
    L["_n"] = c[0]
    return L


VL = vec_layout()
NV = VL["_n"]


def make_consts():
    c = np.zeros((128, NCST), np.float32)
    i = np.arange(128)
    c[:, C_ID:C_ID + 128] = np.eye(128)
    same = (i[:, None] // 64) == (i[None, :] // 64)
    c[:, C_MP:C_MP + 128] = same & (i[:, None] <= i[None, :])
    c[:, C_MPS:C_MPS + 128] = same & (i[:, None] < i[None, :])
    c[:, C_BLK:C_BLK + 128] = same
    t = np.arange(512)
    c[:, C_SEGP:C_SEGP + 512] = (t % 64 != 0)[None, :]
    t = np.arange(32)
    c[:, C_SEGS:C_SEGS + 32] = (t % 8 != 0)[None, :]
    j = np.arange(32)
    same8 = (j[:, None] // 8) == (j[None, :] // 8)
    c[:32, C_MS:C_MS + 32] = same8 & (j[:, None] <= j[None, :])
    c[:32, C_MSS:C_MSS + 32] = same8 & (j[:, None] < j[None, :])
    for b in range(4):
        c[8 * b:8 * b + 8, C_ROWS + b] = 1.0
    c[:, C_MPL:C_MPL + 128] = same & (i[:, None] > i[None, :])
    c[:32, C_MSL:C_MSL + 32] = same8 & (j[:, None] > j[None, :])
    c[:, C_TRI:C_TRI + 128] = (i[:, None] <= i[None, :])
    return c


DBG = {}


class Chan:
    def __init__(self, sem):
        self.sem = sem
        self.count = 0


class Eng:
    def __init__(self, name, b, chan, selfsync):
        self.name = name
        self.b = b
        self.chan = chan
        self.seen = {}
        self.selfsync = selfsync


class TT:
    def __init__(self, t, name):
        self.t = t
        self.name = name
        self.w = None
        self.r = []
        self.dchan = None

    def __getitem__(self, idx):
        return self.t[idx]


class Sched:
    def __init__(self, nc, es):
        self.nc = nc
        self.es = es
        self.nsem = 0
        self.eng = {}
        for name, b, ss in (("pe", nc.tensor, False), ("act", nc.scalar, DBG.get("ss", True)), ("dve", nc.vector, DBG.get("ss", True)),
                            ("pool", nc.gpsimd, True), ("sp", nc.sync, False)):
            self.eng[name] = Eng(name, b, self.new_chan("e_" + name), ss)
        self.dchans = []
        self.named = {}
        self.rec = None
        self.ntile = 0

    def new_chan(self, name):
        self.nsem += 1
        return Chan(self.es.enter_context(self.nc.semaphore(name)))

    def sb(self, shape, dt, name, es=None):
        self.ntile += 1
        t = (es or self.es).enter_context(self.nc.sbuf_tensor(f"{name}_{self.ntile}", list(shape), dt))
        return TT(t, name)

    def ps(self, shape, dt, name, es=None):
        self.ntile += 1
        t = (es or self.es).enter_context(self.nc.psum_tensor(f"{name}_{self.ntile}", list(shape), dt))
        return TT(t, name)

    def _wait(self, E, reads, writes):
        deps = {}

        def add(d):
            ch, cnt = d
            if deps.get(ch, 0) < cnt:
                deps[ch] = cnt
        for t in reads:
            if t.w is not None:
                add(t.w)
        for t in writes:
            if t.w is not None:
                add(t.w)
            for r in t.r:
                add(r)
        for ch, cnt in deps.items():
            if ch is E.chan and not E.selfsync:
                continue
            if E.seen.get(ch, 0) < cnt:
                E.b.wait_ge(ch.sem, cnt)
                E.seen[ch] = cnt

    def op(self, eng, fn, reads=(), writes=()):
        if self.rec is not None:
            rec, self_ = self.rec, self
            rec.append(lambda: self_._replay(self_.op, eng, fn, reads, writes))
            return None
        E = self.eng[eng]
        self._wait(E, reads, writes)
        if E.selfsync and DBG.get("serial", True) and E.seen.get(E.chan, 0) < E.chan.count:
            E.b.wait_ge(E.chan.sem, E.chan.count)
            E.seen[E.chan] = E.chan.count
        ins = fn(E.b)
        E.chan.count += 1
        ins.then_inc(E.chan.sem, 1)
        stamp = (E.chan, E.chan.count)
        for t in writes:
            t.w = stamp
            t.r = []
        for t in reads:
            t.r.append(stamp)
        return ins

    def _replay(self, f, *a, **kw):
        saved, self.rec = self.rec, None
        try:
            return f(*a, **kw)
        finally:
            self.rec = saved

    def dma(self, q, out, in_, reads=(), writes=(), **kw):
        if self.rec is not None:
            rec, self_ = self.rec, self
            rec.append(lambda: self_._replay(self_.dma, q, out, in_, reads, writes, **kw))
            return None
        E = self.eng[q]
        self._wait(E, reads, writes)
        owner = (list(writes) + list(reads))[0]
        if owner.dchan is None:
            if owner.name not in self.named:
                self.named[owner.name] = self.new_chan("d_" + owner.name)
                self.dchans.append(self.named[owner.name])
            owner.dchan = self.named[owner.name]
        ch = owner.dchan
        ins = E.b.dma_start(out=out, in_=in_, **kw)
        ch.count += 16
        ins.then_inc(ch.sem, 16)
        stamp = (ch, ch.count)
        for t in writes:
            t.w = stamp
            t.r = []
        for t in reads:
            t.r.append(stamp)
        return ins

    def barrier(self, engines=("pe", "act", "dve", "sp", "pool"), dchans=True):
        chans = [self.eng[e].chan for e in self.eng] + (self.dchans if dchans else [])
        for e in engines:
            E = self.eng[e]
            for ch in chans:
                if ch is E.chan:
                    continue
                if E.seen.get(ch, 0) < ch.count:
                    E.b.wait_ge(ch.sem, ch.count)
                    E.seen[ch] = ch.count

    def finish(self):
        self.barrier(engines=("sp",))


def build():
    nc = bass.Bass("TRN2", target_bir_lowering=False)
    dt_in = lambda name, shape, dt=F32: nc.dram_tensor(name, list(shape), dt, kind="ExternalInput").ap()
    dt_out = lambda name, shape, dt=F32: nc.dram_tensor(name, list(shape), dt, kind="ExternalOutput").ap()

    xpT = dt_in("xpT", [D, SEQ])
    xsT = dt_in("xsT", [D, NS])
    wi_d = [[dt_in(f"wi{l}{f}", [D, 2 * DFF]) for f in range(2)] for l in range(DEPTH)]
    wo_d = [[dt_in(f"wo{l}{f}", [DFF, D]) for f in range(2)] for l in range(DEPTH)]
    vec_d = dt_in("vecs", [128, NV])
    cst_d = dt_in("cst", [128, NCST])
    win_d = [dt_in(f"win{l}", [D, WIN_COLS]) for l in range(DEPTH)]
    wout_d = [dt_in(f"wout{l}", [D, D]) for l in range(DEPTH)]
    hgst_d = dt_in("hgst", [DEPTH, 4, 4, 64, 64])
    hg_p_d = dt_out("hg_p", [DEPTH, 4, 64, 64])
    lora_d = dt_in("lora", [DEPTH, 128, 256])
    wqb_d = dt_in("wqb", [DEPTH, 384, 1280])
    wuk_d = dt_in("wuk", [DEPTH, 256, 512])
    wuv_d = dt_in("wuv", [DEPTH, 256, 512])
    rope_d = dt_in("rope", [128, 2, SEQ + NS])
    NPOOL = DBG.get('npool', 5120)
    ckvc_d = [dt_in(f"ckvc{l}", [NPOOL, 128, 256]) for l in range(DEPTH)]
    kpec_d = [dt_in(f"kpec{l}", [NPOOL, 128, 32]) for l in range(DEPTH)]
    pt_d = dt_in("pt", [128, 4], I32)
    wukT_d = dt_in("wukT", [DEPTH, 128, 4, 256])
    ckv_p_d = dt_out("ckv_p", [DEPTH, 256, SEQ])
    ckv_s_d = dt_out("ckv_s", [DEPTH, 256, NS])
    kpe_p_d = dt_out("kpe_p", [DEPTH, 32, SEQ])
    kpe_s_d = dt_out("kpe_s", [DEPTH, 32, NS])
    rwst_d = dt_in("rwst", [DEPTH, 4, 4, 64, 64])
    rwsh_d = dt_in("rwsh", [DEPTH, 128, 7, 4])
    rw_p_d = dt_out("rw_p", [DEPTH, 4, 64, 64])
    rw_s_d = dt_out("rw_s", [DEPTH, 4, 4, 64, 64])
    sh_p_d = dt_out("sh_p", [DEPTH, 128, 7])
    sh_s_d = dt_out("sh_s", [DEPTH, 128, 7, 4])
    hg_s_d = dt_out("hg_s", [DEPTH, 4, 4, 64, 64])
    yT_p = dt_out("yT_p", [D, SEQ])
    yT_s = dt_out("yT_s", [D, NS])

    with contextlib.ExitStack() as es:
        S = Sched(nc, es)
        vecs = S.sb([128, NV], F32, "vecs")
        S.dma("sp", vecs[:], vec_d[:, :], writes=[vecs])
        cst = S.sb([128, NCST], F32, "cst")
        S.dma("sp", cst[:], cst_d[:, :], writes=[cst])
        cstb = S.sb([128, NCST], BF16, "cstb")
        S.op("dve", lambda e: e.tensor_copy(out=cstb[:], in_=cst[:]), reads=[cst], writes=[cstb])
        ones_bf = S.sb([128, 128], BF16, "ones")
        S.op("dve", lambda e: e.memset(ones_bf[:], 1.0), writes=[ones_bf])

        x = S.sb([128, KC, TB], F32, "x")
        NWI, NWO = 2, 4
        wi_buf = [S.sb([128, KC, 2, GJ * 128], BF16, f"wi{i}") for i in range(NWI)]
        wo_buf = [S.sb([128, GJ, 512], BF16, f"wo{i}") for i in range(NWO)]
        wi_ctr = [0]
        wo_ctr = [0]
        PS = [S.ps([128, 512], F32, f"ps{i}") for i in range(8)]

        def rmsnorm(xt, ncols, gcol, out_bf, sq, rstd, psb):
            for kc in range(KC):
                S.op("act", lambda e, kc=kc: e.activation(out=sq[:, kc, :ncols], in_=xt[:, kc, :ncols], func=AF.Square),
                     reads=[xt], writes=[sq])

            def mm(e):
                ins = None
                for kc in range(KC):
                    ins = e.matmul(psb[:, :ncols], lhsT=ones_bf[:], rhs=sq[:, kc, :ncols], start=(kc == 0), stop=(kc == KC - 1))
                return ins
            S.op("pe", mm, reads=[sq, ones_bf], writes=[psb])
            S.op("act", lambda e: e.activation(out=rstd[:, :ncols], in_=psb[:, :ncols], func=AF.Sqrt, scale=1.0 / D, bias=epsb[:, 0:1]),
                 reads=[psb, epsb], writes=[rstd])
            S.op("dve", lambda e: e.reciprocal(out=rstd[:, :ncols], in_=rstd[:, :ncols]), reads=[rstd], writes=[rstd])
            for kc in range(KC):
                S.op("dve", lambda e, kc=kc: e.scalar_tensor_tensor(out=out_bf[:, kc, :ncols], in0=xt[:, kc, :ncols],
                                                                  scalar=vecs[:, gcol + kc:gcol + kc + 1], in1=rstd[:, :ncols],
                                                                  op0=ALU.mult, op1=ALU.mult),
                     reads=[xt, vecs, rstd], writes=[out_bf])

        epsb = S.sb([128, 1], F32, "epsb")
        S.op("dve", lambda e: e.memset(epsb[:], EPS), writes=[epsb])

        def ffn(l, f, ncols, gcol):
            with contextlib.ExitStack() as fs:
                xn = S.sb([128, KC, TB], BF16, "xn", fs)
                sq = S.sb([128, KC, TB], BF16, "sq", fs)
                rstd = S.sb([128, TB], F32, "rstd", fs)
                h = [S.sb([128, TB], BF16, f"h{j}", fs) for j in range(NJ)]
                sa = [S.sb([128, TB], BF16, f"sa{i}", fs) for i in range(2)]
                rmsnorm(x, ncols, gcol, xn, sq, rstd, PS[7])
                wi_v = wi_d[l][f].rearrange("(kc p) (two n) -> p kc two n", p=128, two=2)
                for g in range(NJ // GJ):
                    wb = wi_buf[wi_ctr[0] % NWI]
                    wi_ctr[0] += 1
                    for two in range(2):
                        S.dma("pool", wb[:, :, two, :], wi_v[:, :, two, g * GJ * 128:(g + 1) * GJ * 128], writes=[wb])
                    for jj in range(GJ):
                        j = g * GJ + jj
                        pa, pb = PS[(j % 2) * 2], PS[(j % 2) * 2 + 1]

                        def mma(e, wb=wb, jj=jj, pa=pa):
                            ins = None
                            for kc in range(KC):
                                ins = e.matmul(pa[:, :ncols], lhsT=wb[:, kc, 0, jj * 128:(jj + 1) * 128], rhs=xn[:, kc, :ncols],
                                               start=(kc == 0), stop=(kc == KC - 1))
                            return ins

                        def mmb(e, wb=wb, jj=jj, pb=pb):
                            ins = None
                            for kc in range(KC):
                                ins = e.matmul(pb[:, :ncols], lhsT=wb[:, kc, 1, jj * 128:(jj + 1) * 128], rhs=xn[:, kc, :ncols],
                                               start=(kc == 0), stop=(kc == KC - 1))
                            return ins
                        S.op("pe", mma, reads=[wb, xn], writes=[pa])
                        S.op("pe", mmb, reads=[wb, xn], writes=[pb])
                        st = sa[j % 2]
                        S.op("act", lambda e, pa=pa, st=st: e.activation(out=st[:, :ncols], in_=pa[:, :ncols], func=AF.Silu),
                             reads=[pa], writes=[st])
                        S.op("dve", lambda e, pb=pb, st=st, j=j: e.tensor_tensor(out=h[j][:, :ncols], in0=pb[:, :ncols], in1=st[:, :ncols], op=ALU.mult),
                             reads=[pb, st], writes=[h[j]])
                wo_v = wo_d[l][f].rearrange("(j p) n -> p j n", p=128)
                for half in range(2):
                    acc = [PS[4 + i] for i in range(4)]
                    for g in range(NJ // GJ):
                        wb = wo_buf[wo_ctr[0] % NWO]
                        wo_ctr[0] += 1
                        S.dma("pool", wb[:], wo_v[:, g * GJ:(g + 1) * GJ, half * 512:(half + 1) * 512], writes=[wb])
                        for jj in range(GJ):
                            j = g * GJ + jj
                            for fo in range(4):
                                S.op("pe", lambda e, wb=wb, jj=jj, j=j, fo=fo: e.matmul(
                                    acc[fo][:, :ncols], lhsT=wb[:, jj, fo * 128:(fo + 1) * 128], rhs=h[j][:, :ncols],
                                    start=(j == 0), stop=(j == NJ - 1)),
                                    reads=[wb, h[j]], writes=[acc[fo]])
                    for fo in range(4):
                        kc = half * 4 + fo
                        S.op("dve", lambda e, fo=fo, kc=kc: e.scalar_tensor_tensor(
                            out=x[:, kc, :ncols], in0=acc[fo][:, :ncols], scalar=0.5, in1=x[:, kc, :ncols],
                            op0=ALU.mult, op1=ALU.add), reads=[acc[fo], x], writes=[x])
                S.barrier()


        NWB = 2
        win_buf = [S.sb([128, KC, 256], BF16, f"win{i}") for i in range(NWB)]
        win_ctr = [0]
        lbt = S.sb([128, 8], F32, "lbt")
        S.op("dve", lambda e: e.memset(lbt[:], 0.0), writes=[lbt])
        S.op("dve", lambda e: e.tensor_tensor(out=lbt[:, 2:4], in0=vecs[:, VL["hglog_1"]:VL["hglog_1"] + 2],
                                              in1=vecs[:, VL["hglog_0"]:VL["hglog_0"] + 2], op=ALU.subtract),
             reads=[vecs, lbt], writes=[lbt])
        S.op("act", lambda e: e.activation(out=lbt[:, 2:4], in_=lbt[:, 2:4], func=AF.Sigmoid), reads=[lbt], writes=[lbt])
        S.op("dve", lambda e: e.tensor_scalar(out=lbt[:, 4:8], in0=lbt[:, 0:4], scalar1=-1.0, scalar2=1.0, op0=ALU.mult, op1=ALU.add),
             reads=[lbt], writes=[lbt])
        S_hg = [S.sb([128, 2, 64], F32, f"S_hg{l}") for l in range(DEPTH)]
        for l in range(DEPTH):
            S.op("dve", lambda e, l=l: e.memset(S_hg[l][:], 0.0), writes=[S_hg[l]])

        def wgroup(l, col0, ncol):
            wb = win_buf[win_ctr[0] % NWB]
            win_ctr[0] += 1
            S.dma("pool", wb[:, :, :ncol], win_d[l].rearrange("(kc p) n -> p kc n", p=128)[:, :, col0:col0 + ncol], writes=[wb])
            return wb

        def proj_fm(wb, wcol, M, ps, xn, ncols, pbase=0):
            def f(e):
                ins = None
                for kc in range(KC):
                    ins = e.matmul(ps[pbase:pbase + M, :ncols], lhsT=wb[:, kc, wcol:wcol + M], rhs=xn[:, kc, :ncols],
                                   start=(kc == 0), stop=(kc == KC - 1))
                return ins
            S.op("pe", f, reads=[wb, xn], writes=[ps])

        def A(eng, out, in_, func, R, W, **kw):
            S.op(eng, lambda e: e.activation(out=out, in_=in_, func=func, **kw), reads=R, writes=W)

        def TTo(out, in0, in1, op, R, W, eng="dve"):
            S.op(eng, lambda e: e.tensor_tensor(out=out, in0=in0, in1=in1, op=op), reads=R, writes=W)

        def TS(out, in0, s1, s2, op0, op1, R, W, eng="dve"):
            if op1 is None:
                S.op(eng, lambda e: e.tensor_scalar(out=out, in0=in0, scalar1=s1, scalar2=None, op0=op0), reads=R, writes=W)
            else:
                S.op(eng, lambda e: e.tensor_scalar(out=out, in0=in0, scalar1=s1, scalar2=s2, op0=op0, op1=op1), reads=R, writes=W)

        def STT(out, in0, scalar, in1, op0, op1, R, W):
            S.op("dve", lambda e: e.scalar_tensor_tensor(out=out, in0=in0, scalar=scalar, in1=in1, op0=op0, op1=op1), reads=R, writes=W)

        def MM(out, lhsT, rhs, R, W, start=True, stop=True):
            S.op("pe", lambda e: e.matmul(out, lhsT=lhsT, rhs=rhs, start=start, stop=stop), reads=R, writes=W)

        def hgrn(l, ncols, xn, o_all):
            samp = (ncols == NS)
            C = 8 if samp else 64
            nseg = ncols // C
            groups = [(0, 32, [0, 1, 2, 3])] if samp else [(g * 128, 128, [2 * g, 2 * g + 1]) for g in range(4)]
            ng = len(groups)
            segm = cst[:, C_SEGS:C_SEGS + 32] if samp else cst[:, C_SEGP:C_SEGP + 512]
            mcol = C_MS if samp else C_MP
            with contextlib.ExitStack() as hs:
                qT = S.sb([128, 2, TB], F32, "hq", hs)
                sg_ = S.sb([128, 2, TB], F32, "hsig", hs)
                sn = S.sb([128, 2, TB], F32, "hsn", hs)
                bb = S.sb([128, 2, TB], F32, "hb", hs)
                t1 = S.sb([128, 2, TB], F32, "ht1", hs)
                t2 = S.sb([128, 2, TB], F32, "ht2", hs)
                Qi = S.sb([128, 2, TB], BF16, "hQi", hs)
                Ki = S.sb([128, 2, TB], BF16, "hKi", hs)
                Qs = S.sb([128, 2, TB], BF16, "hQs", hs)
                Kd = S.sb([128, 2, TB], BF16, "hKd", hs)
                gate = S.sb([128, 2, TB], F32, "hgate", hs)
                Vt = S.sb([128, 4, 256], BF16, "hVt", hs)
                KdT = S.sb([128, 4, 2, 128], BF16, "hKdT", hs)
                KdTm = S.sb([32, 4, 2, 128], BF16, "hKdTm", hs)
                attm = S.sb([128, 4, 2, 2, 128], BF16, "hattm", hs)
                Sbf = S.sb([128, 8, 2, 64], BF16, "hSbf", hs)
                dseg = S.sb([128, 2, 8], F32, "hdseg", hs)
                osb = S.sb([128, 2, TB], F32, "hosb", hs)
                o2 = S.sb([128, 2, TB], BF16, "ho2", hs)
                Ssm = [S.sb([128, 2, 64], F32, f"hSs{b}", hs) for b in range(4)] if samp else None
                lbc = lambda pc: lbt[:, l * 2 + pc:l * 2 + pc + 1]
                omlc = lambda pc: lbt[:, 4 + l * 2 + pc:4 + l * 2 + pc + 1]
                wb = wgroup(l, 0, 256)
                for pc in range(2):
                    proj_fm(wb, pc * 128, 128, PS[pc], xn, ncols)
                    A("act", qT[:, pc, :ncols], PS[pc][:, :ncols], AF.Copy, [PS[pc]], [qT])
                wb = wgroup(l, 256, 256)
                for pc in range(2):
                    p_ = PS[2 + pc]
                    proj_fm(wb, pc * 128, 128, p_, xn, ncols)
                    A("act", sg_[:, pc, :ncols], p_[:, :ncols], AF.Sigmoid, [p_], [sg_])
                    A("act", sn[:, pc, :ncols], p_[:, :ncols], AF.Sigmoid, [p_], [sn], scale=-1.0)
                    TS(sg_[:, pc, :ncols], sg_[:, pc, :ncols], omlc(pc), lbc(pc), ALU.mult, ALU.add, [sg_, lbt], [sg_])
                    TS(sg_[:, pc, :ncols], sg_[:, pc, :ncols], 1e-30, None, ALU.max, None, [sg_], [sg_])
                    A("act", sg_[:, pc, :ncols], sg_[:, pc, :ncols], AF.Ln, [sg_], [sg_])
                    TS(sn[:, pc, :ncols], sn[:, pc, :ncols], omlc(pc), None, ALU.mult, None, [sn, lbt], [sn])
                    S.op("dve", lambda e, pc=pc: e.tensor_tensor_scan(out=bb[:, pc, :ncols], data0=segm[:, :ncols], data1=sg_[:, pc, :ncols],
                                                                     initial=0.0, op0=ALU.mult, op1=ALU.add), reads=[sg_, cst], writes=[bb])
                    bv = bb[:, pc, :ncols].rearrange("p (s c) -> p s c", c=C)
                    v3 = lambda t, pc=pc: t[:, pc, :ncols].rearrange("p (s c) -> p s c", c=C)
                    TTo(v3(t1), bv, bv[:, :, C // 2 - 1:C // 2].to_broadcast([128, nseg, C]), ALU.subtract, [bb], [t1])
                    A("act", t2[:, pc, :ncols], t1[:, pc, :ncols], AF.Exp, [t1], [t2])
                    TTo(Qi[:, pc, :ncols], qT[:, pc, :ncols], t2[:, pc, :ncols], ALU.mult, [qT, t2], [Qi])
                    A("act", t2[:, pc, :ncols], t1[:, pc, :ncols], AF.Exp, [t1], [t2], scale=-1.0)
                    TTo(Ki[:, pc, :ncols], sn[:, pc, :ncols], t2[:, pc, :ncols], ALU.mult, [sn, t2], [Ki])
                    A("act", t2[:, pc, :ncols], bb[:, pc, :ncols], AF.Exp, [bb], [t2])
                    TTo(Qs[:, pc, :ncols], qT[:, pc, :ncols], t2[:, pc, :ncols], ALU.mult, [qT, t2], [Qs])
                    TTo(v3(t1), bv, bv[:, :, C - 1:C].to_broadcast([128, nseg, C]), ALU.subtract, [bb], [t1])
                    A("act", t2[:, pc, :ncols], t1[:, pc, :ncols], AF.Exp, [t1], [t2], scale=-1.0)
                    TTo(Kd[:, pc, :ncols], sn[:, pc, :ncols], t2[:, pc, :ncols], ALU.mult, [sn, t2], [Kd])
                    A("act", dseg[:, pc, :nseg], bv[:, :, C - 1], AF.Exp, [bb], [dseg])
                wb = wgroup(l, 512, 256)
                for g, (c0g, gsz, segs) in enumerate(groups):
                    def f(e, c0g=c0g, gsz=gsz, wb=wb):
                        ins = None
                        for kc in range(KC):
                            ins = e.matmul(PS[4][:gsz, 0:256], lhsT=xn[:, kc, c0g:c0g + gsz], rhs=wb[:, kc, 0:256], start=(kc == 0), stop=(kc == KC - 1))
                        return ins
                    S.op("pe", f, reads=[xn, wb], writes=[PS[4]])
                    A("act", Vt[:gsz, g, :], PS[4][:gsz, 0:256], AF.Copy, [PS[4]], [Vt])
                wb = wgroup(l, 768, 256)
                for pc in range(2):
                    proj_fm(wb, pc * 128, 128, PS[pc], xn, ncols)
                    A("act", gate[:, pc, :ncols], PS[pc][:, :ncols], AF.Silu, [PS[pc]], [gate])
                if DBG.get('hg_stage', 9) < 1:
                    S.barrier()
                    return
                psT = PS[5]
                for g, (c0g, gsz, segs) in enumerate(groups):
                    for pc in range(2):
                        S.op("pe", lambda e, g=g, pc=pc, c0g=c0g, gsz=gsz: e.transpose(
                            psT[:, :].bitcast(BF16)[:gsz, pc * 128:(pc + 1) * 128], Kd[:, pc, c0g:c0g + gsz], cstb[:, C_ID:C_ID + 128]),
                            reads=[Kd, cstb], writes=[psT])
                    S.op("dve", lambda e, g=g, gsz=gsz: e.tensor_copy(out=KdT[:gsz, g, :, :].rearrange("p a b -> p (a b)"),
                                                                     in_=psT[:, :].bitcast(BF16)[:gsz, 0:256]), reads=[psT], writes=[KdT])
                if samp:
                    for b in range(4):
                        TS(KdTm[:, b, :, :].rearrange("p a b -> p (a b)"), KdT[:32, 0, :, :].rearrange("p a b -> p (a b)"),
                           cst[:32, C_ROWS + b:C_ROWS + b + 1], None, ALU.mult, None, [KdT, cst], [KdTm])
                if DBG.get('hg_stage', 9) < 2:
                    S.barrier()
                    return
                psA = [PS[6], PS[5]]
                for g, (c0g, gsz, segs) in enumerate(groups):
                    for hb in range(2):
                        base = hb * 64

                        def f(e, c0g=c0g, gsz=gsz, hb=hb, base=base):
                            ins = None
                            for pc in range(2):
                                ins = e.matmul(psA[hb][:gsz, pc * 128:pc * 128 + gsz], lhsT=Ki[base:base + 64, pc, c0g:c0g + gsz],
                                               rhs=Qi[base:base + 64, pc, c0g:c0g + gsz], start=True, stop=True)
                            return ins
                        S.op("pe", f, reads=[Ki, Qi], writes=[psA[hb]])
                        TTo(attm[:gsz, g, hb, :, :gsz], psA[hb][:gsz, 0:256].rearrange("p (h t) -> p h t", h=2)[:, :, :gsz],
                            cst[:gsz, mcol:mcol + gsz].unsqueeze(1).to_broadcast([gsz, 2, gsz]), ALU.mult, [psA[hb], cst], [attm])
                if DBG.get('hg_stage', 9) < 3:
                    S.barrier()
                    return
                psU = PS[7]
                if not samp:
                    Sst = S_hg[l]
                    for seg in range(nseg):
                        g, r0 = seg // 2, (seg % 2) * 64
                        psU = PS[7] if r0 == 0 else PS[4]
                        A("act", Sbf[:, seg, :, :].rearrange("p a b -> p (a b)"), Sst[:, :, :].rearrange("p a b -> p (a b)"), AF.Copy, [Sst], [Sbf])

                        def f(e, g=g, r0=r0):
                            ins = None
                            for h in range(4):
                                pc, base = h // 2, (h % 2) * 64
                                ins = e.matmul(psU[base:base + 64, pc * 64:(pc + 1) * 64], lhsT=KdT[r0:r0 + 64, g, pc, base:base + 64],
                                               rhs=Vt[r0:r0 + 64, g, h * 64:(h + 1) * 64], start=True, stop=True)
                            return ins
                        S.op("pe", f, reads=[KdT, Vt], writes=[psU])
                        for pc in range(2):
                            STT(Sst[:, pc, :], Sst[:, pc, :], dseg[:, pc, seg:seg + 1], psU[:, pc * 64:(pc + 1) * 64], ALU.mult, ALU.add,
                                [Sst, dseg, psU], [Sst])
                else:
                    for b in range(4):
                        Sst = Ssm[b]
                        S.dma("sp", Sst[:], hgst_d[l, b].rearrange("(pc hb) k v -> (hb k) pc v", hb=2), writes=[Sst])
                        A("act", Sbf[:, b, :, :].rearrange("p a b -> p (a b)"), Sst[:, :, :].rearrange("p a b -> p (a b)"), AF.Copy, [Sst], [Sbf])

                        def f(e, b=b):
                            ins = None
                            for h in range(4):
                                pc, base = h // 2, (h % 2) * 64
                                ins = e.matmul(psU[base:base + 64, pc * 64:(pc + 1) * 64], lhsT=KdTm[:32, b, pc, base:base + 64],
                                               rhs=Vt[:32, 0, h * 64:(h + 1) * 64], start=True, stop=True)
                            return ins
                        S.op("pe", f, reads=[KdTm, Vt], writes=[psU])
                        for pc in range(2):
                            STT(Sst[:, pc, :], Sst[:, pc, :], dseg[:, pc, b:b + 1], psU[:, pc * 64:(pc + 1) * 64], ALU.mult, ALU.add,
                                [Sst, dseg, psU], [Sst])
                        S.dma("sp", hg_s_d[l, b].rearrange("(pc hb) k v -> (hb k) pc v", hb=2), Sst[:], reads=[Sst])
                if DBG.get('hg_stage', 9) < 4:
                    S.barrier()
                    return
                for g, (c0g, gsz, segs) in enumerate(groups):
                    for h in range(4):
                        pc, hb, base = h // 2, h % 2, (h % 2) * 64

                        def f(e, g=g, h=h, pc=pc, hb=hb, base=base, c0g=c0g, gsz=gsz, segs=segs):
                            ins = e.matmul(PS[h][base:base + 64, c0g:c0g + gsz], lhsT=Vt[:gsz, g, h * 64:(h + 1) * 64],
                                           rhs=attm[:gsz, g, hb, pc, :gsz], start=True, stop=False)
                            for si, seg in enumerate(segs):
                                ins = e.matmul(PS[h][base:base + 64, seg * C:(seg + 1) * C], lhsT=Sbf[base:base + 64, seg, pc, :],
                                               rhs=Qs[base:base + 64, pc, seg * C:(seg + 1) * C], start=False, stop=(si == len(segs) - 1))
                            return ins
                        S.op("pe", f, reads=[Vt, attm, Sbf, Qs], writes=[PS[h]])
                for pc in range(2):
                    for hb in range(2):
                        h, base = 2 * pc + hb, hb * 64
                        A("act", osb[base:base + 64, pc, :ncols], PS[h][base:base + 64, :ncols], AF.Copy, [PS[h]], [osb])
                        A("act", o2[base:base + 64, pc, :ncols], PS[h][base:base + 64, :ncols], AF.Square, [PS[h]], [o2])
                for pc in range(2):
                    MM(PS[4 + pc][:, :ncols], cstb[:, C_BLK:C_BLK + 128], o2[:, pc, :ncols], [cstb, o2], [PS[4 + pc]])
                    A("act", t1[:, pc, :ncols], PS[4 + pc][:, :ncols], AF.Sqrt, [PS[4 + pc], epsb], [t1], scale=1.0 / 64, bias=epsb[:, 0:1])
                    S.op("dve", lambda e, pc=pc: e.reciprocal(out=t1[:, pc, :ncols], in_=t1[:, pc, :ncols]), reads=[t1], writes=[t1])
                    STT(t2[:, pc, :ncols], osb[:, pc, :ncols], vecs[:, VL[f"hgn_{l}"]:VL[f"hgn_{l}"] + 1], t1[:, pc, :ncols], ALU.mult, ALU.mult,
                        [osb, vecs, t1], [t2])
                    TTo(o_all[:, pc, :ncols], t2[:, pc, :ncols], gate[:, pc, :ncols], ALU.mult, [t2, gate], [o_all])
                S.barrier()


        H_rw = [S.sb([128, 2, 64], F32, f"H_rw{l}") for l in range(DEPTH)]
        rw_prev = [S.sb([128, 8], F32, f"rwprev{l}") for l in range(DEPTH)]
        lora = [S.sb([128, 256], BF16, f"lora{l}") for l in range(DEPTH)]
        omka = S.sb([128, 4], F32, "omka")
        H_rw_v = [[TT(H_rw[l].t, f"H_rw{l}_{hb}") for hb in range(2)] for l in range(DEPTH)]
        for l in range(DEPTH):
            S.op("dve", lambda e, l=l: e.memset(H_rw[l][:], 0.0), writes=[H_rw[l], H_rw_v[l][0], H_rw_v[l][1]])
            S.op("dve", lambda e, l=l: e.memset(rw_prev[l][:], 0.0), writes=[rw_prev[l]])
            S.dma("pool", lora[l][:], lora_d[l], writes=[lora[l]])
            TS(omka[:, 2 * l:2 * l + 2], vecs[:, VL[f"ka_{l}"]:VL[f"ka_{l}"] + 2], -1.0, 1.0, ALU.mult, ALU.add, [vecs], [omka])

        def rwkv(l, bi, ncols, xn, o_all):
            samp = (ncols == NS)
            C = 8 if samp else 64
            nseg = ncols // C
            groups = [(0, 32, [0, 1, 2, 3])] if samp else [(g * 128, 128, [2 * g, 2 * g + 1]) for g in range(4)]
            segm = cst[:, C_SEGS:C_SEGS + 32] if samp else cst[:, C_SEGP:C_SEGP + 512]
            m_incl, m_str, m_low = (C_MS, C_MSS, C_MSL) if samp else (C_MP, C_MPS, C_MPL)
            nlev = 2 if samp else 5
            V_ = lambda nm, pc=0: vecs[:, VL[f"{nm}_{l}"] + pc:VL[f"{nm}_{l}"] + pc + 1]
            with contextlib.ExitStack() as rs:
                rkv = S.sb([128, 6, TB], F32, "rw_rkv", rs)
                l6 = S.sb([128, TB], BF16, "rw_l6", rs)
                sh0 = S.sb([128, 7, 4], F32, "rw_sh0", rs)
                shs = S.sb([128, 7, 4], F32, "rw_shs", rs)
                pa_ = contextlib.ExitStack()
                pb = S.sb([128, TB + 1], F32, "rw_pb", pa_)
                prevb = S.sb([128, TB], F32, "rw_prevb", pa_)
                dtmp = S.sb([128, TB], F32, "rw_d", pa_)
                l6f = S.sb([128, TB], F32, "rw_l6f", pa_)
                if samp:
                    S.dma("sp", sh0[:], rwsh_d[l], writes=[sh0])
                for c in range(7):
                    if c % 2 == 0:
                        wb = wgroup(l, 1024 + 128 * c, 256 if c < 6 else 128)
                    pp = PS[c % 2]
                    proj_fm(wb, (c % 2) * 128, 128, pp, xn, ncols)
                    A("act", pb[:, 1:ncols + 1], pp[:, :ncols], AF.Copy, [pp], [pb])
                    dest = rkv[:, c, :ncols] if c < 6 else l6f[:, :ncols]
                    dT = rkv if c < 6 else l6f
                    if not samp:
                        S.op("dve", lambda e, c=c: e.tensor_copy(out=pb[:, 0:1], in_=rw_prev[l][:, c:c + 1]), reads=[rw_prev[l], pb], writes=[pb])
                        TTo(dtmp[:, :ncols], pb[:, 0:ncols], pb[:, 1:ncols + 1], ALU.subtract, [pb], [dtmp])
                        S.op("dve", lambda e, c=c: e.tensor_copy(out=rw_prev[l][:, c:c + 1], in_=pb[:, ncols:ncols + 1]), reads=[pb, rw_prev[l]], writes=[rw_prev[l]])
                    else:
                        S.op("dve", lambda e: e.tensor_copy(out=prevb[:, 1:ncols], in_=pb[:, 1:ncols]), reads=[pb], writes=[prevb])
                        S.op("dve", lambda e, c=c: e.tensor_copy(out=prevb[:, :ncols].rearrange("p (b t) -> p b t", t=8)[:, :, 0], in_=sh0[:, c, :]),
                             reads=[sh0, prevb], writes=[prevb])
                        TTo(dtmp[:, :ncols], prevb[:, :ncols], pb[:, 1:ncols + 1], ALU.subtract, [pb, prevb], [dtmp])
                        S.op("dve", lambda e, c=c: e.tensor_copy(out=shs[:, c, :], in_=pb[:, 1:ncols + 1].rearrange("p (b t) -> p b t", t=8)[:, :, 7]),
                             reads=[pb, shs], writes=[shs])
                    STT(dest, dtmp[:, :ncols], V_("mu", c), pb[:, 1:ncols + 1], ALU.mult, ALU.add, [dtmp, vecs, pb], [dT])
                if samp:
                    S.dma("sp", sh_s_d[l], shs[:], reads=[shs])
                elif bi == NPB - 1:
                    S.dma("sp", sh_p_d[l], rw_prev[l][:, 0:7], reads=[rw_prev[l]])
                A("act", l6[0:32, :ncols], l6f[0:32, :ncols], AF.Tanh, [l6f], [l6])
                A("act", l6[32:64, :ncols], l6f[32:64, :ncols], AF.Copy, [l6f], [l6])
                A("act", l6[64:128, :ncols], l6f[64:128, :ncols], AF.Sigmoid, [l6f], [l6])
                S.barrier()
                pa_.close()
                if DBG.get('rw_stage', 9) < 1:
                    S.barrier(); return
                for pc in range(2):
                    with contextlib.ExitStack() as bs:
                        f2 = lambda nm: S.sb([128, TB], F32, nm, bs)
                        b2 = lambda nm: S.sb([128, TB], BF16, nm, bs)
                        lw, aa, gg, al, be, km, cw, tA, tB_ = f2("rw_lw"), f2("rw_a"), f2("rw_g"), f2("rw_al"), f2("rw_be"), f2("rw_km"), f2("rw_cw"), f2("rw_tA"), f2("rw_tB")
                        At, Bt, Kt, Rt = b2("rw_At"), b2("rw_Bt"), b2("rw_Kt"), b2("rw_Rt")
                        vb = b2("rw_vb")
                        tokT = S.sb([128, 4, 4, 128], BF16, "rw_tokT", bs)
                        tokM = S.sb([32, 4, 2, 128], BF16, "rw_tokM", bs)
                        pCt = S.sb([128, 8], F32, "rw_pC", bs)
                        ysb = f2("rw_y")
                        Hsm = [S.sb([128, 64], F32, f"rw_Hs{b}", bs) for b in range(4)] if samp else None
                        r_, k_, v_ = rkv[:, pc, :ncols], rkv[:, 2 + pc, :ncols], rkv[:, 4 + pc, :ncols]
                        n = ncols
                        MM(PS[2][:, :n], lora[l][0:32, pc * 128:(pc + 1) * 128], l6[0:32, :n], [lora[l], l6], [PS[2]])
                        MM(PS[3][:, :n], lora[l][32:64, pc * 128:(pc + 1) * 128], l6[32:64, :n], [lora[l], l6], [PS[3]])
                        MM(PS[4][:, :n], lora[l][64:128, pc * 128:(pc + 1) * 128], l6[64:128, :n], [lora[l], l6], [PS[4]])
                        A("act", lw[:, :n], PS[2][:, :n], AF.Sigmoid, [PS[2], vecs], [lw], bias=V_("w0", pc))
                        TS(lw[:, :n], lw[:, :n], -float(np.exp(-0.5)), None, ALU.mult, None, [lw], [lw])
                        A("act", aa[:, :n], PS[3][:, :n], AF.Sigmoid, [PS[3], vecs], [aa], bias=V_("a0", pc))
                        A("act", gg[:, :n], PS[4][:, :n], AF.Copy, [PS[4]], [gg])
                        TS(al[:, :n], k_, V_("kk", pc), None, ALU.mult, None, [rkv, vecs], [al])
                        A("act", tA[:, :n], al[:, :n], AF.Square, [al], [tA])
                        MM(PS[5][:, :n], cst[:, C_BLK:C_BLK + 128], tA[:, :n], [cst, tA], [PS[5]])
                        A("act", tA[:, :n], PS[5][:, :n], AF.Sqrt, [PS[5]], [tA])
                        TS(tA[:, :n], tA[:, :n], 1e-12, None, ALU.max, None, [tA], [tA])
                        S.op("dve", lambda e: e.reciprocal(out=tA[:, :n], in_=tA[:, :n]), reads=[tA], writes=[tA])
                        TTo(al[:, :n], al[:, :n], tA[:, :n], ALU.mult, [al, tA], [al])
                        TS(tA[:, :n], aa[:, :n], V_("ka", pc), omka[:, 2 * l + pc:2 * l + pc + 1], ALU.mult, ALU.add, [aa, vecs, omka], [tA])
                        TTo(km[:, :n], k_, tA[:, :n], ALU.mult, [rkv, tA], [km])
                        STT(be[:, :n], al[:, :n], -1.0, aa[:, :n], ALU.mult, ALU.mult, [al, aa], [be])
                        STT(tA[:, :n], r_, V_("rk", pc), km[:, :n], ALU.mult, ALU.mult, [rkv, vecs, km], [tA])
                        MM(PS[6][:, :n], cst[:, C_BLK:C_BLK + 128], tA[:, :n], [cst, tA], [PS[6]])
                        TTo(tB_[:, :n], PS[6][:, :n], v_, ALU.mult, [PS[6], rkv], [tB_])
                        S.op("dve", lambda e: e.tensor_tensor_scan(out=cw[:, :n], data0=segm[:, :n], data1=lw[:, :n], initial=0.0,
                                                                   op0=ALU.mult, op1=ALU.add), reads=[lw, cst], writes=[cw])
                        TTo(tA[:, :n], cw[:, :n], lw[:, :n], ALU.subtract, [cw, lw], [tA])
                        A("act", tA[:, :n], tA[:, :n], AF.Exp, [tA], [tA])
                        TTo(At[:, :n], al[:, :n], tA[:, :n], ALU.mult, [al, tA], [At])
                        A("act", tA[:, :n], cw[:, :n], AF.Exp, [cw], [tA], scale=-1.0)
                        TTo(Bt[:, :n], be[:, :n], tA[:, :n], ALU.mult, [be, tA], [Bt])
                        TTo(Kt[:, :n], km[:, :n], tA[:, :n], ALU.mult, [km, tA], [Kt])
                        A("act", tA[:, :n], cw[:, :n], AF.Exp, [cw], [tA])
                        TTo(Rt[:, :n], r_, tA[:, :n], ALU.mult, [rkv, tA], [Rt])
                        A("act", pCt[:, :nseg], cw[:, :n].rearrange("p (s c) -> p s c", c=C)[:, :, C - 1], AF.Exp, [cw], [pCt])
                        A("act", vb[:, :n], v_, AF.Copy, [rkv], [vb])
                        if DBG.get('rw_stage', 9) < 2:
                            S.barrier(); continue
                        for g, (c0g, gsz, segs) in enumerate(groups):
                            for wi_, src in enumerate((At, Bt, Kt, vb)):
                                S.op("pe", lambda e, wi_=wi_, src=src, c0g=c0g, gsz=gsz: e.transpose(
                                    PS[7][:, :].bitcast(BF16)[:gsz, wi_ * 128:(wi_ + 1) * 128], src[:, c0g:c0g + gsz], cstb[:, C_ID:C_ID + 128]),
                                    reads=[src, cstb], writes=[PS[7]])
                            S.op("dve", lambda e, g=g, gsz=gsz: e.tensor_copy(out=tokT[:gsz, g, :, :].rearrange("p a b -> p (a b)"),
                                                                             in_=PS[7][:, :].bitcast(BF16)[:gsz, 0:512]), reads=[PS[7]], writes=[tokT])
                        if samp:
                            for b in range(4):
                                TS(tokM[:, b, :, :].rearrange("p a b -> p (a b)"), tokT[:32, 0, 1:3, :].rearrange("p a b -> p (a b)"),
                                   cst[:32, C_ROWS + b:C_ROWS + b + 1], None, ALU.mult, None, [tokT, cst], [tokM])
                        if DBG.get('rw_stage', 9) < 3:
                            S.barrier(); continue
                        Tl = []
                        for hb in range(2):
                            sq_ = lambda nm: S.sb([128, 128], BF16, f"{nm}{hb}", bs)
                            Tl.append(dict(Nn=sq_("rw_N"), Aa=sq_("rw_A"), IA=sq_("rw_IA"), Pp=sq_("rw_P"), AakT=sq_("rw_AakT"), ArbT=sq_("rw_ArbT"),
                                           ArkT=sq_("rw_ArkT"), WT=sq_("rw_WT"), X0=S.sb([128, 64], BF16, f"rw_X0{hb}", bs),
                                           Ut=S.sb([128, 64], F32, f"rw_Ut{hb}", bs), Usb=S.sb([128, 64], BF16, f"rw_Usb{hb}", bs),
                                           Uf=S.sb([128, 64], F32, f"rw_Uf{hb}", bs), Hc=S.sb([128, 8, 64], BF16, f"rw_Hc{hb}", bs),
                                           Hp=S.sb([128, 64], F32, f"rw_Hp{hb}", bs)))
                            if samp:
                                for b in range(4):
                                    S.dma("sp", Hsm[b][hb * 64:hb * 64 + 64, :], rwst_d[l, b, 2 * pc + hb], writes=[Hsm[b]])

                        def solve(hb, g, c0g, gsz, segs):
                            h, base = 2 * pc + hb, hb * 64
                            bk = PS[0:4] if hb == 0 else PS[4:8]
                            T_ = Tl[hb]
                            Nn, Aa, IA, Pp, AakT, ArbT, ArkT, WT = (T_["Nn"], T_["Aa"], T_["IA"], T_["Pp"], T_["AakT"], T_["ArbT"], T_["ArkT"], T_["WT"])
                            X0, Ut, Usb, Uf, Hc_, Hp_ = T_["X0"], T_["Ut"], T_["Usb"], T_["Uf"], T_["Hc"], T_["Hp"]
                            gsl = slice(c0g, c0g + gsz)
                            fm = lambda t: t[base:base + 64, gsl]
                            mk = lambda col: cst[:gsz, col:col + gsz]
                            MM(bk[0][:gsz, :gsz], fm(Bt), fm(At), [Bt, At], [bk[0]])
                            TTo(Nn[:gsz, :gsz], bk[0][:gsz, :gsz], mk(m_str), ALU.mult, [bk[0], cst], [Nn])
                            MM(bk[1][:gsz, :gsz], fm(At), fm(Bt), [Bt, At], [bk[1]])
                            TTo(Aa[:gsz, :gsz], bk[1][:gsz, :gsz], mk(m_low), ALU.mult, [bk[1], cst], [Aa])
                            MM(bk[2][:gsz, :gsz], fm(Kt), fm(At), [Kt, At], [bk[2]])
                            TTo(AakT[:gsz, :gsz], bk[2][:gsz, :gsz], mk(m_str), ALU.mult, [bk[2], cst], [AakT])
                            MM(bk[3][:gsz, :gsz], fm(Bt), fm(Rt), [Bt, Rt], [bk[3]])
                            TTo(ArbT[:gsz, :gsz], bk[3][:gsz, :gsz], mk(m_incl), ALU.mult, [bk[3], cst], [ArbT])
                            MM(bk[0][:gsz, :gsz], fm(Kt), fm(Rt), [Kt, Rt], [bk[0]])
                            TTo(ArkT[:gsz, :gsz], bk[0][:gsz, :gsz], mk(m_incl), ALU.mult, [bk[0], cst], [ArkT])
                            TTo(Pp[:gsz, :gsz], Nn[:gsz, :gsz], mk(C_ID), ALU.add, [Nn, cst], [Pp])
                            for j in range(1, nlev + 1):
                                MM(bk[1][:gsz, :gsz], Nn[:gsz, :gsz], Aa[:gsz, :gsz], [Nn, Aa], [bk[1]])
                                if j < nlev:
                                    MM(bk[2][:gsz, :gsz], Aa[:gsz, :gsz], Nn[:gsz, :gsz], [Nn, Aa], [bk[2]])
                                TTo(IA[:gsz, :gsz], bk[1][:gsz, :gsz], mk(C_ID), ALU.add, [bk[1], cst], [IA])
                                if j < nlev:
                                    S.op("dve", lambda e, gsz=gsz: e.tensor_copy(out=Aa[:gsz, :gsz], in_=bk[1][:gsz, :gsz]), reads=[bk[1]], writes=[Aa])
                                    A("act", Nn[:gsz, :gsz], bk[2][:gsz, :gsz], AF.Copy, [bk[2]], [Nn])
                                MM(bk[3][:gsz, :gsz], IA[:gsz, :gsz], Pp[:gsz, :gsz], [IA, Pp], [bk[3]])
                                A("act", Pp[:gsz, :gsz], bk[3][:gsz, :gsz], AF.Copy, [bk[3]], [Pp])
                            Vtok = tokT[:gsz, g, 3, base:base + 64]
                            MM(bk[0][:gsz, 0:64], AakT[:gsz, :gsz], Vtok, [AakT, tokT], [bk[0]])
                            A("act", X0[:gsz, :], bk[0][:gsz, 0:64], AF.Copy, [bk[0]], [X0])
                            MM(bk[1][:gsz, 0:64], Pp[:gsz, :gsz], X0[:gsz, :], [Pp, X0], [bk[1]])
                            A("act", Ut[:gsz, :], bk[1][:gsz, 0:64], AF.Copy, [bk[1]], [Ut])
                            MM(bk[2][base:base + 64, :gsz], tokT[:gsz, g, 0, base:base + 64], Pp[:gsz, :gsz], [tokT, Pp], [bk[2]])
                            A("act", WT[base:base + 64, :gsz], bk[2][base:base + 64, :gsz], AF.Copy, [bk[2]], [WT])
                            if not samp:
                                Hst = H_rw_v[l][hb]
                                for si, seg in enumerate(segs):
                                    r0 = si * 64
                                    pu = bk[si % 2]
                                    ph = bk[2 + si % 2]
                                    A("act", Hc_[base:base + 64, seg, :], Hst[base:base + 64, pc, :], AF.Copy, [Hst], [Hc_])
                                    A("act", Hp_[base:base + 64, :], Hst[base:base + 64, pc, :], AF.Identity, [Hst, pCt], [Hp_], scale=pCt[base:base + 64, seg:seg + 1])
                                    MM(pu[r0:r0 + 64, 0:64], WT[base:base + 64, r0:r0 + 64], Hc_[base:base + 64, seg, :], [WT, Hc_], [pu])
                                    TTo(Uf[r0:r0 + 64, :], pu[r0:r0 + 64, 0:64], Ut[r0:r0 + 64, :], ALU.add, [pu, Ut], [Uf])
                                    A("act", Usb[r0:r0 + 64, :], Uf[r0:r0 + 64, :], AF.Copy, [Uf], [Usb])

                                    def fH(e, r0=r0, g=g, ph=ph):
                                        e.matmul(ph[base:base + 64, 0:64], lhsT=tokT[r0:r0 + 64, g, 2, base:base + 64], rhs=tokT[r0:r0 + 64, g, 3, base:base + 64],
                                                 start=True, stop=False)
                                        return e.matmul(ph[base:base + 64, 0:64], lhsT=tokT[r0:r0 + 64, g, 1, base:base + 64], rhs=Usb[r0:r0 + 64, :],
                                                        start=False, stop=True)
                                    S.op("pe", fH, reads=[tokT, Usb], writes=[ph])
                                    STT(Hst[base:base + 64, pc, :], ph[base:base + 64, 0:64], pCt[base:base + 64, seg:seg + 1], Hp_[base:base + 64, :],
                                        ALU.mult, ALU.add, [ph, pCt, Hp_, Hst], [Hst])
                            else:
                                for b in range(4):
                                    A("act", Hc_[base:base + 64, b, :], Hsm[b][base:base + 64, :], AF.Copy, [Hsm[b]], [Hc_])

                                def fU(e):
                                    ins = None
                                    for b in range(4):
                                        ins = e.matmul(bk[0][:32, b * 64:(b + 1) * 64], lhsT=WT[base:base + 64, 0:32], rhs=Hc_[base:base + 64, b, :], start=True, stop=True)
                                    return ins
                                S.op("pe", fU, reads=[WT, Hc_], writes=[bk[0]])
                                S.op("dve", lambda e: e.tensor_copy(out=Uf[:32, :], in_=Ut[:32, :]), reads=[Ut], writes=[Uf])
                                for b in range(4):
                                    STT(Uf[:32, :], bk[0][:32, b * 64:(b + 1) * 64], cst[:32, C_ROWS + b:C_ROWS + b + 1], Uf[:32, :], ALU.mult, ALU.add,
                                        [bk[0], cst, Uf], [Uf])
                                A("act", Usb[:32, :], Uf[:32, :], AF.Copy, [Uf], [Usb])
                                for b in range(4):
                                    ph = bk[2 + b % 2]

                                    def fH(e, b=b, ph=ph):
                                        e.matmul(ph[base:base + 64, 0:64], lhsT=tokM[:32, b, 1, base:base + 64], rhs=tokT[:32, 0, 3, base:base + 64], start=True, stop=False)
                                        return e.matmul(ph[base:base + 64, 0:64], lhsT=tokM[:32, b, 0, base:base + 64], rhs=Usb[:32, :], start=False, stop=True)
                                    S.op("pe", fH, reads=[tokM, tokT, Usb], writes=[ph])
                                    TTo(Hp_[base:base + 64, :], ph[base:base + 64, 0:64], Hsm[b][base:base + 64, :], ALU.add, [ph, Hsm[b]], [Hp_])
                                    TS(Hsm[b][base:base + 64, :], Hp_[base:base + 64, :], pCt[base:base + 64, b:b + 1], None, ALU.mult, None, [Hp_, pCt], [Hsm[b]])
                                    S.dma("sp", rw_s_d[l, b, h], Hsm[b][base:base + 64, :], reads=[Hsm[b]])
                            py = bk[1]

                            def fY(e, g=g, gsz=gsz, segs=segs, py=py):
                                e.matmul(py[base:base + 64, :gsz], lhsT=tokT[:gsz, g, 3, base:base + 64], rhs=ArkT[:gsz, :gsz], start=True, stop=False)
                                ins = e.matmul(py[base:base + 64, :gsz], lhsT=Usb[:gsz, :], rhs=ArbT[:gsz, :gsz], start=False, stop=False)
                                for si, seg in enumerate(segs):
                                    ins = e.matmul(py[base:base + 64, si * C:(si + 1) * C], lhsT=Hc_[base:base + 64, seg, :],
                                                   rhs=Rt[base:base + 64, c0g + si * C:c0g + (si + 1) * C], start=False, stop=(si == len(segs) - 1))
                                return ins
                            S.op("pe", fY, reads=[tokT, ArkT, Usb, ArbT, Hc_, Rt], writes=[py])
                            A("act", ysb[base:base + 64, gsl], py[base:base + 64, :gsz], AF.Copy, [py], [ysb])
                        for g, (c0g, gsz, segs) in enumerate(groups):
                            progs = []
                            for hb in range(2):
                                S.rec = []
                                solve(hb, g, c0g, gsz, segs)
                                progs.append(S.rec)
                                S.rec = None
                            for i_ in range(max(len(p_) for p_ in progs)):
                                for p_ in progs:
                                    if i_ < len(p_):
                                        p_[i_]()
                        if DBG.get('rw_stage', 9) < 8:
                            S.barrier(); continue
                        if DBG.get('rw_post', 99) > 0:
                            MM(PS[0][:, :n], cst[:, C_BLK:C_BLK + 128], ysb[:, :n], [cst, ysb], [PS[0]])
                        if DBG.get('rw_post', 99) > 1:
                            A("act", tA[:, :n], ysb[:, :n], AF.Square, [ysb], [tA])
                        if DBG.get('rw_post', 99) > 2:
                            MM(PS[1][:, :n], cst[:, C_BLK:C_BLK + 128], tA[:, :n], [cst, tA], [PS[1]])
                        if DBG.get('rw_post', 99) > 3:
                            TS(cw[:, :n], PS[0][:, :n], 1.0 / 64, None, ALU.mult, None, [PS[0]], [cw])
                        if DBG.get('rw_post', 99) > 4:
                            TTo(tA[:, :n], cw[:, :n], cw[:, :n], ALU.mult, [cw], [tA])
                        if DBG.get('rw_post', 99) > 5:
                            STT(tA[:, :n], PS[1][:, :n], 1.0 / 64, tA[:, :n], ALU.mult, ALU.subtract, [PS[1], tA], [tA])
                        if DBG.get('rw_post', 99) > 6:
                            TS(tA[:, :n], tA[:, :n], 0.0, 64e-5, ALU.max, ALU.add, [tA], [tA])
                        if DBG.get('rw_post', 99) > 7:
                            A("act", tA[:, :n], tA[:, :n], AF.Sqrt, [tA], [tA])
                        if DBG.get('rw_post', 99) > 8:
                            S.op("dve", lambda e: e.reciprocal(out=tA[:, :n], in_=tA[:, :n]), reads=[tA], writes=[tA])
                        if DBG.get('rw_post', 99) > 9:
                            TTo(ysb[:, :n], ysb[:, :n], cw[:, :n], ALU.subtract, [ysb, cw], [ysb])
                        if DBG.get('rw_post', 99) > 10:
                            if DBG.get('exp1'):
                                TTo(ysb[:, :n], ysb[:, :n], cw[:, :n], ALU.mult, [ysb, cw], [ysb])
                            else:
                                TTo(ysb[:, :n], ysb[:, :n], tA[:, :n], ALU.mult, [ysb, tA], [ysb])
                        if DBG.get('rw_post', 99) > 11:
                            TS(ysb[:, :n], ysb[:, :n], V_("lnw", pc), V_("lnb", pc), ALU.mult, ALU.add, [ysb, vecs], [ysb])
                        if DBG.get('rw_post', 99) > 12:
                            TTo(ysb[:, :n], ysb[:, :n], tB_[:, :n], ALU.add, [ysb, tB_], [ysb])
                        if DBG.get('rw_post', 99) > 13:
                            TTo(o_all[:, 2 + pc, :n], ysb[:, :n], gg[:, :n], ALU.mult, [ysb, gg], [o_all])
                        S.barrier()
                if (not samp) and bi == NPB - 1:
                    S.dma("sp", rw_p_d[l].rearrange("(pc hb) k v -> (hb k) pc v", hb=2), H_rw[l][:], reads=[H_rw[l], H_rw_v[l][0], H_rw_v[l][1]])
                S.barrier()


        MLA_SCALE = float((64 + 32) ** -0.5)
        knope_h, v_h, kpe_h = [], [], []
        kv_es = contextlib.ExitStack()

        def alloc_kv():
            for l in range(DEPTH):
                knope_h.append(S.sb([128, 4, SEQ], BF16, f"knope{l}", kv_es))
                v_h.append(S.sb([128, 16, 8, 65], BF16, f"vh{l}", kv_es))
                kpe_h.append(S.sb([128, SEQ], BF16, f"kpeh{l}", kv_es))
                S.op("dve", lambda e, l=l: e.memset(v_h[l][:, :, :, 64:65], 1.0), writes=[v_h[l]])

        def norm_fm(src, nch, ncols, gname, l, dst_f, dst_b, sqm, rs):
            for c in range(nch):
                A("act", sqm[:, c, :ncols], src[:, c, :ncols], AF.Square, [src], [sqm])

            def f(e):
                ins = None
                for c in range(nch):
                    ins = e.matmul(PS[7][:, :ncols], lhsT=ones_bf[:], rhs=sqm[:, c, :ncols], start=(c == 0), stop=(c == nch - 1))
                return ins
            S.op("pe", f, reads=[sqm, ones_bf], writes=[PS[7]])
            A("act", rs[:, :ncols], PS[7][:, :ncols], AF.Sqrt, [PS[7], epsb], [rs], scale=1.0 / (nch * 128), bias=epsb[:, 0:1])
            S.op("dve", lambda e: e.reciprocal(out=rs[:, :ncols], in_=rs[:, :ncols]), reads=[rs], writes=[rs])
            for c in range(nch):
                g_ = vecs[:, VL[f"{gname}_{l}"] + c:VL[f"{gname}_{l}"] + c + 1]
                if dst_f is not None:
                    STT(dst_f[:, c, :ncols], src[:, c, :ncols], g_, rs[:, :ncols], ALU.mult, ALU.mult, [src, vecs, rs], [dst_f])
                    if dst_b is not None:
                        A("act", dst_b[:, c, :ncols], dst_f[:, c, :ncols], AF.Copy, [dst_f], [dst_b])
                else:
                    STT(dst_b[:, c, :ncols], src[:, c, :ncols], g_, rs[:, :ncols], ALU.mult, ALU.mult, [src, vecs, rs], [dst_b])

        def mla(l, bi, ncols, c0, xn, o_all):
            samp = (ncols == NS)
            n = ncols
            with contextlib.ExitStack() as ms_:
                wqb = S.sb([128, 3, 1280], BF16, "mla_wqb", ms_)
                S.dma("pool", wqb[:], wqb_d[l].rearrange("(kc p) n -> p kc n", p=128), writes=[wqb])
                wuk = S.sb([128, 2, 512], BF16, "mla_wuk", ms_)
                wuv = S.sb([128, 2, 512], BF16, "mla_wuv", ms_)
                S.dma("pool", wuk[:], wuk_d[l].rearrange("(kc p) n -> p kc n", p=128), writes=[wuk])
                S.dma("pool", wuv[:], wuv_d[l].rearrange("(kc p) n -> p kc n", p=128), writes=[wuv])
                ropt = S.sb([128, 2, TB], F32, "mla_rope", ms_)
                rc0 = SEQ if samp else c0
                S.dma("sp", ropt[:, :, :n], rope_d[:, :, rc0:rc0 + n], writes=[ropt])
                qn = S.sb([128, 3, TB], BF16, "mla_qn", ms_)
                cb = S.sb([128, 2, TB], BF16, "mla_cb", ms_)
                qnp = S.sb([128, 4, TB], BF16, "mla_qnp", ms_)
                qpe = S.sb([128, 8, TB], BF16, "mla_qpe", ms_)
                rt1 = S.sb([128, TB], F32, "mla_rt1", ms_)
                rt2 = S.sb([128, TB], F32, "mla_rt2", ms_)
                kpef = S.sb([32, TB], F32, "mla_kpef", ms_)
                with contextlib.ExitStack() as ps_:
                    qa_f = S.sb([128, 3, TB], F32, "mla_qaf", ps_)
                    kv_f = S.sb([128, 2, TB], F32, "mla_kvf", ps_)
                    c_f = kv_f
                    sqm = S.sb([128, 3, TB], BF16, "mla_sq", ps_)
                    rs = S.sb([128, TB], F32, "mla_rs", ps_)
                    wb = wgroup(l, 1920, 256)
                    for c in range(3):
                        if c == 2:
                            wb = wgroup(l, 2176, 128)
                        proj_fm(wb, (c % 2) * 128, 128, PS[c % 2], xn, n)
                        A("act", qa_f[:, c, :n], PS[c % 2][:, :n], AF.Copy, [PS[c % 2]], [qa_f])
                    wb = wgroup(l, 2304, 256)
                    for c in range(2):
                        proj_fm(wb, c * 128, 128, PS[2 + c], xn, n)
                        A("act", kv_f[:, c, :n], PS[2 + c][:, :n], AF.Copy, [PS[2 + c]], [kv_f])
                    norm_fm(qa_f, 3, n, "qn", l, None, qn, sqm, rs)
                    norm_fm(kv_f, 2, n, "kvn", l, c_f, cb, sqm, rs)
                    cdst = ckv_s_d[l] if samp else ckv_p_d[l][:, c0:c0 + n]
                    S.dma("sp", cdst.rearrange("(c p) t -> p c t", p=128), c_f[:, :, :n], reads=[c_f])
                    wb = wgroup(l, 2560, 96)
                    proj_fm(wb, 0, 64, PS[4], xn, n)
                    proj_fm(wb, 32, 64, PS[5], xn, n)
                    TTo(rt1[0:32, :n], PS[4][0:32, :n], ropt[0:32, 0, :n], ALU.mult, [PS[4], ropt], [rt1])
                    TTo(rt2[0:32, :n], PS[5][0:32, :n], ropt[0:32, 1, :n], ALU.mult, [PS[5], ropt], [rt2])
                    TTo(kpef[:, :n], rt1[0:32, :n], rt2[0:32, :n], ALU.add, [rt1, rt2], [kpef])
                    if DBG.get("kpe_dbg") == 1:
                        S.op("dve", lambda e: e.tensor_copy(out=kpef[:, :n], in_=PS[4][0:32, :n]), reads=[PS[4]], writes=[kpef])
                    if DBG.get("kpe_dbg") == 2:
                        S.op("dve", lambda e: e.tensor_copy(out=kpef[:, :n], in_=ropt[0:32, 0, :n]), reads=[ropt], writes=[kpef])
                    kdst = kpe_s_d[l] if samp else kpe_p_d[l][:, c0:c0 + n]
                    S.dma("sp", kdst, kpef[:, :n], reads=[kpef])
                    S.barrier()
                for j in range(4):
                    pp = PS[j % 2]

                    def f(e, j=j, pp=pp):
                        ins = None
                        for kc in range(3):
                            ins = e.matmul(pp[:, :n], lhsT=wqb[:, kc, j * 128:(j + 1) * 128], rhs=qn[:, kc, :n], start=(kc == 0), stop=(kc == 2))
                        return ins
                    S.op("pe", f, reads=[wqb, qn], writes=[pp])
                    A("act", qnp[:, j, :n], pp[:, :n], AF.Copy, [pp], [qnp], scale=MLA_SCALE)
                for h in range(8):
                    pb_ = 0 if samp else (h % 2) * 64
                    pa, pbk = PS[2 + (h % 2) * 2], PS[3 + (h % 2) * 2]

                    def f(e, h=h, pb_=pb_, pa=pa, pbk=pbk):
                        ins = None
                        for which, pt in ((0, pa), (1, pbk)):
                            for kc in range(3):
                                col = 512 + h * 96 + which * 32
                                ins = e.matmul(pt[pb_:pb_ + 64, :n], lhsT=wqb[:, kc, col:col + 64], rhs=qn[:, kc, :n], start=(kc == 0), stop=(kc == 2))
                        return ins
                    S.op("pe", f, reads=[wqb, qn], writes=[pa, pbk])
                    TTo(rt1[pb_:pb_ + 32, :n], pa[pb_:pb_ + 32, :n], ropt[pb_:pb_ + 32, 0, :n], ALU.mult, [pa, ropt], [rt1])
                    TTo(rt2[pb_:pb_ + 32, :n], pbk[pb_:pb_ + 32, :n], ropt[pb_:pb_ + 32, 1, :n], ALU.mult, [pbk, ropt], [rt2])
                    TTo(rt1[pb_:pb_ + 32, :n], rt1[pb_:pb_ + 32, :n], rt2[pb_:pb_ + 32, :n], ALU.add, [rt1, rt2], [rt1])
                    TS(qpe[pb_:pb_ + 32, h, :n], rt1[pb_:pb_ + 32, :n], MLA_SCALE, None, ALU.mult, None, [rt1], [qpe])
                if not samp:
                    mla_prompt_attn(l, bi, c0, wuk, wuv, cb, kpef, qnp, qpe, o_all, ms_)
                elif DBG.get("mla_s", 1):
                    mla_sample_attn(l, wuv, cb, kpef, qnp, qpe, o_all, ms_)
                S.barrier()

        def mla_sample_attn(l, wuv, cb, kpef, qnp, qpe, o_all, ms_):
            NT = 16
            NCH = 128 // NT
            wukT = S.sb([128, 4, 256], BF16, "mla_wukT", ms_)
            S.dma("pool", wukT[:], wukT_d[l], writes=[wukT])
            idx = S.sb([128, 4], I32, "mla_idx", ms_)
            S.dma("sp", idx[:], pt_d[:, :], writes=[idx])
            idxa = S.sb([128, 4, NCH], I32, "mla_idxa", ms_)
            for a in range(NCH):
                TS(idxa[:, :, a], idx[:, :], float(NCH), float(a), ALU.mult, ALU.add, [idx], [idxa])
            qlat = S.sb([128, 2, 8, NS], BF16, "mla_qlat", ms_)
            for hb in range(2):
                base = hb * 64
                pq = PS[hb]

                def f(e, hb=hb, base=base, pq=pq):
                    ins = None
                    for jp in range(4):
                        for rc in range(2):
                            ins = e.matmul(pq[:, (rc * 4 + jp) * 32:(rc * 4 + jp + 1) * 32], lhsT=wukT[base:base + 64, jp, rc * 128:(rc + 1) * 128],
                                           rhs=qnp[base:base + 64, jp, 0:NS], start=True, stop=True)
                    return ins
                S.op("pe", f, reads=[wukT, qnp], writes=[pq])
                A("act", qlat[:, :, :, :].rearrange("p r (j two) t -> p r j two t", two=2)[:, :, :, hb, :],
                  pq[:, 0:256].rearrange("p (r j t) -> p r j t", r=2, j=4), AF.Copy, [pq], [qlat])
            if DBG.get("ms_stage", 9) < 1:
                return
            kpeb = S.sb([32, NS], BF16, "mla_kpeb", ms_)
            A("act", kpeb[:, :], kpef[:, :NS], AF.Copy, [kpef], [kpeb])
            cnew = S.sb([32, 256], BF16, "mla_cnew", ms_)
            for rc in range(2):
                S.op("pe", lambda e, rc=rc: e.transpose(PS[2][:, :].bitcast(BF16)[:NS, rc * 128:(rc + 1) * 128], cb[:, rc, 0:NS], cstb[:, C_ID:C_ID + 128]),
                     reads=[cb, cstb], writes=[PS[2]])
            A("act", cnew[:, :], PS[2][:, :].bitcast(BF16)[:NS, 0:256], AF.Copy, [PS[2]], [cnew])
            cbuf = [S.sb([128, NT * 256], BF16, f"mla_cbuf{i}", ms_) for i in range(2)]
            kbuf = S.sb([128, 128 * 32 + 32], BF16, "mla_kbuf", ms_)
            S.op("dve", lambda e: e.memset(kbuf[:, 4096:4128], 0.0), writes=[kbuf])
            G4 = 4
            cT = [S.sb([128, G4 * 256], BF16, f"mla_cT{i}", ms_) for i in range(2)]
            kT = [S.sb([128, G4 * 128], BF16, f"mla_kT{i}", ms_) for i in range(2)]
            qpep = S.sb([128, 8, NS], BF16, "mla_qpep", ms_)
            kpebp = S.sb([128, NS], BF16, "mla_kpebp", ms_)
            for t_ in (kT[0], kT[1], qpep, kpebp):
                S.op("dve", lambda e, t_=t_: e.memset(t_[:], 0.0), writes=[t_])
            A("act", qpep[0:32, :, :], qpe[0:32, :, 0:NS], AF.Copy, [qpe, qpep], [qpep])
            A("act", kpebp[0:32, :], kpef[:, :NS], AF.Copy, [kpef, kpebp], [kpebp])
            Pt = [S.sb([128, G4 * 64], BF16, f"mla_Pt{i}", ms_) for i in range(2)]
            Ptn = S.sb([32, 64], BF16, "mla_Ptn", ms_)
            olat = S.sb([64, 256], BF16, "mla_olat", ms_)
            olatT = S.sb([128, 2, 64], BF16, "mla_olatT", ms_)
            rcs = S.sb([64, 1], F32, "mla_rcs", ms_)
            ckv2 = ckvc_d[l].rearrange("n (a t) d -> (n a) (t d)", t=NT)
            kpe2 = kpec_d[l].rearrange("n t d -> n (t d)")
            pool_e = S.eng["pool"]

            def gather(dst, src2, idx_ap, R):
                E = pool_e
                S._wait(E, R, [dst])
                if dst.dchan is None:
                    if dst.name not in S.named:
                        S.named[dst.name] = S.new_chan("d_" + dst.name)
                        S.dchans.append(S.named[dst.name])
                    dst.dchan = S.named[dst.name]
                ch = dst.dchan
                ins = E.b.indirect_dma_start(out=dst[:, 0:src2.shape[1]], out_offset=None, in_=src2, in_offset=bass.IndirectOffsetOnAxis(ap=idx_ap, axis=0))
                ch.count += 16
                ins.then_inc(ch.sem, 16)
                dst.w = (ch, ch.count)
                dst.r = []
                for t in R:
                    t.r.append((ch, ch.count))
            pO, pS_ = PS[7], PS[6]
            pK = PS[1]
            it = 0
            for b in range(4):
                qsl = slice(8 * b, 8 * b + 8)
                gather(kbuf, kpe2, idx[:, b:b + 1], [idx])
                for a in range(NCH):
                    cbf = cbuf[a % 2]
                    gather(cbf, ckv2, idxa[:, b, a:a + 1], [idxa])
                    for t4 in range(NT // G4):
                        par = it % 2
                        it += 1
                        pT = PS[2 + par]
                        tl = [a * NT + t4 * G4 + i for i in range(G4)]
                        ct = [cbf[:, (t4 * G4 + i) * 256:(t4 * G4 + i + 1) * 256] for i in range(G4)]

                        def fT(e, pT=pT, ct=ct, tl=tl):
                            ins = None
                            for i in range(G4):
                                e.transpose(pT[:, :].bitcast(BF16)[:, i * 256:i * 256 + 128], ct[i][:, 0:128], cstb[:, C_ID:C_ID + 128])
                                e.transpose(pT[:, :].bitcast(BF16)[:, i * 256 + 128:i * 256 + 256], ct[i][:, 128:256], cstb[:, C_ID:C_ID + 128])
                            for i in range(G4):
                                ins = e.transpose(pK[:, :].bitcast(BF16)[:64, i * 128:(i + 1) * 128], kbuf[:, tl[i] * 32:tl[i] * 32 + 64], cstb[:, C_ID:C_ID + 128])
                            return ins
                        S.op("pe", fT, reads=[cbf, kbuf, cstb], writes=[pT, pK])
                        A("act", cT[par][:, :], pT[:, :].bitcast(BF16)[:, 0:G4 * 256], AF.Copy, [pT], [cT[par]])
                        A("act", kT[par][0:32, :], pK[:, :].bitcast(BF16)[:32, 0:G4 * 128], AF.Copy, [pK], [kT[par]])
                        pSc = PS[4 + par]

                        def fS(e, par=par, pSc=pSc):
                            ins = None
                            for i in range(G4):
                                o_ = pSc[:, i * 64:(i + 1) * 64]
                                e.matmul(o_, lhsT=cT[par][:, i * 256:i * 256 + 128], rhs=qlat[:, 0, :, qsl], start=True, stop=False)
                                e.matmul(o_, lhsT=cT[par][:, i * 256 + 128:i * 256 + 256], rhs=qlat[:, 1, :, qsl], start=False, stop=False)
                                ins = e.matmul(o_, lhsT=kT[par][:, i * 128:(i + 1) * 128], rhs=qpep[:, :, qsl], start=False, stop=True)
                            return ins
                        S.op("pe", fS, reads=[cT[par], kT[par], qlat, qpep], writes=[pSc])
                        A("act", Pt[par][:, :], pSc[:, 0:G4 * 64], AF.Exp, [pSc], [Pt[par]])

                        def fP(e, par=par, ct=ct, first=(a == 0 and t4 == 0)):
                            ins = None
                            for i in range(G4):
                                st = first and i == 0
                                e.matmul(pO[0:64, 0:256], lhsT=Pt[par][:, i * 64:(i + 1) * 64], rhs=ct[i], start=st, stop=False)
                                ins = e.matmul(pS_[0:64, 0:2], lhsT=Pt[par][:, i * 64:(i + 1) * 64], rhs=ones_bf[:, 0:2], start=st, stop=False)
                            return ins
                        S.op("pe", fP, reads=[Pt[par], cbf, ones_bf], writes=[pO, pS_])
                if DBG.get("ms_stage", 9) < 6:
                    continue
                pSc = PS[4]

                def fSn(e):
                    e.matmul(pSc[:NS, 0:64], lhsT=cb[:, 0, 0:NS], rhs=qlat[:, 0, :, qsl], start=True, stop=False)
                    e.matmul(pSc[:NS, 0:64], lhsT=cb[:, 1, 0:NS], rhs=qlat[:, 1, :, qsl], start=False, stop=False)
                    return e.matmul(pSc[:NS, 0:64], lhsT=kpebp[:, 0:NS], rhs=qpep[:, :, qsl], start=False, stop=True)
                S.op("pe", fSn, reads=[cb, kpebp, qlat, qpep], writes=[pSc])
                A("act", Ptn[:, :], pSc[:NS, 0:64], AF.Exp, [pSc], [Ptn])
                TTo(Ptn[:, :].rearrange("p (h t) -> p h t", h=8), Ptn[:, :].rearrange("p (h t) -> p h t", h=8),
                    cstb[:NS, C_MS + 8 * b:C_MS + 8 * b + 8].unsqueeze(1).to_broadcast([NS, 8, 8]), ALU.mult, [Ptn, cstb], [Ptn])

                def fPn(e):
                    e.matmul(pO[0:64, 0:256], lhsT=Ptn[:, :], rhs=cnew[:, :], start=False, stop=True)
                    return e.matmul(pS_[0:64, 0:2], lhsT=Ptn[:, :], rhs=ones_bf[:NS, 0:2], start=False, stop=True)
                S.op("pe", fPn, reads=[Ptn, cnew, ones_bf], writes=[pO, pS_])
                S.op("dve", lambda e: e.reciprocal(out=rcs[:, :], in_=pS_[0:64, 0:1]), reads=[pS_], writes=[rcs])
                TS(olat[:, :], pO[0:64, 0:256], rcs[:, 0:1], None, ALU.mult, None, [pO, rcs], [olat])
                for rc in range(2):
                    S.op("pe", lambda e, rc=rc: e.transpose(PS[2][:, :].bitcast(BF16)[:, rc * 64:(rc + 1) * 64], olat[:, rc * 128:(rc + 1) * 128], cstb[:64, C_ID:C_ID + 64]),
                         reads=[olat, cstb], writes=[PS[2]])
                A("act", olatT[:, :, :].rearrange("p a b -> p (a b)"), PS[2][:, :].bitcast(BF16)[:, 0:128], AF.Copy, [PS[2]], [olatT])

                def fO(e, b=b):
                    ins = None
                    for h in range(8):
                        jp, base = h // 2, (h % 2) * 64
                        for rc in range(2):
                            ins = e.matmul(PS[0][base:base + 64, jp * 32 + 8 * b:jp * 32 + 8 * b + 8], lhsT=wuv[:, rc, h * 64:(h + 1) * 64],
                                           rhs=olatT[:, rc, h * 8:(h + 1) * 8], start=(rc == 0), stop=(rc == 1))
                    return ins
                S.op("pe", fO, reads=[wuv, olatT], writes=[PS[0]])
            if DBG.get("ms_stage", 9) < 6:
                return
            A("act", o_all[:, 4:8, 0:NS], PS[0][:, 0:128].rearrange("p (j t) -> p j t", j=4), AF.Copy, [PS[0]], [o_all])

        def mla_prompt_attn(l, bi, c0, wuk, wuv, cb, kpef, qnp, qpe, o_all, ms_):
            n = TB
            for j in range(4):
                pp = PS[j % 2]

                def f(e, j=j, pp=pp):
                    ins = None
                    for rc in range(2):
                        ins = e.matmul(pp[:, :n], lhsT=wuk[:, rc, j * 128:(j + 1) * 128], rhs=cb[:, rc, :n], start=(rc == 0), stop=(rc == 1))
                    return ins
                S.op("pe", f, reads=[wuk, cb], writes=[pp])
                A("act", knope_h[l][:, j, c0:c0 + n], pp[:, :n], AF.Copy, [pp], [knope_h[l]])
            A("act", kpe_h[l][0:32, c0:c0 + n], kpef[:, :n], AF.Copy, [kpef], [kpe_h[l]])
            S.dma("sp", kpe_h[l][64:96, c0:c0 + n], kpe_h[l][0:32, c0:c0 + n], reads=[kpe_h[l]], writes=[kpe_h[l]])
            for g in range(4):
                pp = PS[2 + g % 2]

                def f(e, g=g, pp=pp):
                    ins = None
                    for rc in range(2):
                        ins = e.matmul(pp[:, :512], lhsT=cb[:, rc, g * 128:(g + 1) * 128], rhs=wuv[:, rc, :], start=(rc == 0), stop=(rc == 1))
                    return ins
                S.op("pe", f, reads=[wuv, cb], writes=[pp])
                A("act", v_h[l][:, bi * 4 + g, :, 0:64], pp[:, :512].rearrange("p (h d) -> p h d", h=8), AF.Copy, [pp], [v_h[l]])
            QR = 256
            PT = S.sb([128, 16, QR], BF16, "mla_PT", ms_)
            o_tok = S.sb([128, 4, 512], BF16, "mla_otok", ms_)
            rcp = S.sb([128, 1], F32, "mla_rcp", ms_)
            for h in range(8):
                j, base = h // 2, (h % 2) * 64
                pss = (PS[0], PS[1]) if h % 2 == 0 else (PS[2], PS[3])
                for qr in range(TB // QR):
                    q0 = qr * QR
                    kb0 = (c0 + q0) // 128
                    nkb = kb0 + QR // 128
                    for kb in range(nkb):
                        pst = pss[kb % 2]

                        def f(e, kb=kb, pst=pst):
                            e.matmul(pst[:, :QR], lhsT=knope_h[l][base:base + 64, j, kb * 128:(kb + 1) * 128], rhs=qnp[base:base + 64, j, q0:q0 + QR],
                                     start=True, stop=False)
                            return e.matmul(pst[:, :QR], lhsT=kpe_h[l][base:base + 32, kb * 128:(kb + 1) * 128], rhs=qpe[base:base + 32, h, q0:q0 + QR],
                                            start=False, stop=True)
                        S.op("pe", f, reads=[knope_h[l], kpe_h[l], qnp, qpe], writes=[pst])
                        A("act", PT[:, kb, :], pst[:, :QR], AF.Exp, [pst], [PT])
                        dj = kb - kb0
                        if dj >= 0:
                            TTo(PT[:, kb, dj * 128:(dj + 1) * 128], PT[:, kb, dj * 128:(dj + 1) * 128], cstb[:, C_TRI:C_TRI + 128], ALU.mult, [PT, cstb], [PT])
                    for qs in range(QR // 128):
                        po = PS[4 + qs % 2]
                        nk = kb0 + qs + 1

                        def f(e, qs=qs, po=po, nk=nk):
                            ins = None
                            for kb in range(nk):
                                ins = e.matmul(po[:, 0:65], lhsT=PT[:, kb, qs * 128:(qs + 1) * 128], rhs=v_h[l][:, kb, h, :], start=(kb == 0), stop=(kb == nk - 1))
                            return ins
                        S.op("pe", f, reads=[PT, v_h[l]], writes=[po])
                        S.op("dve", lambda e, po=po: e.reciprocal(out=rcp[:, :], in_=po[:, 64:65]), reads=[po], writes=[rcp])
                        TS(o_tok[:, qr * 2 + qs, h * 64:(h + 1) * 64], po[:, 0:64], rcp[:, 0:1], None, ALU.mult, None, [po, rcp], [o_tok])
            for qs in range(4):
                for m in range(4):
                    S.op("pe", lambda e, qs=qs, m=m: e.transpose(PS[6][:, :].bitcast(BF16)[:, m * 128:(m + 1) * 128], o_tok[:, qs, m * 128:(m + 1) * 128],
                                                               cstb[:, C_ID:C_ID + 128]), reads=[o_tok, cstb], writes=[PS[6]])
                for m in range(4):
                    A("act", o_all[:, 4 + m, qs * 128:(qs + 1) * 128], PS[6][:, :].bitcast(BF16)[:, m * 128:(m + 1) * 128], AF.Copy, [PS[6]], [o_all])

        def mixer(l, bi, ncols, c0):
            with contextlib.ExitStack() as ms:
                xn = S.sb([128, KC, TB], BF16, "mxn", ms)
                o_all = S.sb([128, KC, TB], BF16, "oall", ms)
                S.op("dve", lambda e: e.memset(o_all[:], 0.0), writes=[o_all])
                with contextlib.ExitStack() as ns:
                    sq = S.sb([128, KC, TB], BF16, "msq", ns)
                    rstd = S.sb([128, TB], F32, "mrstd", ns)
                    rmsnorm(x, ncols, VL[f"nmix_{l}"], xn, sq, rstd, PS[7])
                    S.barrier()
                if DBG.get('hg', 1):
                    hgrn(l, ncols, xn, o_all)
                if DBG.get('rw', 1):
                    rwkv(l, bi, ncols, xn, o_all)
                if DBG.get('mla', 1):
                    mla(l, bi, ncols, c0, xn, o_all)
                with contextlib.ExitStack() as ws:
                    wout = S.sb([128, KC, D], BF16, "wout", ws)
                    S.dma("pool", wout[:], wout_d[l].rearrange("(kc p) n -> p kc n", p=128), writes=[wout])
                    for fo in range(KC):
                        pp = PS[fo % 2]

                        def f(e, fo=fo, pp=pp):
                            ins = None
                            for c in range(KC):
                                ins = e.matmul(pp[:, :ncols], lhsT=wout[:, c, fo * 128:(fo + 1) * 128], rhs=o_all[:, c, :ncols],
                                               start=(c == 0), stop=(c == KC - 1))
                            return ins
                        S.op("pe", f, reads=[wout, o_all], writes=[pp])
                        TTo(x[:, fo, :ncols], pp[:, :ncols], x[:, fo, :ncols], ALU.add, [pp, x], [x])
                    if bi == NPB - 1:
                        S.dma("sp", hg_p_d[l].rearrange("(pc hb) k v -> (hb k) pc v", hb=2), S_hg[l][:], reads=[S_hg[l]])
                    S.barrier()

        blocks = [(xpT, yT_p, b * TB, TB) for b in range(NPB)] + [(xsT, yT_s, 0, NS)]
        alloc_kv()
        for bi, (src, dst, c0, ncols) in enumerate(blocks):
            if bi == NPB:
                S.barrier()
                kv_es.close()
            if bi not in DBG.get('blocks', range(10)):
                continue
            S.dma("sp", x[:, :, :ncols], src.rearrange("(kc p) t -> p kc t", p=128)[:, :, c0:c0 + ncols], writes=[x])
            for l in DBG.get('layers', range(DEPTH)):
                ffn(l, 0, ncols, 0 + l * 8)
                mixer(l, bi, ncols, c0)
                ffn(l, 1, ncols, 32 + l * 8)
            with contextlib.ExitStack() as fs:
                sq = S.sb([128, KC, TB], BF16, "sq", fs)
                rstd = S.sb([128, TB], F32, "rstd", fs)
                yo = S.sb([128, KC, TB], F32, "yo", fs)
                for kc in range(KC):
                    S.op("act", lambda e, kc=kc: e.activation(out=sq[:, kc, :ncols], in_=x[:, kc, :ncols], func=AF.Square),
                         reads=[x], writes=[sq])

                def mm(e):
                    ins = None
                    for kc in range(KC):
                        ins = e.matmul(PS[7][:, :ncols], lhsT=ones_bf[:], rhs=sq[:, kc, :ncols], start=(kc == 0), stop=(kc == KC - 1))
                    return ins
                S.op("pe", mm, reads=[sq, ones_bf], writes=[PS[7]])
                S.op("act", lambda e: e.activation(out=rstd[:, :ncols], in_=PS[7][:, :ncols], func=AF.Sqrt, scale=1.0 / D, bias=epsb[:, 0:1]),
                     reads=[PS[7], epsb], writes=[rstd])
                S.op("dve", lambda e: e.reciprocal(out=rstd[:, :ncols], in_=rstd[:, :ncols]), reads=[rstd], writes=[rstd])
                for kc in range(KC):
                    S.op("dve", lambda e, kc=kc: e.scalar_tensor_tensor(out=yo[:, kc, :ncols], in0=x[:, kc, :ncols],
                                                                      scalar=vecs[:, VL["nfin"] + kc:VL["nfin"] + kc + 1], in1=rstd[:, :ncols],
                                                                      op0=ALU.mult, op1=ALU.mult),
                         reads=[x, vecs, rstd], writes=[yo])
                S.dma("sp", dst.rearrange("(kc p) t -> p kc t", p=128)[:, :, c0:c0 + ncols], yo[:, :, :ncols], reads=[yo])
                S.barrier()
        S.finish()
    return nc


def _fm(v):
    return np.ascontiguousarray(np.asarray(v, np.float32).reshape(-1, 128).T)


def kernel(**inp):
    ncores = DBG.get('ncores', 8)
    nc = build()
    f32 = np.float32
    vecs = np.zeros((128, NV), f32)

    def put(name, v):
        a = _fm(v)
        vecs[:, VL[name]:VL[name] + a.shape[1]] = a
    for l in range(DEPTH):
        put(f"nf1_{l}", inp["norm_ffn1"][l])
        put(f"nmix_{l}", inp["norm_mix"][l])
        put(f"nf2_{l}", inp["norm_ffn2"][l])
        put(f"hglog_{l}", inp["hg_lb_logits"][l])
        put(f"hgn_{l}", np.tile(np.asarray(inp["hg_norm"][l], f32), 2))
        put(f"mu_{l}", inp["rw_mu"][l])
        put(f"w0_{l}", inp["rw_w0"][l])
        put(f"a0_{l}", inp["rw_a0"][l])
        put(f"kk_{l}", inp["rw_kk"][l])
        put(f"ka_{l}", inp["rw_ka"][l])
        put(f"rk_{l}", np.asarray(inp["rw_rk"][l], f32).reshape(-1))
        put(f"lnw_{l}", inp["rw_ln_w"][l])
        put(f"lnb_{l}", inp["rw_ln_b"][l])
        put(f"qn_{l}", inp["mla_q_norm"][l])
        put(f"kvn_{l}", inp["mla_kv_norm"][l])
    put("nfin", inp["norm_final"])
    shared = {"vecs": vecs, "cst": make_consts()}
    shared["lora"] = np.ascontiguousarray(np.stack([np.concatenate([inp["rw_w2"][l], inp["rw_a2"][l], inp["rw_g2"][l]], axis=0)
                                                    for l in range(DEPTH)]), f32)
    wq = np.asarray(inp["mla_wqb"], f32).reshape(DEPTH, 384, 8, 96)
    sw = (np.arange(32) + 16) % 32
    wr = wq[..., 64:]
    shared["wqb"] = np.ascontiguousarray(np.concatenate([wq[..., :64].reshape(DEPTH, 384, 512),
                                                         np.concatenate([wr, wr[..., sw], wr], axis=-1).reshape(DEPTH, 384, 768)], axis=2))
    shared["wuk"] = np.ascontiguousarray(np.asarray(inp["mla_wuk"], f32).reshape(DEPTH, 256, 512))
    shared["wuv"] = np.ascontiguousarray(np.asarray(inp["mla_wuv"], f32).reshape(DEPTH, 256, 512))
    wk = np.asarray(inp["mla_wuk"], f32).transpose(0, 2, 3, 1).reshape(DEPTH, 4, 2, 64, 256)
    shared["wukT"] = np.ascontiguousarray(wk.transpose(0, 2, 3, 1, 4).reshape(DEPTH, 128, 4, 256))
    for l in range(DEPTH):
        shared[f"ckvc{l}"] = np.ascontiguousarray(inp["cache_mla_ckv"][l][:DBG.get('npool', 5120)])
        shared[f"kpec{l}"] = np.ascontiguousarray(inp["cache_mla_kpe"][l][:DBG.get('npool', 5120)])
    past_len = int(inp["page_table"].shape[1]) * int(inp["cache_mla_ckv"].shape[2])
    pos = np.concatenate([np.arange(SEQ, dtype=f32), (past_len + np.tile(np.arange(8, dtype=f32), 4)).astype(f32)])
    inv = np.exp(-np.log(f32(10000.0)) * np.arange(16, dtype=f32) / f32(16)).astype(f32)
    ang = (pos[None, :] * inv[:, None]).astype(f32)
    cos2 = np.concatenate([np.cos(ang), np.cos(ang)], axis=0).astype(f32)
    sin2 = np.concatenate([-np.sin(ang), np.sin(ang)], axis=0).astype(f32)
    rope = np.zeros((128, 2, SEQ + NS), f32)
    for r0 in (0, 64):
        rope[r0:r0 + 32, 0] = cos2
        rope[r0:r0 + 32, 1] = sin2
    shared["rope"] = rope
    for l in range(DEPTH):
        w = np.asarray(inp["w_in"][l], f32)
        shared[f"win{l}"] = np.ascontiguousarray(np.concatenate([w, w[:, 2576:2592], w[:, 2560:2576], w[:, 2560:2592]], axis=1))
        shared[f"wout{l}"] = np.ascontiguousarray(inp["w_out"][l], f32)
    for l in range(DEPTH):
        shared[f"wi{l}0"] = np.ascontiguousarray(inp["ffn1_wi"][l], f32)
        shared[f"wi{l}1"] = np.ascontiguousarray(inp["ffn2_wi"][l], f32)
        shared[f"wo{l}0"] = np.ascontiguousarray(inp["ffn1_wo"][l], f32)
        shared[f"wo{l}1"] = np.ascontiguousarray(inp["ffn2_wo"][l], f32)
    in_maps = []
    for c in range(ncores):
        m = dict(shared)
        m["xpT"] = np.ascontiguousarray(np.asarray(inp["x_prompt"][c], f32).T)
        m["xsT"] = np.ascontiguousarray(np.asarray(inp["x_sample"][4 * c:4 * c + 4], f32).reshape(NS, D).T)
        m["hgst"] = np.ascontiguousarray(np.asarray(inp["state_hgrn"][:, 4 * c:4 * c + 4], f32))
        m["pt"] = np.ascontiguousarray((np.asarray(inp["page_table"][4 * c:4 * c + 4]) % DBG.get('npool', 1 << 30)).astype(np.int32).T)
        m["rwst"] = np.ascontiguousarray(np.asarray(inp["state_rwkv"][:, 4 * c:4 * c + 4], f32).transpose(0, 1, 2, 4, 3))
        sh = np.asarray(inp["state_rwkv_shift"][:, 4 * c:4 * c + 4], f32)
        m["rwsh"] = np.ascontiguousarray(sh.reshape(DEPTH, 4, 7, 128).transpose(0, 3, 2, 1))
        in_maps.append(m)
    res = run_bass_kernel_spmd(nc, in_maps, core_ids=list(range(ncores)))
    R = res.results
    y_p = np.stack([R[c]["yT_p"].T for c in range(ncores)]).astype(f32)
    y_s = np.concatenate([R[c]["yT_s"].T.reshape(4, 8, D) for c in range(ncores)]).astype(f32)
    hg_p = np.stack([R[c]["hg_p"] for c in range(ncores)], axis=1).astype(f32)
    hg_s = np.concatenate([R[c]["hg_s"] for c in range(ncores)], axis=1).astype(f32)
    z = lambda *sh: np.zeros(sh, f32)
    rw_p = np.stack([R[c]["rw_p"].transpose(0, 1, 3, 2) for c in range(ncores)], axis=1).astype(f32)
    rw_s = np.concatenate([R[c]["rw_s"].transpose(0, 1, 2, 4, 3) for c in range(ncores)], axis=1).astype(f32)
    sh_p = np.stack([R[c]["sh_p"].transpose(0, 2, 1).reshape(DEPTH, 896) for c in range(ncores)], axis=1).astype(f32)
    sh_s = np.concatenate([R[c]["sh_s"].transpose(0, 3, 2, 1).reshape(DEPTH, 4, 896) for c in range(ncores)], axis=1).astype(f32)
    ckv_p = np.stack([R[c]["ckv_p"].transpose(0, 2, 1) for c in range(ncores)], axis=1).astype(f32)
    ckv_s = np.concatenate([R[c]["ckv_s"].transpose(0, 2, 1).reshape(DEPTH, 4, 8, 256) for c in range(ncores)], axis=1).astype(f32)
    kpe_p = np.stack([R[c]["kpe_p"].transpose(0, 2, 1) for c in range(ncores)], axis=1).astype(f32)
    kpe_s = np.concatenate([R[c]["kpe_s"].transpose(0, 2, 1).reshape(DEPTH, 4, 8, 32) for c in range(ncores)], axis=1).astype(f32)
    return (y_p, y_s, hg_p, hg_s, rw_p, rw_s, sh_p, sh_s, ckv_p, ckv_s, kpe_p, kpe_s)
```

```python
import contextlib
import numpy as np
import concourse.bass as bass
import concourse.mybir as mybir
from concourse.bass_utils import run_bass_kernel_spmd

F32 = mybir.dt.float32
BF16 = mybir.dt.bfloat16
I32 = mybir.dt.int32
U32 = mybir.dt.uint32
AF = mybir.ActivationFunctionType
ALU = mybir.AluOpType
AX = mybir.AxisListType

D = 1024
SEQ = 2048
DEPTH = 2
DFF = 2816
NJ = DFF // 128
KC = D // 128
EPS = 1e-6
TB = 512
NPB = SEQ // TB
NS = 32
IN_COLS = 2592
GJ = 2


WIN_COLS = 2656
WIN_GROUPS = [(0, 256), (256, 256), (512, 256), (768, 256), (1024, 256), (1280, 256), (1536, 256), (1792, 128),
              (1920, 256), (2176, 128), (2304, 256), (2560, 96)]
WIN_OFF = {}
_o = 0
for _c0, _n in WIN_GROUPS:
    WIN_OFF[(_c0, _n)] = _o
    _o += KC * _n
WIN_FLAT = _o
C_ID, C_MP, C_MPS, C_BLK, C_SEGP, C_SEGS, C_MS, C_MSS, C_ROWS = 0, 128, 256, 384, 512, 1024, 1056, 1088, 1120
C_MPL, C_MSL, C_TRI = 1124, 1252, 1284
NCST = 1412


def vec_layout():
    L = {}
    c = [0]

    def add(name, n):
        L[name] = c[0]
        c[0] += n
    for l in range(DEPTH):
        add(f"nf1_{l}", 8)
    for l in range(DEPTH):
        add(f"nmix_{l}", 8)
    for l in range(DEPTH):
        add(f"nf2_{l}", 8)
    add("nfin", 8)
    for l in range(DEPTH):
        add(f"hglog_{l}", 2)
    for l in range(DEPTH):
        add(f"hgn_{l}", 1)
    for l in range(DEPTH):
        for nm, n in (("mu", 7), ("w0", 2), ("a0", 2), ("kk", 2), ("ka", 2), ("rk", 2), ("lnw", 2), ("lnb", 2), ("qn", 3), ("kvn", 2)):
            add(f"{nm}_{l}", n)
    L["_n"] = c[0]
    return L


VL = vec_layout()
NV = VL["_n"]


def make_consts():
    c = np.zeros((128, NCST), np.float32)
    i = np.arange(128)
    c[:, C_ID:C_ID + 128] = np.eye(128)
    same = (i[:, None] // 64) == (i[None, :] // 64)
    c[:, C_MP:C_MP + 128] = same & (i[:, None] <= i[None, :])
    c[:, C_MPS:C_MPS + 128] = same & (i[:, None] < i[None, :])
    c[:, C_BLK:C_BLK + 128] = same
    t = np.arange(512)
    c[:, C_SEGP:C_SEGP + 512] = (t % 64 != 0)[None, :]
    t = np.arange(32)
    c[:, C_SEGS:C_SEGS + 32] = (t % 8 != 0)[None, :]
    j = np.arange(32)
    same8 = (j[:, None] // 8) == (j[None, :] // 8)
    c[:32, C_MS:C_MS + 32] = same8 & (j[:, None] <= j[None, :])
    c[:32, C_MSS:C_MSS + 32] = same8 & (j[:, None] < j[None, :])
    for b in range(4):
        c[8 * b:8 * b + 8, C_ROWS + b] = 1.0
    c[:, C_MPL:C_MPL + 128] = same & (i[:, None] > i[None, :])
    c[:32, C_MSL:C_MSL + 32] = same8 & (j[:, None] > j[None, :])
    c[:, C_TRI:C_TRI + 128] = (i[:, None] <= i[None, :])
    return c


DBG = {}


class Chan:
    def __init__(self, sem):
        self.sem = sem
        self.count = 0


class Eng:
    def __init__(self, name, b, chan, selfsync):
        self.name = name
        self.b = b
        self.chan = chan
        self.seen = {}
        self.selfsync = selfsync


class TT:
    def __init__(self, t, name):
        self.t = t
        self.name = name
        self.w = None
        self.r = []
        self.dchan = None

    def __getitem__(self, idx):
        return self.t[idx]


class Sched:
    def __init__(self, nc, es):
        self.nc = nc
        self.es = es
        self.nsem = 0
        self.eng = {}
        for name, b, ss in (("pe", nc.tensor, False), ("act", nc.scalar, DBG.get("ss", True)), ("dve", nc.vector, DBG.get("ss", True)),
                            ("pool", nc.gpsimd, True), ("sp", nc.sync, False)):
            self.eng[name] = Eng(name, b, self.new_chan("e_" + name), ss)
        self.dchans = []
        self.named = {}
        self.rec = None
        self.epoch = {}
        self.ntile = 0

    def new_chan(self, name):
        self.nsem += 1
        return Chan(self.es.enter_context(self.nc.semaphore(name)))

    def sb(self, shape, dt, name, es=None):
        self.ntile += 1
        t = (es or self.es).enter_context(self.nc.sbuf_tensor(f"{name}_{self.ntile}", list(shape), dt))
        return TT(t, name)

    def ps(self, shape, dt, name, es=None):
        self.ntile += 1
        t = (es or self.es).enter_context(self.nc.psum_tensor(f"{name}_{self.ntile}", list(shape), dt))
        return TT(t, name)

    def _wait(self, E, reads, writes):
        deps = {}

        def add(d):
            ch, cnt = d
            if deps.get(ch, 0) < cnt:
                deps[ch] = cnt
        for t in reads:
            if t.w is not None:
                add(t.w)
        for t in writes:
            if t.w is not None:
                add(t.w)
            for r in t.r:
                add(r)
        for ch, cnt in deps.items():
            if ch is E.chan and not E.selfsync:
                continue
            if E.seen.get(ch, 0) < cnt:
                E.b.wait_ge(ch.sem, cnt)
                E.seen[ch] = cnt

    def op(self, eng, fn, reads=(), writes=()):
        if self.rec is not None:
            rec, self_ = self.rec, self
            rec.append(lambda: self_._replay(self_.op, eng, fn, reads, writes))
            return None
        E = self.eng[eng]
        self._wait(E, reads, writes)
        if E.selfsync and DBG.get("serial", True) and E.seen.get(E.chan, 0) < E.chan.count:
            E.b.wait_ge(E.chan.sem, E.chan.count)
            E.seen[E.chan] = E.chan.count
        ins = fn(E.b)
        E.chan.count += 1
        ins.then_inc(E.chan.sem, 1)
        stamp = (E.chan, E.chan.count)
        for t in writes:
            t.w = stamp
            t.r = []
        for t in reads:
            t.r.append(stamp)
        return ins

    def _replay(self, f, *a, **kw):
        saved, self.rec = self.rec, None
        try:
            return f(*a, **kw)
        finally:
            self.rec = saved

    def dma(self, q, out, in_, reads=(), writes=(), **kw):
        if self.rec is not None:
            rec, self_ = self.rec, self
            rec.append(lambda: self_._replay(self_.dma, q, out, in_, reads, writes, **kw))
            return None
        E = self.eng[q]
        self._wait(E, reads, writes)
        if q == "pool" and not kw.pop("persistent", False):
            self.pool_epoch_wait()
        owner = (list(writes) + list(reads))[0]
        if owner.dchan is None:
            if owner.name not in self.named:
                self.named[owner.name] = self.new_chan("d_" + owner.name)
                self.dchans.append(self.named[owner.name])
            owner.dchan = self.named[owner.name]
        ch = owner.dchan
        ins = E.b.dma_start(out=out, in_=in_, **kw)
        ch.count += 16
        ins.then_inc(ch.sem, 16)
        stamp = (ch, ch.count)
        for t in writes:
            t.w = stamp
            t.r = []
        for t in reads:
            t.r.append(stamp)
        return ins

    def barrier(self, engines=("pe", "act", "dve", "sp"), dchans=True):
        chans = [self.eng[e].chan for e in self.eng] + (self.dchans if dchans else [])
        if dchans:
            self.epoch = {ch: ch.count for ch in chans if ch is not self.eng["pool"].chan}
        for e in engines:
            E = self.eng[e]
            for ch in chans:
                if ch is E.chan:
                    continue
                if E.seen.get(ch, 0) < ch.count:
                    E.b.wait_ge(ch.sem, ch.count)
                    E.seen[ch] = ch.count

    def pool_epoch_wait(self):
        E = self.eng["pool"]
        for ch, cnt in self.epoch.items():
            if E.seen.get(ch, 0) < cnt:
                E.b.wait_ge(ch.sem, cnt)
                E.seen[ch] = cnt

    def finish(self):
        self.barrier(engines=("sp",))


def build():
    nc = bass.Bass("TRN2", target_bir_lowering=False)
    dt_in = lambda name, shape, dt=F32: nc.dram_tensor(name, list(shape), dt, kind="ExternalInput").ap()
    dt_out = lambda name, shape, dt=F32: nc.dram_tensor(name, list(shape), dt, kind="ExternalOutput").ap()

    xpT = dt_in("xpT", [D, SEQ])
    xsT = dt_in("xsT", [D, NS])
    wi_d = [[dt_in(f"wi{l}{f}", [NJ // GJ, 128, KC * 2 * GJ * 128]) for f in range(2)] for l in range(DEPTH)]
    wo_d = [[dt_in(f"wo{l}{f}", [2, NJ // GJ, 128, GJ * 512]) for f in range(2)] for l in range(DEPTH)]
    vec_d = dt_in("vecs", [128, NV])
    cst_d = dt_in("cst", [128, NCST])
    win_d = [dt_in(f"win{l}", [128, WIN_FLAT]) for l in range(DEPTH)]
    wout_d = [dt_in(f"wout{l}", [D, D]) for l in range(DEPTH)]
    hgst_d = dt_in("hgst", [DEPTH, 4, 4, 64, 64])
    hg_p_d = dt_out("hg_p", [DEPTH, 4, 64, 64])
    lora_d = dt_in("lora", [DEPTH, 128, 256])
    wqb_d = dt_in("wqb", [DEPTH, 384, 1280])
    wuk_d = dt_in("wuk", [DEPTH, 256, 512])
    wuv_d = dt_in("wuv", [DEPTH, 256, 512])
    rope_d = dt_in("rope", [128, 2, SEQ + NS])
    NPOOL = DBG.get('npool', 5120)
    ckvc_d = [dt_in(f"ckvc{l}", [NPOOL, 128, 256]) for l in range(DEPTH)]
    kpec_d = [dt_in(f"kpec{l}", [NPOOL, 128, 32]) for l in range(DEPTH)]
    pt_d = dt_in("pt", [128, 4], I32)
    wukT_d = dt_in("wukT", [DEPTH, 128, 4, 256])
    ckv_p_d = dt_out("ckv_p", [DEPTH, 256, SEQ])
    ckv_s_d = dt_out("ckv_s", [DEPTH, 256, NS])
    kpe_p_d = dt_out("kpe_p", [DEPTH, 32, SEQ])
    kpe_s_d = dt_out("kpe_s", [DEPTH, 32, NS])
    rwst_d = dt_in("rwst", [DEPTH, 4, 4, 64, 64])
    rwsh_d = dt_in("rwsh", [DEPTH, 128, 7, 4])
    rw_p_d = dt_out("rw_p", [DEPTH, 4, 64, 64])
    rw_s_d = dt_out("rw_s", [DEPTH, 4, 4, 64, 64])
    sh_p_d = dt_out("sh_p", [DEPTH, 128, 7])
    sh_s_d = dt_out("sh_s", [DEPTH, 128, 7, 4])
    hg_s_d = dt_out("hg_s", [DEPTH, 4, 4, 64, 64])
    yT_p = dt_out("yT_p", [D, SEQ])
    yT_s = dt_out("yT_s", [D, NS])

    with contextlib.ExitStack() as es:
        S = Sched(nc, es)
        vecs = S.sb([128, NV], F32, "vecs")
        S.dma("sp", vecs[:], vec_d[:, :], writes=[vecs])
        cst = S.sb([128, NCST], F32, "cst")
        S.dma("sp", cst[:], cst_d[:, :], writes=[cst])
        cstb = S.sb([128, NCST], BF16, "cstb")
        S.op("dve", lambda e: e.tensor_copy(out=cstb[:], in_=cst[:]), reads=[cst], writes=[cstb])
        ones_bf = S.sb([128, 128], BF16, "ones")
        S.op("dve", lambda e: e.memset(ones_bf[:], 1.0), writes=[ones_bf])

        x = S.sb([128, KC, TB], F32, "x")
        NWI, NWO = 2, 4
        wi_buf = [S.sb([128, KC, 2, GJ * 128], BF16, f"wi{i}") for i in range(NWI)]
        wo_buf = [S.sb([128, GJ, 512], BF16, f"wo{i}") for i in range(NWO)]
        wi_ctr = [0]
        wo_ctr = [0]
        PS = [S.ps([128, 512], F32, f"ps{i}") for i in range(8)]

        def rmsnorm(xt, ncols, gcol, out_bf, sq, rstd, psb):
            for kc in range(KC):
                S.op("act", lambda e, kc=kc: e.activation(out=sq[:, kc, :ncols], in_=xt[:, kc, :ncols], func=AF.Square),
                     reads=[xt], writes=[sq])

            def mm(e):
                ins = None
                for kc in range(KC):
                    ins = e.matmul(psb[:, :ncols], lhsT=ones_bf[:], rhs=sq[:, kc, :ncols], start=(kc == 0), stop=(kc == KC - 1))
                return ins
            S.op("pe", mm, reads=[sq, ones_bf], writes=[psb])
            S.op("act", lambda e: e.activation(out=rstd[:, :ncols], in_=psb[:, :ncols], func=AF.Sqrt, scale=1.0 / D, bias=epsb[:, 0:1]),
                 reads=[psb, epsb], writes=[rstd])
            S.op("dve", lambda e: e.reciprocal(out=rstd[:, :ncols], in_=rstd[:, :ncols]), reads=[rstd], writes=[rstd])
            for kc in range(KC):
                S.op("dve", lambda e, kc=kc: e.scalar_tensor_tensor(out=out_bf[:, kc, :ncols], in0=xt[:, kc, :ncols],
                                                                  scalar=vecs[:, gcol + kc:gcol + kc + 1], in1=rstd[:, :ncols],
                                                                  op0=ALU.mult, op1=ALU.mult),
                     reads=[xt, vecs, rstd], writes=[out_bf])

        epsb = S.sb([128, 1], F32, "epsb")
        S.op("dve", lambda e: e.memset(epsb[:], EPS), writes=[epsb])

        def ffn(l, f, ncols, gcol):
            with contextlib.ExitStack() as fs:
                xn = S.sb([128, KC, TB], BF16, "xn", fs)
                sq = S.sb([128, KC, TB], BF16, "sq", fs)
                rstd = S.sb([128, TB], F32, "rstd", fs)
                h = [S.sb([128, TB], BF16, f"h{j}", fs) for j in range(NJ)]
                sa = [S.sb([128, TB], BF16, f"sa{i}", fs) for i in range(2)]
                rmsnorm(x, ncols, gcol, xn, sq, rstd, PS[7])
                for g in range(NJ // GJ):
                    wb = wi_buf[wi_ctr[0] % NWI]
                    wi_ctr[0] += 1
                    if not (DBG.get("nodma") and wi_ctr[0] > NWI):
                        S.dma("pool", wb[:, :, :, :].rearrange("p a b c -> p (a b c)"), wi_d[l][f][g], writes=[wb], persistent=True)
                    for jj in range(GJ):
                        j = g * GJ + jj
                        pa, pb = PS[(j % 2) * 2], PS[(j % 2) * 2 + 1]

                        def mma(e, wb=wb, jj=jj, pa=pa):
                            ins = None
                            for kc in range(KC):
                                ins = e.matmul(pa[:, :ncols], lhsT=wb[:, kc, 0, jj * 128:(jj + 1) * 128], rhs=xn[:, kc, :ncols],
                                               start=(kc == 0), stop=(kc == KC - 1))
                            return ins

                        def mmb(e, wb=wb, jj=jj, pb=pb):
                            ins = None
                            for kc in range(KC):
                                ins = e.matmul(pb[:, :ncols], lhsT=wb[:, kc, 1, jj * 128:(jj + 1) * 128], rhs=xn[:, kc, :ncols],
                                               start=(kc == 0), stop=(kc == KC - 1))
                            return ins
                        S.op("pe", mma, reads=[wb, xn], writes=[pa])
                        S.op("pe", mmb, reads=[wb, xn], writes=[pb])
                        st = sa[j % 2]
                        S.op("act", lambda e, pa=pa, st=st: e.activation(out=st[:, :ncols], in_=pa[:, :ncols], func=AF.Silu),
                             reads=[pa], writes=[st])
                        S.op("dve", lambda e, pb=pb, st=st, j=j: e.tensor_tensor(out=h[j][:, :ncols], in0=pb[:, :ncols], in1=st[:, :ncols], op=ALU.mult),
                             reads=[pb, st], writes=[h[j]])
                for half in range(2):
                    acc = [PS[4 + i] for i in range(4)]
                    for g in range(NJ // GJ):
                        wb = wo_buf[wo_ctr[0] % NWO]
                        wo_ctr[0] += 1
                        if not (DBG.get("nodma") and wo_ctr[0] > NWO):
                            S.dma("pool", wb[:, :, :].rearrange("p a b -> p (a b)"), wo_d[l][f][half, g], writes=[wb], persistent=True)
                        for jj in range(GJ):
                            j = g * GJ + jj
                            for fo in range(4):
                                S.op("pe", lambda e, wb=wb, jj=jj, j=j, fo=fo: e.matmul(
                                    acc[fo][:, :ncols], lhsT=wb[:, jj, fo * 128:(fo + 1) * 128], rhs=h[j][:, :ncols],
                                    start=(j == 0), stop=(j == NJ - 1)),
                                    reads=[wb, h[j]], writes=[acc[fo]])
                    for fo in range(4):
                        kc = half * 4 + fo
                        S.op("dve", lambda e, fo=fo, kc=kc: e.scalar_tensor_tensor(
                            out=x[:, kc, :ncols], in0=acc[fo][:, :ncols], scalar=0.5, in1=x[:, kc, :ncols],
                            op0=ALU.mult, op1=ALU.add), reads=[acc[fo], x], writes=[x])
                S.barrier()


        NWB = 2
        win_buf = [S.sb([128, KC, 256], BF16, f"win{i}") for i in range(NWB)]
        win_ctr = [0]
        lbt = S.sb([128, 8], F32, "lbt")
        S.op("dve", lambda e: e.memset(lbt[:], 0.0), writes=[lbt])
        S.op("dve", lambda e: e.tensor_tensor(out=lbt[:, 2:4], in0=vecs[:, VL["hglog_1"]:VL["hglog_1"] + 2],
                                              in1=vecs[:, VL["hglog_0"]:VL["hglog_0"] + 2], op=ALU.subtract),
             reads=[vecs, lbt], writes=[lbt])
        S.op("act", lambda e: e.activation(out=lbt[:, 2:4], in_=lbt[:, 2:4], func=AF.Sigmoid), reads=[lbt], writes=[lbt])
        S.op("dve", lambda e: e.tensor_scalar(out=lbt[:, 4:8], in0=lbt[:, 0:4], scalar1=-1.0, scalar2=1.0, op0=ALU.mult, op1=ALU.add),
             reads=[lbt], writes=[lbt])
        S_hg = [S.sb([128, 2, 64], F32, f"S_hg{l}") for l in range(DEPTH)]
        for l in range(DEPTH):
            S.op("dve", lambda e, l=l: e.memset(S_hg[l][:], 0.0), writes=[S_hg[l]])

        def wgroup(l, col0, ncol):
            wb = win_buf[win_ctr[0] % NWB]
            win_ctr[0] += 1
            off = WIN_OFF[(col0, ncol)]
            S.dma("pool", wb[:, :, :ncol], win_d[l][:, off:off + KC * ncol].rearrange("p (k n) -> p k n", k=KC), writes=[wb], persistent=True)
            return wb

        def proj_fm(wb, wcol, M, ps, xn, ncols, pbase=0):
            def f(e):
                ins = None
                for kc in range(KC):
                    ins = e.matmul(ps[pbase:pbase + M, :ncols], lhsT=wb[:, kc, wcol:wcol + M], rhs=xn[:, kc, :ncols],
                                   start=(kc == 0), stop=(kc == KC - 1))
                return ins
            S.op("pe", f, reads=[wb, xn], writes=[ps])

        def A(eng, out, in_, func, R, W, **kw):
            S.op(eng, lambda e: e.activation(out=out, in_=in_, func=func, **kw), reads=R, writes=W)

        def TTo(out, in0, in1, op, R, W, eng="dve"):
            S.op(eng, lambda e: e.tensor_tensor(out=out, in0=in0, in1=in1, op=op), reads=R, writes=W)

        def TS(out, in0, s1, s2, op0, op1, R, W, eng="dve"):
            if op1 is None:
                S.op(eng, lambda e: e.tensor_scalar(out=out, in0=in0, scalar1=s1, scalar2=None, op0=op0), reads=R, writes=W)
            else:
                S.op(eng, lambda e: e.tensor_scalar(out=out, in0=in0, scalar1=s1, scalar2=s2, op0=op0, op1=op1), reads=R, writes=W)

        def STT(out, in0, scalar, in1, op0, op1, R, W):
            S.op("dve", lambda e: e.scalar_tensor_tensor(out=out, in0=in0, scalar=scalar, in1=in1, op0=op0, op1=op1), reads=R, writes=W)

        def MM(out, lhsT, rhs, R, W, start=True, stop=True):
            S.op("pe", lambda e: e.matmul(out, lhsT=lhsT, rhs=rhs, start=start, stop=stop), reads=R, writes=W)

        def hgrn(l, ncols, xn, o_all):
            samp = (ncols == NS)
            C = 8 if samp else 64
            nseg = ncols // C
            groups = [(0, 32, [0, 1, 2, 3])] if samp else [(g * 128, 128, [2 * g, 2 * g + 1]) for g in range(4)]
            ng = len(groups)
            segm = cst[:, C_SEGS:C_SEGS + 32] if samp else cst[:, C_SEGP:C_SEGP + 512]
            mcol = C_MS if samp else C_MP
            with contextlib.ExitStack() as hs:
                qT = S.sb([128, 2, TB], F32, "hq", hs)
                sg_ = S.sb([128, 2, TB], F32, "hsig", hs)
                sn = S.sb([128, 2, TB], F32, "hsn", hs)
                bb = S.sb([128, 2, TB], F32, "hb", hs)
                t1 = S.sb([128, 2, TB], F32, "ht1", hs)
                t2 = S.sb([128, 2, TB], F32, "ht2", hs)
                Qi = S.sb([128, 2, TB], BF16, "hQi", hs)
                Ki = S.sb([128, 2, TB], BF16, "hKi", hs)
                Qs = S.sb([128, 2, TB], BF16, "hQs", hs)
                Kd = S.sb([128, 2, TB], BF16, "hKd", hs)
                gate = S.sb([128, 2, TB], F32, "hgate", hs)
                Vt = S.sb([128, 4, 256], BF16, "hVt", hs)
                KdT = S.sb([128, 4, 2, 128], BF16, "hKdT", hs)
                KdTm = S.sb([32, 4, 2, 128], BF16, "hKdTm", hs)
                attm = S.sb([128, 4, 2, 2, 128], BF16, "hattm", hs)
                Sbf = S.sb([128, 8, 2, 64], BF16, "hSbf", hs)
                dseg = S.sb([128, 2, 8], F32, "hdseg", hs)
                osb = S.sb([128, 2, TB], F32, "hosb", hs)
                o2 = S.sb([128, 2, TB], BF16, "ho2", hs)
                Ssm = [S.sb([128, 2, 64], F32, f"hSs{b}", hs) for b in range(4)] if samp else None
                lbc = lambda pc: lbt[:, l * 2 + pc:l * 2 + pc + 1]
                omlc = lambda pc: lbt[:, 4 + l * 2 + pc:4 + l * 2 + pc + 1]
                wb = wgroup(l, 0, 256)
                for pc in range(2):
                    proj_fm(wb, pc * 128, 128, PS[pc], xn, ncols)
                    A("act", qT[:, pc, :ncols], PS[pc][:, :ncols], AF.Copy, [PS[pc]], [qT])
                wb = wgroup(l, 256, 256)
                for pc in range(2):
                    p_ = PS[2 + pc]
                    proj_fm(wb, pc * 128, 128, p_, xn, ncols)
                    A("act", sg_[:, pc, :ncols], p_[:, :ncols], AF.Sigmoid, [p_], [sg_])
                    A("act", sn[:, pc, :ncols], p_[:, :ncols], AF.Sigmoid, [p_], [sn], scale=-1.0)
                    TS(sg_[:, pc, :ncols], sg_[:, pc, :ncols], omlc(pc), lbc(pc), ALU.mult, ALU.add, [sg_, lbt], [sg_])
                    TS(sg_[:, pc, :ncols], sg_[:, pc, :ncols], 1e-30, None, ALU.max, None, [sg_], [sg_])
                    A("act", sg_[:, pc, :ncols], sg_[:, pc, :ncols], AF.Ln, [sg_], [sg_])
                    TS(sn[:, pc, :ncols], sn[:, pc, :ncols], omlc(pc), None, ALU.mult, None, [sn, lbt], [sn])
                    S.op("dve", lambda e, pc=pc: e.tensor_tensor_scan(out=bb[:, pc, :ncols], data0=segm[:, :ncols], data1=sg_[:, pc, :ncols],
                                                                     initial=0.0, op0=ALU.mult, op1=ALU.add), reads=[sg_, cst], writes=[bb])
                    bv = bb[:, pc, :ncols].rearrange("p (s c) -> p s c", c=C)
                    v3 = lambda t, pc=pc: t[:, pc, :ncols].rearrange("p (s c) -> p s c", c=C)
                    TTo(v3(t1), bv, bv[:, :, C // 2 - 1:C // 2].to_broadcast([128, nseg, C]), ALU.subtract, [bb], [t1])
                    A("act", t2[:, pc, :ncols], t1[:, pc, :ncols], AF.Exp, [t1], [t2])
                    TTo(Qi[:, pc, :ncols], qT[:, pc, :ncols], t2[:, pc, :ncols], ALU.mult, [qT, t2], [Qi])
                    A("act", t2[:, pc, :ncols], t1[:, pc, :ncols], AF.Exp, [t1], [t2], scale=-1.0)
                    TTo(Ki[:, pc, :ncols], sn[:, pc, :ncols], t2[:, pc, :ncols], ALU.mult, [sn, t2], [Ki])
                    A("act", t2[:, pc, :ncols], bb[:, pc, :ncols], AF.Exp, [bb], [t2])
                    TTo(Qs[:, pc, :ncols], qT[:, pc, :ncols], t2[:, pc, :ncols], ALU.mult, [qT, t2], [Qs])
                    TTo(v3(t1), bv, bv[:, :, C - 1:C].to_broadcast([128, nseg, C]), ALU.subtract, [bb], [t1])
                    A("act", t2[:, pc, :ncols], t1[:, pc, :ncols], AF.Exp, [t1], [t2], scale=-1.0)
                    TTo(Kd[:, pc, :ncols], sn[:, pc, :ncols], t2[:, pc, :ncols], ALU.mult, [sn, t2], [Kd])
                    A("act", dseg[:, pc, :nseg], bv[:, :, C - 1], AF.Exp, [bb], [dseg])
                wb = wgroup(l, 512, 256)
                for g, (c0g, gsz, segs) in enumerate(groups):
                    def f(e, c0g=c0g, gsz=gsz, wb=wb):
                        ins = None
                        for kc in range(KC):
                            ins = e.matmul(PS[4][:gsz, 0:256], lhsT=xn[:, kc, c0g:c0g + gsz], rhs=wb[:, kc, 0:256], start=(kc == 0), stop=(kc == KC - 1))
                        return ins
                    S.op("pe", f, reads=[xn, wb], writes=[PS[4]])
                    A("act", Vt[:gsz, g, :], PS[4][:gsz, 0:256], AF.Copy, [PS[4]], [Vt])
                wb = wgroup(l, 768, 256)
                for pc in range(2):
                    proj_fm(wb, pc * 128, 128, PS[pc], xn, ncols)
                    A("act", gate[:, pc, :ncols], PS[pc][:, :ncols], AF.Silu, [PS[pc]], [gate])
                if DBG.get('hg_stage', 9) < 1:
                    S.barrier()
                    return
                psT = PS[5]
                for g, (c0g, gsz, segs) in enumerate(groups):
                    for pc in range(2):
                        S.op("pe", lambda e, g=g, pc=pc, c0g=c0g, gsz=gsz: e.transpose(
                            psT[:, :].bitcast(BF16)[:gsz, pc * 128:(pc + 1) * 128], Kd[:, pc, c0g:c0g + gsz], cstb[:, C_ID:C_ID + 128]),
                            reads=[Kd, cstb], writes=[psT])
                    S.op("dve", lambda e, g=g, gsz=gsz: e.tensor_copy(out=KdT[:gsz, g, :, :].rearrange("p a b -> p (a b)"),
                                                                     in_=psT[:, :].bitcast(BF16)[:gsz, 0:256]), reads=[psT], writes=[KdT])
                if samp:
                    for b in range(4):
                        TS(KdTm[:, b, :, :].rearrange("p a b -> p (a b)"), KdT[:32, 0, :, :].rearrange("p a b -> p (a b)"),
                           cst[:32, C_ROWS + b:C_ROWS + b + 1], None, ALU.mult, None, [KdT, cst], [KdTm])
                if DBG.get('hg_stage', 9) < 2:
                    S.barrier()
                    return
                psA = [PS[6], PS[5]]
                for g, (c0g, gsz, segs) in enumerate(groups):
                    for hb in range(2):
                        base = hb * 64

                        def f(e, c0g=c0g, gsz=gsz, hb=hb, base=base):
                            ins = None
                            for pc in range(2):
                                ins = e.matmul(psA[hb][:gsz, pc * 128:pc * 128 + gsz], lhsT=Ki[base:base + 64, pc, c0g:c0g + gsz],
                                               rhs=Qi[base:base + 64, pc, c0g:c0g + gsz], start=True, stop=True)
                            return ins
                        S.op("pe", f, reads=[Ki, Qi], writes=[psA[hb]])
                        TTo(attm[:gsz, g, hb, :, :gsz], psA[hb][:gsz, 0:256].rearrange("p (h t) -> p h t", h=2)[:, :, :gsz],
                            cst[:gsz, mcol:mcol + gsz].unsqueeze(1).to_broadcast([gsz, 2, gsz]), ALU.mult, [psA[hb], cst], [attm])
                if DBG.get('hg_stage', 9) < 3:
                    S.barrier()
                    return
                psU = PS[7]
                if not samp:
                    Sst = S_hg[l]
                    for seg in range(nseg):
                        g, r0 = seg // 2, (seg % 2) * 64
                        psU = PS[7] if r0 == 0 else PS[4]
                        A("act", Sbf[:, seg, :, :].rearrange("p a b -> p (a b)"), Sst[:, :, :].rearrange("p a b -> p (a b)"), AF.Copy, [Sst], [Sbf])

                        def f(e, g=g, r0=r0):
                            ins = None
                            for h in range(4):
                                pc, base = h // 2, (h % 2) * 64
                                ins = e.matmul(psU[base:base + 64, pc * 64:(pc + 1) * 64], lhsT=KdT[r0:r0 + 64, g, pc, base:base + 64],
                                               rhs=Vt[r0:r0 + 64, g, h * 64:(h + 1) * 64], start=True, stop=True)
                            return ins
                        S.op("pe", f, reads=[KdT, Vt], writes=[psU])
                        for pc in range(2):
                            STT(Sst[:, pc, :], Sst[:, pc, :], dseg[:, pc, seg:seg + 1], psU[:, pc * 64:(pc + 1) * 64], ALU.mult, ALU.add,
                                [Sst, dseg, psU], [Sst])
                else:
                    for b in range(4):
                        Sst = Ssm[b]
                        S.dma("sp", Sst[:], hgst_d[l, b].rearrange("(pc hb) k v -> (hb k) pc v", hb=2), writes=[Sst])
                        A("act", Sbf[:, b, :, :].rearrange("p a b -> p (a b)"), Sst[:, :, :].rearrange("p a b -> p (a b)"), AF.Copy, [Sst], [Sbf])

                        def f(e, b=b):
                            ins = None
                            for h in range(4):
                                pc, base = h // 2, (h % 2) * 64
                                ins = e.matmul(psU[base:base + 64, pc * 64:(pc + 1) * 64], lhsT=KdTm[:32, b, pc, base:base + 64],
                                               rhs=Vt[:32, 0, h * 64:(h + 1) * 64], start=True, stop=True)
                            return ins
                        S.op("pe", f, reads=[KdTm, Vt], writes=[psU])
                        for pc in range(2):
                            STT(Sst[:, pc, :], Sst[:, pc, :], dseg[:, pc, b:b + 1], psU[:, pc * 64:(pc + 1) * 64], ALU.mult, ALU.add,
                                [Sst, dseg, psU], [Sst])
                        S.dma("sp", hg_s_d[l, b].rearrange("(pc hb) k v -> (hb k) pc v", hb=2), Sst[:], reads=[Sst])
                if DBG.get('hg_stage', 9) < 4:
                    S.barrier()
                    return
                for g, (c0g, gsz, segs) in enumerate(groups):
                    for h in range(4):
                        pc, hb, base = h // 2, h % 2, (h % 2) * 64

                        def f(e, g=g, h=h, pc=pc, hb=hb, base=base, c0g=c0g, gsz=gsz, segs=segs):
                            ins = e.matmul(PS[h][base:base + 64, c0g:c0g + gsz], lhsT=Vt[:gsz, g, h * 64:(h + 1) * 64],
                                           rhs=attm[:gsz, g, hb, pc, :gsz], start=True, stop=False)
                            for si, seg in enumerate(segs):
                                ins = e.matmul(PS[h][base:base + 64, seg * C:(seg + 1) * C], lhsT=Sbf[base:base + 64, seg, pc, :],
                                               rhs=Qs[base:base + 64, pc, seg * C:(seg + 1) * C], start=False, stop=(si == len(segs) - 1))
                            return ins
                        S.op("pe", f, reads=[Vt, attm, Sbf, Qs], writes=[PS[h]])
                for pc in range(2):
                    for hb in range(2):
                        h, base = 2 * pc + hb, hb * 64
                        A("act", osb[base:base + 64, pc, :ncols], PS[h][base:base + 64, :ncols], AF.Copy, [PS[h]], [osb])
                        A("act", o2[base:base + 64, pc, :ncols], PS[h][base:base + 64, :ncols], AF.Square, [PS[h]], [o2])
                for pc in range(2):
                    MM(PS[4 + pc][:, :ncols], cstb[:, C_BLK:C_BLK + 128], o2[:, pc, :ncols], [cstb, o2], [PS[4 + pc]])
                    A("act", t1[:, pc, :ncols], PS[4 + pc][:, :ncols], AF.Sqrt, [PS[4 + pc], epsb], [t1], scale=1.0 / 64, bias=epsb[:, 0:1])
                    S.op("dve", lambda e, pc=pc: e.reciprocal(out=t1[:, pc, :ncols], in_=t1[:, pc, :ncols]), reads=[t1], writes=[t1])
                    STT(t2[:, pc, :ncols], osb[:, pc, :ncols], vecs[:, VL[f"hgn_{l}"]:VL[f"hgn_{l}"] + 1], t1[:, pc, :ncols], ALU.mult, ALU.mult,
                        [osb, vecs, t1], [t2])
                    TTo(o_all[:, pc, :ncols], t2[:, pc, :ncols], gate[:, pc, :ncols], ALU.mult, [t2, gate], [o_all])
                S.barrier()


        H_rw = [S.sb([128, 2, 64], F32, f"H_rw{l}") for l in range(DEPTH)]
        rw_prev = [S.sb([128, 8], F32, f"rwprev{l}") for l in range(DEPTH)]
        lora = [S.sb([128, 256], BF16, f"lora{l}") for l in range(DEPTH)]
        omka = S.sb([128, 4], F32, "omka")
        H_rw_v = [[TT(H_rw[l].t, f"H_rw{l}_{hb}") for hb in range(2)] for l in range(DEPTH)]
        for l in range(DEPTH):
            S.op("dve", lambda e, l=l: e.memset(H_rw[l][:], 0.0), writes=[H_rw[l], H_rw_v[l][0], H_rw_v[l][1]])
            S.op("dve", lambda e, l=l: e.memset(rw_prev[l][:], 0.0), writes=[rw_prev[l]])
            S.dma("pool", lora[l][:], lora_d[l], writes=[lora[l]], persistent=True)
            TS(omka[:, 2 * l:2 * l + 2], vecs[:, VL[f"ka_{l}"]:VL[f"ka_{l}"] + 2], -1.0, 1.0, ALU.mult, ALU.add, [vecs], [omka])

        def rwkv(l, bi, ncols, xn, o_all):
            samp = (ncols == NS)
            C = 8 if samp else 64
            nseg = ncols // C
            groups = [(0, 32, [0, 1, 2, 3])] if samp else [(g * 128, 128, [2 * g, 2 * g + 1]) for g in range(4)]
            segm = cst[:, C_SEGS:C_SEGS + 32] if samp else cst[:, C_SEGP:C_SEGP + 512]
            m_incl, m_str, m_low = (C_MS, C_MSS, C_MSL) if samp else (C_MP, C_MPS, C_MPL)
            nlev = 2 if samp else 5
            V_ = lambda nm, pc=0: vecs[:, VL[f"{nm}_{l}"] + pc:VL[f"{nm}_{l}"] + pc + 1]
            with contextlib.ExitStack() as rs:
                rkv = S.sb([128, 6, TB], F32, "rw_rkv", rs)
                l6 = S.sb([128, TB], BF16, "rw_l6", rs)
                sh0 = S.sb([128, 7, 4], F32, "rw_sh0", rs)
                shs = S.sb([128, 7, 4], F32, "rw_shs", rs)
                pa_ = contextlib.ExitStack()
                pb = S.sb([128, TB + 1], F32, "rw_pb", pa_)
                prevb = S.sb([128, TB], F32, "rw_prevb", pa_)
                dtmp = S.sb([128, TB], F32, "rw_d", pa_)
                l6f = S.sb([128, TB], F32, "rw_l6f", pa_)
                if samp:
                    S.dma("sp", sh0[:], rwsh_d[l], writes=[sh0])
                for c in range(7):
                    if c % 2 == 0:
                        wb = wgroup(l, 1024 + 128 * c, 256 if c < 6 else 128)
                    pp = PS[c % 2]
                    proj_fm(wb, (c % 2) * 128, 128, pp, xn, ncols)
                    A("act", pb[:, 1:ncols + 1], pp[:, :ncols], AF.Copy, [pp], [pb])
                    dest = rkv[:, c, :ncols] if c < 6 else l6f[:, :ncols]
                    dT = rkv if c < 6 else l6f
                    if not samp:
                        S.op("dve", lambda e, c=c: e.tensor_copy(out=pb[:, 0:1], in_=rw_prev[l][:, c:c + 1]), reads=[rw_prev[l], pb], writes=[pb])
                        TTo(dtmp[:, :ncols], pb[:, 0:ncols], pb[:, 1:ncols + 1], ALU.subtract, [pb], [dtmp])
                        S.op("dve", lambda e, c=c: e.tensor_copy(out=rw_prev[l][:, c:c + 1], in_=pb[:, ncols:ncols + 1]), reads=[pb, rw_prev[l]], writes=[rw_prev[l]])
                    else:
                        S.op("dve", lambda e: e.tensor_copy(out=prevb[:, 1:ncols], in_=pb[:, 1:ncols]), reads=[pb], writes=[prevb])
                        S.op("dve", lambda e, c=c: e.tensor_copy(out=prevb[:, :ncols].rearrange("p (b t) -> p b t", t=8)[:, :, 0], in_=sh0[:, c, :]),
                             reads=[sh0, prevb], writes=[prevb])
                        TTo(dtmp[:, :ncols], prevb[:, :ncols], pb[:, 1:ncols + 1], ALU.subtract, [pb, prevb], [dtmp])
                        S.op("dve", lambda e, c=c: e.tensor_copy(out=shs[:, c, :], in_=pb[:, 1:ncols + 1].rearrange("p (b t) -> p b t", t=8)[:, :, 7]),
                             reads=[pb, shs], writes=[shs])
                    STT(dest, dtmp[:, :ncols], V_("mu", c), pb[:, 1:ncols + 1], ALU.mult, ALU.add, [dtmp, vecs, pb], [dT])
                if samp:
                    S.dma("sp", sh_s_d[l], shs[:], reads=[shs])
                elif bi == NPB - 1:
                    S.dma("sp", sh_p_d[l], rw_prev[l][:, 0:7], reads=[rw_prev[l]])
                A("act", l6[0:32, :ncols], l6f[0:32, :ncols], AF.Tanh, [l6f], [l6])
                A("act", l6[32:64, :ncols], l6f[32:64, :ncols], AF.Copy, [l6f], [l6])
                A("act", l6[64:128, :ncols], l6f[64:128, :ncols], AF.Sigmoid, [l6f], [l6])
                S.barrier()
                pa_.close()
                if DBG.get('rw_stage', 9) < 1:
                    S.barrier(); return
                for pc in range(2):
                    with contextlib.ExitStack() as bs:
                        f2 = lambda nm: S.sb([128, TB], F32, nm, bs)
                        b2 = lambda nm: S.sb([128, TB], BF16, nm, bs)
                        lw, aa, gg, al, be, km, cw, tA, tB_ = f2("rw_lw"), f2("rw_a"), f2("rw_g"), f2("rw_al"), f2("rw_be"), f2("rw_km"), f2("rw_cw"), f2("rw_tA"), f2("rw_tB")
                        At, Bt, Kt, Rt = b2("rw_At"), b2("rw_Bt"), b2("rw_Kt"), b2("rw_Rt")
                        vb = b2("rw_vb")
                        tokT = S.sb([128, 4, 4, 128], BF16, "rw_tokT", bs)
                        tokM = S.sb([32, 4, 2, 128], BF16, "rw_tokM", bs)
                        pCt = S.sb([128, 8], F32, "rw_pC", bs)
                        ysb = f2("rw_y")
                        Hsm = [S.sb([128, 64], F32, f"rw_Hs{b}", bs) for b in range(4)] if samp else None
                        r_, k_, v_ = rkv[:, pc, :ncols], rkv[:, 2 + pc, :ncols], rkv[:, 4 + pc, :ncols]
                        n = ncols
                        MM(PS[2][:, :n], lora[l][0:32, pc * 128:(pc + 1) * 128], l6[0:32, :n], [lora[l], l6], [PS[2]])
                        MM(PS[3][:, :n], lora[l][32:64, pc * 128:(pc + 1) * 128], l6[32:64, :n], [lora[l], l6], [PS[3]])
                        MM(PS[4][:, :n], lora[l][64:128, pc * 128:(pc + 1) * 128], l6[64:128, :n], [lora[l], l6], [PS[4]])
                        A("act", lw[:, :n], PS[2][:, :n], AF.Sigmoid, [PS[2], vecs], [lw], bias=V_("w0", pc))
                        TS(lw[:, :n], lw[:, :n], -float(np.exp(-0.5)), None, ALU.mult, None, [lw], [lw])
                        A("act", aa[:, :n], PS[3][:, :n], AF.Sigmoid, [PS[3], vecs], [aa], bias=V_("a0", pc))
                        A("act", gg[:, :n], PS[4][:, :n], AF.Copy, [PS[4]], [gg])
                        TS(al[:, :n], k_, V_("kk", pc), None, ALU.mult, None, [rkv, vecs], [al])
                        A("act", tA[:, :n], al[:, :n], AF.Square, [al], [tA])
                        MM(PS[5][:, :n], cst[:, C_BLK:C_BLK + 128], tA[:, :n], [cst, tA], [PS[5]])
                        A("act", tA[:, :n], PS[5][:, :n], AF.Sqrt, [PS[5]], [tA])
                        TS(tA[:, :n], tA[:, :n], 1e-12, None, ALU.max, None, [tA], [tA])
                        S.op("dve", lambda e: e.reciprocal(out=tA[:, :n], in_=tA[:, :n]), reads=[tA], writes=[tA])
                        TTo(al[:, :n], al[:, :n], tA[:, :n], ALU.mult, [al, tA], [al])
                        TS(tA[:, :n], aa[:, :n], V_("ka", pc), omka[:, 2 * l + pc:2 * l + pc + 1], ALU.mult, ALU.add, [aa, vecs, omka], [tA])
                        TTo(km[:, :n], k_, tA[:, :n], ALU.mult, [rkv, tA], [km])
                        STT(be[:, :n], al[:, :n], -1.0, aa[:, :n], ALU.mult, ALU.mult, [al, aa], [be])
                        STT(tA[:, :n], r_, V_("rk", pc), km[:, :n], ALU.mult, ALU.mult, [rkv, vecs, km], [tA])
                        MM(PS[6][:, :n], cst[:, C_BLK:C_BLK + 128], tA[:, :n], [cst, tA], [PS[6]])
                        TTo(tB_[:, :n], PS[6][:, :n], v_, ALU.mult, [PS[6], rkv], [tB_])
                        S.op("dve", lambda e: e.tensor_tensor_scan(out=cw[:, :n], data0=segm[:, :n], data1=lw[:, :n], initial=0.0,
                                                                   op0=ALU.mult, op1=ALU.add), reads=[lw, cst], writes=[cw])
                        TTo(tA[:, :n], cw[:, :n], lw[:, :n], ALU.subtract, [cw, lw], [tA])
                        A("act", tA[:, :n], tA[:, :n], AF.Exp, [tA], [tA])
                        TTo(At[:, :n], al[:, :n], tA[:, :n], ALU.mult, [al, tA], [At])
                        A("act", tA[:, :n], cw[:, :n], AF.Exp, [cw], [tA], scale=-1.0)
                        TTo(Bt[:, :n], be[:, :n], tA[:, :n], ALU.mult, [be, tA], [Bt])
                        TTo(Kt[:, :n], km[:, :n], tA[:, :n], ALU.mult, [km, tA], [Kt])
                        A("act", tA[:, :n], cw[:, :n], AF.Exp, [cw], [tA])
                        TTo(Rt[:, :n], r_, tA[:, :n], ALU.mult, [rkv, tA], [Rt])
                        A("act", pCt[:, :nseg], cw[:, :n].rearrange("p (s c) -> p s c", c=C)[:, :, C - 1], AF.Exp, [cw], [pCt])
                        A("act", vb[:, :n], v_, AF.Copy, [rkv], [vb])
                        if DBG.get('rw_stage', 9) < 2:
                            S.barrier(); continue
                        for g, (c0g, gsz, segs) in enumerate(groups):
                            for wi_, src in enumerate((At, Bt, Kt, vb)):
                                S.op("pe", lambda e, wi_=wi_, src=src, c0g=c0g, gsz=gsz: e.transpose(
                                    PS[7][:, :].bitcast(BF16)[:gsz, wi_ * 128:(wi_ + 1) * 128], src[:, c0g:c0g + gsz], cstb[:, C_ID:C_ID + 128]),
                                    reads=[src, cstb], writes=[PS[7]])
                            S.op("dve", lambda e, g=g, gsz=gsz: e.tensor_copy(out=tokT[:gsz, g, :, :].rearrange("p a b -> p (a b)"),
                                                                             in_=PS[7][:, :].bitcast(BF16)[:gsz, 0:512]), reads=[PS[7]], writes=[tokT])
                        if samp:
                            for b in range(4):
                                TS(tokM[:, b, :, :].rearrange("p a b -> p (a b)"), tokT[:32, 0, 1:3, :].rearrange("p a b -> p (a b)"),
                                   cst[:32, C_ROWS + b:C_ROWS + b + 1], None, ALU.mult, None, [tokT, cst], [tokM])
                        if DBG.get('rw_stage', 9) < 3:
                            S.barrier(); continue
                        Tl = []
                        for hb in range(2):
                            sq_ = lambda nm: S.sb([128, 128], BF16, f"{nm}{hb}", bs)
                            Tl.append(dict(Nn=sq_("rw_N"), Aa=sq_("rw_A"), IA=sq_("rw_IA"), Pp=sq_("rw_P"), AakT=sq_("rw_AakT"), ArbT=sq_("rw_ArbT"),
                                           ArkT=sq_("rw_ArkT"), WT=sq_("rw_WT"), X0=S.sb([128, 64], BF16, f"rw_X0{hb}", bs),
                                           Ut=S.sb([128, 64], F32, f"rw_Ut{hb}", bs), Usb=S.sb([128, 64], BF16, f"rw_Usb{hb}", bs),
                                           Uf=S.sb([128, 64], F32, f"rw_Uf{hb}", bs), Hc=S.sb([128, 8, 64], BF16, f"rw_Hc{hb}", bs),
                                           Hp=S.sb([128, 64], F32, f"rw_Hp{hb}", bs)))
                            if samp:
                                for b in range(4):
                                    S.dma("sp", Hsm[b][hb * 64:hb * 64 + 64, :], rwst_d[l, b, 2 * pc + hb], writes=[Hsm[b]])

                        def solve(hb, g, c0g, gsz, segs):
                            h, base = 2 * pc + hb, hb * 64
                            bk = PS[0:4] if hb == 0 else PS[4:8]
                            T_ = Tl[hb]
                            Nn, Aa, IA, Pp, AakT, ArbT, ArkT, WT = (T_["Nn"], T_["Aa"], T_["IA"], T_["Pp"], T_["AakT"], T_["ArbT"], T_["ArkT"], T_["WT"])
                            X0, Ut, Usb, Uf, Hc_, Hp_ = T_["X0"], T_["Ut"], T_["Usb"], T_["Uf"], T_["Hc"], T_["Hp"]
                            gsl = slice(c0g, c0g + gsz)
                            fm = lambda t: t[base:base + 64, gsl]
                            mk = lambda col: cst[:gsz, col:col + gsz]
                            MM(bk[0][:gsz, :gsz], fm(Bt), fm(At), [Bt, At], [bk[0]])
                            TTo(Nn[:gsz, :gsz], bk[0][:gsz, :gsz], mk(m_str), ALU.mult, [bk[0], cst], [Nn])
                            MM(bk[1][:gsz, :gsz], fm(At), fm(Bt), [Bt, At], [bk[1]])
                            TTo(Aa[:gsz, :gsz], bk[1][:gsz, :gsz], mk(m_low), ALU.mult, [bk[1], cst], [Aa])
                            MM(bk[2][:gsz, :gsz], fm(Kt), fm(At), [Kt, At], [bk[2]])
                            TTo(AakT[:gsz, :gsz], bk[2][:gsz, :gsz], mk(m_str), ALU.mult, [bk[2], cst], [AakT])
                            MM(bk[3][:gsz, :gsz], fm(Bt), fm(Rt), [Bt, Rt], [bk[3]])
                            TTo(ArbT[:gsz, :gsz], bk[3][:gsz, :gsz], mk(m_incl), ALU.mult, [bk[3], cst], [ArbT])
                            MM(bk[0][:gsz, :gsz], fm(Kt), fm(Rt), [Kt, Rt], [bk[0]])
                            TTo(ArkT[:gsz, :gsz], bk[0][:gsz, :gsz], mk(m_incl), ALU.mult, [bk[0], cst], [ArkT])
                            TTo(Pp[:gsz, :gsz], Nn[:gsz, :gsz], mk(C_ID), ALU.add, [Nn, cst], [Pp])
                            for j in range(1, nlev + 1):
                                MM(bk[1][:gsz, :gsz], Nn[:gsz, :gsz], Aa[:gsz, :gsz], [Nn, Aa], [bk[1]])
                                if j < nlev:
                                    MM(bk[2][:gsz, :gsz], Aa[:gsz, :gsz], Nn[:gsz, :gsz], [Nn, Aa], [bk[2]])
                                TTo(IA[:gsz, :gsz], bk[1][:gsz, :gsz], mk(C_ID), ALU.add, [bk[1], cst], [IA])
                                if j < nlev:
                                    S.op("dve", lambda e, gsz=gsz: e.tensor_copy(out=Aa[:gsz, :gsz], in_=bk[1][:gsz, :gsz]), reads=[bk[1]], writes=[Aa])
                                    A("act", Nn[:gsz, :gsz], bk[2][:gsz, :gsz], AF.Copy, [bk[2]], [Nn])
                                MM(bk[3][:gsz, :gsz], IA[:gsz, :gsz], Pp[:gsz, :gsz], [IA, Pp], [bk[3]])
                                A("act", Pp[:gsz, :gsz], bk[3][:gsz, :gsz], AF.Copy, [bk[3]], [Pp])
                            Vtok = tokT[:gsz, g, 3, base:base + 64]
                            MM(bk[0][:gsz, 0:64], AakT[:gsz, :gsz], Vtok, [AakT, tokT], [bk[0]])
                            A("act", X0[:gsz, :], bk[0][:gsz, 0:64], AF.Copy, [bk[0]], [X0])
                            MM(bk[1][:gsz, 0:64], Pp[:gsz, :gsz], X0[:gsz, :], [Pp, X0], [bk[1]])
                            A("act", Ut[:gsz, :], bk[1][:gsz, 0:64], AF.Copy, [bk[1]], [Ut])
                            MM(bk[2][base:base + 64, :gsz], tokT[:gsz, g, 0, base:base + 64], Pp[:gsz, :gsz], [tokT, Pp], [bk[2]])
                            A("act", WT[base:base + 64, :gsz], bk[2][base:base + 64, :gsz], AF.Copy, [bk[2]], [WT])
                            if not samp:
                                Hst = H_rw_v[l][hb]
                                for si, seg in enumerate(segs):
                                    r0 = si * 64
                                    pu = bk[si % 2]
                                    ph = bk[2 + si % 2]
                                    A("act", Hc_[base:base + 64, seg, :], Hst[base:base + 64, pc, :], AF.Copy, [Hst], [Hc_])
                                    A("act", Hp_[base:base + 64, :], Hst[base:base + 64, pc, :], AF.Identity, [Hst, pCt], [Hp_], scale=pCt[base:base + 64, seg:seg + 1])
                                    MM(pu[r0:r0 + 64, 0:64], WT[base:base + 64, r0:r0 + 64], Hc_[base:base + 64, seg, :], [WT, Hc_], [pu])
                                    TTo(Uf[r0:r0 + 64, :], pu[r0:r0 + 64, 0:64], Ut[r0:r0 + 64, :], ALU.add, [pu, Ut], [Uf])
                                    A("act", Usb[r0:r0 + 64, :], Uf[r0:r0 + 64, :], AF.Copy, [Uf], [Usb])

                                    def fH(e, r0=r0, g=g, ph=ph):
                                        e.matmul(ph[base:base + 64, 0:64], lhsT=tokT[r0:r0 + 64, g, 2, base:base + 64], rhs=tokT[r0:r0 + 64, g, 3, base:base + 64],
                                                 start=True, stop=False)
                                        return e.matmul(ph[base:base + 64, 0:64], lhsT=tokT[r0:r0 + 64, g, 1, base:base + 64], rhs=Usb[r0:r0 + 64, :],
                                                        start=False, stop=True)
                                    S.op("pe", fH, reads=[tokT, Usb], writes=[ph])
                                    STT(Hst[base:base + 64, pc, :], ph[base:base + 64, 0:64], pCt[base:base + 64, seg:seg + 1], Hp_[base:base + 64, :],
                                        ALU.mult, ALU.add, [ph, pCt, Hp_, Hst], [Hst])
                            else:
                                for b in range(4):
                                    A("act", Hc_[base:base + 64, b, :], Hsm[b][base:base + 64, :], AF.Copy, [Hsm[b]], [Hc_])

                                def fU(e):
                                    ins = None
                                    for b in range(4):
                                        ins = e.matmul(bk[0][:32, b * 64:(b + 1) * 64], lhsT=WT[base:base + 64, 0:32], rhs=Hc_[base:base + 64, b, :], start=True, stop=True)
                                    return ins
                                S.op("pe", fU, reads=[WT, Hc_], writes=[bk[0]])
                                S.op("dve", lambda e: e.tensor_copy(out=Uf[:32, :], in_=Ut[:32, :]), reads=[Ut], writes=[Uf])
                                for b in range(4):
                                    STT(Uf[:32, :], bk[0][:32, b * 64:(b + 1) * 64], cst[:32, C_ROWS + b:C_ROWS + b + 1], Uf[:32, :], ALU.mult, ALU.add,
                                        [bk[0], cst, Uf], [Uf])
                                A("act", Usb[:32, :], Uf[:32, :], AF.Copy, [Uf], [Usb])
                                for b in range(4):
                                    ph = bk[2 + b % 2]

                                    def fH(e, b=b, ph=ph):
                                        e.matmul(ph[base:base + 64, 0:64], lhsT=tokM[:32, b, 1, base:base + 64], rhs=tokT[:32, 0, 3, base:base + 64], start=True, stop=False)
                                        return e.matmul(ph[base:base + 64, 0:64], lhsT=tokM[:32, b, 0, base:base + 64], rhs=Usb[:32, :], start=False, stop=True)
                                    S.op("pe", fH, reads=[tokM, tokT, Usb], writes=[ph])
                                    TTo(Hp_[base:base + 64, :], ph[base:base + 64, 0:64], Hsm[b][base:base + 64, :], ALU.add, [ph, Hsm[b]], [Hp_])
                                    TS(Hsm[b][base:base + 64, :], Hp_[base:base + 64, :], pCt[base:base + 64, b:b + 1], None, ALU.mult, None, [Hp_, pCt], [Hsm[b]])
                                    S.dma("sp", rw_s_d[l, b, h], Hsm[b][base:base + 64, :], reads=[Hsm[b]])
                            py = bk[1]

                            def fY(e, g=g, gsz=gsz, segs=segs, py=py):
                                e.matmul(py[base:base + 64, :gsz], lhsT=tokT[:gsz, g, 3, base:base + 64], rhs=ArkT[:gsz, :gsz], start=True, stop=False)
                                ins = e.matmul(py[base:base + 64, :gsz], lhsT=Usb[:gsz, :], rhs=ArbT[:gsz, :gsz], start=False, stop=False)
                                for si, seg in enumerate(segs):
                                    ins = e.matmul(py[base:base + 64, si * C:(si + 1) * C], lhsT=Hc_[base:base + 64, seg, :],
                                                   rhs=Rt[base:base + 64, c0g + si * C:c0g + (si + 1) * C], start=False, stop=(si == len(segs) - 1))
                                return ins
                            S.op("pe", fY, reads=[tokT, ArkT, Usb, ArbT, Hc_, Rt], writes=[py])
                            A("act", ysb[base:base + 64, gsl], py[base:base + 64, :gsz], AF.Copy, [py], [ysb])
                        for g, (c0g, gsz, segs) in enumerate(groups):
                            progs = []
                            for hb in range(2):
                                S.rec = []
                                solve(hb, g, c0g, gsz, segs)
                                progs.append(S.rec)
                                S.rec = None
                            for i_ in range(max(len(p_) for p_ in progs)):
                                for p_ in progs:
                                    if i_ < len(p_):
                                        p_[i_]()
                        if DBG.get('rw_stage', 9) < 8:
                            S.barrier(); continue
                        if DBG.get('rw_post', 99) > 0:
                            MM(PS[0][:, :n], cst[:, C_BLK:C_BLK + 128], ysb[:, :n], [cst, ysb], [PS[0]])
                        if DBG.get('rw_post', 99) > 1:
                            A("act", tA[:, :n], ysb[:, :n], AF.Square, [ysb], [tA])
                        if DBG.get('rw_post', 99) > 2:
                            MM(PS[1][:, :n], cst[:, C_BLK:C_BLK + 128], tA[:, :n], [cst, tA], [PS[1]])
                        if DBG.get('rw_post', 99) > 3:
                            TS(cw[:, :n], PS[0][:, :n], 1.0 / 64, None, ALU.mult, None, [PS[0]], [cw])
                        if DBG.get('rw_post', 99) > 4:
                            TTo(tA[:, :n], cw[:, :n], cw[:, :n], ALU.mult, [cw], [tA])
                        if DBG.get('rw_post', 99) > 5:
                            STT(tA[:, :n], PS[1][:, :n], 1.0 / 64, tA[:, :n], ALU.mult, ALU.subtract, [PS[1], tA], [tA])
                        if DBG.get('rw_post', 99) > 6:
                            TS(tA[:, :n], tA[:, :n], 0.0, 64e-5, ALU.max, ALU.add, [tA], [tA])
                        if DBG.get('rw_post', 99) > 7:
                            A("act", tA[:, :n], tA[:, :n], AF.Sqrt, [tA], [tA])
                        if DBG.get('rw_post', 99) > 8:
                            S.op("dve", lambda e: e.reciprocal(out=tA[:, :n], in_=tA[:, :n]), reads=[tA], writes=[tA])
                        if DBG.get('rw_post', 99) > 9:
                            TTo(ysb[:, :n], ysb[:, :n], cw[:, :n], ALU.subtract, [ysb, cw], [ysb])
                        if DBG.get('rw_post', 99) > 10:
                            if DBG.get('exp1'):
                                TTo(ysb[:, :n], ysb[:, :n], cw[:, :n], ALU.mult, [ysb, cw], [ysb])
                            else:
                                TTo(ysb[:, :n], ysb[:, :n], tA[:, :n], ALU.mult, [ysb, tA], [ysb])
                        if DBG.get('rw_post', 99) > 11:
                            TS(ysb[:, :n], ysb[:, :n], V_("lnw", pc), V_("lnb", pc), ALU.mult, ALU.add, [ysb, vecs], [ysb])
                        if DBG.get('rw_post', 99) > 12:
                            TTo(ysb[:, :n], ysb[:, :n], tB_[:, :n], ALU.add, [ysb, tB_], [ysb])
                        if DBG.get('rw_post', 99) > 13:
                            TTo(o_all[:, 2 + pc, :n], ysb[:, :n], gg[:, :n], ALU.mult, [ysb, gg], [o_all])
                        S.barrier()
                if (not samp) and bi == NPB - 1:
                    S.dma("sp", rw_p_d[l].rearrange("(pc hb) k v -> (hb k) pc v", hb=2), H_rw[l][:], reads=[H_rw[l], H_rw_v[l][0], H_rw_v[l][1]])
                S.barrier()


        MLA_SCALE = float((64 + 32) ** -0.5)
        knope_h, v_h, kpe_h = [], [], []
        kv_es = contextlib.ExitStack()

        def alloc_kv():
            for l in range(DEPTH):
                knope_h.append(S.sb([128, 4, SEQ], BF16, f"knope{l}", kv_es))
                v_h.append(S.sb([128, 16, 8, 65], BF16, f"vh{l}", kv_es))
                kpe_h.append(S.sb([128, SEQ], BF16, f"kpeh{l}", kv_es))
                S.op("dve", lambda e, l=l: e.memset(v_h[l][:, :, :, 64:65], 1.0), writes=[v_h[l]])

        def norm_fm(src, nch, ncols, gname, l, dst_f, dst_b, sqm, rs):
            for c in range(nch):
                A("act", sqm[:, c, :ncols], src[:, c, :ncols], AF.Square, [src], [sqm])

            def f(e):
                ins = None
                for c in range(nch):
                    ins = e.matmul(PS[7][:, :ncols], lhsT=ones_bf[:], rhs=sqm[:, c, :ncols], start=(c == 0), stop=(c == nch - 1))
                return ins
            S.op("pe", f, reads=[sqm, ones_bf], writes=[PS[7]])
            A("act", rs[:, :ncols], PS[7][:, :ncols], AF.Sqrt, [PS[7], epsb], [rs], scale=1.0 / (nch * 128), bias=epsb[:, 0:1])
            S.op("dve", lambda e: e.reciprocal(out=rs[:, :ncols], in_=rs[:, :ncols]), reads=[rs], writes=[rs])
            for c in range(nch):
                g_ = vecs[:, VL[f"{gname}_{l}"] + c:VL[f"{gname}_{l}"] + c + 1]
                if dst_f is not None:
                    STT(dst_f[:, c, :ncols], src[:, c, :ncols], g_, rs[:, :ncols], ALU.mult, ALU.mult, [src, vecs, rs], [dst_f])
                    if dst_b is not None:
                        A("act", dst_b[:, c, :ncols], dst_f[:, c, :ncols], AF.Copy, [dst_f], [dst_b])
                else:
                    STT(dst_b[:, c, :ncols], src[:, c, :ncols], g_, rs[:, :ncols], ALU.mult, ALU.mult, [src, vecs, rs], [dst_b])

        def mla(l, bi, ncols, c0, xn, o_all):
            samp = (ncols == NS)
            n = ncols
            with contextlib.ExitStack() as ms_:
                wqb = S.sb([128, 3, 1280], BF16, "mla_wqb", ms_)
                S.dma("pool", wqb[:], wqb_d[l].rearrange("(kc p) n -> p kc n", p=128), writes=[wqb])
                wuk = S.sb([128, 2, 512], BF16, "mla_wuk", ms_)
                wuv = S.sb([128, 2, 512], BF16, "mla_wuv", ms_)
                S.dma("pool", wuk[:], wuk_d[l].rearrange("(kc p) n -> p kc n", p=128), writes=[wuk])
                S.dma("pool", wuv[:], wuv_d[l].rearrange("(kc p) n -> p kc n", p=128), writes=[wuv])
                ropt = S.sb([128, 2, TB], F32, "mla_rope", ms_)
                rc0 = SEQ if samp else c0
                S.dma("sp", ropt[:, :, :n], rope_d[:, :, rc0:rc0 + n], writes=[ropt])
                qn = S.sb([128, 3, TB], BF16, "mla_qn", ms_)
                cb = S.sb([128, 2, TB], BF16, "mla_cb", ms_)
                qnp = S.sb([128, 4, TB], BF16, "mla_qnp", ms_)
                qpe = S.sb([128, 8, TB], BF16, "mla_qpe", ms_)
                rt1 = S.sb([128, TB], F32, "mla_rt1", ms_)
                rt2 = S.sb([128, TB], F32, "mla_rt2", ms_)
                kpef = S.sb([32, TB], F32, "mla_kpef", ms_)
                with contextlib.ExitStack() as ps_:
                    qa_f = S.sb([128, 3, TB], F32, "mla_qaf", ps_)
                    kv_f = S.sb([128, 2, TB], F32, "mla_kvf", ps_)
                    c_f = kv_f
                    sqm = S.sb([128, 3, TB], BF16, "mla_sq", ps_)
                    rs = S.sb([128, TB], F32, "mla_rs", ps_)
                    wb = wgroup(l, 1920, 256)
                    for c in range(3):
                        if c == 2:
                            wb = wgroup(l, 2176, 128)
                        proj_fm(wb, (c % 2) * 128, 128, PS[c % 2], xn, n)
                        A("act", qa_f[:, c, :n], PS[c % 2][:, :n], AF.Copy, [PS[c % 2]], [qa_f])
                    wb = wgroup(l, 2304, 256)
                    for c in range(2):
                        proj_fm(wb, c * 128, 128, PS[2 + c], xn, n)
                        A("act", kv_f[:, c, :n], PS[2 + c][:, :n], AF.Copy, [PS[2 + c]], [kv_f])
                    norm_fm(qa_f, 3, n, "qn", l, None, qn, sqm, rs)
                    norm_fm(kv_f, 2, n, "kvn", l, c_f, cb, sqm, rs)
                    cdst = ckv_s_d[l] if samp else ckv_p_d[l][:, c0:c0 + n]
                    S.dma("sp", cdst.rearrange("(c p) t -> p c t", p=128), c_f[:, :, :n], reads=[c_f])
                    wb = wgroup(l, 2560, 96)
                    proj_fm(wb, 0, 64, PS[4], xn, n)
                    proj_fm(wb, 32, 64, PS[5], xn, n)
                    TTo(rt1[0:32, :n], PS[4][0:32, :n], ropt[0:32, 0, :n], ALU.mult, [PS[4], ropt], [rt1])
                    TTo(rt2[0:32, :n], PS[5][0:32, :n], ropt[0:32, 1, :n], ALU.mult, [PS[5], ropt], [rt2])
                    TTo(kpef[:, :n], rt1[0:32, :n], rt2[0:32, :n], ALU.add, [rt1, rt2], [kpef])
                    if DBG.get("kpe_dbg") == 1:
                        S.op("dve", lambda e: e.tensor_copy(out=kpef[:, :n], in_=PS[4][0:32, :n]), reads=[PS[4]], writes=[kpef])
                    if DBG.get("kpe_dbg") == 2:
                        S.op("dve", lambda e: e.tensor_copy(out=kpef[:, :n], in_=ropt[0:32, 0, :n]), reads=[ropt], writes=[kpef])
                    kdst = kpe_s_d[l] if samp else kpe_p_d[l][:, c0:c0 + n]
                    S.dma("sp", kdst, kpef[:, :n], reads=[kpef])
                    S.barrier()
                for j in range(4):
                    pp = PS[j % 2]

                    def f(e, j=j, pp=pp):
                        ins = None
                        for kc in range(3):
                            ins = e.matmul(pp[:, :n], lhsT=wqb[:, kc, j * 128:(j + 1) * 128], rhs=qn[:, kc, :n], start=(kc == 0), stop=(kc == 2))
                        return ins
                    S.op("pe", f, reads=[wqb, qn], writes=[pp])
                    A("act", qnp[:, j, :n], pp[:, :n], AF.Copy, [pp], [qnp], scale=MLA_SCALE)
                for h in range(8):
                    pb_ = 0 if samp else (h % 2) * 64
                    pa, pbk = PS[2 + (h % 2) * 2], PS[3 + (h % 2) * 2]

                    def f(e, h=h, pb_=pb_, pa=pa, pbk=pbk):
                        ins = None
                        for which, pt in ((0, pa), (1, pbk)):
                            for kc in range(3):
                                col = 512 + h * 96 + which * 32
                                ins = e.matmul(pt[pb_:pb_ + 64, :n], lhsT=wqb[:, kc, col:col + 64], rhs=qn[:, kc, :n], start=(kc == 0), stop=(kc == 2))
                        return ins
                    S.op("pe", f, reads=[wqb, qn], writes=[pa, pbk])
                    TTo(rt1[pb_:pb_ + 32, :n], pa[pb_:pb_ + 32, :n], ropt[pb_:pb_ + 32, 0, :n], ALU.mult, [pa, ropt], [rt1])
                    TTo(rt2[pb_:pb_ + 32, :n], pbk[pb_:pb_ + 32, :n], ropt[pb_:pb_ + 32, 1, :n], ALU.mult, [pbk, ropt], [rt2])
                    TTo(rt1[pb_:pb_ + 32, :n], rt1[pb_:pb_ + 32, :n], rt2[pb_:pb_ + 32, :n], ALU.add, [rt1, rt2], [rt1])
                    TS(qpe[pb_:pb_ + 32, h, :n], rt1[pb_:pb_ + 32, :n], MLA_SCALE, None, ALU.mult, None, [rt1], [qpe])
                if not samp:
                    mla_prompt_attn(l, bi, c0, wuk, wuv, cb, kpef, qnp, qpe, o_all, ms_)
                elif DBG.get("mla_s", 1):
                    mla_sample_attn(l, wuv, cb, kpef, qnp, qpe, o_all, ms_)
                S.barrier()

        def mla_sample_attn(l, wuv, cb, kpef, qnp, qpe, o_all, ms_):
            NT = 16
            NCH = 128 // NT
            wukT = S.sb([128, 4, 256], BF16, "mla_wukT", ms_)
            S.dma("pool", wukT[:], wukT_d[l], writes=[wukT])
            idx = S.sb([128, 4], I32, "mla_idx", ms_)
            S.dma("sp", idx[:], pt_d[:, :], writes=[idx])
            idxa = S.sb([128, 4, NCH], I32, "mla_idxa", ms_)
            for a in range(NCH):
                TS(idxa[:, :, a], idx[:, :], float(NCH), float(a), ALU.mult, ALU.add, [idx], [idxa])
            qlat = S.sb([128, 2, 8, NS], BF16, "mla_qlat", ms_)
            for hb in range(2):
                base = hb * 64
                pq = PS[hb]

                def f(e, hb=hb, base=base, pq=pq):
                    ins = None
                    for jp in range(4):
                        for rc in range(2):
                            ins = e.matmul(pq[:, (rc * 4 + jp) * 32:(rc * 4 + jp + 1) * 32], lhsT=wukT[base:base + 64, jp, rc * 128:(rc + 1) * 128],
                                           rhs=qnp[base:base + 64, jp, 0:NS], start=True, stop=True)
                    return ins
                S.op("pe", f, reads=[wukT, qnp], writes=[pq])
                A("act", qlat[:, :, :, :].rearrange("p r (j two) t -> p r j two t", two=2)[:, :, :, hb, :],
                  pq[:, 0:256].rearrange("p (r j t) -> p r j t", r=2, j=4), AF.Copy, [pq], [qlat])
            if DBG.get("ms_stage", 9) < 1:
                return
            kpeb = S.sb([32, NS], BF16, "mla_kpeb", ms_)
            A("act", kpeb[:, :], kpef[:, :NS], AF.Copy, [kpef], [kpeb])
            cnew = S.sb([32, 256], BF16, "mla_cnew", ms_)
            for rc in range(2):
                S.op("pe", lambda e, rc=rc: e.transpose(PS[2][:, :].bitcast(BF16)[:NS, rc * 128:(rc + 1) * 128], cb[:, rc, 0:NS], cstb[:, C_ID:C_ID + 128]),
                     reads=[cb, cstb], writes=[PS[2]])
            A("act", cnew[:, :], PS[2][:, :].bitcast(BF16)[:NS, 0:256], AF.Copy, [PS[2]], [cnew])
            cbuf = [S.sb([128, NT * 256], BF16, f"mla_cbuf{i}", ms_) for i in range(2)]
            kbuf = S.sb([128, 128 * 32 + 32], BF16, "mla_kbuf", ms_)
            S.op("dve", lambda e: e.memset(kbuf[:, 4096:4128], 0.0), writes=[kbuf])
            G4 = 4
            cT = [S.sb([128, G4 * 256], BF16, f"mla_cT{i}", ms_) for i in range(2)]
            kT = [S.sb([128, G4 * 128], BF16, f"mla_kT{i}", ms_) for i in range(2)]
            qpep = S.sb([128, 8, NS], BF16, "mla_qpep", ms_)
            kpebp = S.sb([128, NS], BF16, "mla_kpebp", ms_)
            for t_ in (kT[0], kT[1], qpep, kpebp):
                S.op("dve", lambda e, t_=t_: e.memset(t_[:], 0.0), writes=[t_])
            A("act", qpep[0:32, :, :], qpe[0:32, :, 0:NS], AF.Copy, [qpe, qpep], [qpep])
            A("act", kpebp[0:32, :], kpef[:, :NS], AF.Copy, [kpef, kpebp], [kpebp])
            Pt = [S.sb([128, G4 * 64], BF16, f"mla_Pt{i}", ms_) for i in range(2)]
            Ptn = S.sb([32, 64], BF16, "mla_Ptn", ms_)
            olat = S.sb([64, 256], BF16, "mla_olat", ms_)
            olatT = S.sb([128, 2, 64], BF16, "mla_olatT", ms_)
            rcs = S.sb([64, 1], F32, "mla_rcs", ms_)
            ckv2 = ckvc_d[l].rearrange("n (a t) d -> (n a) (t d)", t=NT)
            kpe2 = kpec_d[l].rearrange("n t d -> n (t d)")
            pool_e = S.eng["pool"]

            def gather(dst, src2, idx_ap, R):
                E = pool_e
                S._wait(E, R, [dst])
                S.pool_epoch_wait()
                if dst.dchan is None:
                    if dst.name not in S.named:
                        S.named[dst.name] = S.new_chan("d_" + dst.name)
                        S.dchans.append(S.named[dst.name])
                    dst.dchan = S.named[dst.name]
                ch = dst.dchan
                ins = E.b.indirect_dma_start(out=dst[:, 0:src2.shape[1]], out_offset=None, in_=src2, in_offset=bass.IndirectOffsetOnAxis(ap=idx_ap, axis=0))
                ch.count += 16
                ins.then_inc(ch.sem, 16)
                dst.w = (ch, ch.count)
                dst.r = []
                for t in R:
                    t.r.append((ch, ch.count))
            pO, pS_ = PS[7], PS[6]
            pK = PS[1]
            it = 0
            for b in range(4):
                qsl = slice(8 * b, 8 * b + 8)
                gather(kbuf, kpe2, idx[:, b:b + 1], [idx])
                for a in range(NCH):
                    cbf = cbuf[a % 2]
                    gather(cbf, ckv2, idxa[:, b, a:a + 1], [idxa])
                    for t4 in range(NT // G4):
                        par = it % 2
                        it += 1
                        pT = PS[2 + par]
                        tl = [a * NT + t4 * G4 + i for i in range(G4)]
                        ct = [cbf[:, (t4 * G4 + i) * 256:(t4 * G4 + i + 1) * 256] for i in range(G4)]

                        def fT(e, pT=pT, ct=ct, tl=tl):
                            ins = None
                            for i in range(G4):
                                e.transpose(pT[:, :].bitcast(BF16)[:, i * 256:i * 256 + 128], ct[i][:, 0:128], cstb[:, C_ID:C_ID + 128])
                                e.transpose(pT[:, :].bitcast(BF16)[:, i * 256 + 128:i * 256 + 256], ct[i][:, 128:256], cstb[:, C_ID:C_ID + 128])
                            for i in range(G4):
                                ins = e.transpose(pK[:, :].bitcast(BF16)[:64, i * 128:(i + 1) * 128], kbuf[:, tl[i] * 32:tl[i] * 32 + 64], cstb[:, C_ID:C_ID + 128])
                            return ins
                        S.op("pe", fT, reads=[cbf, kbuf, cstb], writes=[pT, pK])
                        A("act", cT[par][:, :], pT[:, :].bitcast(BF16)[:, 0:G4 * 256], AF.Copy, [pT], [cT[par]])
                        A("act", kT[par][0:32, :], pK[:, :].bitcast(BF16)[:32, 0:G4 * 128], AF.Copy, [pK], [kT[par]])
                        pSc = PS[4 + par]

                        def fS(e, par=par, pSc=pSc):
                            ins = None
                            for i in range(G4):
                                o_ = pSc[:, i * 64:(i + 1) * 64]
                                e.matmul(o_, lhsT=cT[par][:, i * 256:i * 256 + 128], rhs=qlat[:, 0, :, qsl], start=True, stop=False)
                                e.matmul(o_, lhsT=cT[par][:, i * 256 + 128:i * 256 + 256], rhs=qlat[:, 1, :, qsl], start=False, stop=False)
                                ins = e.matmul(o_, lhsT=kT[par][:, i * 128:(i + 1) * 128], rhs=qpep[:, :, qsl], start=False, stop=True)
                            return ins
                        S.op("pe", fS, reads=[cT[par], kT[par], qlat, qpep], writes=[pSc])
                        A("act", Pt[par][:, :], pSc[:, 0:G4 * 64], AF.Exp, [pSc], [Pt[par]])

                        def fP(e, par=par, ct=ct, first=(a == 0 and t4 == 0)):
                            ins = None
                            for i in range(G4):
                                st = first and i == 0
                                e.matmul(pO[0:64, 0:256], lhsT=Pt[par][:, i * 64:(i + 1) * 64], rhs=ct[i], start=st, stop=False)
                                ins = e.matmul(pS_[0:64, 0:2], lhsT=Pt[par][:, i * 64:(i + 1) * 64], rhs=ones_bf[:, 0:2], start=st, stop=False)
                            return ins
                        S.op("pe", fP, reads=[Pt[par], cbf, ones_bf], writes=[pO, pS_])
                if DBG.get("ms_stage", 9) < 6:
                    continue
                pSc = PS[4]

                def fSn(e):
                    e.matmul(pSc[:NS, 0:64], lhsT=cb[:, 0, 0:NS], rhs=qlat[:, 0, :, qsl], start=True, stop=False)
                    e.matmul(pSc[:NS, 0:64], lhsT=cb[:, 1, 0:NS], rhs=qlat[:, 1, :, qsl], start=False, stop=False)
                    return e.matmul(pSc[:NS, 0:64], lhsT=kpebp[:, 0:NS], rhs=qpep[:, :, qsl], start=False, stop=True)
                S.op("pe", fSn, reads=[cb, kpebp, qlat, qpep], writes=[pSc])
                A("act", Ptn[:, :], pSc[:NS, 0:64], AF.Exp, [pSc], [Ptn])
                TTo(Ptn[:, :].rearrange("p (h t) -> p h t", h=8), Ptn[:, :].rearrange("p (h t) -> p h t", h=8),
                    cstb[:NS, C_MS + 8 * b:C_MS + 8 * b + 8].unsqueeze(1).to_broadcast([NS, 8, 8]), ALU.mult, [Ptn, cstb], [Ptn])

                def fPn(e):
                    e.matmul(pO[0:64, 0:256], lhsT=Ptn[:, :], rhs=cnew[:, :], start=False, stop=True)
                    return e.matmul(pS_[0:64, 0:2], lhsT=Ptn[:, :], rhs=ones_bf[:NS, 0:2], start=False, stop=True)
                S.op("pe", fPn, reads=[Ptn, cnew, ones_bf], writes=[pO, pS_])
                S.op("dve", lambda e: e.reciprocal(out=rcs[:, :], in_=pS_[0:64, 0:1]), reads=[pS_], writes=[rcs])
                TS(olat[:, :], pO[0:64, 0:256], rcs[:, 0:1], None, ALU.mult, None, [pO, rcs], [olat])
                for rc in range(2):
                    S.op("pe", lambda e, rc=rc: e.transpose(PS[2][:, :].bitcast(BF16)[:, rc * 64:(rc + 1) * 64], olat[:, rc * 128:(rc + 1) * 128], cstb[:64, C_ID:C_ID + 64]),
                         reads=[olat, cstb], writes=[PS[2]])
                A("act", olatT[:, :, :].rearrange("p a b -> p (a b)"), PS[2][:, :].bitcast(BF16)[:, 0:128], AF.Copy, [PS[2]], [olatT])

                def fO(e, b=b):
                    ins = None
                    for h in range(8):
                        jp, base = h // 2, (h % 2) * 64
                        for rc in range(2):
                            ins = e.matmul(PS[0][base:base + 64, jp * 32 + 8 * b:jp * 32 + 8 * b + 8], lhsT=wuv[:, rc, h * 64:(h + 1) * 64],
                                           rhs=olatT[:, rc, h * 8:(h + 1) * 8], start=(rc == 0), stop=(rc == 1))
                    return ins
                S.op("pe", fO, reads=[wuv, olatT], writes=[PS[0]])
            if DBG.get("ms_stage", 9) < 6:
                return
            A("act", o_all[:, 4:8, 0:NS], PS[0][:, 0:128].rearrange("p (j t) -> p j t", j=4), AF.Copy, [PS[0]], [o_all])

        def mla_prompt_attn(l, bi, c0, wuk, wuv, cb, kpef, qnp, qpe, o_all, ms_):
            n = TB
            for j in range(4):
                pp = PS[j % 2]

                def f(e, j=j, pp=pp):
                    ins = None
                    for rc in range(2):
                        ins = e.matmul(pp[:, :n], lhsT=wuk[:, rc, j * 128:(j + 1) * 128], rhs=cb[:, rc, :n], start=(rc == 0), stop=(rc == 1))
                    return ins
                S.op("pe", f, reads=[wuk, cb], writes=[pp])
                A("act", knope_h[l][:, j, c0:c0 + n], pp[:, :n], AF.Copy, [pp], [knope_h[l]])
            A("act", kpe_h[l][0:32, c0:c0 + n], kpef[:, :n], AF.Copy, [kpef], [kpe_h[l]])
            S.dma("sp", kpe_h[l][64:96, c0:c0 + n], kpe_h[l][0:32, c0:c0 + n], reads=[kpe_h[l]], writes=[kpe_h[l]])
            for g in range(4):
                pp = PS[2 + g % 2]

                def f(e, g=g, pp=pp):
                    ins = None
                    for rc in range(2):
                        ins = e.matmul(pp[:, :512], lhsT=cb[:, rc, g * 128:(g + 1) * 128], rhs=wuv[:, rc, :], start=(rc == 0), stop=(rc == 1))
                    return ins
                S.op("pe", f, reads=[wuv, cb], writes=[pp])
                A("act", v_h[l][:, bi * 4 + g, :, 0:64], pp[:, :512].rearrange("p (h d) -> p h d", h=8), AF.Copy, [pp], [v_h[l]])
            QR = 256
            PT = S.sb([128, 16, QR], BF16, "mla_PT", ms_)
            o_tok = S.sb([128, 4, 512], BF16, "mla_otok", ms_)
            rcp = S.sb([128, 1], F32, "mla_rcp", ms_)
            for h in range(8):
                j, base = h // 2, (h % 2) * 64
                pss = (PS[0], PS[1]) if h % 2 == 0 else (PS[2], PS[3])
                for qr in range(TB // QR):
                    q0 = qr * QR
                    kb0 = (c0 + q0) // 128
                    nkb = kb0 + QR // 128
                    for kb in range(nkb):
                        pst = pss[kb % 2]

                        def f(e, kb=kb, pst=pst):
                            e.matmul(pst[:, :QR], lhsT=knope_h[l][base:base + 64, j, kb * 128:(kb + 1) * 128], rhs=qnp[base:base + 64, j, q0:q0 + QR],
                                     start=True, stop=False)
                            return e.matmul(pst[:, :QR], lhsT=kpe_h[l][base:base + 32, kb * 128:(kb + 1) * 128], rhs=qpe[base:base + 32, h, q0:q0 + QR],
                                            start=False, stop=True)
                        S.op("pe", f, reads=[knope_h[l], kpe_h[l], qnp, qpe], writes=[pst])
                        A("act", PT[:, kb, :], pst[:, :QR], AF.Exp, [pst], [PT])
                        dj = kb - kb0
                        if dj >= 0:
                            TTo(PT[:, kb, dj * 128:(dj + 1) * 128], PT[:, kb, dj * 128:(dj + 1) * 128], cstb[:, C_TRI:C_TRI + 128], ALU.mult, [PT, cstb], [PT])
                    for qs in range(QR // 128):
                        po = PS[4 + qs % 2]
                        nk = kb0 + qs + 1

                        def f(e, qs=qs, po=po, nk=nk):
                            ins = None
                            for kb in range(nk):
                                ins = e.matmul(po[:, 0:65], lhsT=PT[:, kb, qs * 128:(qs + 1) * 128], rhs=v_h[l][:, kb, h, :], start=(kb == 0), stop=(kb == nk - 1))
                            return ins
                        S.op("pe", f, reads=[PT, v_h[l]], writes=[po])
                        S.op("dve", lambda e, po=po: e.reciprocal(out=rcp[:, :], in_=po[:, 64:65]), reads=[po], writes=[rcp])
                        TS(o_tok[:, qr * 2 + qs, h * 64:(h + 1) * 64], po[:, 0:64], rcp[:, 0:1], None, ALU.mult, None, [po, rcp], [o_tok])
            for qs in range(4):
                for m in range(4):
                    S.op("pe", lambda e, qs=qs, m=m: e.transpose(PS[6][:, :].bitcast(BF16)[:, m * 128:(m + 1) * 128], o_tok[:, qs, m * 128:(m + 1) * 128],
                                                               cstb[:, C_ID:C_ID + 128]), reads=[o_tok, cstb], writes=[PS[6]])
                for m in range(4):
                    A("act", o_all[:, 4 + m, qs * 128:(qs + 1) * 128], PS[6][:, :].bitcast(BF16)[:, m * 128:(m + 1) * 128], AF.Copy, [PS[6]], [o_all])

        def mixer(l, bi, ncols, c0):
            with contextlib.ExitStack() as ms:
                xn = S.sb([128, KC, TB], BF16, "mxn", ms)
                o_all = S.sb([128, KC, TB], BF16, "oall", ms)
                S.op("dve", lambda e: e.memset(o_all[:], 0.0), writes=[o_all])
                with contextlib.ExitStack() as ns:
                    sq = S.sb([128, KC, TB], BF16, "msq", ns)
                    rstd = S.sb([128, TB], F32, "mrstd", ns)
                    rmsnorm(x, ncols, VL[f"nmix_{l}"], xn, sq, rstd, PS[7])
                    S.barrier()
                if DBG.get('hg', 1):
                    hgrn(l, ncols, xn, o_all)
                if DBG.get('rw', 1):
                    rwkv(l, bi, ncols, xn, o_all)
                if DBG.get('mla', 1):
                    mla(l, bi, ncols, c0, xn, o_all)
                with contextlib.ExitStack() as ws:
                    wout = S.sb([128, KC, D], BF16, "wout", ws)
                    S.dma("pool", wout[:], wout_d[l].rearrange("(kc p) n -> p kc n", p=128), writes=[wout])
                    for fo in range(KC):
                        pp = PS[fo % 2]

                        def f(e, fo=fo, pp=pp):
                            ins = None
                            for c in range(KC):
                                ins = e.matmul(pp[:, :ncols], lhsT=wout[:, c, fo * 128:(fo + 1) * 128], rhs=o_all[:, c, :ncols],
                                               start=(c == 0), stop=(c == KC - 1))
                            return ins
                        S.op("pe", f, reads=[wout, o_all], writes=[pp])
                        TTo(x[:, fo, :ncols], pp[:, :ncols], x[:, fo, :ncols], ALU.add, [pp, x], [x])
                    if bi == NPB - 1:
                        S.dma("sp", hg_p_d[l].rearrange("(pc hb) k v -> (hb k) pc v", hb=2), S_hg[l][:], reads=[S_hg[l]])
                    S.barrier()

        blocks = [(xpT, yT_p, b * TB, TB) for b in range(NPB)] + [(xsT, yT_s, 0, NS)]
        alloc_kv()
        for bi, (src, dst, c0, ncols) in enumerate(blocks):
            if bi == NPB:
                S.barrier()
                kv_es.close()
            if bi not in DBG.get('blocks', range(10)):
                continue
            S.dma("sp", x[:, :, :ncols], src.rearrange("(kc p) t -> p kc t", p=128)[:, :, c0:c0 + ncols], writes=[x])
            for l in DBG.get('layers', range(DEPTH)):
                ffn(l, 0, ncols, 0 + l * 8)
                mixer(l, bi, ncols, c0)
                ffn(l, 1, ncols, 32 + l * 8)
            with contextlib.ExitStack() as fs:
                sq = S.sb([128, KC, TB], BF16, "sq", fs)
                rstd = S.sb([128, TB], F32, "rstd", fs)
                yo = S.sb([128, KC, TB], F32, "yo", fs)
                for kc in range(KC):
                    S.op("act", lambda e, kc=kc: e.activation(out=sq[:, kc, :ncols], in_=x[:, kc, :ncols], func=AF.Square),
                         reads=[x], writes=[sq])

                def mm(e):
                    ins = None
                    for kc in range(KC):
                        ins = e.matmul(PS[7][:, :ncols], lhsT=ones_bf[:], rhs=sq[:, kc, :ncols], start=(kc == 0), stop=(kc == KC - 1))
                    return ins
                S.op("pe", mm, reads=[sq, ones_bf], writes=[PS[7]])
                S.op("act", lambda e: e.activation(out=rstd[:, :ncols], in_=PS[7][:, :ncols], func=AF.Sqrt, scale=1.0 / D, bias=epsb[:, 0:1]),
                     reads=[PS[7], epsb], writes=[rstd])
                S.op("dve", lambda e: e.reciprocal(out=rstd[:, :ncols], in_=rstd[:, :ncols]), reads=[rstd], writes=[rstd])
                for kc in range(KC):
                    S.op("dve", lambda e, kc=kc: e.scalar_tensor_tensor(out=yo[:, kc, :ncols], in0=x[:, kc, :ncols],
                                                                      scalar=vecs[:, VL["nfin"] + kc:VL["nfin"] + kc + 1], in1=rstd[:, :ncols],
                                                                      op0=ALU.mult, op1=ALU.mult),
                         reads=[x, vecs, rstd], writes=[yo])
                S.dma("sp", dst.rearrange("(kc p) t -> p kc t", p=128)[:, :, c0:c0 + ncols], yo[:, :, :ncols], reads=[yo])
                S.barrier()
        S.finish()
    return nc


def _fm(v):
    return np.ascontiguousarray(np.asarray(v, np.float32).reshape(-1, 128).T)


def kernel(**inp):
    ncores = DBG.get('ncores', 8)
    nc = build()
    f32 = np.float32
    vecs = np.zeros((128, NV), f32)

    def put(name, v):
        a = _fm(v)
        vecs[:, VL[name]:VL[name] + a.shape[1]] = a
    for l in range(DEPTH):
        put(f"nf1_{l}", inp["norm_ffn1"][l])
        put(f"nmix_{l}", inp["norm_mix"][l])
        put(f"nf2_{l}", inp["norm_ffn2"][l])
        put(f"hglog_{l}", inp["hg_lb_logits"][l])
        put(f"hgn_{l}", np.tile(np.asarray(inp["hg_norm"][l], f32), 2))
        put(f"mu_{l}", inp["rw_mu"][l])
        put(f"w0_{l}", inp["rw_w0"][l])
        put(f"a0_{l}", inp["rw_a0"][l])
        put(f"kk_{l}", inp["rw_kk"][l])
        put(f"ka_{l}", inp["rw_ka"][l])
        put(f"rk_{l}", np.asarray(inp["rw_rk"][l], f32).reshape(-1))
        put(f"lnw_{l}", inp["rw_ln_w"][l])
        put(f"lnb_{l}", inp["rw_ln_b"][l])
        put(f"qn_{l}", inp["mla_q_norm"][l])
        put(f"kvn_{l}", inp["mla_kv_norm"][l])
    put("nfin", inp["norm_final"])
    shared = {"vecs": vecs, "cst": make_consts()}
    shared["lora"] = np.ascontiguousarray(np.stack([np.concatenate([inp["rw_w2"][l], inp["rw_a2"][l], inp["rw_g2"][l]], axis=0)
                                                    for l in range(DEPTH)]), f32)
    wq = np.asarray(inp["mla_wqb"], f32).reshape(DEPTH, 384, 8, 96)
    sw = (np.arange(32) + 16) % 32
    wr = wq[..., 64:]
    shared["wqb"] = np.ascontiguousarray(np.concatenate([wq[..., :64].reshape(DEPTH, 384, 512),
                                                         np.concatenate([wr, wr[..., sw], wr], axis=-1).reshape(DEPTH, 384, 768)], axis=2))
    shared["wuk"] = np.ascontiguousarray(np.asarray(inp["mla_wuk"], f32).reshape(DEPTH, 256, 512))
    shared["wuv"] = np.ascontiguousarray(np.asarray(inp["mla_wuv"], f32).reshape(DEPTH, 256, 512))
    wk = np.asarray(inp["mla_wuk"], f32).transpose(0, 2, 3, 1).reshape(DEPTH, 4, 2, 64, 256)
    shared["wukT"] = np.ascontiguousarray(wk.transpose(0, 2, 3, 1, 4).reshape(DEPTH, 128, 4, 256))
    for l in range(DEPTH):
        shared[f"ckvc{l}"] = np.ascontiguousarray(inp["cache_mla_ckv"][l][:DBG.get('npool', 5120)])
        shared[f"kpec{l}"] = np.ascontiguousarray(inp["cache_mla_kpe"][l][:DBG.get('npool', 5120)])
    past_len = int(inp["page_table"].shape[1]) * int(inp["cache_mla_ckv"].shape[2])
    pos = np.concatenate([np.arange(SEQ, dtype=f32), (past_len + np.tile(np.arange(8, dtype=f32), 4)).astype(f32)])
    inv = np.exp(-np.log(f32(10000.0)) * np.arange(16, dtype=f32) / f32(16)).astype(f32)
    ang = (pos[None, :] * inv[:, None]).astype(f32)
    cos2 = np.concatenate([np.cos(ang), np.cos(ang)], axis=0).astype(f32)
    sin2 = np.concatenate([-np.sin(ang), np.sin(ang)], axis=0).astype(f32)
    rope = np.zeros((128, 2, SEQ + NS), f32)
    for r0 in (0, 64):
        rope[r0:r0 + 32, 0] = cos2
        rope[r0:r0 + 32, 1] = sin2
    shared["rope"] = rope
    for l in range(DEPTH):
        w = np.asarray(inp["w_in"][l], f32)
        wfull = np.concatenate([w, w[:, 2576:2592], w[:, 2560:2576], w[:, 2560:2592]], axis=1).reshape(KC, 128, WIN_COLS)
        shared[f"win{l}"] = np.ascontiguousarray(np.concatenate(
            [wfull[:, :, c0_:c0_ + n_].transpose(1, 0, 2).reshape(128, KC * n_) for c0_, n_ in WIN_GROUPS], axis=1))
        shared[f"wout{l}"] = np.ascontiguousarray(inp["w_out"][l], f32)
    for l in range(DEPTH):
        for f_, (kwi, kwo) in enumerate((("ffn1_wi", "ffn1_wo"), ("ffn2_wi", "ffn2_wo"))):
            wi = np.asarray(inp[kwi][l], f32).reshape(KC, 128, 2, NJ // GJ, GJ * 128)
            shared[f"wi{l}{f_}"] = np.ascontiguousarray(wi.transpose(3, 1, 0, 2, 4).reshape(NJ // GJ, 128, KC * 2 * GJ * 128))
            wo = np.asarray(inp[kwo][l], f32).reshape(NJ // GJ, GJ, 128, 2, 512)
            shared[f"wo{l}{f_}"] = np.ascontiguousarray(wo.transpose(3, 0, 2, 1, 4).reshape(2, NJ // GJ, 128, GJ * 512))
    in_maps = []
    for c in range(ncores):
        m = dict(shared)
        m["xpT"] = np.ascontiguousarray(np.asarray(inp["x_prompt"][c], f32).T)
        m["xsT"] = np.ascontiguousarray(np.asarray(inp["x_sample"][4 * c:4 * c + 4], f32).reshape(NS, D).T)
        m["hgst"] = np.ascontiguousarray(np.asarray(inp["state_hgrn"][:, 4 * c:4 * c + 4], f32))
        m["pt"] = np.ascontiguousarray((np.asarray(inp["page_table"][4 * c:4 * c + 4]) % DBG.get('npool', 1 << 30)).astype(np.int32).T)
        m["rwst"] = np.ascontiguousarray(np.asarray(inp["state_rwkv"][:, 4 * c:4 * c + 4], f32).transpose(0, 1, 2, 4, 3))
        sh = np.asarray(inp["state_rwkv_shift"][:, 4 * c:4 * c + 4], f32)
        m["rwsh"] = np.ascontiguousarray(sh.reshape(DEPTH, 4, 7, 128).transpose(0, 3, 2, 1))
        in_maps.append(m)
    res = run_bass_kernel_spmd(nc, in_maps, core_ids=list(range(ncores)))
    R = res.results
    y_p = np.stack([R[c]["yT_p"].T for c in range(ncores)]).astype(f32)
    y_s = np.concatenate([R[c]["yT_s"].T.reshape(4, 8, D) for c in range(ncores)]).astype(f32)
    hg_p = np.stack([R[c]["hg_p"] for c in range(ncores)], axis=1).astype(f32)
    hg_s = np.concatenate([R[c]["hg_s"] for c in range(ncores)], axis=1).astype(f32)
    z = lambda *sh: np.zeros(sh, f32)
    rw_p = np.stack([R[c]["rw_p"].transpose(0, 1, 3, 2) for c in range(ncores)], axis=1).astype(f32)
    rw_s = np.concatenate([R[c]["rw_s"].transpose(0, 1, 2, 4, 3) for c in range(ncores)], axis=1).astype(f32)
    sh_p = np.stack([R[c]["sh_p"].transpose(0, 2, 1).reshape(DEPTH, 896) for c in range(ncores)], axis=1).astype(f32)
    sh_s = np.concatenate([R[c]["sh_s"].transpose(0, 3, 2, 1).reshape(DEPTH, 4, 896) for c in range(ncores)], axis=1).astype(f32)
    ckv_p = np.stack([R[c]["ckv_p"].transpose(0, 2, 1) for c in range(ncores)], axis=1).astype(f32)
    ckv_s = np.concatenate([R[c]["ckv_s"].transpose(0, 2, 1).reshape(DEPTH, 4, 8, 256) for c in range(ncores)], axis=1).astype(f32)
    kpe_p = np.stack([R[c]["kpe_p"].transpose(0, 2, 1) for c in range(ncores)], axis=1).astype(f32)
    kpe_s = np.concatenate([R[c]["kpe_s"].transpose(0, 2, 1).reshape(DEPTH, 4, 8, 32) for c in range(ncores)], axis=1).astype(f32)
    return (y_p, y_s, hg_p, hg_s, rw_p, rw_s, sh_p, sh_s, ckv_p, ckv_s, kpe_p, kpe_s)
```

```python
import contextlib
import numpy as np
import concourse.bass as bass
import concourse.mybir as mybir
from concourse.bass_utils import run_bass_kernel_spmd

F32 = mybir.dt.float32
BF16 = mybir.dt.bfloat16
I32 = mybir.dt.int32
U32 = mybir.dt.uint32
AF = mybir.ActivationFunctionType
ALU = mybir.AluOpType
AX = mybir.AxisListType

D = 1024
SEQ = 2048
DEPTH = 2
DFF = 2816
NJ = DFF // 128
KC = D // 128
EPS = 1e-6
TB = 512
NPB = SEQ // TB
NS = 32
IN_COLS = 2592
GJ = 2


WIN_COLS = 2656
WIN_GROUPS = [(0, 256), (256, 256), (512, 256), (768, 256), (1024, 256), (1280, 256), (1536, 256), (1792, 128),
              (1920, 256), (2176, 128), (2304, 256), (2560, 96)]
WIN_OFF = {}
_o = 0
for _c0, _n in WIN_GROUPS:
    WIN_OFF[(_c0, _n)] = _o
    _o += KC * _n
WIN_FLAT = _o
C_ID, C_MP, C_MPS, C_BLK, C_SEGP, C_SEGS, C_MS, C_MSS, C_ROWS = 0, 128, 256, 384, 512, 1024, 1056, 1088, 1120
C_MPL, C_MSL, C_TRI = 1124, 1252, 1284
NCST = 1412


def vec_layout():
    L = {}
    c = [0]

    def add(name, n):
        L[name] = c[0]
        c[0] += n
    for l in range(DEPTH):
        add(f"nf1_{l}", 8)
    for l in range(DEPTH):
        add(f"nmix_{l}", 8)
    for l in range(DEPTH):
        add(f"nf2_{l}", 8)
    add("nfin", 8)
    for l in range(DEPTH):
        add(f"hglog_{l}", 2)
    for l in range(DEPTH):
        add(f"hgn_{l}", 1)
    for l in range(DEPTH):
        for nm, n in (("mu", 7), ("w0", 2), ("a0", 2), ("kk", 2), ("ka", 2), ("rk", 2), ("lnw", 2), ("lnb", 2), ("qn", 3), ("kvn", 2)):
            add(f"{nm}_{l}", n)
    L["_n"] = c[0]
    return L


VL = vec_layout()
NV = VL["_n"]


def make_consts():
    c = np.zeros((128, NCST), np.float32)
    i = np.arange(128)
    c[:, C_ID:C_ID + 128] = np.eye(128)
    same = (i[:, None] // 64) == (i[None, :] // 64)
    c[:, C_MP:C_MP + 128] = same & (i[:, None] <= i[None, :])
    c[:, C_MPS:C_MPS + 128] = same & (i[:, None] < i[None, :])
    c[:, C_BLK:C_BLK + 128] = same
    t = np.arange(512)
    c[:, C_SEGP:C_SEGP + 512] = (t % 64 != 0)[None, :]
    t = np.arange(32)
    c[:, C_SEGS:C_SEGS + 32] = (t % 8 != 0)[None, :]
    j = np.arange(32)
    same8 = (j[:, None] // 8) == (j[None, :] // 8)
    c[:32, C_MS:C_MS + 32] = same8 & (j[:, None] <= j[None, :])
    c[:32, C_MSS:C_MSS + 32] = same8 & (j[:, None] < j[None, :])
    for b in range(4):
        c[8 * b:8 * b + 8, C_ROWS + b] = 1.0
    c[:, C_MPL:C_MPL + 128] = same & (i[:, None] > i[None, :])
    c[:32, C_MSL:C_MSL + 32] = same8 & (j[:, None] > j[None, :])
    c[:, C_TRI:C_TRI + 128] = (i[:, None] <= i[None, :])
    return c


DBG = {}


class Chan:
    def __init__(self, sem):
        self.sem = sem
        self.count = 0


class Eng:
    def __init__(self, name, b, chan, selfsync):
        self.name = name
        self.b = b
        self.chan = chan
        self.seen = {}
        self.selfsync = selfsync


class TT:
    def __init__(self, t, name):
        self.t = t
        self.name = name
        self.w = None
        self.r = []
        self.dchan = None

    def __getitem__(self, idx):
        return self.t[idx]


class Sched:
    def __init__(self, nc, es):
        self.nc = nc
        self.es = es
        self.nsem = 0
        self.eng = {}
        for name, b, ss in (("pe", nc.tensor, False), ("act", nc.scalar, DBG.get("ss", True)), ("dve", nc.vector, DBG.get("ss", True)),
                            ("pool", nc.gpsimd, True), ("sp", nc.sync, False)):
            self.eng[name] = Eng(name, b, self.new_chan("e_" + name), ss)
        self.dchans = []
        self.named = {}
        self.rec = None
        self.epoch = {}
        self.ntile = 0

    def new_chan(self, name):
        self.nsem += 1
        return Chan(self.es.enter_context(self.nc.semaphore(name)))

    def sb(self, shape, dt, name, es=None):
        self.ntile += 1
        t = (es or self.es).enter_context(self.nc.sbuf_tensor(f"{name}_{self.ntile}", list(shape), dt))
        return TT(t, name)

    def ps(self, shape, dt, name, es=None):
        self.ntile += 1
        t = (es or self.es).enter_context(self.nc.psum_tensor(f"{name}_{self.ntile}", list(shape), dt))
        return TT(t, name)

    def _wait(self, E, reads, writes):
        deps = {}

        def add(d):
            ch, cnt = d
            if deps.get(ch, 0) < cnt:
                deps[ch] = cnt
        for t in reads:
            if t.w is not None:
                add(t.w)
        for t in writes:
            if t.w is not None:
                add(t.w)
            for r in t.r:
                add(r)
        for ch, cnt in deps.items():
            if ch is E.chan and not E.selfsync:
                continue
            if E.seen.get(ch, 0) < cnt:
                E.b.wait_ge(ch.sem, cnt)
                E.seen[ch] = cnt

    def op(self, eng, fn, reads=(), writes=()):
        if self.rec is not None:
            rec, self_ = self.rec, self
            rec.append(lambda: self_._replay(self_.op, eng, fn, reads, writes))
            return None
        E = self.eng[eng]
        self._wait(E, reads, writes)
        if E.selfsync and DBG.get("serial", True) and E.seen.get(E.chan, 0) < E.chan.count:
            E.b.wait_ge(E.chan.sem, E.chan.count)
            E.seen[E.chan] = E.chan.count
        ins = fn(E.b)
        E.chan.count += 1
        ins.then_inc(E.chan.sem, 1)
        stamp = (E.chan, E.chan.count)
        for t in writes:
            t.w = stamp
            t.r = []
        for t in reads:
            t.r.append(stamp)
        return ins

    def _replay(self, f, *a, **kw):
        saved, self.rec = self.rec, None
        try:
            return f(*a, **kw)
        finally:
            self.rec = saved

    def dma(self, q, out, in_, reads=(), writes=(), **kw):
        if self.rec is not None:
            rec, self_ = self.rec, self
            rec.append(lambda: self_._replay(self_.dma, q, out, in_, reads, writes, **kw))
            return None
        E = self.eng[q]
        self._wait(E, reads, writes)
        if q == "pool" and not kw.pop("persistent", False):
            self.pool_epoch_wait()
        owner = (list(writes) + list(reads))[0]
        if owner.dchan is None:
            if owner.name not in self.named:
                self.named[owner.name] = self.new_chan("d_" + owner.name)
                self.dchans.append(self.named[owner.name])
            owner.dchan = self.named[owner.name]
        ch = owner.dchan
        ins = E.b.dma_start(out=out, in_=in_, **kw)
        ch.count += 16
        ins.then_inc(ch.sem, 16)
        stamp = (ch, ch.count)
        for t in writes:
            t.w = stamp
            t.r = []
        for t in reads:
            t.r.append(stamp)
        return ins

    def barrier(self, engines=("pe", "act", "dve", "sp"), dchans=True):
        chans = [self.eng[e].chan for e in self.eng] + (self.dchans if dchans else [])
        if dchans:
            self.epoch = {ch: ch.count for ch in chans if ch is not self.eng["pool"].chan}
        for e in engines:
            E = self.eng[e]
            for ch in chans:
                if ch is E.chan:
                    continue
                if E.seen.get(ch, 0) < ch.count:
                    E.b.wait_ge(ch.sem, ch.count)
                    E.seen[ch] = ch.count

    def pool_epoch_wait(self):
        E = self.eng["pool"]
        for ch, cnt in self.epoch.items():
            if E.seen.get(ch, 0) < cnt:
                E.b.wait_ge(ch.sem, cnt)
                E.seen[ch] = cnt

    def finish(self):
        self.barrier(engines=("sp",))


def build():
    nc = bass.Bass("TRN2", target_bir_lowering=False)
    dt_in = lambda name, shape, dt=F32: nc.dram_tensor(name, list(shape), dt, kind="ExternalInput").ap()
    dt_out = lambda name, shape, dt=F32: nc.dram_tensor(name, list(shape), dt, kind="ExternalOutput").ap()

    xpT = dt_in("xpT", [D, SEQ])
    xsT = dt_in("xsT", [D, NS])
    wi_d = [[dt_in(f"wi{l}{f}", [NJ // GJ, 128, KC * 2 * GJ * 128]) for f in range(2)] for l in range(DEPTH)]
    wo_d = [[dt_in(f"wo{l}{f}", [2, NJ // GJ, 128, GJ * 512]) for f in range(2)] for l in range(DEPTH)]
    vec_d = dt_in("vecs", [128, NV])
    cst_d = dt_in("cst", [128, NCST])
    win_d = [dt_in(f"win{l}", [128, WIN_FLAT]) for l in range(DEPTH)]
    wout_d = [dt_in(f"wout{l}", [D, D]) for l in range(DEPTH)]
    hgst_d = dt_in("hgst", [DEPTH, 4, 4, 64, 64])
    hg_p_d = dt_out("hg_p", [DEPTH, 4, 64, 64])
    lora_d = dt_in("lora", [DEPTH, 128, 256])
    wqb_d = dt_in("wqb", [DEPTH, 384, 1280])
    wuk_d = dt_in("wuk", [DEPTH, 256, 512])
    wuv_d = dt_in("wuv", [DEPTH, 256, 512])
    rope_d = dt_in("rope", [128, 2, SEQ + NS])
    NPOOL = DBG.get('npool', 5120)
    ckvc_d = [dt_in(f"ckvc{l}", [NPOOL, 128, 256]) for l in range(DEPTH)]
    kpec_d = [dt_in(f"kpec{l}", [NPOOL, 128, 32]) for l in range(DEPTH)]
    pt_d = dt_in("pt", [128, 4], I32)
    wukT_d = dt_in("wukT", [DEPTH, 128, 4, 256])
    ckv_p_d = dt_out("ckv_p", [DEPTH, 256, SEQ])
    ckv_s_d = dt_out("ckv_s", [DEPTH, 256, NS])
    kpe_p_d = dt_out("kpe_p", [DEPTH, 32, SEQ])
    kpe_s_d = dt_out("kpe_s", [DEPTH, 32, NS])
    rwst_d = dt_in("rwst", [DEPTH, 4, 4, 64, 64])
    rwsh_d = dt_in("rwsh", [DEPTH, 128, 7, 4])
    rw_p_d = dt_out("rw_p", [DEPTH, 4, 64, 64])
    rw_s_d = dt_out("rw_s", [DEPTH, 4, 4, 64, 64])
    sh_p_d = dt_out("sh_p", [DEPTH, 128, 7])
    sh_s_d = dt_out("sh_s", [DEPTH, 128, 7, 4])
    hg_s_d = dt_out("hg_s", [DEPTH, 4, 4, 64, 64])
    yT_p = dt_out("yT_p", [D, SEQ])
    yT_s = dt_out("yT_s", [D, NS])

    with contextlib.ExitStack() as es:
        S = Sched(nc, es)
        vecs = S.sb([128, NV], F32, "vecs")
        S.dma("sp", vecs[:], vec_d[:, :], writes=[vecs])
        cst = S.sb([128, NCST], F32, "cst")
        S.dma("sp", cst[:], cst_d[:, :], writes=[cst])
        cstb = S.sb([128, NCST], BF16, "cstb")
        S.op("dve", lambda e: e.tensor_copy(out=cstb[:], in_=cst[:]), reads=[cst], writes=[cstb])
        ones_bf = S.sb([128, 128], BF16, "ones")
        S.op("dve", lambda e: e.memset(ones_bf[:], 1.0), writes=[ones_bf])

        x = S.sb([128, KC, TB], F32, "x")
        NWI, NWO = 3, 4
        wi_buf = [S.sb([128, KC, 2, GJ * 128], BF16, f"wi{i}") for i in range(NWI)]
        wo_buf = [S.sb([128, GJ, 512], BF16, f"wo{i}") for i in range(NWO)]
        wi_ctr = [0]
        wo_ctr = [0]
        PS = [S.ps([128, 512], F32, f"ps{i}") for i in range(8)]

        def rmsnorm(xt, ncols, gcol, out_bf, sq, rstd, psb):
            for kc in range(KC):
                S.op("act", lambda e, kc=kc: e.activation(out=sq[:, kc, :ncols], in_=xt[:, kc, :ncols], func=AF.Square),
                     reads=[xt], writes=[sq])

            def mm(e):
                ins = None
                for kc in range(KC):
                    ins = e.matmul(psb[:, :ncols], lhsT=ones_bf[:], rhs=sq[:, kc, :ncols], start=(kc == 0), stop=(kc == KC - 1))
                return ins
            S.op("pe", mm, reads=[sq, ones_bf], writes=[psb])
            S.op("act", lambda e: e.activation(out=rstd[:, :ncols], in_=psb[:, :ncols], func=AF.Sqrt, scale=1.0 / D, bias=epsb[:, 0:1]),
                 reads=[psb, epsb], writes=[rstd])
            S.op("dve", lambda e: e.reciprocal(out=rstd[:, :ncols], in_=rstd[:, :ncols]), reads=[rstd], writes=[rstd])
            for kc in range(KC):
                S.op("dve", lambda e, kc=kc: e.scalar_tensor_tensor(out=out_bf[:, kc, :ncols], in0=xt[:, kc, :ncols],
                                                                  scalar=vecs[:, gcol + kc:gcol + kc + 1], in1=rstd[:, :ncols],
                                                                  op0=ALU.mult, op1=ALU.mult),
                     reads=[xt, vecs, rstd], writes=[out_bf])

        epsb = S.sb([128, 1], F32, "epsb")
        S.op("dve", lambda e: e.memset(epsb[:], EPS), writes=[epsb])

        def ffn(l, f, ncols, gcol):
            with contextlib.ExitStack() as fs:
                xn = S.sb([128, KC, TB], BF16, "xn", fs)
                sq = S.sb([128, KC, TB], BF16, "sq", fs)
                rstd = S.sb([128, TB], F32, "rstd", fs)
                h = [S.sb([128, TB], BF16, f"h{j}", fs) for j in range(NJ)]
                sa = [S.sb([128, TB], BF16, f"sa{i}", fs) for i in range(2)]
                rmsnorm(x, ncols, gcol, xn, sq, rstd, PS[7])
                for g in range(NJ // GJ):
                    wb = wi_buf[wi_ctr[0] % NWI]
                    wi_ctr[0] += 1
                    if not (DBG.get("nodma") and wi_ctr[0] > NWI):
                        S.dma("pool", wb[:, :, :, :].rearrange("p a b c -> p (a b c)"), wi_d[l][f][g], writes=[wb], persistent=True)
                    for jj in range(GJ):
                        j = g * GJ + jj
                        pa, pb = PS[(j % 2) * 2], PS[(j % 2) * 2 + 1]

                        def mma(e, wb=wb, jj=jj, pa=pa):
                            ins = None
                            for kc in range(KC):
                                ins = e.matmul(pa[:, :ncols], lhsT=wb[:, kc, 0, jj * 128:(jj + 1) * 128], rhs=xn[:, kc, :ncols],
                                               start=(kc == 0), stop=(kc == KC - 1))
                            return ins

                        def mmb(e, wb=wb, jj=jj, pb=pb):
                            ins = None
                            for kc in range(KC):
                                ins = e.matmul(pb[:, :ncols], lhsT=wb[:, kc, 1, jj * 128:(jj + 1) * 128], rhs=xn[:, kc, :ncols],
                                               start=(kc == 0), stop=(kc == KC - 1))
                            return ins
                        S.op("pe", mma, reads=[wb, xn], writes=[pa])
                        S.op("pe", mmb, reads=[wb, xn], writes=[pb])
                        st = sa[j % 2]
                        S.op("act", lambda e, pa=pa, st=st: e.activation(out=st[:, :ncols], in_=pa[:, :ncols], func=AF.Silu),
                             reads=[pa], writes=[st])
                        S.op("dve", lambda e, pb=pb, st=st, j=j: e.tensor_tensor(out=h[j][:, :ncols], in0=pb[:, :ncols], in1=st[:, :ncols], op=ALU.mult),
                             reads=[pb, st], writes=[h[j]])
                for half in range(2):
                    acc = [PS[4 + i] for i in range(4)]
                    for g in range(NJ // GJ):
                        wb = wo_buf[wo_ctr[0] % NWO]
                        wo_ctr[0] += 1
                        if not (DBG.get("nodma") and wo_ctr[0] > NWO):
                            S.dma("pool", wb[:, :, :].rearrange("p a b -> p (a b)"), wo_d[l][f][half, g], writes=[wb], persistent=True)
                        for jj in range(GJ):
                            j = g * GJ + jj
                            for fo in range(4):
                                S.op("pe", lambda e, wb=wb, jj=jj, j=j, fo=fo: e.matmul(
                                    acc[fo][:, :ncols], lhsT=wb[:, jj, fo * 128:(fo + 1) * 128], rhs=h[j][:, :ncols],
                                    start=(j == 0), stop=(j == NJ - 1)),
                                    reads=[wb, h[j]], writes=[acc[fo]])
                    for fo in range(4):
                        kc = half * 4 + fo
                        S.op("dve", lambda e, fo=fo, kc=kc: e.scalar_tensor_tensor(
                            out=x[:, kc, :ncols], in0=acc[fo][:, :ncols], scalar=0.5, in1=x[:, kc, :ncols],
                            op0=ALU.mult, op1=ALU.add), reads=[acc[fo], x], writes=[x])
                S.barrier()


        NWB = 2
        win_buf = [S.sb([128, KC, 256], BF16, f"win{i}") for i in range(NWB)]
        win_ctr = [0]
        lbt = S.sb([128, 8], F32, "lbt")
        S.op("dve", lambda e: e.memset(lbt[:], 0.0), writes=[lbt])
        S.op("dve", lambda e: e.tensor_tensor(out=lbt[:, 2:4], in0=vecs[:, VL["hglog_1"]:VL["hglog_1"] + 2],
                                              in1=vecs[:, VL["hglog_0"]:VL["hglog_0"] + 2], op=ALU.subtract),
             reads=[vecs, lbt], writes=[lbt])
        S.op("act", lambda e: e.activation(out=lbt[:, 2:4], in_=lbt[:, 2:4], func=AF.Sigmoid), reads=[lbt], writes=[lbt])
        S.op("dve", lambda e: e.tensor_scalar(out=lbt[:, 4:8], in0=lbt[:, 0:4], scalar1=-1.0, scalar2=1.0, op0=ALU.mult, op1=ALU.add),
             reads=[lbt], writes=[lbt])
        S_hg = [S.sb([128, 2, 64], F32, f"S_hg{l}") for l in range(DEPTH)]
        for l in range(DEPTH):
            S.op("dve", lambda e, l=l: e.memset(S_hg[l][:], 0.0), writes=[S_hg[l]])

        def wgroup(l, col0, ncol):
            wb = win_buf[win_ctr[0] % NWB]
            win_ctr[0] += 1
            off = WIN_OFF[(col0, ncol)]
            S.dma("pool", wb[:, :, :ncol], win_d[l][:, off:off + KC * ncol].rearrange("p (k n) -> p k n", k=KC), writes=[wb], persistent=True)
            return wb

        def proj_fm(wb, wcol, M, ps, xn, ncols, pbase=0):
            def f(e):
                ins = None
                for kc in range(KC):
                    ins = e.matmul(ps[pbase:pbase + M, :ncols], lhsT=wb[:, kc, wcol:wcol + M], rhs=xn[:, kc, :ncols],
                                   start=(kc == 0), stop=(kc == KC - 1))
                return ins
            S.op("pe", f, reads=[wb, xn], writes=[ps])

        def A(eng, out, in_, func, R, W, **kw):
            S.op(eng, lambda e: e.activation(out=out, in_=in_, func=func, **kw), reads=R, writes=W)

        def TTo(out, in0, in1, op, R, W, eng="dve"):
            S.op(eng, lambda e: e.tensor_tensor(out=out, in0=in0, in1=in1, op=op), reads=R, writes=W)

        def TS(out, in0, s1, s2, op0, op1, R, W, eng="dve"):
            if op1 is None:
                S.op(eng, lambda e: e.tensor_scalar(out=out, in0=in0, scalar1=s1, scalar2=None, op0=op0), reads=R, writes=W)
            else:
                S.op(eng, lambda e: e.tensor_scalar(out=out, in0=in0, scalar1=s1, scalar2=s2, op0=op0, op1=op1), reads=R, writes=W)

        def STT(out, in0, scalar, in1, op0, op1, R, W):
            S.op("dve", lambda e: e.scalar_tensor_tensor(out=out, in0=in0, scalar=scalar, in1=in1, op0=op0, op1=op1), reads=R, writes=W)

        def MM(out, lhsT, rhs, R, W, start=True, stop=True):
            S.op("pe", lambda e: e.matmul(out, lhsT=lhsT, rhs=rhs, start=start, stop=stop), reads=R, writes=W)

        def hgrn(l, ncols, xn, o_all):
            samp = (ncols == NS)
            C = 8 if samp else 64
            nseg = ncols // C
            groups = [(0, 32, [0, 1, 2, 3])] if samp else [(g * 128, 128, [2 * g, 2 * g + 1]) for g in range(4)]
            ng = len(groups)
            segm = cst[:, C_SEGS:C_SEGS + 32] if samp else cst[:, C_SEGP:C_SEGP + 512]
            mcol = C_MS if samp else C_MP
            with contextlib.ExitStack() as hs:
                qT = S.sb([128, 2, TB], F32, "hq", hs)
                sg_ = S.sb([128, 2, TB], F32, "hsig", hs)
                sn = S.sb([128, 2, TB], F32, "hsn", hs)
                bb = S.sb([128, 2, TB], F32, "hb", hs)
                t1 = S.sb([128, 2, TB], F32, "ht1", hs)
                t2 = S.sb([128, 2, TB], F32, "ht2", hs)
                Qi = S.sb([128, 2, TB], BF16, "hQi", hs)
                Ki = S.sb([128, 2, TB], BF16, "hKi", hs)
                Qs = S.sb([128, 2, TB], BF16, "hQs", hs)
                Kd = S.sb([128, 2, TB], BF16, "hKd", hs)
                gate = S.sb([128, 2, TB], F32, "hgate", hs)
                Vt = S.sb([128, 4, 256], BF16, "hVt", hs)
                KdT = S.sb([128, 4, 2, 128], BF16, "hKdT", hs)
                KdTm = S.sb([32, 4, 2, 128], BF16, "hKdTm", hs) if samp else None
                attm = S.sb([128, 4, 2, 2, 128], BF16, "hattm", hs)
                Sbf = S.sb([128, 8, 2, 64], BF16, "hSbf", hs)
                dseg = S.sb([128, 2, 8], F32, "hdseg", hs)
                osb = S.sb([128, 2, TB], F32, "hosb", hs)
                o2 = Qi
                Ssm = [S.sb([128, 2, 64], F32, f"hSs{b}", hs) for b in range(4)] if samp else None
                lbc = lambda pc: lbt[:, l * 2 + pc:l * 2 + pc + 1]
                omlc = lambda pc: lbt[:, 4 + l * 2 + pc:4 + l * 2 + pc + 1]
                wb = wgroup(l, 0, 256)
                for pc in range(2):
                    proj_fm(wb, pc * 128, 128, PS[pc], xn, ncols)
                    A("act", qT[:, pc, :ncols], PS[pc][:, :ncols], AF.Copy, [PS[pc]], [qT])
                wb = wgroup(l, 256, 256)
                for pc in range(2):
                    p_ = PS[2 + pc]
                    proj_fm(wb, pc * 128, 128, p_, xn, ncols)
                    A("act", sg_[:, pc, :ncols], p_[:, :ncols], AF.Sigmoid, [p_], [sg_])
                    A("act", sn[:, pc, :ncols], p_[:, :ncols], AF.Sigmoid, [p_], [sn], scale=-1.0)
                    TS(sg_[:, pc, :ncols], sg_[:, pc, :ncols], omlc(pc), lbc(pc), ALU.mult, ALU.add, [sg_, lbt], [sg_])
                    TS(sg_[:, pc, :ncols], sg_[:, pc, :ncols], 1e-30, None, ALU.max, None, [sg_], [sg_])
                    A("act", sg_[:, pc, :ncols], sg_[:, pc, :ncols], AF.Ln, [sg_], [sg_])
                    TS(sn[:, pc, :ncols], sn[:, pc, :ncols], omlc(pc), None, ALU.mult, None, [sn, lbt], [sn])
                    S.op("dve", lambda e, pc=pc: e.tensor_tensor_scan(out=bb[:, pc, :ncols], data0=segm[:, :ncols], data1=sg_[:, pc, :ncols],
                                                                     initial=0.0, op0=ALU.mult, op1=ALU.add), reads=[sg_, cst], writes=[bb])
                    bv = bb[:, pc, :ncols].rearrange("p (s c) -> p s c", c=C)
                    v3 = lambda t, pc=pc: t[:, pc, :ncols].rearrange("p (s c) -> p s c", c=C)
                    TTo(v3(t1), bv, bv[:, :, C // 2 - 1:C // 2].to_broadcast([128, nseg, C]), ALU.subtract, [bb], [t1])
                    A("act", t2[:, pc, :ncols], t1[:, pc, :ncols], AF.Exp, [t1], [t2])
                    TTo(Qi[:, pc, :ncols], qT[:, pc, :ncols], t2[:, pc, :ncols], ALU.mult, [qT, t2], [Qi])
                    A("act", t2[:, pc, :ncols], t1[:, pc, :ncols], AF.Exp, [t1], [t2], scale=-1.0)
                    TTo(Ki[:, pc, :ncols], sn[:, pc, :ncols], t2[:, pc, :ncols], ALU.mult, [sn, t2], [Ki])
                    A("act", t2[:, pc, :ncols], bb[:, pc, :ncols], AF.Exp, [bb], [t2])
                    TTo(Qs[:, pc, :ncols], qT[:, pc, :ncols], t2[:, pc, :ncols], ALU.mult, [qT, t2], [Qs])
                    TTo(v3(t1), bv, bv[:, :, C - 1:C].to_broadcast([128, nseg, C]), ALU.subtract, [bb], [t1])
                    A("act", t2[:, pc, :ncols], t1[:, pc, :ncols], AF.Exp, [t1], [t2], scale=-1.0)
                    TTo(Kd[:, pc, :ncols], sn[:, pc, :ncols], t2[:, pc, :ncols], ALU.mult, [sn, t2], [Kd])
                    A("act", dseg[:, pc, :nseg], bv[:, :, C - 1], AF.Exp, [bb], [dseg])
                wb = wgroup(l, 512, 256)
                for g, (c0g, gsz, segs) in enumerate(groups):
                    def f(e, c0g=c0g, gsz=gsz, wb=wb):
                        ins = None
                        for kc in range(KC):
                            ins = e.matmul(PS[4][:gsz, 0:256], lhsT=xn[:, kc, c0g:c0g + gsz], rhs=wb[:, kc, 0:256], start=(kc == 0), stop=(kc == KC - 1))
                        return ins
                    S.op("pe", f, reads=[xn, wb], writes=[PS[4]])
                    A("act", Vt[:gsz, g, :], PS[4][:gsz, 0:256], AF.Copy, [PS[4]], [Vt])
                wb = wgroup(l, 768, 256)
                for pc in range(2):
                    proj_fm(wb, pc * 128, 128, PS[pc], xn, ncols)
                    A("act", gate[:, pc, :ncols], PS[pc][:, :ncols], AF.Silu, [PS[pc]], [gate])
                if DBG.get('hg_stage', 9) < 1:
                    S.barrier()
                    return
                psT = PS[5]
                for g, (c0g, gsz, segs) in enumerate(groups):
                    for pc in range(2):
                        S.op("pe", lambda e, g=g, pc=pc, c0g=c0g, gsz=gsz: e.transpose(
                            psT[:, :].bitcast(BF16)[:gsz, pc * 128:(pc + 1) * 128], Kd[:, pc, c0g:c0g + gsz], cstb[:, C_ID:C_ID + 128]),
                            reads=[Kd, cstb], writes=[psT])
                    S.op("dve", lambda e, g=g, gsz=gsz: e.tensor_copy(out=KdT[:gsz, g, :, :].rearrange("p a b -> p (a b)"),
                                                                     in_=psT[:, :].bitcast(BF16)[:gsz, 0:256]), reads=[psT], writes=[KdT])
                if samp:
                    for b in range(4):
                        TS(KdTm[:, b, :, :].rearrange("p a b -> p (a b)"), KdT[:32, 0, :, :].rearrange("p a b -> p (a b)"),
                           cst[:32, C_ROWS + b:C_ROWS + b + 1], None, ALU.mult, None, [KdT, cst], [KdTm])
                if DBG.get('hg_stage', 9) < 2:
                    S.barrier()
                    return
                psA = [PS[6], PS[5]]
                for g, (c0g, gsz, segs) in enumerate(groups):
                    for hb in range(2):
                        base = hb * 64

                        def f(e, c0g=c0g, gsz=gsz, hb=hb, base=base):
                            ins = None
                            for pc in range(2):
                                ins = e.matmul(psA[hb][:gsz, pc * 128:pc * 128 + gsz], lhsT=Ki[base:base + 64, pc, c0g:c0g + gsz],
                                               rhs=Qi[base:base + 64, pc, c0g:c0g + gsz], start=True, stop=True)
                            return ins
                        S.op("pe", f, reads=[Ki, Qi], writes=[psA[hb]])
                        TTo(attm[:gsz, g, hb, :, :gsz], psA[hb][:gsz, 0:256].rearrange("p (h t) -> p h t", h=2)[:, :, :gsz],
                            cst[:gsz, mcol:mcol + gsz].unsqueeze(1).to_broadcast([gsz, 2, gsz]), ALU.mult, [psA[hb], cst], [attm])
                if DBG.get('hg_stage', 9) < 3:
                    S.barrier()
                    return
                psU = PS[7]
                if not samp:
                    Sst = S_hg[l]
                    for seg in range(nseg):
                        g, r0 = seg // 2, (seg % 2) * 64
                        psU = PS[7] if r0 == 0 else PS[4]
                        A("act", Sbf[:, seg, :, :].rearrange("p a b -> p (a b)"), Sst[:, :, :].rearrange("p a b -> p (a b)"), AF.Copy, [Sst], [Sbf])

                        def f(e, g=g, r0=r0):
                            ins = None
                            for h in range(4):
                                pc, base = h // 2, (h % 2) * 64
                                ins = e.matmul(psU[base:base + 64, pc * 64:(pc + 1) * 64], lhsT=KdT[r0:r0 + 64, g, pc, base:base + 64],
                                               rhs=Vt[r0:r0 + 64, g, h * 64:(h + 1) * 64], start=True, stop=True)
                            return ins
                        S.op("pe", f, reads=[KdT, Vt], writes=[psU])
                        for pc in range(2):
                            STT(Sst[:, pc, :], Sst[:, pc, :], dseg[:, pc, seg:seg + 1], psU[:, pc * 64:(pc + 1) * 64], ALU.mult, ALU.add,
                                [Sst, dseg, psU], [Sst])
                else:
                    for b in range(4):
                        Sst = Ssm[b]
                        S.dma("sp", Sst[:], hgst_d[l, b].rearrange("(pc hb) k v -> (hb k) pc v", hb=2), writes=[Sst])
                        A("act", Sbf[:, b, :, :].rearrange("p a b -> p (a b)"), Sst[:, :, :].rearrange("p a b -> p (a b)"), AF.Copy, [Sst], [Sbf])

                        def f(e, b=b):
                            ins = None
                            for h in range(4):
                                pc, base = h // 2, (h % 2) * 64
                                ins = e.matmul(psU[base:base + 64, pc * 64:(pc + 1) * 64], lhsT=KdTm[:32, b, pc, base:base + 64],
                                               rhs=Vt[:32, 0, h * 64:(h + 1) * 64], start=True, stop=True)
                            return ins
                        S.op("pe", f, reads=[KdTm, Vt], writes=[psU])
                        for pc in range(2):
                            STT(Sst[:, pc, :], Sst[:, pc, :], dseg[:, pc, b:b + 1], psU[:, pc * 64:(pc + 1) * 64], ALU.mult, ALU.add,
                                [Sst, dseg, psU], [Sst])
                        S.dma("sp", hg_s_d[l, b].rearrange("(pc hb) k v -> (hb k) pc v", hb=2), Sst[:], reads=[Sst])
                if DBG.get('hg_stage', 9) < 4:
                    S.barrier()
                    return
                for g, (c0g, gsz, segs) in enumerate(groups):
                    for h in range(4):
                        pc, hb, base = h // 2, h % 2, (h % 2) * 64

                        def f(e, g=g, h=h, pc=pc, hb=hb, base=base, c0g=c0g, gsz=gsz, segs=segs):
                            ins = e.matmul(PS[h][base:base + 64, c0g:c0g + gsz], lhsT=Vt[:gsz, g, h * 64:(h + 1) * 64],
                                           rhs=attm[:gsz, g, hb, pc, :gsz], start=True, stop=False)
                            for si, seg in enumerate(segs):
                                ins = e.matmul(PS[h][base:base + 64, seg * C:(seg + 1) * C], lhsT=Sbf[base:base + 64, seg, pc, :],
                                               rhs=Qs[base:base + 64, pc, seg * C:(seg + 1) * C], start=False, stop=(si == len(segs) - 1))
                            return ins
                        S.op("pe", f, reads=[Vt, attm, Sbf, Qs], writes=[PS[h]])
                for pc in range(2):
                    for hb in range(2):
                        h, base = 2 * pc + hb, hb * 64
                        A("act", osb[base:base + 64, pc, :ncols], PS[h][base:base + 64, :ncols], AF.Copy, [PS[h]], [osb])
                        A("act", o2[base:base + 64, pc, :ncols], PS[h][base:base + 64, :ncols], AF.Square, [PS[h]], [o2])
                for pc in range(2):
                    MM(PS[4 + pc][:, :ncols], cstb[:, C_BLK:C_BLK + 128], o2[:, pc, :ncols], [cstb, o2], [PS[4 + pc]])
                    A("act", t1[:, pc, :ncols], PS[4 + pc][:, :ncols], AF.Sqrt, [PS[4 + pc], epsb], [t1], scale=1.0 / 64, bias=epsb[:, 0:1])
                    S.op("dve", lambda e, pc=pc: e.reciprocal(out=t1[:, pc, :ncols], in_=t1[:, pc, :ncols]), reads=[t1], writes=[t1])
                    STT(t2[:, pc, :ncols], osb[:, pc, :ncols], vecs[:, VL[f"hgn_{l}"]:VL[f"hgn_{l}"] + 1], t1[:, pc, :ncols], ALU.mult, ALU.mult,
                        [osb, vecs, t1], [t2])
                    TTo(o_all[:, pc, :ncols], t2[:, pc, :ncols], gate[:, pc, :ncols], ALU.mult, [t2, gate], [o_all])
                S.barrier()


        H_rw = [S.sb([128, 2, 64], F32, f"H_rw{l}") for l in range(DEPTH)]
        rw_prev = [S.sb([128, 8], F32, f"rwprev{l}") for l in range(DEPTH)]
        lora = [S.sb([128, 256], BF16, f"lora{l}") for l in range(DEPTH)]
        omka = S.sb([128, 4], F32, "omka")
        H_rw_v = [[TT(H_rw[l].t, f"H_rw{l}_{hb}") for hb in range(2)] for l in range(DEPTH)]
        for l in range(DEPTH):
            S.op("dve", lambda e, l=l: e.memset(H_rw[l][:], 0.0), writes=[H_rw[l], H_rw_v[l][0], H_rw_v[l][1]])
            S.op("dve", lambda e, l=l: e.memset(rw_prev[l][:], 0.0), writes=[rw_prev[l]])
            S.dma("pool", lora[l][:], lora_d[l], writes=[lora[l]], persistent=True)
            TS(omka[:, 2 * l:2 * l + 2], vecs[:, VL[f"ka_{l}"]:VL[f"ka_{l}"] + 2], -1.0, 1.0, ALU.mult, ALU.add, [vecs], [omka])

        def rwkv(l, bi, ncols, xn, o_all):
            samp = (ncols == NS)
            C = 8 if samp else 64
            nseg = ncols // C
            groups = [(0, 32, [0, 1, 2, 3])] if samp else [(g * 128, 128, [2 * g, 2 * g + 1]) for g in range(4)]
            segm = cst[:, C_SEGS:C_SEGS + 32] if samp else cst[:, C_SEGP:C_SEGP + 512]
            m_incl, m_str, m_low = (C_MS, C_MSS, C_MSL) if samp else (C_MP, C_MPS, C_MPL)
            nlev = 2 if samp else 5
            V_ = lambda nm, pc=0: vecs[:, VL[f"{nm}_{l}"] + pc:VL[f"{nm}_{l}"] + pc + 1]
            with contextlib.ExitStack() as rs:
                rkv = S.sb([128, 6, TB], F32, "rw_rkv", rs)
                l6 = S.sb([128, TB], BF16, "rw_l6", rs)
                sh0 = S.sb([128, 7, 4], F32, "rw_sh0", rs)
                shs = S.sb([128, 7, 4], F32, "rw_shs", rs)
                pa_ = contextlib.ExitStack()
                pb = S.sb([128, TB + 1], F32, "rw_pb", pa_)
                prevb = S.sb([128, TB], F32, "rw_prevb", pa_)
                dtmp = S.sb([128, TB], F32, "rw_d", pa_)
                l6f = S.sb([128, TB], F32, "rw_l6f", pa_)
                if samp:
                    S.dma("sp", sh0[:], rwsh_d[l], writes=[sh0])
                for c in range(7):
                    if c % 2 == 0:
                        wb = wgroup(l, 1024 + 128 * c, 256 if c < 6 else 128)
                    pp = PS[c % 2]
                    proj_fm(wb, (c % 2) * 128, 128, pp, xn, ncols)
                    A("act", pb[:, 1:ncols + 1], pp[:, :ncols], AF.Copy, [pp], [pb])
                    dest = rkv[:, c, :ncols] if c < 6 else l6f[:, :ncols]
                    dT = rkv if c < 6 else l6f
                    if not samp:
                        S.op("dve", lambda e, c=c: e.tensor_copy(out=pb[:, 0:1], in_=rw_prev[l][:, c:c + 1]), reads=[rw_prev[l], pb], writes=[pb])
                        TTo(dtmp[:, :ncols], pb[:, 0:ncols], pb[:, 1:ncols + 1], ALU.subtract, [pb], [dtmp])
                        S.op("dve", lambda e, c=c: e.tensor_copy(out=rw_prev[l][:, c:c + 1], in_=pb[:, ncols:ncols + 1]), reads=[pb, rw_prev[l]], writes=[rw_prev[l]])
                    else:
                        S.op("dve", lambda e: e.tensor_copy(out=prevb[:, 1:ncols], in_=pb[:, 1:ncols]), reads=[pb], writes=[prevb])
                        S.op("dve", lambda e, c=c: e.tensor_copy(out=prevb[:, :ncols].rearrange("p (b t) -> p b t", t=8)[:, :, 0], in_=sh0[:, c, :]),
                             reads=[sh0, prevb], writes=[prevb])
                        TTo(dtmp[:, :ncols], prevb[:, :ncols], pb[:, 1:ncols + 1], ALU.subtract, [pb, prevb], [dtmp])
                        S.op("dve", lambda e, c=c: e.tensor_copy(out=shs[:, c, :], in_=pb[:, 1:ncols + 1].rearrange("p (b t) -> p b t", t=8)[:, :, 7]),
                             reads=[pb, shs], writes=[shs])
                    STT(dest, dtmp[:, :ncols], V_("mu", c), pb[:, 1:ncols + 1], ALU.mult, ALU.add, [dtmp, vecs, pb], [dT])
                if samp:
                    S.dma("sp", sh_s_d[l], shs[:], reads=[shs])
                elif bi == NPB - 1:
                    S.dma("sp", sh_p_d[l], rw_prev[l][:, 0:7], reads=[rw_prev[l]])
                A("act", l6[0:32, :ncols], l6f[0:32, :ncols], AF.Tanh, [l6f], [l6])
                A("act", l6[32:64, :ncols], l6f[32:64, :ncols], AF.Copy, [l6f], [l6])
                A("act", l6[64:128, :ncols], l6f[64:128, :ncols], AF.Sigmoid, [l6f], [l6])
                S.barrier()
                pa_.close()
                if DBG.get('rw_stage', 9) < 1:
                    S.barrier(); return
                for pc in range(2):
                    with contextlib.ExitStack() as bs:
                        f2 = lambda nm: S.sb([128, TB], F32, nm, bs)
                        b2 = lambda nm: S.sb([128, TB], BF16, nm, bs)
                        lw, aa, gg, al, be, km, cw, tA, tB_ = f2("rw_lw"), f2("rw_a"), f2("rw_g"), f2("rw_al"), f2("rw_be"), f2("rw_km"), f2("rw_cw"), f2("rw_tA"), f2("rw_tB")
                        At, Bt, Kt, Rt = b2("rw_At"), b2("rw_Bt"), b2("rw_Kt"), b2("rw_Rt")
                        vb = b2("rw_vb")
                        tokT = S.sb([128, 4, 4, 128], BF16, "rw_tokT", bs)
                        tokM = S.sb([32, 4, 2, 128], BF16, "rw_tokM", bs) if samp else None
                        pCt = S.sb([128, 8], F32, "rw_pC", bs)
                        ysb = f2("rw_y")
                        Hsm = [S.sb([128, 64], F32, f"rw_Hs{b}", bs) for b in range(4)] if samp else None
                        r_, k_, v_ = rkv[:, pc, :ncols], rkv[:, 2 + pc, :ncols], rkv[:, 4 + pc, :ncols]
                        n = ncols
                        MM(PS[2][:, :n], lora[l][0:32, pc * 128:(pc + 1) * 128], l6[0:32, :n], [lora[l], l6], [PS[2]])
                        MM(PS[3][:, :n], lora[l][32:64, pc * 128:(pc + 1) * 128], l6[32:64, :n], [lora[l], l6], [PS[3]])
                        MM(PS[4][:, :n], lora[l][64:128, pc * 128:(pc + 1) * 128], l6[64:128, :n], [lora[l], l6], [PS[4]])
                        A("act", lw[:, :n], PS[2][:, :n], AF.Sigmoid, [PS[2], vecs], [lw], bias=V_("w0", pc))
                        TS(lw[:, :n], lw[:, :n], -float(np.exp(-0.5)), None, ALU.mult, None, [lw], [lw])
                        A("act", aa[:, :n], PS[3][:, :n], AF.Sigmoid, [PS[3], vecs], [aa], bias=V_("a0", pc))
                        A("act", gg[:, :n], PS[4][:, :n], AF.Copy, [PS[4]], [gg])
                        TS(al[:, :n], k_, V_("kk", pc), None, ALU.mult, None, [rkv, vecs], [al])
                        A("act", tA[:, :n], al[:, :n], AF.Square, [al], [tA])
                        MM(PS[5][:, :n], cst[:, C_BLK:C_BLK + 128], tA[:, :n], [cst, tA], [PS[5]])
                        A("act", tA[:, :n], PS[5][:, :n], AF.Sqrt, [PS[5]], [tA])
                        TS(tA[:, :n], tA[:, :n], 1e-12, None, ALU.max, None, [tA], [tA])
                        S.op("dve", lambda e: e.reciprocal(out=tA[:, :n], in_=tA[:, :n]), reads=[tA], writes=[tA])
                        TTo(al[:, :n], al[:, :n], tA[:, :n], ALU.mult, [al, tA], [al])
                        TS(tA[:, :n], aa[:, :n], V_("ka", pc), omka[:, 2 * l + pc:2 * l + pc + 1], ALU.mult, ALU.add, [aa, vecs, omka], [tA])
                        TTo(km[:, :n], k_, tA[:, :n], ALU.mult, [rkv, tA], [km])
                        STT(be[:, :n], al[:, :n], -1.0, aa[:, :n], ALU.mult, ALU.mult, [al, aa], [be])
                        STT(tA[:, :n], r_, V_("rk", pc), km[:, :n], ALU.mult, ALU.mult, [rkv, vecs, km], [tA])
                        MM(PS[6][:, :n], cst[:, C_BLK:C_BLK + 128], tA[:, :n], [cst, tA], [PS[6]])
                        TTo(tB_[:, :n], PS[6][:, :n], v_, ALU.mult, [PS[6], rkv], [tB_])
                        S.op("dve", lambda e: e.tensor_tensor_scan(out=cw[:, :n], data0=segm[:, :n], data1=lw[:, :n], initial=0.0,
                                                                   op0=ALU.mult, op1=ALU.add), reads=[lw, cst], writes=[cw])
                        TTo(tA[:, :n], cw[:, :n], lw[:, :n], ALU.subtract, [cw, lw], [tA])
                        A("act", tA[:, :n], tA[:, :n], AF.Exp, [tA], [tA])
                        TTo(At[:, :n], al[:, :n], tA[:, :n], ALU.mult, [al, tA], [At])
                        A("act", tA[:, :n], cw[:, :n], AF.Exp, [cw], [tA], scale=-1.0)
                        TTo(Bt[:, :n], be[:, :n], tA[:, :n], ALU.mult, [be, tA], [Bt])
                        TTo(Kt[:, :n], km[:, :n], tA[:, :n], ALU.mult, [km, tA], [Kt])
                        A("act", tA[:, :n], cw[:, :n], AF.Exp, [cw], [tA])
                        TTo(Rt[:, :n], r_, tA[:, :n], ALU.mult, [rkv, tA], [Rt])
                        A("act", pCt[:, :nseg], cw[:, :n].rearrange("p (s c) -> p s c", c=C)[:, :, C - 1], AF.Exp, [cw], [pCt])
                        A("act", vb[:, :n], v_, AF.Copy, [rkv], [vb])
                        if DBG.get('rw_stage', 9) < 2:
                            S.barrier(); continue
                        for g, (c0g, gsz, segs) in enumerate(groups):
                            for wi_, src in enumerate((At, Bt, Kt, vb)):
                                S.op("pe", lambda e, wi_=wi_, src=src, c0g=c0g, gsz=gsz: e.transpose(
                                    PS[7][:, :].bitcast(BF16)[:gsz, wi_ * 128:(wi_ + 1) * 128], src[:, c0g:c0g + gsz], cstb[:, C_ID:C_ID + 128]),
                                    reads=[src, cstb], writes=[PS[7]])
                            S.op("dve", lambda e, g=g, gsz=gsz: e.tensor_copy(out=tokT[:gsz, g, :, :].rearrange("p a b -> p (a b)"),
                                                                             in_=PS[7][:, :].bitcast(BF16)[:gsz, 0:512]), reads=[PS[7]], writes=[tokT])
                        if samp:
                            for b in range(4):
                                TS(tokM[:, b, :, :].rearrange("p a b -> p (a b)"), tokT[:32, 0, 1:3, :].rearrange("p a b -> p (a b)"),
                                   cst[:32, C_ROWS + b:C_ROWS + b + 1], None, ALU.mult, None, [tokT, cst], [tokM])
                        if DBG.get('rw_stage', 9) < 3:
                            S.barrier(); continue
                        Tl = []
                        for hb in range(2):
                            sq_ = lambda nm: S.sb([128, 128], BF16, f"{nm}{hb}", bs)
                            Tl.append(dict(Nn=sq_("rw_N"), Aa=sq_("rw_A"), IA=sq_("rw_IA"), Pp=sq_("rw_P"), AakT=sq_("rw_AakT"), ArbT=sq_("rw_ArbT"),
                                           ArkT=sq_("rw_ArkT"), WT=sq_("rw_WT"), X0=S.sb([128, 64], BF16, f"rw_X0{hb}", bs),
                                           Ut=S.sb([128, 64], F32, f"rw_Ut{hb}", bs), Usb=S.sb([128, 64], BF16, f"rw_Usb{hb}", bs),
                                           Uf=S.sb([128, 64], F32, f"rw_Uf{hb}", bs), Hc=S.sb([128, 8, 64], BF16, f"rw_Hc{hb}", bs),
                                           Hp=S.sb([128, 64], F32, f"rw_Hp{hb}", bs)))
                            if samp:
                                for b in range(4):
                                    S.dma("sp", Hsm[b][hb * 64:hb * 64 + 64, :], rwst_d[l, b, 2 * pc + hb], writes=[Hsm[b]])

                        def solve(hb, g, c0g, gsz, segs):
                            h, base = 2 * pc + hb, hb * 64
                            bk = PS[0:4] if hb == 0 else PS[4:8]
                            T_ = Tl[hb]
                            Nn, Aa, IA, Pp, AakT, ArbT, ArkT, WT = (T_["Nn"], T_["Aa"], T_["IA"], T_["Pp"], T_["AakT"], T_["ArbT"], T_["ArkT"], T_["WT"])
                            X0, Ut, Usb, Uf, Hc_, Hp_ = T_["X0"], T_["Ut"], T_["Usb"], T_["Uf"], T_["Hc"], T_["Hp"]
                            gsl = slice(c0g, c0g + gsz)
                            fm = lambda t: t[base:base + 64, gsl]
                            mk = lambda col: cst[:gsz, col:col + gsz]
                            MM(bk[0][:gsz, :gsz], fm(Bt), fm(At), [Bt, At], [bk[0]])
                            TTo(Nn[:gsz, :gsz], bk[0][:gsz, :gsz], mk(m_str), ALU.mult, [bk[0], cst], [Nn])
                            MM(bk[1][:gsz, :gsz], fm(At), fm(Bt), [Bt, At], [bk[1]])
                            TTo(Aa[:gsz, :gsz], bk[1][:gsz, :gsz], mk(m_low), ALU.mult, [bk[1], cst], [Aa])
                            MM(bk[2][:gsz, :gsz], fm(Kt), fm(At), [Kt, At], [bk[2]])
                            TTo(AakT[:gsz, :gsz], bk[2][:gsz, :gsz], mk(m_str), ALU.mult, [bk[2], cst], [AakT])
                            MM(bk[3][:gsz, :gsz], fm(Bt), fm(Rt), [Bt, Rt], [bk[3]])
                            TTo(ArbT[:gsz, :gsz], bk[3][:gsz, :gsz], mk(m_incl), ALU.mult, [bk[3], cst], [ArbT])
                            MM(bk[0][:gsz, :gsz], fm(Kt), fm(Rt), [Kt, Rt], [bk[0]])
                            TTo(ArkT[:gsz, :gsz], bk[0][:gsz, :gsz], mk(m_incl), ALU.mult, [bk[0], cst], [ArkT])
                            TTo(Pp[:gsz, :gsz], Nn[:gsz, :gsz], mk(C_ID), ALU.add, [Nn, cst], [Pp])
                            for j in range(1, nlev + 1):
                                MM(bk[1][:gsz, :gsz], Nn[:gsz, :gsz], Aa[:gsz, :gsz], [Nn, Aa], [bk[1]])
                                if j < nlev:
                                    MM(bk[2][:gsz, :gsz], Aa[:gsz, :gsz], Nn[:gsz, :gsz], [Nn, Aa], [bk[2]])
                                TTo(IA[:gsz, :gsz], bk[1][:gsz, :gsz], mk(C_ID), ALU.add, [bk[1], cst], [IA])
                                if j < nlev:
                                    S.op("dve", lambda e, gsz=gsz: e.tensor_copy(out=Aa[:gsz, :gsz], in_=bk[1][:gsz, :gsz]), reads=[bk[1]], writes=[Aa])
                                    A("act", Nn[:gsz, :gsz], bk[2][:gsz, :gsz], AF.Copy, [bk[2]], [Nn])
                                MM(bk[3][:gsz, :gsz], IA[:gsz, :gsz], Pp[:gsz, :gsz], [IA, Pp], [bk[3]])
                                A("act", Pp[:gsz, :gsz], bk[3][:gsz, :gsz], AF.Copy, [bk[3]], [Pp])
                            Vtok = tokT[:gsz, g, 3, base:base + 64]
                            MM(bk[0][:gsz, 0:64], AakT[:gsz, :gsz], Vtok, [AakT, tokT], [bk[0]])
                            A("act", X0[:gsz, :], bk[0][:gsz, 0:64], AF.Copy, [bk[0]], [X0])
                            MM(bk[1][:gsz, 0:64], Pp[:gsz, :gsz], X0[:gsz, :], [Pp, X0], [bk[1]])
                            A("act", Ut[:gsz, :], bk[1][:gsz, 0:64], AF.Copy, [bk[1]], [Ut])
                            MM(bk[2][base:base + 64, :gsz], tokT[:gsz, g, 0, base:base + 64], Pp[:gsz, :gsz], [tokT, Pp], [bk[2]])
                            A("act", WT[base:base + 64, :gsz], bk[2][base:base + 64, :gsz], AF.Copy, [bk[2]], [WT])
                            if not samp:
                                Hst = H_rw_v[l][hb]
                                for si, seg in enumerate(segs):
                                    r0 = si * 64
                                    pu = bk[si % 2]
                                    ph = bk[2 + si % 2]
                                    A("act", Hc_[base:base + 64, seg, :], Hst[base:base + 64, pc, :], AF.Copy, [Hst], [Hc_])
                                    A("act", Hp_[base:base + 64, :], Hst[base:base + 64, pc, :], AF.Identity, [Hst, pCt], [Hp_], scale=pCt[base:base + 64, seg:seg + 1])
                                    MM(pu[r0:r0 + 64, 0:64], WT[base:base + 64, r0:r0 + 64], Hc_[base:base + 64, seg, :], [WT, Hc_], [pu])
                                    TTo(Uf[r0:r0 + 64, :], pu[r0:r0 + 64, 0:64], Ut[r0:r0 + 64, :], ALU.add, [pu, Ut], [Uf])
                                    A("act", Usb[r0:r0 + 64, :], Uf[r0:r0 + 64, :], AF.Copy, [Uf], [Usb])

                                    def fH(e, r0=r0, g=g, ph=ph):
                                        e.matmul(ph[base:base + 64, 0:64], lhsT=tokT[r0:r0 + 64, g, 2, base:base + 64], rhs=tokT[r0:r0 + 64, g, 3, base:base + 64],
                                                 start=True, stop=False)
                                        return e.matmul(ph[base:base + 64, 0:64], lhsT=tokT[r0:r0 + 64, g, 1, base:base + 64], rhs=Usb[r0:r0 + 64, :],
                                                        start=False, stop=True)
                                    S.op("pe", fH, reads=[tokT, Usb], writes=[ph])
                                    STT(Hst[base:base + 64, pc, :], ph[base:base + 64, 0:64], pCt[base:base + 64, seg:seg + 1], Hp_[base:base + 64, :],
                                        ALU.mult, ALU.add, [ph, pCt, Hp_, Hst], [Hst])
                            else:
                                for b in range(4):
                                    A("act", Hc_[base:base + 64, b, :], Hsm[b][base:base + 64, :], AF.Copy, [Hsm[b]], [Hc_])

                                def fU(e):
                                    ins = None
                                    for b in range(4):
                                        ins = e.matmul(bk[0][:32, b * 64:(b + 1) * 64], lhsT=WT[base:base + 64, 0:32], rhs=Hc_[base:base + 64, b, :], start=True, stop=True)
                                    return ins
                                S.op("pe", fU, reads=[WT, Hc_], writes=[bk[0]])
                                S.op("dve", lambda e: e.tensor_copy(out=Uf[:32, :], in_=Ut[:32, :]), reads=[Ut], writes=[Uf])
                                for b in range(4):
                                    STT(Uf[:32, :], bk[0][:32, b * 64:(b + 1) * 64], cst[:32, C_ROWS + b:C_ROWS + b + 1], Uf[:32, :], ALU.mult, ALU.add,
                                        [bk[0], cst, Uf], [Uf])
                                A("act", Usb[:32, :], Uf[:32, :], AF.Copy, [Uf], [Usb])
                                for b in range(4):
                                    ph = bk[2 + b % 2]

                                    def fH(e, b=b, ph=ph):
                                        e.matmul(ph[base:base + 64, 0:64], lhsT=tokM[:32, b, 1, base:base + 64], rhs=tokT[:32, 0, 3, base:base + 64], start=True, stop=False)
                                        return e.matmul(ph[base:base + 64, 0:64], lhsT=tokM[:32, b, 0, base:base + 64], rhs=Usb[:32, :], start=False, stop=True)
                                    S.op("pe", fH, reads=[tokM, tokT, Usb], writes=[ph])
                                    TTo(Hp_[base:base + 64, :], ph[base:base + 64, 0:64], Hsm[b][base:base + 64, :], ALU.add, [ph, Hsm[b]], [Hp_])
                                    TS(Hsm[b][base:base + 64, :], Hp_[base:base + 64, :], pCt[base:base + 64, b:b + 1], None, ALU.mult, None, [Hp_, pCt], [Hsm[b]])
                                    S.dma("sp", rw_s_d[l, b, h], Hsm[b][base:base + 64, :], reads=[Hsm[b]])
                            py = bk[1]

                            def fY(e, g=g, gsz=gsz, segs=segs, py=py):
                                e.matmul(py[base:base + 64, :gsz], lhsT=tokT[:gsz, g, 3, base:base + 64], rhs=ArkT[:gsz, :gsz], start=True, stop=False)
                                ins = e.matmul(py[base:base + 64, :gsz], lhsT=Usb[:gsz, :], rhs=ArbT[:gsz, :gsz], start=False, stop=False)
                                for si, seg in enumerate(segs):
                                    ins = e.matmul(py[base:base + 64, si * C:(si + 1) * C], lhsT=Hc_[base:base + 64, seg, :],
                                                   rhs=Rt[base:base + 64, c0g + si * C:c0g + (si + 1) * C], start=False, stop=(si == len(segs) - 1))
                                return ins
                            S.op("pe", fY, reads=[tokT, ArkT, Usb, ArbT, Hc_, Rt], writes=[py])
                            A("act", ysb[base:base + 64, gsl], py[base:base + 64, :gsz], AF.Copy, [py], [ysb])
                        for g, (c0g, gsz, segs) in enumerate(groups):
                            progs = []
                            for hb in range(2):
                                S.rec = []
                                solve(hb, g, c0g, gsz, segs)
                                progs.append(S.rec)
                                S.rec = None
                            for i_ in range(max(len(p_) for p_ in progs)):
                                for p_ in progs:
                                    if i_ < len(p_):
                                        p_[i_]()
                        if DBG.get('rw_stage', 9) < 8:
                            S.barrier(); continue
                        if DBG.get('rw_post', 99) > 0:
                            MM(PS[0][:, :n], cst[:, C_BLK:C_BLK + 128], ysb[:, :n], [cst, ysb], [PS[0]])
                        if DBG.get('rw_post', 99) > 1:
                            A("act", tA[:, :n], ysb[:, :n], AF.Square, [ysb], [tA])
                        if DBG.get('rw_post', 99) > 2:
                            MM(PS[1][:, :n], cst[:, C_BLK:C_BLK + 128], tA[:, :n], [cst, tA], [PS[1]])
                        if DBG.get('rw_post', 99) > 3:
                            TS(cw[:, :n], PS[0][:, :n], 1.0 / 64, None, ALU.mult, None, [PS[0]], [cw])
                        if DBG.get('rw_post', 99) > 4:
                            TTo(tA[:, :n], cw[:, :n], cw[:, :n], ALU.mult, [cw], [tA])
                        if DBG.get('rw_post', 99) > 5:
                            STT(tA[:, :n], PS[1][:, :n], 1.0 / 64, tA[:, :n], ALU.mult, ALU.subtract, [PS[1], tA], [tA])
                        if DBG.get('rw_post', 99) > 6:
                            TS(tA[:, :n], tA[:, :n], 0.0, 64e-5, ALU.max, ALU.add, [tA], [tA])
                        if DBG.get('rw_post', 99) > 7:
                            A("act", tA[:, :n], tA[:, :n], AF.Sqrt, [tA], [tA])
                        if DBG.get('rw_post', 99) > 8:
                            S.op("dve", lambda e: e.reciprocal(out=tA[:, :n], in_=tA[:, :n]), reads=[tA], writes=[tA])
                        if DBG.get('rw_post', 99) > 9:
                            TTo(ysb[:, :n], ysb[:, :n], cw[:, :n], ALU.subtract, [ysb, cw], [ysb])
                        if DBG.get('rw_post', 99) > 10:
                            if DBG.get('exp1'):
                                TTo(ysb[:, :n], ysb[:, :n], cw[:, :n], ALU.mult, [ysb, cw], [ysb])
                            else:
                                TTo(ysb[:, :n], ysb[:, :n], tA[:, :n], ALU.mult, [ysb, tA], [ysb])
                        if DBG.get('rw_post', 99) > 11:
                            TS(ysb[:, :n], ysb[:, :n], V_("lnw", pc), V_("lnb", pc), ALU.mult, ALU.add, [ysb, vecs], [ysb])
                        if DBG.get('rw_post', 99) > 12:
                            TTo(ysb[:, :n], ysb[:, :n], tB_[:, :n], ALU.add, [ysb, tB_], [ysb])
                        if DBG.get('rw_post', 99) > 13:
                            TTo(o_all[:, 2 + pc, :n], ysb[:, :n], gg[:, :n], ALU.mult, [ysb, gg], [o_all])
                        S.barrier()
                if (not samp) and bi == NPB - 1:
                    S.dma("sp", rw_p_d[l].rearrange("(pc hb) k v -> (hb k) pc v", hb=2), H_rw[l][:], reads=[H_rw[l], H_rw_v[l][0], H_rw_v[l][1]])
                S.barrier()


        MLA_SCALE = float((64 + 32) ** -0.5)
        knope_h, v_h, kpe_h = [], [], []
        kv_es = contextlib.ExitStack()

        def alloc_kv():
            for l in range(DEPTH):
                knope_h.append(S.sb([128, 4, SEQ], BF16, f"knope{l}", kv_es))
                v_h.append(S.sb([128, 16, 8, 65], BF16, f"vh{l}", kv_es))
                kpe_h.append(S.sb([128, SEQ], BF16, f"kpeh{l}", kv_es))
                S.op("dve", lambda e, l=l: e.memset(v_h[l][:, :, :, 64:65], 1.0), writes=[v_h[l]])

        def norm_fm(src, nch, ncols, gname, l, dst_f, dst_b, sqm, rs):
            for c in range(nch):
                A("act", sqm[:, c, :ncols], src[:, c, :ncols], AF.Square, [src], [sqm])

            def f(e):
                ins = None
                for c in range(nch):
                    ins = e.matmul(PS[7][:, :ncols], lhsT=ones_bf[:], rhs=sqm[:, c, :ncols], start=(c == 0), stop=(c == nch - 1))
                return ins
            S.op("pe", f, reads=[sqm, ones_bf], writes=[PS[7]])
            A("act", rs[:, :ncols], PS[7][:, :ncols], AF.Sqrt, [PS[7], epsb], [rs], scale=1.0 / (nch * 128), bias=epsb[:, 0:1])
            S.op("dve", lambda e: e.reciprocal(out=rs[:, :ncols], in_=rs[:, :ncols]), reads=[rs], writes=[rs])
            for c in range(nch):
                g_ = vecs[:, VL[f"{gname}_{l}"] + c:VL[f"{gname}_{l}"] + c + 1]
                if dst_f is not None:
                    STT(dst_f[:, c, :ncols], src[:, c, :ncols], g_, rs[:, :ncols], ALU.mult, ALU.mult, [src, vecs, rs], [dst_f])
                    if dst_b is not None:
                        A("act", dst_b[:, c, :ncols], dst_f[:, c, :ncols], AF.Copy, [dst_f], [dst_b])
                else:
                    STT(dst_b[:, c, :ncols], src[:, c, :ncols], g_, rs[:, :ncols], ALU.mult, ALU.mult, [src, vecs, rs], [dst_b])

        def mla(l, bi, ncols, c0, xn, o_all):
            samp = (ncols == NS)
            n = ncols
            with contextlib.ExitStack() as ms_:
                wqb = S.sb([128, 3, 1280], BF16, "mla_wqb", ms_)
                S.dma("pool", wqb[:], wqb_d[l].rearrange("(kc p) n -> p kc n", p=128), writes=[wqb])
                wuk = S.sb([128, 2, 512], BF16, "mla_wuk", ms_)
                wuv = S.sb([128, 2, 512], BF16, "mla_wuv", ms_)
                S.dma("pool", wuk[:], wuk_d[l].rearrange("(kc p) n -> p kc n", p=128), writes=[wuk])
                S.dma("pool", wuv[:], wuv_d[l].rearrange("(kc p) n -> p kc n", p=128), writes=[wuv])
                ropt = S.sb([128, 2, TB], F32, "mla_rope", ms_)
                rc0 = SEQ if samp else c0
                S.dma("sp", ropt[:, :, :n], rope_d[:, :, rc0:rc0 + n], writes=[ropt])
                qn = S.sb([128, 3, TB], BF16, "mla_qn", ms_)
                cb = S.sb([128, 2, TB], BF16, "mla_cb", ms_)
                qnp = S.sb([128, 4, TB], BF16, "mla_qnp", ms_)
                qpe = S.sb([128, 8, TB], BF16, "mla_qpe", ms_)
                rt1 = S.sb([128, TB], F32, "mla_rt1", ms_)
                rt2 = S.sb([128, TB], F32, "mla_rt2", ms_)
                kpef = S.sb([32, TB], F32, "mla_kpef", ms_)
                with contextlib.ExitStack() as ps_:
                    qa_f = S.sb([128, 3, TB], F32, "mla_qaf", ps_)
                    kv_f = S.sb([128, 2, TB], F32, "mla_kvf", ps_)
                    c_f = kv_f
                    sqm = qnp
                    rs = S.sb([128, TB], F32, "mla_rs", ps_)
                    wb = wgroup(l, 1920, 256)
                    for c in range(3):
                        if c == 2:
                            wb = wgroup(l, 2176, 128)
                        proj_fm(wb, (c % 2) * 128, 128, PS[c % 2], xn, n)
                        A("act", qa_f[:, c, :n], PS[c % 2][:, :n], AF.Copy, [PS[c % 2]], [qa_f])
                    wb = wgroup(l, 2304, 256)
                    for c in range(2):
                        proj_fm(wb, c * 128, 128, PS[2 + c], xn, n)
                        A("act", kv_f[:, c, :n], PS[2 + c][:, :n], AF.Copy, [PS[2 + c]], [kv_f])
                    norm_fm(qa_f, 3, n, "qn", l, None, qn, sqm, rs)
                    norm_fm(kv_f, 2, n, "kvn", l, c_f, cb, sqm, rs)
                    cdst = ckv_s_d[l] if samp else ckv_p_d[l][:, c0:c0 + n]
                    S.dma("sp", cdst.rearrange("(c p) t -> p c t", p=128), c_f[:, :, :n], reads=[c_f])
                    wb = wgroup(l, 2560, 96)
                    proj_fm(wb, 0, 64, PS[4], xn, n)
                    proj_fm(wb, 32, 64, PS[5], xn, n)
                    TTo(rt1[0:32, :n], PS[4][0:32, :n], ropt[0:32, 0, :n], ALU.mult, [PS[4], ropt], [rt1])
                    TTo(rt2[0:32, :n], PS[5][0:32, :n], ropt[0:32, 1, :n], ALU.mult, [PS[5], ropt], [rt2])
                    TTo(kpef[:, :n], rt1[0:32, :n], rt2[0:32, :n], ALU.add, [rt1, rt2], [kpef])
                    if DBG.get("kpe_dbg") == 1:
                        S.op("dve", lambda e: e.tensor_copy(out=kpef[:, :n], in_=PS[4][0:32, :n]), reads=[PS[4]], writes=[kpef])
                    if DBG.get("kpe_dbg") == 2:
                        S.op("dve", lambda e: e.tensor_copy(out=kpef[:, :n], in_=ropt[0:32, 0, :n]), reads=[ropt], writes=[kpef])
                    kdst = kpe_s_d[l] if samp else kpe_p_d[l][:, c0:c0 + n]
                    S.dma("sp", kdst, kpef[:, :n], reads=[kpef])
                    S.barrier()
                for j in range(4):
                    pp = PS[j % 2]

                    def f(e, j=j, pp=pp):
                        ins = None
                        for kc in range(3):
                            ins = e.matmul(pp[:, :n], lhsT=wqb[:, kc, j * 128:(j + 1) * 128], rhs=qn[:, kc, :n], start=(kc == 0), stop=(kc == 2))
                        return ins
                    S.op("pe", f, reads=[wqb, qn], writes=[pp])
                    A("act", qnp[:, j, :n], pp[:, :n], AF.Copy, [pp], [qnp], scale=MLA_SCALE)
                for h in range(8):
                    pb_ = 0 if samp else (h % 2) * 64
                    pa, pbk = PS[2 + (h % 2) * 2], PS[3 + (h % 2) * 2]

                    def f(e, h=h, pb_=pb_, pa=pa, pbk=pbk):
                        ins = None
                        for which, pt in ((0, pa), (1, pbk)):
                            for kc in range(3):
                                col = 512 + h * 96 + which * 32
                                ins = e.matmul(pt[pb_:pb_ + 64, :n], lhsT=wqb[:, kc, col:col + 64], rhs=qn[:, kc, :n], start=(kc == 0), stop=(kc == 2))
                        return ins
                    S.op("pe", f, reads=[wqb, qn], writes=[pa, pbk])
                    TTo(rt1[pb_:pb_ + 32, :n], pa[pb_:pb_ + 32, :n], ropt[pb_:pb_ + 32, 0, :n], ALU.mult, [pa, ropt], [rt1])
                    TTo(rt2[pb_:pb_ + 32, :n], pbk[pb_:pb_ + 32, :n], ropt[pb_:pb_ + 32, 1, :n], ALU.mult, [pbk, ropt], [rt2])
                    TTo(rt1[pb_:pb_ + 32, :n], rt1[pb_:pb_ + 32, :n], rt2[pb_:pb_ + 32, :n], ALU.add, [rt1, rt2], [rt1])
                    TS(qpe[pb_:pb_ + 32, h, :n], rt1[pb_:pb_ + 32, :n], MLA_SCALE, None, ALU.mult, None, [rt1], [qpe])
                if not samp:
                    mla_prompt_attn(l, bi, c0, wuk, wuv, cb, kpef, qnp, qpe, o_all, ms_)
                elif DBG.get("mla_s", 1):
                    mla_sample_attn(l, wuv, cb, kpef, qnp, qpe, o_all, ms_)
                S.barrier()

        def mla_sample_attn(l, wuv, cb, kpef, qnp, qpe, o_all, ms_):
            NT = 16
            NCH = 128 // NT
            wukT = S.sb([128, 4, 256], BF16, "mla_wukT", ms_)
            S.dma("pool", wukT[:], wukT_d[l], writes=[wukT])
            idx = S.sb([128, 4], I32, "mla_idx", ms_)
            S.dma("sp", idx[:], pt_d[:, :], writes=[idx])
            idxa = S.sb([128, 4, NCH], I32, "mla_idxa", ms_)
            for a in range(NCH):
                TS(idxa[:, :, a], idx[:, :], float(NCH), float(a), ALU.mult, ALU.add, [idx], [idxa])
            qlat = S.sb([128, 2, 8, NS], BF16, "mla_qlat", ms_)
            for hb in range(2):
                base = hb * 64
                pq = PS[hb]

                def f(e, hb=hb, base=base, pq=pq):
                    ins = None
                    for jp in range(4):
                        for rc in range(2):
                            ins = e.matmul(pq[:, (rc * 4 + jp) * 32:(rc * 4 + jp + 1) * 32], lhsT=wukT[base:base + 64, jp, rc * 128:(rc + 1) * 128],
                                           rhs=qnp[base:base + 64, jp, 0:NS], start=True, stop=True)
                    return ins
                S.op("pe", f, reads=[wukT, qnp], writes=[pq])
                A("act", qlat[:, :, :, :].rearrange("p r (j two) t -> p r j two t", two=2)[:, :, :, hb, :],
                  pq[:, 0:256].rearrange("p (r j t) -> p r j t", r=2, j=4), AF.Copy, [pq], [qlat])
            if DBG.get("ms_stage", 9) < 1:
                return
            kpeb = S.sb([32, NS], BF16, "mla_kpeb", ms_)
            A("act", kpeb[:, :], kpef[:, :NS], AF.Copy, [kpef], [kpeb])
            cnew = S.sb([32, 256], BF16, "mla_cnew", ms_)
            for rc in range(2):
                S.op("pe", lambda e, rc=rc: e.transpose(PS[2][:, :].bitcast(BF16)[:NS, rc * 128:(rc + 1) * 128], cb[:, rc, 0:NS], cstb[:, C_ID:C_ID + 128]),
                     reads=[cb, cstb], writes=[PS[2]])
            A("act", cnew[:, :], PS[2][:, :].bitcast(BF16)[:NS, 0:256], AF.Copy, [PS[2]], [cnew])
            cbuf = [S.sb([128, NT * 256], BF16, f"mla_cbuf{i}", ms_) for i in range(2)]
            kbuf = S.sb([128, 128 * 32 + 32], BF16, "mla_kbuf", ms_)
            S.op("dve", lambda e: e.memset(kbuf[:, 4096:4128], 0.0), writes=[kbuf])
            G4 = 4
            cT = [S.sb([128, G4 * 256], BF16, f"mla_cT{i}", ms_) for i in range(2)]
            kT = [S.sb([128, G4 * 128], BF16, f"mla_kT{i}", ms_) for i in range(2)]
            qpep = S.sb([128, 8, NS], BF16, "mla_qpep", ms_)
            kpebp = S.sb([128, NS], BF16, "mla_kpebp", ms_)
            for t_ in (kT[0], kT[1], qpep, kpebp):
                S.op("dve", lambda e, t_=t_: e.memset(t_[:], 0.0), writes=[t_])
            A("act", qpep[0:32, :, :], qpe[0:32, :, 0:NS], AF.Copy, [qpe, qpep], [qpep])
            A("act", kpebp[0:32, :], kpef[:, :NS], AF.Copy, [kpef, kpebp], [kpebp])
            Pt = [S.sb([128, G4 * 64], BF16, f"mla_Pt{i}", ms_) for i in range(2)]
            Ptn = S.sb([32, 64], BF16, "mla_Ptn", ms_)
            olat = S.sb([64, 256], BF16, "mla_olat", ms_)
            olatT = S.sb([128, 2, 64], BF16, "mla_olatT", ms_)
            rcs = S.sb([64, 1], F32, "mla_rcs", ms_)
            ckv2 = ckvc_d[l].rearrange("n (a t) d -> (n a) (t d)", t=NT)
            kpe2 = kpec_d[l].rearrange("n t d -> n (t d)")
            pool_e = S.eng["pool"]

            def gather(dst, src2, idx_ap, R):
                E = pool_e
                S._wait(E, R, [dst])
                S.pool_epoch_wait()
                if dst.dchan is None:
                    if dst.name not in S.named:
                        S.named[dst.name] = S.new_chan("d_" + dst.name)
                        S.dchans.append(S.named[dst.name])
                    dst.dchan = S.named[dst.name]
                ch = dst.dchan
                ins = E.b.indirect_dma_start(out=dst[:, 0:src2.shape[1]], out_offset=None, in_=src2, in_offset=bass.IndirectOffsetOnAxis(ap=idx_ap, axis=0))
                ch.count += 16
                ins.then_inc(ch.sem, 16)
                dst.w = (ch, ch.count)
                dst.r = []
                for t in R:
                    t.r.append((ch, ch.count))
            pO, pS_ = PS[7], PS[6]
            pK = PS[1]
            it = 0
            for b in range(4):
                qsl = slice(8 * b, 8 * b + 8)
                gather(kbuf, kpe2, idx[:, b:b + 1], [idx])
                for a in range(NCH):
                    cbf = cbuf[a % 2]
                    gather(cbf, ckv2, idxa[:, b, a:a + 1], [idxa])
                    for t4 in range(NT // G4):
                        par = it % 2
                        it += 1
                        pT = PS[2 + par]
                        tl = [a * NT + t4 * G4 + i for i in range(G4)]
                        ct = [cbf[:, (t4 * G4 + i) * 256:(t4 * G4 + i + 1) * 256] for i in range(G4)]

                        def fT(e, pT=pT, ct=ct, tl=tl):
                            ins = None
                            for i in range(G4):
                                e.transpose(pT[:, :].bitcast(BF16)[:, i * 256:i * 256 + 128], ct[i][:, 0:128], cstb[:, C_ID:C_ID + 128])
                                e.transpose(pT[:, :].bitcast(BF16)[:, i * 256 + 128:i * 256 + 256], ct[i][:, 128:256], cstb[:, C_ID:C_ID + 128])
                            for i in range(G4):
                                ins = e.transpose(pK[:, :].bitcast(BF16)[:64, i * 128:(i + 1) * 128], kbuf[:, tl[i] * 32:tl[i] * 32 + 64], cstb[:, C_ID:C_ID + 128])
                            return ins
                        S.op("pe", fT, reads=[cbf, kbuf, cstb], writes=[pT, pK])
                        A("act", cT[par][:, :], pT[:, :].bitcast(BF16)[:, 0:G4 * 256], AF.Copy, [pT], [cT[par]])
                        A("act", kT[par][0:32, :], pK[:, :].bitcast(BF16)[:32, 0:G4 * 128], AF.Copy, [pK], [kT[par]])
                        pSc = PS[4 + par]

                        def fS(e, par=par, pSc=pSc):
                            ins = None
                            for i in range(G4):
                                o_ = pSc[:, i * 64:(i + 1) * 64]
                                e.matmul(o_, lhsT=cT[par][:, i * 256:i * 256 + 128], rhs=qlat[:, 0, :, qsl], start=True, stop=False)
                                e.matmul(o_, lhsT=cT[par][:, i * 256 + 128:i * 256 + 256], rhs=qlat[:, 1, :, qsl], start=False, stop=False)
                                ins = e.matmul(o_, lhsT=kT[par][:, i * 128:(i + 1) * 128], rhs=qpep[:, :, qsl], start=False, stop=True)
                            return ins
                        S.op("pe", fS, reads=[cT[par], kT[par], qlat, qpep], writes=[pSc])
                        A("act", Pt[par][:, :], pSc[:, 0:G4 * 64], AF.Exp, [pSc], [Pt[par]])

                        def fP(e, par=par, ct=ct, first=(a == 0 and t4 == 0)):
                            ins = None
                            for i in range(G4):
                                st = first and i == 0
                                e.matmul(pO[0:64, 0:256], lhsT=Pt[par][:, i * 64:(i + 1) * 64], rhs=ct[i], start=st, stop=False)
                                ins = e.matmul(pS_[0:64, 0:2], lhsT=Pt[par][:, i * 64:(i + 1) * 64], rhs=ones_bf[:, 0:2], start=st, stop=False)
                            return ins
                        S.op("pe", fP, reads=[Pt[par], cbf, ones_bf], writes=[pO, pS_])
                if DBG.get("ms_stage", 9) < 6:
                    continue
                pSc = PS[4]

                def fSn(e):
                    e.matmul(pSc[:NS, 0:64], lhsT=cb[:, 0, 0:NS], rhs=qlat[:, 0, :, qsl], start=True, stop=False)
                    e.matmul(pSc[:NS, 0:64], lhsT=cb[:, 1, 0:NS], rhs=qlat[:, 1, :, qsl], start=False, stop=False)
                    return e.matmul(pSc[:NS, 0:64], lhsT=kpebp[:, 0:NS], rhs=qpep[:, :, qsl], start=False, stop=True)
                S.op("pe", fSn, reads=[cb, kpebp, qlat, qpep], writes=[pSc])
                A("act", Ptn[:, :], pSc[:NS, 0:64], AF.Exp, [pSc], [Ptn])
                TTo(Ptn[:, :].rearrange("p (h t) -> p h t", h=8), Ptn[:, :].rearrange("p (h t) -> p h t", h=8),
                    cstb[:NS, C_MS + 8 * b:C_MS + 8 * b + 8].unsqueeze(1).to_broadcast([NS, 8, 8]), ALU.mult, [Ptn, cstb], [Ptn])

                def fPn(e):
                    e.matmul(pO[0:64, 0:256], lhsT=Ptn[:, :], rhs=cnew[:, :], start=False, stop=True)
                    return e.matmul(pS_[0:64, 0:2], lhsT=Ptn[:, :], rhs=ones_bf[:NS, 0:2], start=False, stop=True)
                S.op("pe", fPn, reads=[Ptn, cnew, ones_bf], writes=[pO, pS_])
                S.op("dve", lambda e: e.reciprocal(out=rcs[:, :], in_=pS_[0:64, 0:1]), reads=[pS_], writes=[rcs])
                TS(olat[:, :], pO[0:64, 0:256], rcs[:, 0:1], None, ALU.mult, None, [pO, rcs], [olat])
                for rc in range(2):
                    S.op("pe", lambda e, rc=rc: e.transpose(PS[2][:, :].bitcast(BF16)[:, rc * 64:(rc + 1) * 64], olat[:, rc * 128:(rc + 1) * 128], cstb[:64, C_ID:C_ID + 64]),
                         reads=[olat, cstb], writes=[PS[2]])
                A("act", olatT[:, :, :].rearrange("p a b -> p (a b)"), PS[2][:, :].bitcast(BF16)[:, 0:128], AF.Copy, [PS[2]], [olatT])

                def fO(e, b=b):
                    ins = None
                    for h in range(8):
                        jp, base = h // 2, (h % 2) * 64
                        for rc in range(2):
                            ins = e.matmul(PS[0][base:base + 64, jp * 32 + 8 * b:jp * 32 + 8 * b + 8], lhsT=wuv[:, rc, h * 64:(h + 1) * 64],
                                           rhs=olatT[:, rc, h * 8:(h + 1) * 8], start=(rc == 0), stop=(rc == 1))
                    return ins
                S.op("pe", fO, reads=[wuv, olatT], writes=[PS[0]])
            if DBG.get("ms_stage", 9) < 6:
                return
            A("act", o_all[:, 4:8, 0:NS], PS[0][:, 0:128].rearrange("p (j t) -> p j t", j=4), AF.Copy, [PS[0]], [o_all])

        def mla_prompt_attn(l, bi, c0, wuk, wuv, cb, kpef, qnp, qpe, o_all, ms_):
            n = TB
            for j in range(4):
                pp = PS[j % 2]

                def f(e, j=j, pp=pp):
                    ins = None
                    for rc in range(2):
                        ins = e.matmul(pp[:, :n], lhsT=wuk[:, rc, j * 128:(j + 1) * 128], rhs=cb[:, rc, :n], start=(rc == 0), stop=(rc == 1))
                    return ins
                S.op("pe", f, reads=[wuk, cb], writes=[pp])
                A("act", knope_h[l][:, j, c0:c0 + n], pp[:, :n], AF.Copy, [pp], [knope_h[l]])
            A("act", kpe_h[l][0:32, c0:c0 + n], kpef[:, :n], AF.Copy, [kpef], [kpe_h[l]])
            S.dma("sp", kpe_h[l][64:96, c0:c0 + n], kpe_h[l][0:32, c0:c0 + n], reads=[kpe_h[l]], writes=[kpe_h[l]])
            for g in range(4):
                pp = PS[2 + g % 2]

                def f(e, g=g, pp=pp):
                    ins = None
                    for rc in range(2):
                        ins = e.matmul(pp[:, :512], lhsT=cb[:, rc, g * 128:(g + 1) * 128], rhs=wuv[:, rc, :], start=(rc == 0), stop=(rc == 1))
                    return ins
                S.op("pe", f, reads=[wuv, cb], writes=[pp])
                A("act", v_h[l][:, bi * 4 + g, :, 0:64], pp[:, :512].rearrange("p (h d) -> p h d", h=8), AF.Copy, [pp], [v_h[l]])
            QR = 256
            PT = S.sb([128, 16, QR], BF16, "mla_PT", ms_)
            o_tok = S.sb([128, 4, 512], BF16, "mla_otok", ms_)
            rcp = S.sb([128, 1], F32, "mla_rcp", ms_)
            for h in range(8):
                j, base = h // 2, (h % 2) * 64
                pss = (PS[0], PS[1]) if h % 2 == 0 else (PS[2], PS[3])
                for qr in range(TB // QR):
                    q0 = qr * QR
                    kb0 = (c0 + q0) // 128
                    nkb = kb0 + QR // 128
                    for kb in range(nkb):
                        pst = pss[kb % 2]

                        def f(e, kb=kb, pst=pst):
                            e.matmul(pst[:, :QR], lhsT=knope_h[l][base:base + 64, j, kb * 128:(kb + 1) * 128], rhs=qnp[base:base + 64, j, q0:q0 + QR],
                                     start=True, stop=False)
                            return e.matmul(pst[:, :QR], lhsT=kpe_h[l][base:base + 32, kb * 128:(kb + 1) * 128], rhs=qpe[base:base + 32, h, q0:q0 + QR],
                                            start=False, stop=True)
                        S.op("pe", f, reads=[knope_h[l], kpe_h[l], qnp, qpe], writes=[pst])
                        A("act", PT[:, kb, :], pst[:, :QR], AF.Exp, [pst], [PT])
                        dj = kb - kb0
                        if dj >= 0:
                            TTo(PT[:, kb, dj * 128:(dj + 1) * 128], PT[:, kb, dj * 128:(dj + 1) * 128], cstb[:, C_TRI:C_TRI + 128], ALU.mult, [PT, cstb], [PT])
                    for qs in range(QR // 128):
                        po = PS[4 + qs % 2]
                        nk = kb0 + qs + 1

                        def f(e, qs=qs, po=po, nk=nk):
                            ins = None
                            for kb in range(nk):
                                ins = e.matmul(po[:, 0:65], lhsT=PT[:, kb, qs * 128:(qs + 1) * 128], rhs=v_h[l][:, kb, h, :], start=(kb == 0), stop=(kb == nk - 1))
                            return ins
                        S.op("pe", f, reads=[PT, v_h[l]], writes=[po])
                        S.op("dve", lambda e, po=po: e.reciprocal(out=rcp[:, :], in_=po[:, 64:65]), reads=[po], writes=[rcp])
                        TS(o_tok[:, qr * 2 + qs, h * 64:(h + 1) * 64], po[:, 0:64], rcp[:, 0:1], None, ALU.mult, None, [po, rcp], [o_tok])
            for qs in range(4):
                for m in range(4):
                    S.op("pe", lambda e, qs=qs, m=m: e.transpose(PS[6][:, :].bitcast(BF16)[:, m * 128:(m + 1) * 128], o_tok[:, qs, m * 128:(m + 1) * 128],
                                                               cstb[:, C_ID:C_ID + 128]), reads=[o_tok, cstb], writes=[PS[6]])
                for m in range(4):
                    A("act", o_all[:, 4 + m, qs * 128:(qs + 1) * 128], PS[6][:, :].bitcast(BF16)[:, m * 128:(m + 1) * 128], AF.Copy, [PS[6]], [o_all])

        def mixer(l, bi, ncols, c0):
            with contextlib.ExitStack() as ms:
                xn = S.sb([128, KC, TB], BF16, "mxn", ms)
                o_all = S.sb([128, KC, TB], BF16, "oall", ms)
                S.op("dve", lambda e: e.memset(o_all[:], 0.0), writes=[o_all])
                with contextlib.ExitStack() as ns:
                    sq = S.sb([128, KC, TB], BF16, "msq", ns)
                    rstd = S.sb([128, TB], F32, "mrstd", ns)
                    rmsnorm(x, ncols, VL[f"nmix_{l}"], xn, sq, rstd, PS[7])
                    S.barrier()
                if DBG.get('hg', 1):
                    hgrn(l, ncols, xn, o_all)
                if DBG.get('rw', 1):
                    rwkv(l, bi, ncols, xn, o_all)
                if DBG.get('mla', 1):
                    mla(l, bi, ncols, c0, xn, o_all)
                with contextlib.ExitStack() as ws:
                    wout = S.sb([128, KC, D], BF16, "wout", ws)
                    S.dma("pool", wout[:], wout_d[l].rearrange("(kc p) n -> p kc n", p=128), writes=[wout])
                    for fo in range(KC):
                        pp = PS[fo % 2]

                        def f(e, fo=fo, pp=pp):
                            ins = None
                            for c in range(KC):
                                ins = e.matmul(pp[:, :ncols], lhsT=wout[:, c, fo * 128:(fo + 1) * 128], rhs=o_all[:, c, :ncols],
                                               start=(c == 0), stop=(c == KC - 1))
                            return ins
                        S.op("pe", f, reads=[wout, o_all], writes=[pp])
                        TTo(x[:, fo, :ncols], pp[:, :ncols], x[:, fo, :ncols], ALU.add, [pp, x], [x])
                    if bi == NPB - 1:
                        S.dma("sp", hg_p_d[l].rearrange("(pc hb) k v -> (hb k) pc v", hb=2), S_hg[l][:], reads=[S_hg[l]])
                    S.barrier()

        blocks = [(xpT, yT_p, b * TB, TB) for b in range(NPB)] + [(xsT, yT_s, 0, NS)]
        alloc_kv()
        for bi, (src, dst, c0, ncols) in enumerate(blocks):
            if bi == NPB:
                S.barrier()
                kv_es.close()
            if bi not in DBG.get('blocks', range(10)):
                continue
            S.dma("sp", x[:, :, :ncols], src.rearrange("(kc p) t -> p kc t", p=128)[:, :, c0:c0 + ncols], writes=[x])
            for l in DBG.get('layers', range(DEPTH)):
                ffn(l, 0, ncols, 0 + l * 8)
                mixer(l, bi, ncols, c0)
                ffn(l, 1, ncols, 32 + l * 8)
            with contextlib.ExitStack() as fs:
                sq = S.sb([128, KC, TB], BF16, "sq", fs)
                rstd = S.sb([128, TB], F32, "rstd", fs)
                yo = S.sb([128, KC, TB], F32, "yo", fs)
                for kc in range(KC):
                    S.op("act", lambda e, kc=kc: e.activation(out=sq[:, kc, :ncols], in_=x[:, kc, :ncols], func=AF.Square),
                         reads=[x], writes=[sq])

                def mm(e):
                    ins = None
                    for kc in range(KC):
                        ins = e.matmul(PS[7][:, :ncols], lhsT=ones_bf[:], rhs=sq[:, kc, :ncols], start=(kc == 0), stop=(kc == KC - 1))
                    return ins
                S.op("pe", mm, reads=[sq, ones_bf], writes=[PS[7]])
                S.op("act", lambda e: e.activation(out=rstd[:, :ncols], in_=PS[7][:, :ncols], func=AF.Sqrt, scale=1.0 / D, bias=epsb[:, 0:1]),
                     reads=[PS[7], epsb], writes=[rstd])
                S.op("dve", lambda e: e.reciprocal(out=rstd[:, :ncols], in_=rstd[:, :ncols]), reads=[rstd], writes=[rstd])
                for kc in range(KC):
                    S.op("dve", lambda e, kc=kc: e.scalar_tensor_tensor(out=yo[:, kc, :ncols], in0=x[:, kc, :ncols],
                                                                      scalar=vecs[:, VL["nfin"] + kc:VL["nfin"] + kc + 1], in1=rstd[:, :ncols],
                                                                      op0=ALU.mult, op1=ALU.mult),
                         reads=[x, vecs, rstd], writes=[yo])
                S.dma("sp", dst.rearrange("(kc p) t -> p kc t", p=128)[:, :, c0:c0 + ncols], yo[:, :, :ncols], reads=[yo])
                S.barrier()
        S.finish()
    return nc


def _fm(v):
    return np.ascontiguousarray(np.asarray(v, np.float32).reshape(-1, 128).T)


def kernel(**inp):
    ncores = DBG.get('ncores', 8)
    nc = build()
    f32 = np.float32
    vecs = np.zeros((128, NV), f32)

    def put(name, v):
        a = _fm(v)
        vecs[:, VL[name]:VL[name] + a.shape[1]] = a
    for l in range(DEPTH):
        put(f"nf1_{l}", inp["norm_ffn1"][l])
        put(f"nmix_{l}", inp["norm_mix"][l])
        put(f"nf2_{l}", inp["norm_ffn2"][l])
        put(f"hglog_{l}", inp["hg_lb_logits"][l])
        put(f"hgn_{l}", np.tile(np.asarray(inp["hg_norm"][l], f32), 2))
        put(f"mu_{l}", inp["rw_mu"][l])
        put(f"w0_{l}", inp["rw_w0"][l])
        put(f"a0_{l}", inp["rw_a0"][l])
        put(f"kk_{l}", inp["rw_kk"][l])
        put(f"ka_{l}", inp["rw_ka"][l])
        put(f"rk_{l}", np.asarray(inp["rw_rk"][l], f32).reshape(-1))
        put(f"lnw_{l}", inp["rw_ln_w"][l])
        put(f"lnb_{l}", inp["rw_ln_b"][l])
        put(f"qn_{l}", inp["mla_q_norm"][l])
        put(f"kvn_{l}", inp["mla_kv_norm"][l])
    put("nfin", inp["norm_final"])
    shared = {"vecs": vecs, "cst": make_consts()}
    shared["lora"] = np.ascontiguousarray(np.stack([np.concatenate([inp["rw_w2"][l], inp["rw_a2"][l], inp["rw_g2"][l]], axis=0)
                                                    for l in range(DEPTH)]), f32)
    wq = np.asarray(inp["mla_wqb"], f32).reshape(DEPTH, 384, 8, 96)
    sw = (np.arange(32) + 16) % 32
    wr = wq[..., 64:]
    shared["wqb"] = np.ascontiguousarray(np.concatenate([wq[..., :64].reshape(DEPTH, 384, 512),
                                                         np.concatenate([wr, wr[..., sw], wr], axis=-1).reshape(DEPTH, 384, 768)], axis=2))
    shared["wuk"] = np.ascontiguousarray(np.asarray(inp["mla_wuk"], f32).reshape(DEPTH, 256, 512))
    shared["wuv"] = np.ascontiguousarray(np.asarray(inp["mla_wuv"], f32).reshape(DEPTH, 256, 512))
    wk = np.asarray(inp["mla_wuk"], f32).transpose(0, 2, 3, 1).reshape(DEPTH, 4, 2, 64, 256)
    shared["wukT"] = np.ascontiguousarray(wk.transpose(0, 2, 3, 1, 4).reshape(DEPTH, 128, 4, 256))
    for l in range(DEPTH):
        shared[f"ckvc{l}"] = np.ascontiguousarray(inp["cache_mla_ckv"][l][:DBG.get('npool', 5120)])
        shared[f"kpec{l}"] = np.ascontiguousarray(inp["cache_mla_kpe"][l][:DBG.get('npool', 5120)])
    past_len = int(inp["page_table"].shape[1]) * int(inp["cache_mla_ckv"].shape[2])
    pos = np.concatenate([np.arange(SEQ, dtype=f32), (past_len + np.tile(np.arange(8, dtype=f32), 4)).astype(f32)])
    inv = np.exp(-np.log(f32(10000.0)) * np.arange(16, dtype=f32) / f32(16)).astype(f32)
    ang = (pos[None, :] * inv[:, None]).astype(f32)
    cos2 = np.concatenate([np.cos(ang), np.cos(ang)], axis=0).astype(f32)
    sin2 = np.concatenate([-np.sin(ang), np.sin(ang)], axis=0).astype(f32)
    rope = np.zeros((128, 2, SEQ + NS), f32)
    for r0 in (0, 64):
        rope[r0:r0 + 32, 0] = cos2
        rope[r0:r0 + 32, 1] = sin2
    shared["rope"] = rope
    for l in range(DEPTH):
        w = np.asarray(inp["w_in"][l], f32)
        wfull = np.concatenate([w, w[:, 2576:2592], w[:, 2560:2576], w[:, 2560:2592]], axis=1).reshape(KC, 128, WIN_COLS)
        shared[f"win{l}"] = np.ascontiguousarray(np.concatenate(
            [wfull[:, :, c0_:c0_ + n_].transpose(1, 0, 2).reshape(128, KC * n_) for c0_, n_ in WIN_GROUPS], axis=1))
        shared[f"wout{l}"] = np.ascontiguousarray(inp["w_out"][l], f32)
    for l in range(DEPTH):
        for f_, (kwi, kwo) in enumerate((("ffn1_wi", "ffn1_wo"), ("ffn2_wi", "ffn2_wo"))):
            wi = np.asarray(inp[kwi][l], f32).reshape(KC, 128, 2, NJ // GJ, GJ * 128)
            shared[f"wi{l}{f_}"] = np.ascontiguousarray(wi.transpose(3, 1, 0, 2, 4).reshape(NJ // GJ, 128, KC * 2 * GJ * 128))
            wo = np.asarray(inp[kwo][l], f32).reshape(NJ // GJ, GJ, 128, 2, 512)
            shared[f"wo{l}{f_}"] = np.ascontiguousarray(wo.transpose(3, 0, 2, 1, 4).reshape(2, NJ // GJ, 128, GJ * 512))
    in_maps = []
    for c in range(ncores):
        m = dict(shared)
        m["xpT"] = np.ascontiguousarray(np.asarray(inp["x_prompt"][c], f32).T)
        m["xsT"] = np.ascontiguousarray(np.asarray(inp["x_sample"][4 * c:4 * c + 4], f32).reshape(NS, D).T)
        m["hgst"] = np.ascontiguousarray(np.asarray(inp["state_hgrn"][:, 4 * c:4 * c + 4], f32))
        m["pt"] = np.ascontiguousarray((np.asarray(inp["page_table"][4 * c:4 * c + 4]) % DBG.get('npool', 1 << 30)).astype(np.int32).T)
        m["rwst"] = np.ascontiguousarray(np.asarray(inp["state_rwkv"][:, 4 * c:4 * c + 4], f32).transpose(0, 1, 2, 4, 3))
        sh = np.asarray(inp["state_rwkv_shift"][:, 4 * c:4 * c + 4], f32)
        m["rwsh"] = np.ascontiguousarray(sh.reshape(DEPTH, 4, 7, 128).transpose(0, 3, 2, 1))
        in_maps.append(m)
    res = run_bass_kernel_spmd(nc, in_maps, core_ids=list(range(ncores)))
    R = res.results
    y_p = np.stack([R[c]["yT_p"].T for c in range(ncores)]).astype(f32)
    y_s = np.concatenate([R[c]["yT_s"].T.reshape(4, 8, D) for c in range(ncores)]).astype(f32)
    hg_p = np.stack([R[c]["hg_p"] for c in range(ncores)], axis=1).astype(f32)
    hg_s = np.concatenate([R[c]["hg_s"] for c in range(ncores)], axis=1).astype(f32)
    z = lambda *sh: np.zeros(sh, f32)
    rw_p = np.stack([R[c]["rw_p"].transpose(0, 1, 3, 2) for c in range(ncores)], axis=1).astype(f32)
    rw_s = np.concatenate([R[c]["rw_s"].transpose(0, 1, 2, 4, 3) for c in range(ncores)], axis=1).astype(f32)
    sh_p = np.stack([R[c]["sh_p"].transpose(0, 2, 1).reshape(DEPTH, 896) for c in range(ncores)], axis=1).astype(f32)
    sh_s = np.concatenate([R[c]["sh_s"].transpose(0, 3, 2, 1).reshape(DEPTH, 4, 896) for c in range(ncores)], axis=1).astype(f32)
    ckv_p = np.stack([R[c]["ckv_p"].transpose(0, 2, 1) for c in range(ncores)], axis=1).astype(f32)
    ckv_s = np.concatenate([R[c]["ckv_s"].transpose(0, 2, 1).reshape(DEPTH, 4, 8, 256) for c in range(ncores)], axis=1).astype(f32)
    kpe_p = np.stack([R[c]["kpe_p"].transpose(0, 2, 1) for c in range(ncores)], axis=1).astype(f32)
    kpe_s = np.concatenate([R[c]["kpe_s"].transpose(0, 2, 1).reshape(DEPTH, 4, 8, 32) for c in range(ncores)], axis=1).astype(f32)
    return (y_p, y_s, hg_p, hg_s, rw_p, rw_s, sh_p, sh_s, ckv_p, ckv_s, kpe_p, kpe_s)
```

```python
import contextlib
import numpy as np
import concourse.bass as bass
import concourse.mybir as mybir
from concourse.bass_utils import run_bass_kernel_spmd

F32 = mybir.dt.float32
BF16 = mybir.dt.bfloat16
I32 = mybir.dt.int32
U32 = mybir.dt.uint32
AF = mybir.ActivationFunctionType
ALU = mybir.AluOpType
AX = mybir.AxisListType

D = 1024
SEQ = 2048
DEPTH = 2
DFF = 2816
NJ = DFF // 128
KC = D // 128
EPS = 1e-6
TB = 512
NPB = SEQ // TB
NS = 32
IN_COLS = 2592
GJ = 2


WIN_COLS = 2656
WIN_GROUPS = [(0, 256), (256, 256), (512, 256), (768, 256), (1024, 256), (1280, 256), (1536, 256), (1792, 128),
              (1920, 256), (2176, 128), (2304, 256), (2560, 96)]
WIN_OFF = {}
_o = 0
for _c0, _n in WIN_GROUPS:
    WIN_OFF[(_c0, _n)] = _o
    _o += KC * _n
WIN_FLAT = _o
C_ID, C_MP, C_MPS, C_BLK, C_SEGP, C_SEGS, C_MS, C_MSS, C_ROWS = 0, 128, 256, 384, 512, 1024, 1056, 1088, 1120
C_MPL, C_MSL, C_TRI = 1124, 1252, 1284
NCST = 1412


def vec_layout():
    L = {}
    c = [0]

    def add(name, n):
        L[name] = c[0]
        c[0] += n
    for l in range(DEPTH):
        add(f"nf1_{l}", 8)
    for l in range(DEPTH):
        add(f"nmix_{l}", 8)
    for l in range(DEPTH):
        add(f"nf2_{l}", 8)
    add("nfin", 8)
    for l in range(DEPTH):
        add(f"hglog_{l}", 2)
    for l in range(DEPTH):
        add(f"hgn_{l}", 1)
    for l in range(DEPTH):
        for nm, n in (("mu", 7), ("w0", 2), ("a0", 2), ("kk", 2), ("ka", 2), ("rk", 2), ("lnw", 2), ("lnb", 2), ("qn", 3), ("kvn", 2)):
            add(f"{nm}_{l}", n)
    L["_n"] = c[0]
    return L


VL = vec_layout()
NV = VL["_n"]


def make_consts():
    c = np.zeros((128, NCST), np.float32)
    i = np.arange(128)
    c[:, C_ID:C_ID + 128] = np.eye(128)
    same = (i[:, None] // 64) == (i[None, :] // 64)
    c[:, C_MP:C_MP + 128] = same & (i[:, None] <= i[None, :])
    c[:, C_MPS:C_MPS + 128] = same & (i[:, None] < i[None, :])
    c[:, C_BLK:C_BLK + 128] = same
    t = np.arange(512)
    c[:, C_SEGP:C_SEGP + 512] = (t % 64 != 0)[None, :]
    t = np.arange(32)
    c[:, C_SEGS:C_SEGS + 32] = (t % 8 != 0)[None, :]
    j = np.arange(32)
    same8 = (j[:, None] // 8) == (j[None, :] // 8)
    c[:32, C_MS:C_MS + 32] = same8 & (j[:, None] <= j[None, :])
    c[:32, C_MSS:C_MSS + 32] = same8 & (j[:, None] < j[None, :])
    for b in range(4):
        c[8 * b:8 * b + 8, C_ROWS + b] = 1.0
    c[:, C_MPL:C_MPL + 128] = same & (i[:, None] > i[None, :])
    c[:32, C_MSL:C_MSL + 32] = same8 & (j[:, None] > j[None, :])
    c[:, C_TRI:C_TRI + 128] = (i[:, None] <= i[None, :])
    return c


DBG = {}


class Chan:
    def __init__(self, sem):
        self.sem = sem
        self.count = 0


class Eng:
    def __init__(self, name, b, chan, selfsync):
        self.name = name
        self.b = b
        self.chan = chan
        self.seen = {}
        self.selfsync = selfsync


class TT:
    def __init__(self, t, name):
        self.t = t
        self.name = name
        self.w = None
        self.r = []
        self.dchan = None

    def __getitem__(self, idx):
        return self.t[idx]


class Sched:
    def __init__(self, nc, es):
        self.nc = nc
        self.es = es
        self.nsem = 0
        self.eng = {}
        for name, b, ss in (("pe", nc.tensor, False), ("act", nc.scalar, DBG.get("ss", True)), ("dve", nc.vector, DBG.get("ss", True)),
                            ("pool", nc.gpsimd, True), ("sp", nc.sync, False)):
            self.eng[name] = Eng(name, b, self.new_chan("e_" + name), ss)
        self.dchans = []
        self.named = {}
        self.rec = None
        self.epoch = {}
        self.ntile = 0

    def new_chan(self, name):
        self.nsem += 1
        return Chan(self.es.enter_context(self.nc.semaphore(name)))

    def sb(self, shape, dt, name, es=None):
        self.ntile += 1
        t = (es or self.es).enter_context(self.nc.sbuf_tensor(f"{name}_{self.ntile}", list(shape), dt))
        return TT(t, name)

    def ps(self, shape, dt, name, es=None):
        self.ntile += 1
        t = (es or self.es).enter_context(self.nc.psum_tensor(f"{name}_{self.ntile}", list(shape), dt))
        return TT(t, name)

    def _wait(self, E, reads, writes):
        deps = {}

        def add(d):
            ch, cnt = d
            if deps.get(ch, 0) < cnt:
                deps[ch] = cnt
        for t in reads:
            if t.w is not None:
                add(t.w)
        for t in writes:
            if t.w is not None:
                add(t.w)
            for r in t.r:
                add(r)
        for ch, cnt in deps.items():
            if ch is E.chan and not E.selfsync:
                continue
            if E.seen.get(ch, 0) < cnt:
                E.b.wait_ge(ch.sem, cnt)
                E.seen[ch] = cnt

    def op(self, eng, fn, reads=(), writes=()):
        if self.rec is not None:
            rec, self_ = self.rec, self
            rec.append(lambda: self_._replay(self_.op, eng, fn, reads, writes))
            return None
        E = self.eng[eng]
        self._wait(E, reads, writes)
        if E.selfsync and DBG.get("serial", True) and E.seen.get(E.chan, 0) < E.chan.count:
            E.b.wait_ge(E.chan.sem, E.chan.count)
            E.seen[E.chan] = E.chan.count
        ins = fn(E.b)
        E.chan.count += 1
        ins.then_inc(E.chan.sem, 1)
        stamp = (E.chan, E.chan.count)
        for t in writes:
            t.w = stamp
            t.r = []
        for t in reads:
            t.r.append(stamp)
        return ins

    def _replay(self, f, *a, **kw):
        saved, self.rec = self.rec, None
        try:
            return f(*a, **kw)
        finally:
            self.rec = saved

    def dma(self, q, out, in_, reads=(), writes=(), **kw):
        if self.rec is not None:
            rec, self_ = self.rec, self
            rec.append(lambda: self_._replay(self_.dma, q, out, in_, reads, writes, **kw))
            return None
        E = self.eng[q]
        self._wait(E, reads, writes)
        if q == "pool" and not kw.pop("persistent", False):
            self.pool_epoch_wait()
        owner = (list(writes) + list(reads))[0]
        if owner.dchan is None:
            if owner.name not in self.named:
                self.named[owner.name] = self.new_chan("d_" + owner.name)
                self.dchans.append(self.named[owner.name])
            owner.dchan = self.named[owner.name]
        ch = owner.dchan
        ins = E.b.dma_start(out=out, in_=in_, **kw)
        ch.count += 16
        ins.then_inc(ch.sem, 16)
        stamp = (ch, ch.count)
        for t in writes:
            t.w = stamp
            t.r = []
        for t in reads:
            t.r.append(stamp)
        return ins

    def barrier(self, engines=("pe", "act", "dve", "sp"), dchans=True):
        chans = [self.eng[e].chan for e in self.eng] + (self.dchans if dchans else [])
        if dchans:
            self.epoch = {ch: ch.count for ch in chans if ch is not self.eng["pool"].chan}
        for e in engines:
            E = self.eng[e]
            for ch in chans:
                if ch is E.chan:
                    continue
                if E.seen.get(ch, 0) < ch.count:
                    E.b.wait_ge(ch.sem, ch.count)
                    E.seen[ch] = ch.count

    def pool_epoch_wait(self):
        E = self.eng["pool"]
        for ch, cnt in self.epoch.items():
            if E.seen.get(ch, 0) < cnt:
                E.b.wait_ge(ch.sem, cnt)
                E.seen[ch] = cnt

    def finish(self):
        self.barrier(engines=("sp",))


def build():
    nc = bass.Bass("TRN2", target_bir_lowering=False)
    dt_in = lambda name, shape, dt=F32: nc.dram_tensor(name, list(shape), dt, kind="ExternalInput").ap()
    dt_out = lambda name, shape, dt=F32: nc.dram_tensor(name, list(shape), dt, kind="ExternalOutput").ap()

    xpT = dt_in("xpT", [D, SEQ])
    xsT = dt_in("xsT", [D, NS])
    wi_d = [[dt_in(f"wi{l}{f}", [NJ // GJ, 128, KC * 2 * GJ * 128]) for f in range(2)] for l in range(DEPTH)]
    wo_d = [[dt_in(f"wo{l}{f}", [2, NJ // GJ, 128, GJ * 512]) for f in range(2)] for l in range(DEPTH)]
    vec_d = dt_in("vecs", [128, NV])
    cst_d = dt_in("cst", [128, NCST])
    win_d = [dt_in(f"win{l}", [128, WIN_FLAT]) for l in range(DEPTH)]
    wout_d = [dt_in(f"wout{l}", [D, D]) for l in range(DEPTH)]
    hgst_d = dt_in("hgst", [DEPTH, 4, 4, 64, 64])
    hg_p_d = dt_out("hg_p", [DEPTH, 4, 64, 64])
    lora_d = dt_in("lora", [DEPTH, 128, 256])
    wqb_d = dt_in("wqb", [DEPTH, 384, 1280])
    wuk_d = dt_in("wuk", [DEPTH, 256, 512])
    wuv_d = dt_in("wuv", [DEPTH, 256, 512])
    rope_d = dt_in("rope", [128, 2, SEQ + NS])
    NPOOL = DBG.get('npool', 5120)
    ckvc_d = [dt_in(f"ckvc{l}", [NPOOL, 128, 256]) for l in range(DEPTH)]
    kpec_d = [dt_in(f"kpec{l}", [NPOOL, 128, 32]) for l in range(DEPTH)]
    pt_d = dt_in("pt", [128, 4], I32)
    wukT_d = dt_in("wukT", [DEPTH, 128, 4, 256])
    ckv_p_d = dt_out("ckv_p", [DEPTH, 256, SEQ])
    ckv_s_d = dt_out("ckv_s", [DEPTH, 256, NS])
    kpe_p_d = dt_out("kpe_p", [DEPTH, 32, SEQ])
    kpe_s_d = dt_out("kpe_s", [DEPTH, 32, NS])
    rwst_d = dt_in("rwst", [DEPTH, 4, 4, 64, 64])
    rwsh_d = dt_in("rwsh", [DEPTH, 128, 7, 4])
    rw_p_d = dt_out("rw_p", [DEPTH, 4, 64, 64])
    rw_s_d = dt_out("rw_s", [DEPTH, 4, 4, 64, 64])
    sh_p_d = dt_out("sh_p", [DEPTH, 128, 7])
    sh_s_d = dt_out("sh_s", [DEPTH, 128, 7, 4])
    hg_s_d = dt_out("hg_s", [DEPTH, 4, 4, 64, 64])
    yT_p = dt_out("yT_p", [D, SEQ])
    yT_s = dt_out("yT_s", [D, NS])

    with contextlib.ExitStack() as es:
        S = Sched(nc, es)
        vecs = S.sb([128, NV], F32, "vecs")
        S.dma("sp", vecs[:], vec_d[:, :], writes=[vecs])
        cst = S.sb([128, NCST], F32, "cst")
        S.dma("sp", cst[:], cst_d[:, :], writes=[cst])
        cstb = S.sb([128, NCST], BF16, "cstb")
        S.op("dve", lambda e: e.tensor_copy(out=cstb[:], in_=cst[:]), reads=[cst], writes=[cstb])
        ones_bf = S.sb([128, 128], BF16, "ones")
        S.op("dve", lambda e: e.memset(ones_bf[:], 1.0), writes=[ones_bf])

        x = S.sb([128, KC, TB], F32, "x")
        NWI, NWO = 3, 4
        wi_buf = [S.sb([128, KC, 2, GJ * 128], BF16, f"wi{i}") for i in range(NWI)]
        wo_buf = [S.sb([128, GJ, 512], BF16, f"wo{i}") for i in range(NWO)]
        wi_ctr = [0]
        wo_ctr = [0]
        PS = [S.ps([128, 512], F32, f"ps{i}") for i in range(8)]

        def rmsnorm(xt, ncols, gcol, out_bf, sq, rstd, psb):
            S.op("act", lambda e: e.activation(out=sq[:, :, :ncols], in_=xt[:, :, :ncols], func=AF.Square), reads=[xt], writes=[sq])

            def mm(e):
                ins = None
                for kc in range(KC):
                    ins = e.matmul(psb[:, :ncols], lhsT=ones_bf[:], rhs=sq[:, kc, :ncols], start=(kc == 0), stop=(kc == KC - 1))
                return ins
            S.op("pe", mm, reads=[sq, ones_bf], writes=[psb])
            S.op("act", lambda e: e.activation(out=rstd[:, :ncols], in_=psb[:, :ncols], func=AF.Sqrt, scale=1.0 / D, bias=epsb[:, 0:1]),
                 reads=[psb, epsb], writes=[rstd])
            S.op("dve", lambda e: e.reciprocal(out=rstd[:, :ncols], in_=rstd[:, :ncols]), reads=[rstd], writes=[rstd])
            for kc in range(KC):
                S.op("dve", lambda e, kc=kc: e.scalar_tensor_tensor(out=out_bf[:, kc, :ncols], in0=xt[:, kc, :ncols],
                                                                  scalar=vecs[:, gcol + kc:gcol + kc + 1], in1=rstd[:, :ncols],
                                                                  op0=ALU.mult, op1=ALU.mult),
                     reads=[xt, vecs, rstd], writes=[out_bf])

        epsb = S.sb([128, 1], F32, "epsb")
        S.op("dve", lambda e: e.memset(epsb[:], EPS), writes=[epsb])

        def ffn(l, f, ncols, gcol):
            with contextlib.ExitStack() as fs:
                xn = S.sb([128, KC, TB], BF16, "xn", fs)
                sq = S.sb([128, KC, TB], BF16, "sq", fs)
                rstd = S.sb([128, TB], F32, "rstd", fs)
                h = [S.sb([128, TB], BF16, f"h{j}", fs) for j in range(NJ)]
                sa = [S.sb([128, TB], BF16, f"sa{i}", fs) for i in range(2)]
                rmsnorm(x, ncols, gcol, xn, sq, rstd, PS[7])
                for g in range(NJ // GJ):
                    wb = wi_buf[wi_ctr[0] % NWI]
                    wi_ctr[0] += 1
                    if not (DBG.get("nodma") and wi_ctr[0] > NWI):
                        S.dma("pool", wb[:, :, :, :].rearrange("p a b c -> p (a b c)"), wi_d[l][f][g], writes=[wb], persistent=True)
                    for jj in range(GJ):
                        j = g * GJ + jj
                        pa, pb = PS[(j % 2) * 2], PS[(j % 2) * 2 + 1]

                        def mma(e, wb=wb, jj=jj, pa=pa):
                            ins = None
                            for kc in range(KC):
                                ins = e.matmul(pa[:, :ncols], lhsT=wb[:, kc, 0, jj * 128:(jj + 1) * 128], rhs=xn[:, kc, :ncols],
                                               start=(kc == 0), stop=(kc == KC - 1))
                            return ins

                        def mmb(e, wb=wb, jj=jj, pb=pb):
                            ins = None
                            for kc in range(KC):
                                ins = e.matmul(pb[:, :ncols], lhsT=wb[:, kc, 1, jj * 128:(jj + 1) * 128], rhs=xn[:, kc, :ncols],
                                               start=(kc == 0), stop=(kc == KC - 1))
                            return ins
                        S.op("pe", mma, reads=[wb, xn], writes=[pa])
                        S.op("pe", mmb, reads=[wb, xn], writes=[pb])
                        st = sa[j % 2]
                        S.op("act", lambda e, pa=pa, st=st: e.activation(out=st[:, :ncols], in_=pa[:, :ncols], func=AF.Silu),
                             reads=[pa], writes=[st])
                        S.op("dve", lambda e, pb=pb, st=st, j=j: e.tensor_tensor(out=h[j][:, :ncols], in0=pb[:, :ncols], in1=st[:, :ncols], op=ALU.mult),
                             reads=[pb, st], writes=[h[j]])
                for half in range(2):
                    acc = [PS[4 + i] for i in range(4)]
                    for g in range(NJ // GJ):
                        wb = wo_buf[wo_ctr[0] % NWO]
                        wo_ctr[0] += 1
                        if not (DBG.get("nodma") and wo_ctr[0] > NWO):
                            S.dma("pool", wb[:, :, :].rearrange("p a b -> p (a b)"), wo_d[l][f][half, g], writes=[wb], persistent=True)
                        for jj in range(GJ):
                            j = g * GJ + jj
                            for fo in range(4):
                                S.op("pe", lambda e, wb=wb, jj=jj, j=j, fo=fo: e.matmul(
                                    acc[fo][:, :ncols], lhsT=wb[:, jj, fo * 128:(fo + 1) * 128], rhs=h[j][:, :ncols],
                                    start=(j == 0), stop=(j == NJ - 1)),
                                    reads=[wb, h[j]], writes=[acc[fo]])
                    for fo in range(4):
                        kc = half * 4 + fo
                        S.op("dve", lambda e, fo=fo, kc=kc: e.scalar_tensor_tensor(
                            out=x[:, kc, :ncols], in0=acc[fo][:, :ncols], scalar=0.5, in1=x[:, kc, :ncols],
                            op0=ALU.mult, op1=ALU.add), reads=[acc[fo], x], writes=[x])
                S.barrier()


        NWB = 2
        win_buf = [S.sb([128, KC, 256], BF16, f"win{i}") for i in range(NWB)]
        win_ctr = [0]
        lbt = S.sb([128, 8], F32, "lbt")
        S.op("dve", lambda e: e.memset(lbt[:], 0.0), writes=[lbt])
        S.op("dve", lambda e: e.tensor_tensor(out=lbt[:, 2:4], in0=vecs[:, VL["hglog_1"]:VL["hglog_1"] + 2],
                                              in1=vecs[:, VL["hglog_0"]:VL["hglog_0"] + 2], op=ALU.subtract),
             reads=[vecs, lbt], writes=[lbt])
        S.op("act", lambda e: e.activation(out=lbt[:, 2:4], in_=lbt[:, 2:4], func=AF.Sigmoid), reads=[lbt], writes=[lbt])
        S.op("dve", lambda e: e.tensor_scalar(out=lbt[:, 4:8], in0=lbt[:, 0:4], scalar1=-1.0, scalar2=1.0, op0=ALU.mult, op1=ALU.add),
             reads=[lbt], writes=[lbt])
        S_hg = [S.sb([128, 2, 64], F32, f"S_hg{l}") for l in range(DEPTH)]
        for l in range(DEPTH):
            S.op("dve", lambda e, l=l: e.memset(S_hg[l][:], 0.0), writes=[S_hg[l]])

        def wgroup(l, col0, ncol):
            wb = win_buf[win_ctr[0] % NWB]
            win_ctr[0] += 1
            off = WIN_OFF[(col0, ncol)]
            S.dma("pool", wb[:, :, :ncol], win_d[l][:, off:off + KC * ncol].rearrange("p (k n) -> p k n", k=KC), writes=[wb], persistent=True)
            return wb

        def proj_fm(wb, wcol, M, ps, xn, ncols, pbase=0):
            def f(e):
                ins = None
                for kc in range(KC):
                    ins = e.matmul(ps[pbase:pbase + M, :ncols], lhsT=wb[:, kc, wcol:wcol + M], rhs=xn[:, kc, :ncols],
                                   start=(kc == 0), stop=(kc == KC - 1))
                return ins
            S.op("pe", f, reads=[wb, xn], writes=[ps])

        def A(eng, out, in_, func, R, W, **kw):
            S.op(eng, lambda e: e.activation(out=out, in_=in_, func=func, **kw), reads=R, writes=W)

        def TTo(out, in0, in1, op, R, W, eng="dve"):
            S.op(eng, lambda e: e.tensor_tensor(out=out, in0=in0, in1=in1, op=op), reads=R, writes=W)

        def TS(out, in0, s1, s2, op0, op1, R, W, eng="dve"):
            if op1 is None:
                S.op(eng, lambda e: e.tensor_scalar(out=out, in0=in0, scalar1=s1, scalar2=None, op0=op0), reads=R, writes=W)
            else:
                S.op(eng, lambda e: e.tensor_scalar(out=out, in0=in0, scalar1=s1, scalar2=s2, op0=op0, op1=op1), reads=R, writes=W)

        def STT(out, in0, scalar, in1, op0, op1, R, W):
            S.op("dve", lambda e: e.scalar_tensor_tensor(out=out, in0=in0, scalar=scalar, in1=in1, op0=op0, op1=op1), reads=R, writes=W)

        def MM(out, lhsT, rhs, R, W, start=True, stop=True):
            S.op("pe", lambda e: e.matmul(out, lhsT=lhsT, rhs=rhs, start=start, stop=stop), reads=R, writes=W)

        def hgrn(l, ncols, xn, o_all):
            samp = (ncols == NS)
            C = 8 if samp else 64
            nseg = ncols // C
            groups = [(0, 32, [0, 1, 2, 3])] if samp else [(g * 128, 128, [2 * g, 2 * g + 1]) for g in range(4)]
            ng = len(groups)
            segm = cst[:, C_SEGS:C_SEGS + 32] if samp else cst[:, C_SEGP:C_SEGP + 512]
            mcol = C_MS if samp else C_MP
            with contextlib.ExitStack() as hs:
                qT = S.sb([128, 2, TB], F32, "hq", hs)
                sg_ = S.sb([128, 2, TB], F32, "hsig", hs)
                sn = S.sb([128, 2, TB], F32, "hsn", hs)
                bb = S.sb([128, 2, TB], F32, "hb", hs)
                t1 = S.sb([128, 2, TB], F32, "ht1", hs)
                t2 = S.sb([128, 2, TB], F32, "ht2", hs)
                Qi = S.sb([128, 2, TB], BF16, "hQi", hs)
                Ki = S.sb([128, 2, TB], BF16, "hKi", hs)
                Qs = S.sb([128, 2, TB], BF16, "hQs", hs)
                Kd = S.sb([128, 2, TB], BF16, "hKd", hs)
                gate = S.sb([128, 2, TB], F32, "hgate", hs)
                Vt = S.sb([128, 4, 256], BF16, "hVt", hs)
                KdT = S.sb([128, 4, 2, 128], BF16, "hKdT", hs)
                KdTm = S.sb([32, 4, 2, 128], BF16, "hKdTm", hs) if samp else None
                attm = S.sb([128, 4, 2, 2, 128], BF16, "hattm", hs)
                Sbf = S.sb([128, 8, 2, 64], BF16, "hSbf", hs)
                dseg = S.sb([128, 2, 8], F32, "hdseg", hs)
                osb = S.sb([128, 2, TB], F32, "hosb", hs)
                o2 = Qi
                Ssm = [S.sb([128, 2, 64], F32, f"hSs{b}", hs) for b in range(4)] if samp else None
                lbc = lambda pc: lbt[:, l * 2 + pc:l * 2 + pc + 1]
                omlc = lambda pc: lbt[:, 4 + l * 2 + pc:4 + l * 2 + pc + 1]
                wb = wgroup(l, 0, 256)
                for pc in range(2):
                    proj_fm(wb, pc * 128, 128, PS[pc], xn, ncols)
                    A("act", qT[:, pc, :ncols], PS[pc][:, :ncols], AF.Copy, [PS[pc]], [qT])
                wb = wgroup(l, 256, 256)
                for pc in range(2):
                    p_ = PS[2 + pc]
                    proj_fm(wb, pc * 128, 128, p_, xn, ncols)
                    A("act", sg_[:, pc, :ncols], p_[:, :ncols], AF.Sigmoid, [p_], [sg_])
                    A("act", sn[:, pc, :ncols], p_[:, :ncols], AF.Sigmoid, [p_], [sn], scale=-1.0)
                    TS(sg_[:, pc, :ncols], sg_[:, pc, :ncols], omlc(pc), lbc(pc), ALU.mult, ALU.add, [sg_, lbt], [sg_])
                    TS(sg_[:, pc, :ncols], sg_[:, pc, :ncols], 1e-30, None, ALU.max, None, [sg_], [sg_])
                    A("act", sg_[:, pc, :ncols], sg_[:, pc, :ncols], AF.Ln, [sg_], [sg_])
                    TS(sn[:, pc, :ncols], sn[:, pc, :ncols], omlc(pc), None, ALU.mult, None, [sn, lbt], [sn])
                    S.op("dve", lambda e, pc=pc: e.tensor_tensor_scan(out=bb[:, pc, :ncols], data0=segm[:, :ncols], data1=sg_[:, pc, :ncols],
                                                                     initial=0.0, op0=ALU.mult, op1=ALU.add), reads=[sg_, cst], writes=[bb])
                    bv = bb[:, pc, :ncols].rearrange("p (s c) -> p s c", c=C)
                    v3 = lambda t, pc=pc: t[:, pc, :ncols].rearrange("p (s c) -> p s c", c=C)
                    TTo(v3(t1), bv, bv[:, :, C // 2 - 1:C // 2].to_broadcast([128, nseg, C]), ALU.subtract, [bb], [t1])
                    A("act", t2[:, pc, :ncols], t1[:, pc, :ncols], AF.Exp, [t1], [t2])
                    TTo(Qi[:, pc, :ncols], qT[:, pc, :ncols], t2[:, pc, :ncols], ALU.mult, [qT, t2], [Qi])
                    A("act", t2[:, pc, :ncols], t1[:, pc, :ncols], AF.Exp, [t1], [t2], scale=-1.0)
                    TTo(Ki[:, pc, :ncols], sn[:, pc, :ncols], t2[:, pc, :ncols], ALU.mult, [sn, t2], [Ki])
                    A("act", t2[:, pc, :ncols], bb[:, pc, :ncols], AF.Exp, [bb], [t2])
                    TTo(Qs[:, pc, :ncols], qT[:, pc, :ncols], t2[:, pc, :ncols], ALU.mult, [qT, t2], [Qs])
                    TTo(v3(t1), bv, bv[:, :, C - 1:C].to_broadcast([128, nseg, C]), ALU.subtract, [bb], [t1])
                    A("act", t2[:, pc, :ncols], t1[:, pc, :ncols], AF.Exp, [t1], [t2], scale=-1.0)
                    TTo(Kd[:, pc, :ncols], sn[:, pc, :ncols], t2[:, pc, :ncols], ALU.mult, [sn, t2], [Kd])
                    A("act", dseg[:, pc, :nseg], bv[:, :, C - 1], AF.Exp, [bb], [dseg])
                wb = wgroup(l, 512, 256)
                for g, (c0g, gsz, segs) in enumerate(groups):
                    def f(e, c0g=c0g, gsz=gsz, wb=wb):
                        ins = None
                        for kc in range(KC):
                            ins = e.matmul(PS[4][:gsz, 0:256], lhsT=xn[:, kc, c0g:c0g + gsz], rhs=wb[:, kc, 0:256], start=(kc == 0), stop=(kc == KC - 1))
                        return ins
                    S.op("pe", f, reads=[xn, wb], writes=[PS[4]])
                    A("act", Vt[:gsz, g, :], PS[4][:gsz, 0:256], AF.Copy, [PS[4]], [Vt])
                wb = wgroup(l, 768, 256)
                for pc in range(2):
                    proj_fm(wb, pc * 128, 128, PS[pc], xn, ncols)
                    A("act", gate[:, pc, :ncols], PS[pc][:, :ncols], AF.Silu, [PS[pc]], [gate])
                if DBG.get('hg_stage', 9) < 1:
                    S.barrier()
                    return
                psT = PS[5]
                for g, (c0g, gsz, segs) in enumerate(groups):
                    for pc in range(2):
                        S.op("pe", lambda e, g=g, pc=pc, c0g=c0g, gsz=gsz: e.transpose(
                            psT[:, :].bitcast(BF16)[:gsz, pc * 128:(pc + 1) * 128], Kd[:, pc, c0g:c0g + gsz], cstb[:, C_ID:C_ID + 128]),
                            reads=[Kd, cstb], writes=[psT])
                    S.op("dve", lambda e, g=g, gsz=gsz: e.tensor_copy(out=KdT[:gsz, g, :, :].rearrange("p a b -> p (a b)"),
                                                                     in_=psT[:, :].bitcast(BF16)[:gsz, 0:256]), reads=[psT], writes=[KdT])
                if samp:
                    for b in range(4):
                        TS(KdTm[:, b, :, :].rearrange("p a b -> p (a b)"), KdT[:32, 0, :, :].rearrange("p a b -> p (a b)"),
                           cst[:32, C_ROWS + b:C_ROWS + b + 1], None, ALU.mult, None, [KdT, cst], [KdTm])
                if DBG.get('hg_stage', 9) < 2:
                    S.barrier()
                    return
                psA = [PS[6], PS[5]]
                for g, (c0g, gsz, segs) in enumerate(groups):
                    for hb in range(2):
                        base = hb * 64

                        def f(e, c0g=c0g, gsz=gsz, hb=hb, base=base):
                            ins = None
                            for pc in range(2):
                                ins = e.matmul(psA[hb][:gsz, pc * 128:pc * 128 + gsz], lhsT=Ki[base:base + 64, pc, c0g:c0g + gsz],
                                               rhs=Qi[base:base + 64, pc, c0g:c0g + gsz], start=True, stop=True)
                            return ins
                        S.op("pe", f, reads=[Ki, Qi], writes=[psA[hb]])
                        TTo(attm[:gsz, g, hb, :, :gsz], psA[hb][:gsz, 0:256].rearrange("p (h t) -> p h t", h=2)[:, :, :gsz],
                            cst[:gsz, mcol:mcol + gsz].unsqueeze(1).to_broadcast([gsz, 2, gsz]), ALU.mult, [psA[hb], cst], [attm])
                if DBG.get('hg_stage', 9) < 3:
                    S.barrier()
                    return
                psU = PS[7]
                if not samp:
                    Sst = S_hg[l]
                    for seg in range(nseg):
                        g, r0 = seg // 2, (seg % 2) * 64
                        psU = PS[7] if r0 == 0 else PS[4]
                        A("act", Sbf[:, seg, :, :].rearrange("p a b -> p (a b)"), Sst[:, :, :].rearrange("p a b -> p (a b)"), AF.Copy, [Sst], [Sbf])

                        def f(e, g=g, r0=r0):
                            ins = None
                            for h in range(4):
                                pc, base = h // 2, (h % 2) * 64
                                ins = e.matmul(psU[base:base + 64, pc * 64:(pc + 1) * 64], lhsT=KdT[r0:r0 + 64, g, pc, base:base + 64],
                                               rhs=Vt[r0:r0 + 64, g, h * 64:(h + 1) * 64], start=True, stop=True)
                            return ins
                        S.op("pe", f, reads=[KdT, Vt], writes=[psU])
                        for pc in range(2):
                            STT(Sst[:, pc, :], Sst[:, pc, :], dseg[:, pc, seg:seg + 1], psU[:, pc * 64:(pc + 1) * 64], ALU.mult, ALU.add,
                                [Sst, dseg, psU], [Sst])
                else:
                    for b in range(4):
                        Sst = Ssm[b]
                        S.dma("sp", Sst[:], hgst_d[l, b].rearrange("(pc hb) k v -> (hb k) pc v", hb=2), writes=[Sst])
                        A("act", Sbf[:, b, :, :].rearrange("p a b -> p (a b)"), Sst[:, :, :].rearrange("p a b -> p (a b)"), AF.Copy, [Sst], [Sbf])

                        def f(e, b=b):
                            ins = None
                            for h in range(4):
                                pc, base = h // 2, (h % 2) * 64
                                ins = e.matmul(psU[base:base + 64, pc * 64:(pc + 1) * 64], lhsT=KdTm[:32, b, pc, base:base + 64],
                                               rhs=Vt[:32, 0, h * 64:(h + 1) * 64], start=True, stop=True)
                            return ins
                        S.op("pe", f, reads=[KdTm, Vt], writes=[psU])
                        for pc in range(2):
                            STT(Sst[:, pc, :], Sst[:, pc, :], dseg[:, pc, b:b + 1], psU[:, pc * 64:(pc + 1) * 64], ALU.mult, ALU.add,
                                [Sst, dseg, psU], [Sst])
                        S.dma("sp", hg_s_d[l, b].rearrange("(pc hb) k v -> (hb k) pc v", hb=2), Sst[:], reads=[Sst])
                if DBG.get('hg_stage', 9) < 4:
                    S.barrier()
                    return
                for g, (c0g, gsz, segs) in enumerate(groups):
                    for h in range(4):
                        pc, hb, base = h // 2, h % 2, (h % 2) * 64

                        def f(e, g=g, h=h, pc=pc, hb=hb, base=base, c0g=c0g, gsz=gsz, segs=segs):
                            ins = e.matmul(PS[h][base:base + 64, c0g:c0g + gsz], lhsT=Vt[:gsz, g, h * 64:(h + 1) * 64],
                                           rhs=attm[:gsz, g, hb, pc, :gsz], start=True, stop=False)
                            for si, seg in enumerate(segs):
                                ins = e.matmul(PS[h][base:base + 64, seg * C:(seg + 1) * C], lhsT=Sbf[base:base + 64, seg, pc, :],
                                               rhs=Qs[base:base + 64, pc, seg * C:(seg + 1) * C], start=False, stop=(si == len(segs) - 1))
                            return ins
                        S.op("pe", f, reads=[Vt, attm, Sbf, Qs], writes=[PS[h]])
                for pc in range(2):
                    for hb in range(2):
                        h, base = 2 * pc + hb, hb * 64
                        A("act", osb[base:base + 64, pc, :ncols], PS[h][base:base + 64, :ncols], AF.Copy, [PS[h]], [osb])
                        A("act", o2[base:base + 64, pc, :ncols], PS[h][base:base + 64, :ncols], AF.Square, [PS[h]], [o2])
                for pc in range(2):
                    MM(PS[4 + pc][:, :ncols], cstb[:, C_BLK:C_BLK + 128], o2[:, pc, :ncols], [cstb, o2], [PS[4 + pc]])
                    A("act", t1[:, pc, :ncols], PS[4 + pc][:, :ncols], AF.Sqrt, [PS[4 + pc], epsb], [t1], scale=1.0 / 64, bias=epsb[:, 0:1])
                    S.op("dve", lambda e, pc=pc: e.reciprocal(out=t1[:, pc, :ncols], in_=t1[:, pc, :ncols]), reads=[t1], writes=[t1])
                    STT(t2[:, pc, :ncols], osb[:, pc, :ncols], vecs[:, VL[f"hgn_{l}"]:VL[f"hgn_{l}"] + 1], t1[:, pc, :ncols], ALU.mult, ALU.mult,
                        [osb, vecs, t1], [t2])
                    TTo(o_all[:, pc, :ncols], t2[:, pc, :ncols], gate[:, pc, :ncols], ALU.mult, [t2, gate], [o_all])
                S.barrier()


        H_rw = [S.sb([128, 2, 64], F32, f"H_rw{l}") for l in range(DEPTH)]
        rw_prev = [S.sb([128, 8], F32, f"rwprev{l}") for l in range(DEPTH)]
        lora = [S.sb([128, 256], BF16, f"lora{l}") for l in range(DEPTH)]
        omka = S.sb([128, 4], F32, "omka")
        H_rw_v = [[TT(H_rw[l].t, f"H_rw{l}_{hb}") for hb in range(2)] for l in range(DEPTH)]
        for l in range(DEPTH):
            S.op("dve", lambda e, l=l: e.memset(H_rw[l][:], 0.0), writes=[H_rw[l], H_rw_v[l][0], H_rw_v[l][1]])
            S.op("dve", lambda e, l=l: e.memset(rw_prev[l][:], 0.0), writes=[rw_prev[l]])
            S.dma("pool", lora[l][:], lora_d[l], writes=[lora[l]], persistent=True)
            TS(omka[:, 2 * l:2 * l + 2], vecs[:, VL[f"ka_{l}"]:VL[f"ka_{l}"] + 2], -1.0, 1.0, ALU.mult, ALU.add, [vecs], [omka])

        def rwkv(l, bi, ncols, xn, o_all):
            samp = (ncols == NS)
            C = 8 if samp else 64
            nseg = ncols // C
            groups = [(0, 32, [0, 1, 2, 3])] if samp else [(g * 128, 128, [2 * g, 2 * g + 1]) for g in range(4)]
            segm = cst[:, C_SEGS:C_SEGS + 32] if samp else cst[:, C_SEGP:C_SEGP + 512]
            m_incl, m_str, m_low = (C_MS, C_MSS, C_MSL) if samp else (C_MP, C_MPS, C_MPL)
            nlev = 2 if samp else 5
            V_ = lambda nm, pc=0: vecs[:, VL[f"{nm}_{l}"] + pc:VL[f"{nm}_{l}"] + pc + 1]
            with contextlib.ExitStack() as rs:
                rkv = S.sb([128, 6, TB], F32, "rw_rkv", rs)
                l6 = S.sb([128, TB], BF16, "rw_l6", rs)
                sh0 = S.sb([128, 7, 4], F32, "rw_sh0", rs)
                shs = S.sb([128, 7, 4], F32, "rw_shs", rs)
                pa_ = contextlib.ExitStack()
                pb = S.sb([128, TB + 1], F32, "rw_pb", pa_)
                prevb = S.sb([128, TB], F32, "rw_prevb", pa_)
                dtmp = S.sb([128, TB], F32, "rw_d", pa_)
                l6f = S.sb([128, TB], F32, "rw_l6f", pa_)
                if samp:
                    S.dma("sp", sh0[:], rwsh_d[l], writes=[sh0])
                for c in range(7):
                    if c % 2 == 0:
                        wb = wgroup(l, 1024 + 128 * c, 256 if c < 6 else 128)
                    pp = PS[c % 2]
                    proj_fm(wb, (c % 2) * 128, 128, pp, xn, ncols)
                    A("act", pb[:, 1:ncols + 1], pp[:, :ncols], AF.Copy, [pp], [pb])
                    dest = rkv[:, c, :ncols] if c < 6 else l6f[:, :ncols]
                    dT = rkv if c < 6 else l6f
                    if not samp:
                        S.op("dve", lambda e, c=c: e.tensor_copy(out=pb[:, 0:1], in_=rw_prev[l][:, c:c + 1]), reads=[rw_prev[l], pb], writes=[pb])
                        TTo(dtmp[:, :ncols], pb[:, 0:ncols], pb[:, 1:ncols + 1], ALU.subtract, [pb], [dtmp])
                        S.op("dve", lambda e, c=c: e.tensor_copy(out=rw_prev[l][:, c:c + 1], in_=pb[:, ncols:ncols + 1]), reads=[pb, rw_prev[l]], writes=[rw_prev[l]])
                    else:
                        S.op("dve", lambda e: e.tensor_copy(out=prevb[:, 1:ncols], in_=pb[:, 1:ncols]), reads=[pb], writes=[prevb])
                        S.op("dve", lambda e, c=c: e.tensor_copy(out=prevb[:, :ncols].rearrange("p (b t) -> p b t", t=8)[:, :, 0], in_=sh0[:, c, :]),
                             reads=[sh0, prevb], writes=[prevb])
                        TTo(dtmp[:, :ncols], prevb[:, :ncols], pb[:, 1:ncols + 1], ALU.subtract, [pb, prevb], [dtmp])
                        S.op("dve", lambda e, c=c: e.tensor_copy(out=shs[:, c, :], in_=pb[:, 1:ncols + 1].rearrange("p (b t) -> p b t", t=8)[:, :, 7]),
                             reads=[pb, shs], writes=[shs])
                    STT(dest, dtmp[:, :ncols], V_("mu", c), pb[:, 1:ncols + 1], ALU.mult, ALU.add, [dtmp, vecs, pb], [dT])
                if samp:
                    S.dma("sp", sh_s_d[l], shs[:], reads=[shs])
                elif bi == NPB - 1:
                    S.dma("sp", sh_p_d[l], rw_prev[l][:, 0:7], reads=[rw_prev[l]])
                A("act", l6[0:32, :ncols], l6f[0:32, :ncols], AF.Tanh, [l6f], [l6])
                A("act", l6[32:64, :ncols], l6f[32:64, :ncols], AF.Copy, [l6f], [l6])
                A("act", l6[64:128, :ncols], l6f[64:128, :ncols], AF.Sigmoid, [l6f], [l6])
                S.barrier()
                pa_.close()
                if DBG.get('rw_stage', 9) < 1:
                    S.barrier(); return
                for pc in range(2):
                    with contextlib.ExitStack() as bs:
                        f2 = lambda nm: S.sb([128, TB], F32, nm, bs)
                        b2 = lambda nm: S.sb([128, TB], BF16, nm, bs)
                        lw, aa, gg, al, be, km, cw, tA, tB_ = f2("rw_lw"), f2("rw_a"), f2("rw_g"), f2("rw_al"), f2("rw_be"), f2("rw_km"), f2("rw_cw"), f2("rw_tA"), f2("rw_tB")
                        At, Bt, Kt, Rt = b2("rw_At"), b2("rw_Bt"), b2("rw_Kt"), b2("rw_Rt")
                        vb = b2("rw_vb")
                        tokT = S.sb([128, 4, 4, 128], BF16, "rw_tokT", bs)
                        tokM = S.sb([32, 4, 2, 128], BF16, "rw_tokM", bs) if samp else None
                        pCt = S.sb([128, 8], F32, "rw_pC", bs)
                        ysb = f2("rw_y")
                        Hsm = [S.sb([128, 64], F32, f"rw_Hs{b}", bs) for b in range(4)] if samp else None
                        r_, k_, v_ = rkv[:, pc, :ncols], rkv[:, 2 + pc, :ncols], rkv[:, 4 + pc, :ncols]
                        n = ncols
                        MM(PS[2][:, :n], lora[l][0:32, pc * 128:(pc + 1) * 128], l6[0:32, :n], [lora[l], l6], [PS[2]])
                        MM(PS[3][:, :n], lora[l][32:64, pc * 128:(pc + 1) * 128], l6[32:64, :n], [lora[l], l6], [PS[3]])
                        MM(PS[4][:, :n], lora[l][64:128, pc * 128:(pc + 1) * 128], l6[64:128, :n], [lora[l], l6], [PS[4]])
                        A("act", lw[:, :n], PS[2][:, :n], AF.Sigmoid, [PS[2], vecs], [lw], bias=V_("w0", pc))
                        TS(lw[:, :n], lw[:, :n], -float(np.exp(-0.5)), None, ALU.mult, None, [lw], [lw])
                        A("act", aa[:, :n], PS[3][:, :n], AF.Sigmoid, [PS[3], vecs], [aa], bias=V_("a0", pc))
                        A("act", gg[:, :n], PS[4][:, :n], AF.Copy, [PS[4]], [gg])
                        TS(al[:, :n], k_, V_("kk", pc), None, ALU.mult, None, [rkv, vecs], [al])
                        A("act", tA[:, :n], al[:, :n], AF.Square, [al], [tA])
                        MM(PS[5][:, :n], cst[:, C_BLK:C_BLK + 128], tA[:, :n], [cst, tA], [PS[5]])
                        A("act", tA[:, :n], PS[5][:, :n], AF.Sqrt, [PS[5]], [tA])
                        TS(tA[:, :n], tA[:, :n], 1e-12, None, ALU.max, None, [tA], [tA])
                        S.op("dve", lambda e: e.reciprocal(out=tA[:, :n], in_=tA[:, :n]), reads=[tA], writes=[tA])
                        TTo(al[:, :n], al[:, :n], tA[:, :n], ALU.mult, [al, tA], [al])
                        TS(tA[:, :n], aa[:, :n], V_("ka", pc), omka[:, 2 * l + pc:2 * l + pc + 1], ALU.mult, ALU.add, [aa, vecs, omka], [tA])
                        TTo(km[:, :n], k_, tA[:, :n], ALU.mult, [rkv, tA], [km])
                        STT(be[:, :n], al[:, :n], -1.0, aa[:, :n], ALU.mult, ALU.mult, [al, aa], [be])
                        STT(tA[:, :n], r_, V_("rk", pc), km[:, :n], ALU.mult, ALU.mult, [rkv, vecs, km], [tA])
                        MM(PS[6][:, :n], cst[:, C_BLK:C_BLK + 128], tA[:, :n], [cst, tA], [PS[6]])
                        TTo(tB_[:, :n], PS[6][:, :n], v_, ALU.mult, [PS[6], rkv], [tB_])
                        S.op("dve", lambda e: e.tensor_tensor_scan(out=cw[:, :n], data0=segm[:, :n], data1=lw[:, :n], initial=0.0,
                                                                   op0=ALU.mult, op1=ALU.add), reads=[lw, cst], writes=[cw])
                        TTo(tA[:, :n], cw[:, :n], lw[:, :n], ALU.subtract, [cw, lw], [tA])
                        A("act", tA[:, :n], tA[:, :n], AF.Exp, [tA], [tA])
                        TTo(At[:, :n], al[:, :n], tA[:, :n], ALU.mult, [al, tA], [At])
                        A("act", tA[:, :n], cw[:, :n], AF.Exp, [cw], [tA], scale=-1.0)
                        TTo(Bt[:, :n], be[:, :n], tA[:, :n], ALU.mult, [be, tA], [Bt])
                        TTo(Kt[:, :n], km[:, :n], tA[:, :n], ALU.mult, [km, tA], [Kt])
                        A("act", tA[:, :n], cw[:, :n], AF.Exp, [cw], [tA])
                        TTo(Rt[:, :n], r_, tA[:, :n], ALU.mult, [rkv, tA], [Rt])
                        A("act", pCt[:, :nseg], cw[:, :n].rearrange("p (s c) -> p s c", c=C)[:, :, C - 1], AF.Exp, [cw], [pCt])
                        A("act", vb[:, :n], v_, AF.Copy, [rkv], [vb])
                        if DBG.get('rw_stage', 9) < 2:
                            S.barrier(); continue
                        for g, (c0g, gsz, segs) in enumerate(groups):
                            for wi_, src in enumerate((At, Bt, Kt, vb)):
                                S.op("pe", lambda e, wi_=wi_, src=src, c0g=c0g, gsz=gsz: e.transpose(
                                    PS[7][:, :].bitcast(BF16)[:gsz, wi_ * 128:(wi_ + 1) * 128], src[:, c0g:c0g + gsz], cstb[:, C_ID:C_ID + 128]),
                                    reads=[src, cstb], writes=[PS[7]])
                            S.op("dve", lambda e, g=g, gsz=gsz: e.tensor_copy(out=tokT[:gsz, g, :, :].rearrange("p a b -> p (a b)"),
                                                                             in_=PS[7][:, :].bitcast(BF16)[:gsz, 0:512]), reads=[PS[7]], writes=[tokT])
                        if samp:
                            for b in range(4):
                                TS(tokM[:, b, :, :].rearrange("p a b -> p (a b)"), tokT[:32, 0, 1:3, :].rearrange("p a b -> p (a b)"),
                                   cst[:32, C_ROWS + b:C_ROWS + b + 1], None, ALU.mult, None, [tokT, cst], [tokM])
                        if DBG.get('rw_stage', 9) < 3:
                            S.barrier(); continue
                        Tl = []
                        for hb in range(2):
                            sq_ = lambda nm: S.sb([128, 128], BF16, f"{nm}{hb}", bs)
                            Tl.append(dict(Nn=sq_("rw_N"), Aa=sq_("rw_A"), IA=sq_("rw_IA"), Pp=sq_("rw_P"), AakT=sq_("rw_AakT"), ArbT=sq_("rw_ArbT"),
                                           ArkT=sq_("rw_ArkT"), WT=sq_("rw_WT"), X0=S.sb([128, 64], BF16, f"rw_X0{hb}", bs),
                                           Ut=S.sb([128, 64], F32, f"rw_Ut{hb}", bs), Usb=S.sb([128, 64], BF16, f"rw_Usb{hb}", bs),
                                           Uf=S.sb([128, 64], F32, f"rw_Uf{hb}", bs), Hc=S.sb([128, 8, 64], BF16, f"rw_Hc{hb}", bs),
                                           Hp=S.sb([128, 64], F32, f"rw_Hp{hb}", bs)))
                            if samp:
                                for b in range(4):
                                    S.dma("sp", Hsm[b][hb * 64:hb * 64 + 64, :], rwst_d[l, b, 2 * pc + hb], writes=[Hsm[b]])

                        def solve(hb, g, c0g, gsz, segs):
                            h, base = 2 * pc + hb, hb * 64
                            bk = PS[0:4] if hb == 0 else PS[4:8]
                            T_ = Tl[hb]
                            Nn, Aa, IA, Pp, AakT, ArbT, ArkT, WT = (T_["Nn"], T_["Aa"], T_["IA"], T_["Pp"], T_["AakT"], T_["ArbT"], T_["ArkT"], T_["WT"])
                            X0, Ut, Usb, Uf, Hc_, Hp_ = T_["X0"], T_["Ut"], T_["Usb"], T_["Uf"], T_["Hc"], T_["Hp"]
                            gsl = slice(c0g, c0g + gsz)
                            fm = lambda t: t[base:base + 64, gsl]
                            mk = lambda col: cst[:gsz, col:col + gsz]
                            MM(bk[0][:gsz, :gsz], fm(Bt), fm(At), [Bt, At], [bk[0]])
                            TTo(Nn[:gsz, :gsz], bk[0][:gsz, :gsz], mk(m_str), ALU.mult, [bk[0], cst], [Nn])
                            MM(bk[1][:gsz, :gsz], fm(At), fm(Bt), [Bt, At], [bk[1]])
                            TTo(Aa[:gsz, :gsz], bk[1][:gsz, :gsz], mk(m_low), ALU.mult, [bk[1], cst], [Aa])
                            MM(bk[2][:gsz, :gsz], fm(Kt), fm(At), [Kt, At], [bk[2]])
                            TTo(AakT[:gsz, :gsz], bk[2][:gsz, :gsz], mk(m_str), ALU.mult, [bk[2], cst], [AakT])
                            MM(bk[3][:gsz, :gsz], fm(Bt), fm(Rt), [Bt, Rt], [bk[3]])
                            TTo(ArbT[:gsz, :gsz], bk[3][:gsz, :gsz], mk(m_incl), ALU.mult, [bk[3], cst], [ArbT])
                            MM(bk[0][:gsz, :gsz], fm(Kt), fm(Rt), [Kt, Rt], [bk[0]])
                            TTo(ArkT[:gsz, :gsz], bk[0][:gsz, :gsz], mk(m_incl), ALU.mult, [bk[0], cst], [ArkT])
                            TTo(Pp[:gsz, :gsz], Nn[:gsz, :gsz], mk(C_ID), ALU.add, [Nn, cst], [Pp])
                            for j in range(1, nlev + 1):
                                MM(bk[1][:gsz, :gsz], Nn[:gsz, :gsz], Aa[:gsz, :gsz], [Nn, Aa], [bk[1]])
                                if j < nlev:
                                    MM(bk[2][:gsz, :gsz], Aa[:gsz, :gsz], Nn[:gsz, :gsz], [Nn, Aa], [bk[2]])
                                TTo(IA[:gsz, :gsz], bk[1][:gsz, :gsz], mk(C_ID), ALU.add, [bk[1], cst], [IA])
                                if j < nlev:
                                    S.op("dve", lambda e, gsz=gsz: e.tensor_copy(out=Aa[:gsz, :gsz], in_=bk[1][:gsz, :gsz]), reads=[bk[1]], writes=[Aa])
                                    A("act", Nn[:gsz, :gsz], bk[2][:gsz, :gsz], AF.Copy, [bk[2]], [Nn])
                                MM(bk[3][:gsz, :gsz], IA[:gsz, :gsz], Pp[:gsz, :gsz], [IA, Pp], [bk[3]])
                                A("act", Pp[:gsz, :gsz], bk[3][:gsz, :gsz], AF.Copy, [bk[3]], [Pp])
                            Vtok = tokT[:gsz, g, 3, base:base + 64]
                            MM(bk[0][:gsz, 0:64], AakT[:gsz, :gsz], Vtok, [AakT, tokT], [bk[0]])
                            A("act", X0[:gsz, :], bk[0][:gsz, 0:64], AF.Copy, [bk[0]], [X0])
                            MM(bk[1][:gsz, 0:64], Pp[:gsz, :gsz], X0[:gsz, :], [Pp, X0], [bk[1]])
                            A("act", Ut[:gsz, :], bk[1][:gsz, 0:64], AF.Copy, [bk[1]], [Ut])
                            MM(bk[2][base:base + 64, :gsz], tokT[:gsz, g, 0, base:base + 64], Pp[:gsz, :gsz], [tokT, Pp], [bk[2]])
                            A("act", WT[base:base + 64, :gsz], bk[2][base:base + 64, :gsz], AF.Copy, [bk[2]], [WT])
                            if not samp:
                                Hst = H_rw_v[l][hb]
                                for si, seg in enumerate(segs):
                                    r0 = si * 64
                                    pu = bk[si % 2]
                                    ph = bk[2 + si % 2]
                                    A("act", Hc_[base:base + 64, seg, :], Hst[base:base + 64, pc, :], AF.Copy, [Hst], [Hc_])
                                    A("act", Hp_[base:base + 64, :], Hst[base:base + 64, pc, :], AF.Identity, [Hst, pCt], [Hp_], scale=pCt[base:base + 64, seg:seg + 1])
                                    MM(pu[r0:r0 + 64, 0:64], WT[base:base + 64, r0:r0 + 64], Hc_[base:base + 64, seg, :], [WT, Hc_], [pu])
                                    TTo(Uf[r0:r0 + 64, :], pu[r0:r0 + 64, 0:64], Ut[r0:r0 + 64, :], ALU.add, [pu, Ut], [Uf])
                                    A("act", Usb[r0:r0 + 64, :], Uf[r0:r0 + 64, :], AF.Copy, [Uf], [Usb])

                                    def fH(e, r0=r0, g=g, ph=ph):
                                        e.matmul(ph[base:base + 64, 0:64], lhsT=tokT[r0:r0 + 64, g, 2, base:base + 64], rhs=tokT[r0:r0 + 64, g, 3, base:base + 64],
                                                 start=True, stop=False)
                                        return e.matmul(ph[base:base + 64, 0:64], lhsT=tokT[r0:r0 + 64, g, 1, base:base + 64], rhs=Usb[r0:r0 + 64, :],
                                                        start=False, stop=True)
                                    S.op("pe", fH, reads=[tokT, Usb], writes=[ph])
                                    STT(Hst[base:base + 64, pc, :], ph[base:base + 64, 0:64], pCt[base:base + 64, seg:seg + 1], Hp_[base:base + 64, :],
                                        ALU.mult, ALU.add, [ph, pCt, Hp_, Hst], [Hst])
                            else:
                                for b in range(4):
                                    A("act", Hc_[base:base + 64, b, :], Hsm[b][base:base + 64, :], AF.Copy, [Hsm[b]], [Hc_])

                                def fU(e):
                                    ins = None
                                    for b in range(4):
                                        ins = e.matmul(bk[0][:32, b * 64:(b + 1) * 64], lhsT=WT[base:base + 64, 0:32], rhs=Hc_[base:base + 64, b, :], start=True, stop=True)
                                    return ins
                                S.op("pe", fU, reads=[WT, Hc_], writes=[bk[0]])
                                S.op("dve", lambda e: e.tensor_copy(out=Uf[:32, :], in_=Ut[:32, :]), reads=[Ut], writes=[Uf])
                                for b in range(4):
                                    STT(Uf[:32, :], bk[0][:32, b * 64:(b + 1) * 64], cst[:32, C_ROWS + b:C_ROWS + b + 1], Uf[:32, :], ALU.mult, ALU.add,
                                        [bk[0], cst, Uf], [Uf])
                                A("act", Usb[:32, :], Uf[:32, :], AF.Copy, [Uf], [Usb])
                                for b in range(4):
                                    ph = bk[2 + b % 2]

                                    def fH(e, b=b, ph=ph):
                                        e.matmul(ph[base:base + 64, 0:64], lhsT=tokM[:32, b, 1, base:base + 64], rhs=tokT[:32, 0, 3, base:base + 64], start=True, stop=False)
                                        return e.matmul(ph[base:base + 64, 0:64], lhsT=tokM[:32, b, 0, base:base + 64], rhs=Usb[:32, :], start=False, stop=True)
                                    S.op("pe", fH, reads=[tokM, tokT, Usb], writes=[ph])
                                    TTo(Hp_[base:base + 64, :], ph[base:base + 64, 0:64], Hsm[b][base:base + 64, :], ALU.add, [ph, Hsm[b]], [Hp_])
                                    TS(Hsm[b][base:base + 64, :], Hp_[base:base + 64, :], pCt[base:base + 64, b:b + 1], None, ALU.mult, None, [Hp_, pCt], [Hsm[b]])
                                    S.dma("sp", rw_s_d[l, b, h], Hsm[b][base:base + 64, :], reads=[Hsm[b]])
                            py = bk[1]

                            def fY(e, g=g, gsz=gsz, segs=segs, py=py):
                                e.matmul(py[base:base + 64, :gsz], lhsT=tokT[:gsz, g, 3, base:base + 64], rhs=ArkT[:gsz, :gsz], start=True, stop=False)
                                ins = e.matmul(py[base:base + 64, :gsz], lhsT=Usb[:gsz, :], rhs=ArbT[:gsz, :gsz], start=False, stop=False)
                                for si, seg in enumerate(segs):
                                    ins = e.matmul(py[base:base + 64, si * C:(si + 1) * C], lhsT=Hc_[base:base + 64, seg, :],
                                                   rhs=Rt[base:base + 64, c0g + si * C:c0g + (si + 1) * C], start=False, stop=(si == len(segs) - 1))
                                return ins
                            S.op("pe", fY, reads=[tokT, ArkT, Usb, ArbT, Hc_, Rt], writes=[py])
                            A("act", ysb[base:base + 64, gsl], py[base:base + 64, :gsz], AF.Copy, [py], [ysb])
                        for g, (c0g, gsz, segs) in enumerate(groups):
                            progs = []
                            for hb in range(2):
                                S.rec = []
                                solve(hb, g, c0g, gsz, segs)
                                progs.append(S.rec)
                                S.rec = None
                            for i_ in range(max(len(p_) for p_ in progs)):
                                for p_ in progs:
                                    if i_ < len(p_):
                                        p_[i_]()
                        if DBG.get('rw_stage', 9) < 8:
                            S.barrier(); continue
                        if DBG.get('rw_post', 99) > 0:
                            MM(PS[0][:, :n], cst[:, C_BLK:C_BLK + 128], ysb[:, :n], [cst, ysb], [PS[0]])
                        if DBG.get('rw_post', 99) > 1:
                            A("act", tA[:, :n], ysb[:, :n], AF.Square, [ysb], [tA])
                        if DBG.get('rw_post', 99) > 2:
                            MM(PS[1][:, :n], cst[:, C_BLK:C_BLK + 128], tA[:, :n], [cst, tA], [PS[1]])
                        if DBG.get('rw_post', 99) > 3:
                            TS(cw[:, :n], PS[0][:, :n], 1.0 / 64, None, ALU.mult, None, [PS[0]], [cw])
                        if DBG.get('rw_post', 99) > 4:
                            TTo(tA[:, :n], cw[:, :n], cw[:, :n], ALU.mult, [cw], [tA])
                        if DBG.get('rw_post', 99) > 5:
                            STT(tA[:, :n], PS[1][:, :n], 1.0 / 64, tA[:, :n], ALU.mult, ALU.subtract, [PS[1], tA], [tA])
                        if DBG.get('rw_post', 99) > 6:
                            TS(tA[:, :n], tA[:, :n], 0.0, 64e-5, ALU.max, ALU.add, [tA], [tA])
                        if DBG.get('rw_post', 99) > 7:
                            A("act", tA[:, :n], tA[:, :n], AF.Sqrt, [tA], [tA])
                        if DBG.get('rw_post', 99) > 8:
                            S.op("dve", lambda e: e.reciprocal(out=tA[:, :n], in_=tA[:, :n]), reads=[tA], writes=[tA])
                        if DBG.get('rw_post', 99) > 9:
                            TTo(ysb[:, :n], ysb[:, :n], cw[:, :n], ALU.subtract, [ysb, cw], [ysb])
                        if DBG.get('rw_post', 99) > 10:
                            if DBG.get('exp1'):
                                TTo(ysb[:, :n], ysb[:, :n], cw[:, :n], ALU.mult, [ysb, cw], [ysb])
                            else:
                                TTo(ysb[:, :n], ysb[:, :n], tA[:, :n], ALU.mult, [ysb, tA], [ysb])
                        if DBG.get('rw_post', 99) > 11:
                            TS(ysb[:, :n], ysb[:, :n], V_("lnw", pc), V_("lnb", pc), ALU.mult, ALU.add, [ysb, vecs], [ysb])
                        if DBG.get('rw_post', 99) > 12:
                            TTo(ysb[:, :n], ysb[:, :n], tB_[:, :n], ALU.add, [ysb, tB_], [ysb])
                        if DBG.get('rw_post', 99) > 13:
                            TTo(o_all[:, 2 + pc, :n], ysb[:, :n], gg[:, :n], ALU.mult, [ysb, gg], [o_all])
                        S.barrier()
                if (not samp) and bi == NPB - 1:
                    S.dma("sp", rw_p_d[l].rearrange("(pc hb) k v -> (hb k) pc v", hb=2), H_rw[l][:], reads=[H_rw[l], H_rw_v[l][0], H_rw_v[l][1]])
                S.barrier()


        MLA_SCALE = float((64 + 32) ** -0.5)
        knope_h, v_h, kpe_h = [], [], []
        kv_es = contextlib.ExitStack()

        def alloc_kv():
            for l in range(DEPTH):
                knope_h.append(S.sb([128, 4, SEQ], BF16, f"knope{l}", kv_es))
                v_h.append(S.sb([128, 16, 8, 65], BF16, f"vh{l}", kv_es))
                kpe_h.append(S.sb([128, SEQ], BF16, f"kpeh{l}", kv_es))
                S.op("dve", lambda e, l=l: e.memset(v_h[l][:, :, :, 64:65], 1.0), writes=[v_h[l]])

        def norm_fm(src, nch, ncols, gname, l, dst_f, dst_b, sqm, rs):
            for c in range(nch):
                A("act", sqm[:, c, :ncols], src[:, c, :ncols], AF.Square, [src], [sqm])

            def f(e):
                ins = None
                for c in range(nch):
                    ins = e.matmul(PS[7][:, :ncols], lhsT=ones_bf[:], rhs=sqm[:, c, :ncols], start=(c == 0), stop=(c == nch - 1))
                return ins
            S.op("pe", f, reads=[sqm, ones_bf], writes=[PS[7]])
            A("act", rs[:, :ncols], PS[7][:, :ncols], AF.Sqrt, [PS[7], epsb], [rs], scale=1.0 / (nch * 128), bias=epsb[:, 0:1])
            S.op("dve", lambda e: e.reciprocal(out=rs[:, :ncols], in_=rs[:, :ncols]), reads=[rs], writes=[rs])
            for c in range(nch):
                g_ = vecs[:, VL[f"{gname}_{l}"] + c:VL[f"{gname}_{l}"] + c + 1]
                if dst_f is not None:
                    STT(dst_f[:, c, :ncols], src[:, c, :ncols], g_, rs[:, :ncols], ALU.mult, ALU.mult, [src, vecs, rs], [dst_f])
                    if dst_b is not None:
                        A("act", dst_b[:, c, :ncols], dst_f[:, c, :ncols], AF.Copy, [dst_f], [dst_b])
                else:
                    STT(dst_b[:, c, :ncols], src[:, c, :ncols], g_, rs[:, :ncols], ALU.mult, ALU.mult, [src, vecs, rs], [dst_b])

        def mla(l, bi, ncols, c0, xn, o_all):
            samp = (ncols == NS)
            n = ncols
            with contextlib.ExitStack() as ms_:
                wqb = S.sb([128, 3, 1280], BF16, "mla_wqb", ms_)
                S.dma("pool", wqb[:], wqb_d[l].rearrange("(kc p) n -> p kc n", p=128), writes=[wqb])
                wuk = S.sb([128, 2, 512], BF16, "mla_wuk", ms_)
                wuv = S.sb([128, 2, 512], BF16, "mla_wuv", ms_)
                S.dma("pool", wuk[:], wuk_d[l].rearrange("(kc p) n -> p kc n", p=128), writes=[wuk])
                S.dma("pool", wuv[:], wuv_d[l].rearrange("(kc p) n -> p kc n", p=128), writes=[wuv])
                ropt = S.sb([128, 2, TB], F32, "mla_rope", ms_)
                rc0 = SEQ if samp else c0
                S.dma("sp", ropt[:, :, :n], rope_d[:, :, rc0:rc0 + n], writes=[ropt])
                qn = S.sb([128, 3, TB], BF16, "mla_qn", ms_)
                cb = S.sb([128, 2, TB], BF16, "mla_cb", ms_)
                qnp = S.sb([128, 4, TB], BF16, "mla_qnp", ms_)
                qpe = S.sb([128, 8, TB], BF16, "mla_qpe", ms_)
                rt1 = S.sb([128, TB], F32, "mla_rt1", ms_)
                rt2 = S.sb([128, TB], F32, "mla_rt2", ms_)
                kpef = S.sb([32, TB], F32, "mla_kpef", ms_)
                with contextlib.ExitStack() as ps_:
                    qa_f = S.sb([128, 3, TB], F32, "mla_qaf", ps_)
                    kv_f = S.sb([128, 2, TB], F32, "mla_kvf", ps_)
                    c_f = kv_f
                    sqm = qnp
                    rs = S.sb([128, TB], F32, "mla_rs", ps_)
                    wb = wgroup(l, 1920, 256)
                    for c in range(3):
                        if c == 2:
                            wb = wgroup(l, 2176, 128)
                        proj_fm(wb, (c % 2) * 128, 128, PS[c % 2], xn, n)
                        A("act", qa_f[:, c, :n], PS[c % 2][:, :n], AF.Copy, [PS[c % 2]], [qa_f])
                    wb = wgroup(l, 2304, 256)
                    for c in range(2):
                        proj_fm(wb, c * 128, 128, PS[2 + c], xn, n)
                        A("act", kv_f[:, c, :n], PS[2 + c][:, :n], AF.Copy, [PS[2 + c]], [kv_f])
                    norm_fm(qa_f, 3, n, "qn", l, None, qn, sqm, rs)
                    norm_fm(kv_f, 2, n, "kvn", l, c_f, cb, sqm, rs)
                    cdst = ckv_s_d[l] if samp else ckv_p_d[l][:, c0:c0 + n]
                    S.dma("sp", cdst.rearrange("(c p) t -> p c t", p=128), c_f[:, :, :n], reads=[c_f])
                    wb = wgroup(l, 2560, 96)
                    proj_fm(wb, 0, 64, PS[4], xn, n)
                    proj_fm(wb, 32, 64, PS[5], xn, n)
                    TTo(rt1[0:32, :n], PS[4][0:32, :n], ropt[0:32, 0, :n], ALU.mult, [PS[4], ropt], [rt1])
                    TTo(rt2[0:32, :n], PS[5][0:32, :n], ropt[0:32, 1, :n], ALU.mult, [PS[5], ropt], [rt2])
                    TTo(kpef[:, :n], rt1[0:32, :n], rt2[0:32, :n], ALU.add, [rt1, rt2], [kpef])
                    if DBG.get("kpe_dbg") == 1:
                        S.op("dve", lambda e: e.tensor_copy(out=kpef[:, :n], in_=PS[4][0:32, :n]), reads=[PS[4]], writes=[kpef])
                    if DBG.get("kpe_dbg") == 2:
                        S.op("dve", lambda e: e.tensor_copy(out=kpef[:, :n], in_=ropt[0:32, 0, :n]), reads=[ropt], writes=[kpef])
                    kdst = kpe_s_d[l] if samp else kpe_p_d[l][:, c0:c0 + n]
                    S.dma("sp", kdst, kpef[:, :n], reads=[kpef])
                    S.barrier()
                for j in range(4):
                    pp = PS[j % 2]

                    def f(e, j=j, pp=pp):
                        ins = None
                        for kc in range(3):
                            ins = e.matmul(pp[:, :n], lhsT=wqb[:, kc, j * 128:(j + 1) * 128], rhs=qn[:, kc, :n], start=(kc == 0), stop=(kc == 2))
                        return ins
                    S.op("pe", f, reads=[wqb, qn], writes=[pp])
                    A("act", qnp[:, j, :n], pp[:, :n], AF.Copy, [pp], [qnp], scale=MLA_SCALE)
                for h in range(8):
                    pb_ = 0 if samp else (h % 2) * 64
                    pa, pbk = PS[2 + (h % 2) * 2], PS[3 + (h % 2) * 2]

                    def f(e, h=h, pb_=pb_, pa=pa, pbk=pbk):
                        ins = None
                        for which, pt in ((0, pa), (1, pbk)):
                            for kc in range(3):
                                col = 512 + h * 96 + which * 32
                                ins = e.matmul(pt[pb_:pb_ + 64, :n], lhsT=wqb[:, kc, col:col + 64], rhs=qn[:, kc, :n], start=(kc == 0), stop=(kc == 2))
                        return ins
                    S.op("pe", f, reads=[wqb, qn], writes=[pa, pbk])
                    TTo(rt1[pb_:pb_ + 32, :n], pa[pb_:pb_ + 32, :n], ropt[pb_:pb_ + 32, 0, :n], ALU.mult, [pa, ropt], [rt1])
                    TTo(rt2[pb_:pb_ + 32, :n], pbk[pb_:pb_ + 32, :n], ropt[pb_:pb_ + 32, 1, :n], ALU.mult, [pbk, ropt], [rt2])
                    TTo(rt1[pb_:pb_ + 32, :n], rt1[pb_:pb_ + 32, :n], rt2[pb_:pb_ + 32, :n], ALU.add, [rt1, rt2], [rt1])
                    TS(qpe[pb_:pb_ + 32, h, :n], rt1[pb_:pb_ + 32, :n], MLA_SCALE, None, ALU.mult, None, [rt1], [qpe])
                if not samp:
                    mla_prompt_attn(l, bi, c0, wuk, wuv, cb, kpef, qnp, qpe, o_all, ms_)
                elif DBG.get("mla_s", 1):
                    mla_sample_attn(l, wuv, cb, kpef, qnp, qpe, o_all, ms_)
                S.barrier()

        def mla_sample_attn(l, wuv, cb, kpef, qnp, qpe, o_all, ms_):
            NT = 16
            NCH = 128 // NT
            wukT = S.sb([128, 4, 256], BF16, "mla_wukT", ms_)
            S.dma("pool", wukT[:], wukT_d[l], writes=[wukT])
            idx = S.sb([128, 4], I32, "mla_idx", ms_)
            S.dma("sp", idx[:], pt_d[:, :], writes=[idx])
            idxa = S.sb([128, 4, NCH], I32, "mla_idxa", ms_)
            for a in range(NCH):
                TS(idxa[:, :, a], idx[:, :], float(NCH), float(a), ALU.mult, ALU.add, [idx], [idxa])
            qlat = S.sb([128, 2, 8, NS], BF16, "mla_qlat", ms_)
            for hb in range(2):
                base = hb * 64
                pq = PS[hb]

                def f(e, hb=hb, base=base, pq=pq):
                    ins = None
                    for jp in range(4):
                        for rc in range(2):
                            ins = e.matmul(pq[:, (rc * 4 + jp) * 32:(rc * 4 + jp + 1) * 32], lhsT=wukT[base:base + 64, jp, rc * 128:(rc + 1) * 128],
                                           rhs=qnp[base:base + 64, jp, 0:NS], start=True, stop=True)
                    return ins
                S.op("pe", f, reads=[wukT, qnp], writes=[pq])
                A("act", qlat[:, :, :, :].rearrange("p r (j two) t -> p r j two t", two=2)[:, :, :, hb, :],
                  pq[:, 0:256].rearrange("p (r j t) -> p r j t", r=2, j=4), AF.Copy, [pq], [qlat])
            if DBG.get("ms_stage", 9) < 1:
                return
            kpeb = S.sb([32, NS], BF16, "mla_kpeb", ms_)
            A("act", kpeb[:, :], kpef[:, :NS], AF.Copy, [kpef], [kpeb])
            cnew = S.sb([32, 256], BF16, "mla_cnew", ms_)
            for rc in range(2):
                S.op("pe", lambda e, rc=rc: e.transpose(PS[2][:, :].bitcast(BF16)[:NS, rc * 128:(rc + 1) * 128], cb[:, rc, 0:NS], cstb[:, C_ID:C_ID + 128]),
                     reads=[cb, cstb], writes=[PS[2]])
            A("act", cnew[:, :], PS[2][:, :].bitcast(BF16)[:NS, 0:256], AF.Copy, [PS[2]], [cnew])
            cbuf = [S.sb([128, NT * 256], BF16, f"mla_cbuf{i}", ms_) for i in range(2)]
            kbuf = S.sb([128, 128 * 32 + 32], BF16, "mla_kbuf", ms_)
            S.op("dve", lambda e: e.memset(kbuf[:, 4096:4128], 0.0), writes=[kbuf])
            G4 = 4
            cT = [S.sb([128, G4 * 256], BF16, f"mla_cT{i}", ms_) for i in range(2)]
            kT = [S.sb([128, G4 * 128], BF16, f"mla_kT{i}", ms_) for i in range(2)]
            qpep = S.sb([128, 8, NS], BF16, "mla_qpep", ms_)
            kpebp = S.sb([128, NS], BF16, "mla_kpebp", ms_)
            for t_ in (kT[0], kT[1], qpep, kpebp):
                S.op("dve", lambda e, t_=t_: e.memset(t_[:], 0.0), writes=[t_])
            A("act", qpep[0:32, :, :], qpe[0:32, :, 0:NS], AF.Copy, [qpe, qpep], [qpep])
            A("act", kpebp[0:32, :], kpef[:, :NS], AF.Copy, [kpef, kpebp], [kpebp])
            Pt = [S.sb([128, G4 * 64], BF16, f"mla_Pt{i}", ms_) for i in range(2)]
            Ptn = S.sb([32, 64], BF16, "mla_Ptn", ms_)
            olat = S.sb([64, 256], BF16, "mla_olat", ms_)
            olatT = S.sb([128, 2, 64], BF16, "mla_olatT", ms_)
            rcs = S.sb([64, 1], F32, "mla_rcs", ms_)
            ckv2 = ckvc_d[l].rearrange("n (a t) d -> (n a) (t d)", t=NT)
            kpe2 = kpec_d[l].rearrange("n t d -> n (t d)")
            pool_e = S.eng["pool"]

            def gather(dst, src2, idx_ap, R):
                E = pool_e
                S._wait(E, R, [dst])
                S.pool_epoch_wait()
                if dst.dchan is None:
                    if dst.name not in S.named:
                        S.named[dst.name] = S.new_chan("d_" + dst.name)
                        S.dchans.append(S.named[dst.name])
                    dst.dchan = S.named[dst.name]
                ch = dst.dchan
                ins = E.b.indirect_dma_start(out=dst[:, 0:src2.shape[1]], out_offset=None, in_=src2, in_offset=bass.IndirectOffsetOnAxis(ap=idx_ap, axis=0))
                ch.count += 16
                ins.then_inc(ch.sem, 16)
                dst.w = (ch, ch.count)
                dst.r = []
                for t in R:
                    t.r.append((ch, ch.count))
            pO, pS_ = PS[7], PS[6]
            pK = PS[1]
            it = 0
            for b in range(4):
                qsl = slice(8 * b, 8 * b + 8)
                gather(kbuf, kpe2, idx[:, b:b + 1], [idx])
                for a in range(NCH):
                    cbf = cbuf[a % 2]
                    gather(cbf, ckv2, idxa[:, b, a:a + 1], [idxa])
                    for t4 in range(NT // G4):
                        par = it % 2
                        it += 1
                        pT = PS[2 + par]
                        tl = [a * NT + t4 * G4 + i for i in range(G4)]
                        ct = [cbf[:, (t4 * G4 + i) * 256:(t4 * G4 + i + 1) * 256] for i in range(G4)]

                        def fT(e, pT=pT, ct=ct, tl=tl):
                            ins = None
                            for i in range(G4):
                                e.transpose(pT[:, :].bitcast(BF16)[:, i * 256:i * 256 + 128], ct[i][:, 0:128], cstb[:, C_ID:C_ID + 128])
                                e.transpose(pT[:, :].bitcast(BF16)[:, i * 256 + 128:i * 256 + 256], ct[i][:, 128:256], cstb[:, C_ID:C_ID + 128])
                            for i in range(G4):
                                ins = e.transpose(pK[:, :].bitcast(BF16)[:64, i * 128:(i + 1) * 128], kbuf[:, tl[i] * 32:tl[i] * 32 + 64], cstb[:, C_ID:C_ID + 128])
                            return ins
                        S.op("pe", fT, reads=[cbf, kbuf, cstb], writes=[pT, pK])
                        A("act", cT[par][:, :], pT[:, :].bitcast(BF16)[:, 0:G4 * 256], AF.Copy, [pT], [cT[par]])
                        A("act", kT[par][0:32, :], pK[:, :].bitcast(BF16)[:32, 0:G4 * 128], AF.Copy, [pK], [kT[par]])
                        pSc = PS[4 + par]

                        def fS(e, par=par, pSc=pSc):
                            ins = None
                            for i in range(G4):
                                o_ = pSc[:, i * 64:(i + 1) * 64]
                                e.matmul(o_, lhsT=cT[par][:, i * 256:i * 256 + 128], rhs=qlat[:, 0, :, qsl], start=True, stop=False)
                                e.matmul(o_, lhsT=cT[par][:, i * 256 + 128:i * 256 + 256], rhs=qlat[:, 1, :, qsl], start=False, stop=False)
                                ins = e.matmul(o_, lhsT=kT[par][:, i * 128:(i + 1) * 128], rhs=qpep[:, :, qsl], start=False, stop=True)
                            return ins
                        S.op("pe", fS, reads=[cT[par], kT[par], qlat, qpep], writes=[pSc])
                        A("act", Pt[par][:, :], pSc[:, 0:G4 * 64], AF.Exp, [pSc], [Pt[par]])

                        def fP(e, par=par, ct=ct, first=(a == 0 and t4 == 0)):
                            ins = None
                            for i in range(G4):
                                st = first and i == 0
                                e.matmul(pO[0:64, 0:256], lhsT=Pt[par][:, i * 64:(i + 1) * 64], rhs=ct[i], start=st, stop=False)
                                ins = e.matmul(pS_[0:64, 0:2], lhsT=Pt[par][:, i * 64:(i + 1) * 64], rhs=ones_bf[:, 0:2], start=st, stop=False)
                            return ins
                        S.op("pe", fP, reads=[Pt[par], cbf, ones_bf], writes=[pO, pS_])
                if DBG.get("ms_stage", 9) < 6:
                    continue
                pSc = PS[4]

                def fSn(e):
                    e.matmul(pSc[:NS, 0:64], lhsT=cb[:, 0, 0:NS], rhs=qlat[:, 0, :, qsl], start=True, stop=False)
                    e.matmul(pSc[:NS, 0:64], lhsT=cb[:, 1, 0:NS], rhs=qlat[:, 1, :, qsl], start=False, stop=False)
                    return e.matmul(pSc[:NS, 0:64], lhsT=kpebp[:, 0:NS], rhs=qpep[:, :, qsl], start=False, stop=True)
                S.op("pe", fSn, reads=[cb, kpebp, qlat, qpep], writes=[pSc])
                A("act", Ptn[:, :], pSc[:NS, 0:64], AF.Exp, [pSc], [Ptn])
                TTo(Ptn[:, :].rearrange("p (h t) -> p h t", h=8), Ptn[:, :].rearrange("p (h t) -> p h t", h=8),
                    cstb[:NS, C_MS + 8 * b:C_MS + 8 * b + 8].unsqueeze(1).to_broadcast([NS, 8, 8]), ALU.mult, [Ptn, cstb], [Ptn])

                def fPn(e):
                    e.matmul(pO[0:64, 0:256], lhsT=Ptn[:, :], rhs=cnew[:, :], start=False, stop=True)
                    return e.matmul(pS_[0:64, 0:2], lhsT=Ptn[:, :], rhs=ones_bf[:NS, 0:2], start=False, stop=True)
                S.op("pe", fPn, reads=[Ptn, cnew, ones_bf], writes=[pO, pS_])
                S.op("dve", lambda e: e.reciprocal(out=rcs[:, :], in_=pS_[0:64, 0:1]), reads=[pS_], writes=[rcs])
                TS(olat[:, :], pO[0:64, 0:256], rcs[:, 0:1], None, ALU.mult, None, [pO, rcs], [olat])
                for rc in range(2):
                    S.op("pe", lambda e, rc=rc: e.transpose(PS[2][:, :].bitcast(BF16)[:, rc * 64:(rc + 1) * 64], olat[:, rc * 128:(rc + 1) * 128], cstb[:64, C_ID:C_ID + 64]),
                         reads=[olat, cstb], writes=[PS[2]])
                A("act", olatT[:, :, :].rearrange("p a b -> p (a b)"), PS[2][:, :].bitcast(BF16)[:, 0:128], AF.Copy, [PS[2]], [olatT])

                def fO(e, b=b):
                    ins = None
                    for h in range(8):
                        jp, base = h // 2, (h % 2) * 64
                        for rc in range(2):
                            ins = e.matmul(PS[0][base:base + 64, jp * 32 + 8 * b:jp * 32 + 8 * b + 8], lhsT=wuv[:, rc, h * 64:(h + 1) * 64],
                                           rhs=olatT[:, rc, h * 8:(h + 1) * 8], start=(rc == 0), stop=(rc == 1))
                    return ins
                S.op("pe", fO, reads=[wuv, olatT], writes=[PS[0]])
            if DBG.get("ms_stage", 9) < 6:
                return
            A("act", o_all[:, 4:8, 0:NS], PS[0][:, 0:128].rearrange("p (j t) -> p j t", j=4), AF.Copy, [PS[0]], [o_all])

        def mla_prompt_attn(l, bi, c0, wuk, wuv, cb, kpef, qnp, qpe, o_all, ms_):
            n = TB
            for j in range(4):
                pp = PS[j % 2]

                def f(e, j=j, pp=pp):
                    ins = None
                    for rc in range(2):
                        ins = e.matmul(pp[:, :n], lhsT=wuk[:, rc, j * 128:(j + 1) * 128], rhs=cb[:, rc, :n], start=(rc == 0), stop=(rc == 1))
                    return ins
                S.op("pe", f, reads=[wuk, cb], writes=[pp])
                A("act", knope_h[l][:, j, c0:c0 + n], pp[:, :n], AF.Copy, [pp], [knope_h[l]])
            A("act", kpe_h[l][0:32, c0:c0 + n], kpef[:, :n], AF.Copy, [kpef], [kpe_h[l]])
            S.dma("sp", kpe_h[l][64:96, c0:c0 + n], kpe_h[l][0:32, c0:c0 + n], reads=[kpe_h[l]], writes=[kpe_h[l]])
            for g in range(4):
                pp = PS[2 + g % 2]

                def f(e, g=g, pp=pp):
                    ins = None
                    for rc in range(2):
                        ins = e.matmul(pp[:, :512], lhsT=cb[:, rc, g * 128:(g + 1) * 128], rhs=wuv[:, rc, :], start=(rc == 0), stop=(rc == 1))
                    return ins
                S.op("pe", f, reads=[wuv, cb], writes=[pp])
                A("act", v_h[l][:, bi * 4 + g, :, 0:64], pp[:, :512].rearrange("p (h d) -> p h d", h=8), AF.Copy, [pp], [v_h[l]])
            QR = 256
            PT = S.sb([128, 16, QR], BF16, "mla_PT", ms_)
            o_tok = S.sb([128, 4, 512], BF16, "mla_otok", ms_)
            rcp = S.sb([128, 1], F32, "mla_rcp", ms_)
            for h in range(8):
                j, base = h // 2, (h % 2) * 64
                pss = (PS[0], PS[1]) if h % 2 == 0 else (PS[2], PS[3])
                for qr in range(TB // QR):
                    q0 = qr * QR
                    kb0 = (c0 + q0) // 128
                    nkb = kb0 + QR // 128
                    for kb in range(nkb):
                        pst = pss[kb % 2]

                        def f(e, kb=kb, pst=pst):
                            e.matmul(pst[:, :QR], lhsT=knope_h[l][base:base + 64, j, kb * 128:(kb + 1) * 128], rhs=qnp[base:base + 64, j, q0:q0 + QR],
                                     start=True, stop=False)
                            return e.matmul(pst[:, :QR], lhsT=kpe_h[l][base:base + 32, kb * 128:(kb + 1) * 128], rhs=qpe[base:base + 32, h, q0:q0 + QR],
                                            start=False, stop=True)
                        S.op("pe", f, reads=[knope_h[l], kpe_h[l], qnp, qpe], writes=[pst])
                        A("act", PT[:, kb, :], pst[:, :QR], AF.Exp, [pst], [PT])
                        dj = kb - kb0
                        if dj >= 0:
                            TTo(PT[:, kb, dj * 128:(dj + 1) * 128], PT[:, kb, dj * 128:(dj + 1) * 128], cstb[:, C_TRI:C_TRI + 128], ALU.mult, [PT, cstb], [PT])
                    for qs in range(QR // 128):
                        po = PS[4 + qs % 2]
                        nk = kb0 + qs + 1

                        def f(e, qs=qs, po=po, nk=nk):
                            ins = None
                            for kb in range(nk):
                                ins = e.matmul(po[:, 0:65], lhsT=PT[:, kb, qs * 128:(qs + 1) * 128], rhs=v_h[l][:, kb, h, :], start=(kb == 0), stop=(kb == nk - 1))
                            return ins
                        S.op("pe", f, reads=[PT, v_h[l]], writes=[po])
                        S.op("dve", lambda e, po=po: e.reciprocal(out=rcp[:, :], in_=po[:, 64:65]), reads=[po], writes=[rcp])
                        TS(o_tok[:, qr * 2 + qs, h * 64:(h + 1) * 64], po[:, 0:64], rcp[:, 0:1], None, ALU.mult, None, [po, rcp], [o_tok])
            for qs in range(4):
                for m in range(4):
                    S.op("pe", lambda e, qs=qs, m=m: e.transpose(PS[6][:, :].bitcast(BF16)[:, m * 128:(m + 1) * 128], o_tok[:, qs, m * 128:(m + 1) * 128],
                                                               cstb[:, C_ID:C_ID + 128]), reads=[o_tok, cstb], writes=[PS[6]])
                for m in range(4):
                    A("act", o_all[:, 4 + m, qs * 128:(qs + 1) * 128], PS[6][:, :].bitcast(BF16)[:, m * 128:(m + 1) * 128], AF.Copy, [PS[6]], [o_all])

        def mixer(l, bi, ncols, c0):
            with contextlib.ExitStack() as ms:
                xn = S.sb([128, KC, TB], BF16, "mxn", ms)
                o_all = S.sb([128, KC, TB], BF16, "oall", ms)
                S.op("dve", lambda e: e.memset(o_all[:], 0.0), writes=[o_all])
                with contextlib.ExitStack() as ns:
                    sq = S.sb([128, KC, TB], BF16, "msq", ns)
                    rstd = S.sb([128, TB], F32, "mrstd", ns)
                    rmsnorm(x, ncols, VL[f"nmix_{l}"], xn, sq, rstd, PS[7])
                    S.barrier()
                if DBG.get('hg', 1):
                    hgrn(l, ncols, xn, o_all)
                if DBG.get('rw', 1):
                    rwkv(l, bi, ncols, xn, o_all)
                if DBG.get('mla', 1):
                    mla(l, bi, ncols, c0, xn, o_all)
                with contextlib.ExitStack() as ws:
                    wout = S.sb([128, KC, D], BF16, "wout", ws)
                    S.dma("pool", wout[:], wout_d[l].rearrange("(kc p) n -> p kc n", p=128), writes=[wout])
                    for fo in range(KC):
                        pp = PS[fo % 2]

                        def f(e, fo=fo, pp=pp):
                            ins = None
                            for c in range(KC):
                                ins = e.matmul(pp[:, :ncols], lhsT=wout[:, c, fo * 128:(fo + 1) * 128], rhs=o_all[:, c, :ncols],
                                               start=(c == 0), stop=(c == KC - 1))
                            return ins
                        S.op("pe", f, reads=[wout, o_all], writes=[pp])
                        TTo(x[:, fo, :ncols], pp[:, :ncols], x[:, fo, :ncols], ALU.add, [pp, x], [x])
                    if bi == NPB - 1:
                        S.dma("sp", hg_p_d[l].rearrange("(pc hb) k v -> (hb k) pc v", hb=2), S_hg[l][:], reads=[S_hg[l]])
                    S.barrier()

        blocks = [(xpT, yT_p, b * TB, TB) for b in range(NPB)] + [(xsT, yT_s, 0, NS)]
        alloc_kv()
        for bi, (src, dst, c0, ncols) in enumerate(blocks):
            if bi == NPB:
                S.barrier()
                kv_es.close()
            if bi not in DBG.get('blocks', range(10)):
                continue
            S.dma("sp", x[:, :, :ncols], src.rearrange("(kc p) t -> p kc t", p=128)[:, :, c0:c0 + ncols], writes=[x])
            for l in DBG.get('layers', range(DEPTH)):
                ffn(l, 0, ncols, 0 + l * 8)
                mixer(l, bi, ncols, c0)
                ffn(l, 1, ncols, 32 + l * 8)
            with contextlib.ExitStack() as fs:
                sq = S.sb([128, KC, TB], BF16, "sq", fs)
                rstd = S.sb([128, TB], F32, "rstd", fs)
                yo = S.sb([128, KC, TB], F32, "yo", fs)
                S.op("act", lambda e: e.activation(out=sq[:, :, :ncols], in_=x[:, :, :ncols], func=AF.Square), reads=[x], writes=[sq])

                def mm(e):
                    ins = None
                    for kc in range(KC):
                        ins = e.matmul(PS[7][:, :ncols], lhsT=ones_bf[:], rhs=sq[:, kc, :ncols], start=(kc == 0), stop=(kc == KC - 1))
                    return ins
                S.op("pe", mm, reads=[sq, ones_bf], writes=[PS[7]])
                S.op("act", lambda e: e.activation(out=rstd[:, :ncols], in_=PS[7][:, :ncols], func=AF.Sqrt, scale=1.0 / D, bias=epsb[:, 0:1]),
                     reads=[PS[7], epsb], writes=[rstd])
                S.op("dve", lambda e: e.reciprocal(out=rstd[:, :ncols], in_=rstd[:, :ncols]), reads=[rstd], writes=[rstd])
                for kc in range(KC):
                    S.op("dve", lambda e, kc=kc: e.scalar_tensor_tensor(out=yo[:, kc, :ncols], in0=x[:, kc, :ncols],
                                                                      scalar=vecs[:, VL["nfin"] + kc:VL["nfin"] + kc + 1], in1=rstd[:, :ncols],
                                                                      op0=ALU.mult, op1=ALU.mult),
                         reads=[x, vecs, rstd], writes=[yo])
                S.dma("sp", dst.rearrange("(kc p) t -> p kc t", p=128)[:, :, c0:c0 + ncols], yo[:, :, :ncols], reads=[yo])
                S.barrier()
        S.finish()
    return nc


def _fm(v):
    return np.ascontiguousarray(np.asarray(v, np.float32).reshape(-1, 128).T)


def kernel(**inp):
    ncores = DBG.get('ncores', 8)
    nc = build()
    f32 = np.float32
    vecs = np.zeros((128, NV), f32)

    def put(name, v):
        a = _fm(v)
        vecs[:, VL[name]:VL[name] + a.shape[1]] = a
    for l in range(DEPTH):
        put(f"nf1_{l}", inp["norm_ffn1"][l])
        put(f"nmix_{l}", inp["norm_mix"][l])
        put(f"nf2_{l}", inp["norm_ffn2"][l])
        put(f"hglog_{l}", inp["hg_lb_logits"][l])
        put(f"hgn_{l}", np.tile(np.asarray(inp["hg_norm"][l], f32), 2))
        put(f"mu_{l}", inp["rw_mu"][l])
        put(f"w0_{l}", inp["rw_w0"][l])
        put(f"a0_{l}", inp["rw_a0"][l])
        put(f"kk_{l}", inp["rw_kk"][l])
        put(f"ka_{l}", inp["rw_ka"][l])
        put(f"rk_{l}", np.asarray(inp["rw_rk"][l], f32).reshape(-1))
        put(f"lnw_{l}", inp["rw_ln_w"][l])
        put(f"lnb_{l}", inp["rw_ln_b"][l])
        put(f"qn_{l}", inp["mla_q_norm"][l])
        put(f"kvn_{l}", inp["mla_kv_norm"][l])
    put("nfin", inp["norm_final"])
    shared = {"vecs": vecs, "cst": make_consts()}
    shared["lora"] = np.ascontiguousarray(np.stack([np.concatenate([inp["rw_w2"][l], inp["rw_a2"][l], inp["rw_g2"][l]], axis=0)
                                                    for l in range(DEPTH)]), f32)
    wq = np.asarray(inp["mla_wqb"], f32).reshape(DEPTH, 384, 8, 96)
    sw = (np.arange(32) + 16) % 32
    wr = wq[..., 64:]
    shared["wqb"] = np.ascontiguousarray(np.concatenate([wq[..., :64].reshape(DEPTH, 384, 512),
                                                         np.concatenate([wr, wr[..., sw], wr], axis=-1).reshape(DEPTH, 384, 768)], axis=2))
    shared["wuk"] = np.ascontiguousarray(np.asarray(inp["mla_wuk"], f32).reshape(DEPTH, 256, 512))
    shared["wuv"] = np.ascontiguousarray(np.asarray(inp["mla_wuv"], f32).reshape(DEPTH, 256, 512))
    wk = np.asarray(inp["mla_wuk"], f32).transpose(0, 2, 3, 1).reshape(DEPTH, 4, 2, 64, 256)
    shared["wukT"] = np.ascontiguousarray(wk.transpose(0, 2, 3, 1, 4).reshape(DEPTH, 128, 4, 256))
    for l in range(DEPTH):
        shared[f"ckvc{l}"] = np.ascontiguousarray(inp["cache_mla_ckv"][l][:DBG.get('npool', 5120)])
        shared[f"kpec{l}"] = np.ascontiguousarray(inp["cache_mla_kpe"][l][:DBG.get('npool', 5120)])
    past_len = int(inp["page_table"].shape[1]) * int(inp["cache_mla_ckv"].shape[2])
    pos = np.concatenate([np.arange(SEQ, dtype=f32), (past_len + np.tile(np.arange(8, dtype=f32), 4)).astype(f32)])
    inv = np.exp(-np.log(f32(10000.0)) * np.arange(16, dtype=f32) / f32(16)).astype(f32)
    ang = (pos[None, :] * inv[:, None]).astype(f32)
    cos2 = np.concatenate([np.cos(ang), np.cos(ang)], axis=0).astype(f32)
    sin2 = np.concatenate([-np.sin(ang), np.sin(ang)], axis=0).astype(f32)
    rope = np.zeros((128, 2, SEQ + NS), f32)
    for r0 in (0, 64):
        rope[r0:r0 + 32, 0] = cos2
        rope[r0:r0 + 32, 1] = sin2
    shared["rope"] = rope
    for l in range(DEPTH):
        w = np.asarray(inp["w_in"][l], f32)
        wfull = np.concatenate([w, w[:, 2576:2592], w[:, 2560:2576], w[:, 2560:2592]], axis=1).reshape(KC, 128, WIN_COLS)
        shared[f"win{l}"] = np.ascontiguousarray(np.concatenate(
            [wfull[:, :, c0_:c0_ + n_].transpose(1, 0, 2).reshape(128, KC * n_) for c0_, n_ in WIN_GROUPS], axis=1))
        shared[f"wout{l}"] = np.ascontiguousarray(inp["w_out"][l], f32)
    for l in range(DEPTH):
        for f_, (kwi, kwo) in enumerate((("ffn1_wi", "ffn1_wo"), ("ffn2_wi", "ffn2_wo"))):
            wi = np.asarray(inp[kwi][l], f32).reshape(KC, 128, 2, NJ // GJ, GJ * 128)
            shared[f"wi{l}{f_}"] = np.ascontiguousarray(wi.transpose(3, 1, 0, 2, 4).reshape(NJ // GJ, 128, KC * 2 * GJ * 128))
            wo = np.asarray(inp[kwo][l], f32).reshape(NJ // GJ, GJ, 128, 2, 512)
            shared[f"wo{l}{f_}"] = np.ascontiguousarray(wo.transpose(3, 0, 2, 1, 4).reshape(2, NJ // GJ, 128, GJ * 512))
    in_maps = []
    for c in range(ncores):
        m = dict(shared)
        m["xpT"] = np.ascontiguousarray(np.asarray(inp["x_prompt"][c], f32).T)
        m["xsT"] = np.ascontiguousarray(np.asarray(inp["x_sample"][4 * c:4 * c + 4], f32).reshape(NS, D).T)
        m["hgst"] = np.ascontiguousarray(np.asarray(inp["state_hgrn"][:, 4 * c:4 * c + 4], f32))
        m["pt"] = np.ascontiguousarray((np.asarray(inp["page_table"][4 * c:4 * c + 4]) % DBG.get('npool', 1 << 30)).astype(np.int32).T)
        m["rwst"] = np.ascontiguousarray(np.asarray(inp["state_rwkv"][:, 4 * c:4 * c + 4], f32).transpose(0, 1, 2, 4, 3))
        sh = np.asarray(inp["state_rwkv_shift"][:, 4 * c:4 * c + 4], f32)
        m["rwsh"] = np.ascontiguousarray(sh.reshape(DEPTH, 4, 7, 128).transpose(0, 3, 2, 1))
        in_maps.append(m)
    res = run_bass_kernel_spmd(nc, in_maps, core_ids=list(range(ncores)))
    R = res.results
    y_p = np.stack([R[c]["yT_p"].T for c in range(ncores)]).astype(f32)
    y_s = np.concatenate([R[c]["yT_s"].T.reshape(4, 8, D) for c in range(ncores)]).astype(f32)
    hg_p = np.stack([R[c]["hg_p"] for c in range(ncores)], axis=1).astype(f32)
    hg_s = np.concatenate([R[c]["hg_s"] for c in range(ncores)], axis=1).astype(f32)
    z = lambda *sh: np.zeros(sh, f32)
    rw_p = np.stack([R[c]["rw_p"].transpose(0, 1, 3, 2) for c in range(ncores)], axis=1).astype(f32)
    rw_s = np.concatenate([R[c]["rw_s"].transpose(0, 1, 2, 4, 3) for c in range(ncores)], axis=1).astype(f32)
    sh_p = np.stack([R[c]["sh_p"].transpose(0, 2, 1).reshape(DEPTH, 896) for c in range(ncores)], axis=1).astype(f32)
    sh_s = np.concatenate([R[c]["sh_s"].transpose(0, 3, 2, 1).reshape(DEPTH, 4, 896) for c in range(ncores)], axis=1).astype(f32)
    ckv_p = np.stack([R[c]["ckv_p"].transpose(0, 2, 1) for c in range(ncores)], axis=1).astype(f32)
    ckv_s = np.concatenate([R[c]["ckv_s"].transpose(0, 2, 1).reshape(DEPTH, 4, 8, 256) for c in range(ncores)], axis=1).astype(f32)
    kpe_p = np.stack([R[c]["kpe_p"].transpose(0, 2, 1) for c in range(ncores)], axis=1).astype(f32)
    kpe_s = np.concatenate([R[c]["kpe_s"].transpose(0, 2, 1).reshape(DEPTH, 4, 8, 32) for c in range(ncores)], axis=1).astype(f32)
    return (y_p, y_s, hg_p, hg_s, rw_p, rw_s, sh_p, sh_s, ckv_p, ckv_s, kpe_p, kpe_s)
```
